# Optimizing a Trainium2 kernel written in Bass

```python
import math
import jax, jax.numpy as jnp
from jax import lax
import numpy as np

D_MODEL = 1024
BATCH = 32
SEQ = 2048
DEPTH = 2

CHUNK = 64
Q_BLOCK = 128
N_A_LAYERS = DEPTH // 2
N_B_LAYERS = DEPTH - N_A_LAYERS
SSM_GROUP = 16
N_GROUPS = D_MODEL // SSM_GROUP
SSM_STATE = 64
HEAD_DIM = 64
N_HEADS = D_MODEL // HEAD_DIM
D_FF = 2816
CONV_W = 3
EPS = 1e-6
DT_MIN = 1e-3
DT_MAX = 1e-1

kernel_name = "yoco_s5_stickbreak_hybrid"


def rms_norm(x, g):
    xf = x.astype(jnp.float32)
    y = xf * lax.rsqrt(jnp.mean(xf * xf, axis=-1, keepdims=True) + EPS)
    return (y * g.astype(jnp.float32)).astype(x.dtype)


def s5_mixer(xn, w_in, lam_re, lam_im, b_re, b_im, c_re, c_im, d_skip, log_dt, w_glu):
    f32 = jnp.float32
    bsz, seq, _ = xn.shape
    u = (xn @ w_in).astype(f32)
    lam = lax.complex(lam_re.astype(f32), lam_im.astype(f32))
    dt = jnp.exp(log_dt.astype(f32))[:, None]
    a_bar = jnp.exp(lam * dt)
    b_mat = lax.complex(b_re.astype(f32), b_im.astype(f32))
    b_bar = ((a_bar - 1.0) / lam)[..., None] * b_mat
    c_mat = lax.complex(c_re.astype(f32), c_im.astype(f32))
    n_chunks = seq // CHUNK
    u_c = u.reshape(bsz, n_chunks, CHUNK, N_GROUPS, SSM_GROUP).transpose(1, 2, 0, 3, 4)
    a_elems = jnp.broadcast_to(a_bar, (CHUNK, 1, N_GROUPS, SSM_STATE))

    def binop(left, right):
        a_l, b_l = left
        a_r, b_r = right
        return a_r * a_l, a_r * b_l + b_r

    def chunk_step(h0, u_t):
        bu = jnp.einsum('tbgh,gph->tbgp', u_t.astype(jnp.complex64), b_bar)
        a_cum, s = lax.associative_scan(binop, (a_elems, bu), axis=0)
        s = s + a_cum * h0[None]
        y = jnp.einsum('tbgp,ghp->tbgh', s, c_mat).real
        return s[-1], y

    h0 = jnp.zeros((bsz, N_GROUPS, SSM_STATE), jnp.complex64)
    _, y = lax.scan(chunk_step, h0, u_c)
    y = y.transpose(2, 0, 1, 3, 4).reshape(bsz, seq, D_MODEL)
    y = y + d_skip.astype(f32) * u
    g = jax.nn.gelu(y).astype(xn.dtype)
    val, gate = jnp.split(g @ w_glu, 2, axis=-1)
    return val * jax.nn.sigmoid(gate)


def shared_kv(h, kv_norm, w_kv, k_norm):
    bsz, seq, _ = h.shape
    k, v = jnp.split(rms_norm(h, kv_norm) @ w_kv, 2, axis=-1)
    k = rms_norm(k.reshape(bsz, seq, N_HEADS, HEAD_DIM), k_norm)
    v = v.reshape(bsz, seq, N_HEADS, HEAD_DIM)
    return k, v


def stick_breaking_attention(q, k, v):
    seq = q.shape[1]
    scale = HEAD_DIM ** -0.5
    outs = []
    for i in range(seq // Q_BLOCK):
        q0 = i * Q_BLOCK
        kv_len = q0 + Q_BLOCK
        z = jnp.einsum('bqhd,bkhd->bhqk', q[:, q0:kv_len], k[:, :kv_len]).astype(jnp.float32) * scale
        q_pos = q0 + jnp.arange(Q_BLOCK)[:, None]
        k_pos = jnp.arange(kv_len)[None, :]
        causal = k_pos < q_pos
        log_beta = jax.nn.log_sigmoid(z)
        log_1m = jnp.where(causal, log_beta - z, 0.0)
        later = lax.cumsum(log_1m, axis=3, reverse=True) - log_1m
        w = jnp.where(causal, jnp.exp(log_beta + later), 0.0)
        outs.append(jnp.einsum('bhqk,bkhd->bqhd', w.astype(v.dtype), v[:, :kv_len]))
    return jnp.concatenate(outs, axis=1)


def sb_mixer(xn, w_q, q_norm, k, v, w_o):
    bsz, seq, _ = xn.shape
    q = rms_norm((xn @ w_q).reshape(bsz, seq, N_HEADS, HEAD_DIM), q_norm)
    o = stick_breaking_attention(q, k, v)
    return o.reshape(bsz, seq, D_MODEL) @ w_o


def conv_ffn(xn, w_up, conv_w, conv_b, w_down):
    seq = xn.shape[1]
    val, gate = jnp.split(xn @ w_up, 2, axis=-1)
    gp = jnp.pad(gate, ((0, 0), (CONV_W - 1, 0), (0, 0)))
    gc = conv_b + conv_w[0] * gp[:, 0:seq]
    for j in range(1, CONV_W):
        gc = gc + conv_w[j] * gp[:, j:j + seq]
    return (jax.nn.silu(gc) * val) @ w_down


def setup_inputs(seed: int = 0) -> dict:
    key = jax.random.key(seed)
    ks = iter(jax.random.split(key, 32))

    def nrm(shape, scale):
        return scale * jax.random.normal(next(ks), shape, jnp.float32)

    def gain(shape):
        return 1.0 + nrm(shape, 0.02)

    d, f = D_MODEL, D_FF
    na, nb = N_A_LAYERS, N_B_LAYERS
    G, P, H = N_GROUPS, SSM_STATE, SSM_GROUP
    n_idx = jnp.arange(P, dtype=jnp.float32)
    x = nrm((BATCH, SEQ, d), 1.0)
    a_norm = gain((na, d))
    a_w_in = nrm((na, d, d), d ** -0.5)
    a_lam_re = -0.5 + nrm((na, G, P), 0.01)
    a_lam_im = math.pi * n_idx + nrm((na, G, P), 0.01)
    a_b_re = nrm((na, G, P, H), (2 * H) ** -0.5)
    a_b_im = nrm((na, G, P, H), (2 * H) ** -0.5)
    a_c_re = nrm((na, G, H, P), P ** -0.5)
    a_c_im = nrm((na, G, H, P), P ** -0.5)
    a_d = nrm((na, d), 1.0)
    a_log_dt = jax.random.uniform(next(ks), (na, G), jnp.float32, math.log(DT_MIN), math.log(DT_MAX))
    a_w_glu = nrm((na, d, 2 * d), d ** -0.5)
    kv_norm = gain((d,))
    w_kv = nrm((d, 2 * d), d ** -0.5)
    k_norm = gain((HEAD_DIM,))
    b_norm = gain((nb, d))
    b_w_q = nrm((nb, d, d), d ** -0.5)
    b_q_norm = gain((nb, HEAD_DIM))
    b_w_o = nrm((nb, d, d), d ** -0.5)
    ffn_norm = gain((DEPTH, d))
    ffn_w_up = nrm((DEPTH, d, 2 * f), d ** -0.5)
    ffn_conv_w = nrm((DEPTH, CONV_W, f), CONV_W ** -0.5)
    ffn_conv_b = nrm((DEPTH, f), 0.02)
    ffn_w_down = nrm((DEPTH, f, d), f ** -0.5)
    return {"x": x, "a_norm": a_norm, "a_w_in": a_w_in, "a_lam_re": a_lam_re, "a_lam_im": a_lam_im,
            "a_b_re": a_b_re, "a_b_im": a_b_im, "a_c_re": a_c_re, "a_c_im": a_c_im, "a_d": a_d,
            "a_log_dt": a_log_dt, "a_w_glu": a_w_glu, "kv_norm": kv_norm, "w_kv": w_kv, "k_norm": k_norm,
            "b_norm": b_norm, "b_w_q": b_w_q, "b_q_norm": b_q_norm, "b_w_o": b_w_o,
            "ffn_norm": ffn_norm, "ffn_w_up": ffn_w_up, "ffn_conv_w": ffn_conv_w,
            "ffn_conv_b": ffn_conv_b, "ffn_w_down": ffn_w_down}


def reference(x, a_norm, a_w_in, a_lam_re, a_lam_im, a_b_re, a_b_im, a_c_re, a_c_im, a_d,
              a_log_dt, a_w_glu, kv_norm, w_kv, k_norm, b_norm, b_w_q, b_q_norm, b_w_o,
              ffn_norm, ffn_w_up, ffn_conv_w, ffn_conv_b, ffn_w_down):
    h = x
    k = None
    v = None
    for layer in range(DEPTH):
        if layer < N_A_LAYERS:
            i = layer
            h = h + s5_mixer(rms_norm(h, a_norm[i]), a_w_in[i], a_lam_re[i], a_lam_im[i],
                             a_b_re[i], a_b_im[i], a_c_re[i], a_c_im[i], a_d[i], a_log_dt[i], a_w_glu[i])
        else:
            j = layer - N_A_LAYERS
            if j == 0:
                k, v = shared_kv(h, kv_norm, w_kv, k_norm)
            h = h + sb_mixer(rms_norm(h, b_norm[j]), b_w_q[j], b_q_norm[j], k, v, b_w_o[j])
        h = h + conv_ffn(rms_norm(h, ffn_norm[layer]), ffn_w_up[layer], ffn_conv_w[layer],
                         ffn_conv_b[layer], ffn_w_down[layer])
    return h
```

```python
import contextlib
import numpy as np
import concourse.bass as bass
import concourse.mybir as mybir
from concourse.bass_utils import run_bass_kernel_spmd

F32 = mybir.dt.float32
BF16 = mybir.dt.bfloat16
AF = mybir.ActivationFunctionType
ALU = mybir.AluOpType
AX = mybir.AxisListType

ENGS = ("pe", "act", "dve", "pool", "sp")
N_DMA_SEMS = 4

D = 1024
DFF = 2816
NFT = DFF // 128
EPS = 1e-6


class T:
    __slots__ = ("name", "writes", "reads")

    def __init__(self, name="t"):
        self.name = name
        self.writes = {}
        self.reads = {}


class Prog:
    _stage = 0

    def __init__(self, nc):
        self.nc = nc
        Prog._stage += 1
        self.sid = Prog._stage
        self.stack = contextlib.ExitStack()
        self.ops = {e: [] for e in ENGS}
        self.known = {e: {} for e in ENGS}
        self.ndma = {e: 0 for e in ENGS}
        self.cnt = {e: 0 for e in ENGS}
        self.milestones = {e: set() for e in ENGS}
        self._n = 0

    def sbuf(self, name, shape, dtype):
        return self.stack.enter_context(self.nc.sbuf_tensor(f"s{self.sid}_{name}", list(shape), dtype))

    def psum(self, name, shape, dtype=F32):
        return self.stack.enter_context(self.nc.psum_tensor(f"s{self.sid}_{name}", list(shape), dtype))

    def tile(self, name=None):
        return T(name or "t")

    def tiles(self, n):
        return [T() for _ in range(n)]

    def _collect(self, eng, reads, writes):
        need = {}
        for t in reads:
            for k, v in t.writes.items():
                if need.get(k, 0) < v:
                    need[k] = v
        for t in writes:
            for d in (t.writes, t.reads):
                for k, v in d.items():
                    if need.get(k, 0) < v:
                        need[k] = v
        waits = []
        kn = self.known[eng]
        for k, v in need.items():
            if k == ("e", eng) and eng == "pe":
                continue
            if kn.get(k, 0) >= v:
                continue
            kn[k] = v
            waits.append((k, v))
            if k[0] == "e":
                self.milestones[k[1]].add(v)
        return waits

    def op(self, eng, fn, reads=(), writes=()):
        waits = self._collect(eng, reads, writes)
        self.cnt[eng] += 1
        idx = self.cnt[eng]
        key = ("e", eng)
        self.ops[eng].append(dict(kind="op", fn=fn, waits=waits, idx=idx))
        for t in reads:
            if t.reads.get(key, 0) < idx:
                t.reads[key] = idx
        for t in writes:
            t.writes = {key: idx}
            t.reads = {}

    def dma(self, eng, fn, reads=(), writes=()):
        i = self.ndma[eng]
        self.ndma[eng] += 1
        slot = i % N_DMA_SEMS
        gen = i // N_DMA_SEMS
        key = ("d", eng, slot)
        waits = self._collect(eng, reads, writes)
        kn = self.known[eng]
        if gen > 0 and kn.get(key, 0) < 16 * gen:
            kn[key] = 16 * gen
            waits.append((key, 16 * gen))
        val = 16 * (gen + 1)
        self.ops[eng].append(dict(kind="dma", fn=fn, waits=waits, key=key))
        for t in reads:
            if t.reads.get(key, 0) < val:
                t.reads[key] = val
        for t in writes:
            t.writes = {key: val}
            t.reads = {}

    def wait_all(self, eng, tiles):
        waits = self._collect(eng, tiles, ())
        self.ops[eng].append(dict(kind="wait", waits=waits))

    def finish(self, out_tiles):
        self.wait_all("sp", out_tiles)
        nc = self.nc
        with nc.cleanup_on_exit():
            sems = {}
            for e in ENGS:
                if self.milestones[e]:
                    sems[("e", e)] = nc.alloc_semaphore(f"p{self.sid}_{e}")
                for s in range(min(N_DMA_SEMS, self.ndma[e])):
                    sems[("d", e, s)] = nc.alloc_semaphore(f"d{self.sid}_{e}_{s}")
            mmap = {e: {v: i + 1 for i, v in enumerate(sorted(self.milestones[e]))} for e in ENGS}

            def replay(e):
                def body(engine):
                    for o in self.ops[e]:
                        for k, v in o["waits"]:
                            if k[0] == "e":
                                v = mmap[k[1]][v]
                            engine.wait_ge(sems[k], v)
                        if o["kind"] == "op":
                            ins = o["fn"](engine)
                            if o["idx"] in mmap[e]:
                                ins.then_inc(sems[("e", e)], 1)
                        elif o["kind"] == "dma":
                            ins = o["fn"](engine)
                            ins.then_inc(sems[o["key"]], 16)
                return body

            with nc.Block() as block:
                block.tensor(replay("pe"))
                block.scalar(replay("act"))
                block.vector(replay("dve"))
                block.gpsimd(replay("pool"))
                block.sync(replay("sp"))
            nc.all_engine_barrier()
        self.stack.close()


def load_weight_bf16(P, dst, tdst, w_ap, kchunks, gain_sb=None, tgain=None, eng="pool"):
    n = w_ap.shape[-1]
    src = w_ap.rearrange("(c p) n -> p c n", p=128)
    step = max(1, 4096 // n)
    for c0 in range(0, kchunks, step):
        c1 = min(kchunks, c0 + step)
        P.dma("pool", lambda e, c0=c0, c1=c1: e.dma_start(out=dst[:, c0:c1, :], in_=src[:, c0:c1, :]), writes=[tdst])
    if gain_sb is not None:
        for c in range(kchunks):
            P.op(eng, lambda e, c=c: e.tensor_scalar(out=dst[:, c, :], in0=dst[:, c, :], scalar1=gain_sb[:, c:c + 1],
                                                      scalar2=None, op0=ALU.mult), reads=[tdst, tgain], writes=[tdst])


def load_gain(P, name, g_ap):
    g_sb = P.sbuf(name, [128, 8], F32)
    t = P.tile()
    P.dma("sp", lambda e: e.dma_start(out=g_sb[:], in_=g_ap.rearrange("(c p) -> p c", p=128), allow_slow_non_contiguous=True), writes=[t])
    return g_sb, t


class NormT:
    def __init__(self, P, ident, tident, keep_x=True):
        self.P = P
        self.ident, self.tident = ident, tident
        self.xt = [P.sbuf(f"nt_xt{i}", [128, 4, D], F32) for i in range(2)]
        self.txt = P.tiles(2)
        self.junk = P.sbuf("nt_junk", [128, D], BF16)
        self.tjunk = P.tile()
        self.ss = P.sbuf("nt_ss", [128, 8], F32)
        self.tss = P.tile()
        self.xn = [P.sbuf(f"nt_xn{i}", [128, D], BF16) for i in range(1)] * 2
        self.txn = P.tiles(1) * 2
        self.pst = [P.psum(f"nt_pst{i}", [128, D], BF16) for i in range(2)]
        self.tpst = P.tiles(2)
        self.xnT = [P.sbuf(f"nt_xnT{i}", [128, 8, 512], BF16) for i in range(1)] * 2
        self.txnT = P.tiles(1) * 2
        self.k = 0
        self.n = 0

    def run(self, src_rows):
        P = self.P
        b = self.n % 2
        self.n += 1
        xt, txt, xnT, txnT = self.xt[b], self.txt[b], self.xnT[b], self.txnT[b]
        P.dma("sp", lambda e: e.dma_start(out=xt[:], in_=src_rows.rearrange("(t p) d -> p t d", p=128)), writes=[txt])
        ss, tss = self.ss, self.tss
        for t in range(4):
            kk = self.k % 2
            self.k += 1
            xn, txn, pst, tpst = self.xn[kk], self.txn[kk], self.pst[kk], self.tpst[kk]
            col = (self.k % 4) * 2
            P.op("dve", lambda e, col=col: e.memset(ss[:, col:col + 1], 0.0), writes=[tss])
            P.op("act", lambda e, t=t, col=col: e.activation(out=self.junk[:], in_=xt[:, t, :], func=AF.Square,
                                                             accum_out=ss[:, col:col + 1]),
                 reads=[txt], writes=[self.tjunk, tss])
            P.op("dve", lambda e, col=col: e.tensor_scalar(out=ss[:, col + 1:col + 2], in0=ss[:, col:col + 1], scalar1=1.0 / D,
                                                           scalar2=EPS, op0=ALU.mult, op1=ALU.add), reads=[tss], writes=[tss])
            P.op("act", lambda e, col=col: e.activation(out=ss[:, col + 1:col + 2], in_=ss[:, col + 1:col + 2], func=AF.Sqrt),
                 reads=[tss], writes=[tss])
            P.op("dve", lambda e, col=col: e.reciprocal(out=ss[:, col + 1:col + 2], in_=ss[:, col + 1:col + 2]), reads=[tss], writes=[tss])
            P.op("act", lambda e, t=t, col=col, xn=xn: e.activation(out=xn[:], in_=xt[:, t, :], func=AF.Copy,
                                                                     scale=ss[:, col + 1:col + 2]),
                 reads=[txt, tss], writes=[txn])
            for c in range(8):
                P.op("pe", lambda e, c=c, xn=xn, pst=pst: e.transpose(out=pst[:, c * 128:(c + 1) * 128],
                                                                       in_=xn[:, c * 128:(c + 1) * 128], identity=self.ident[:]),
                     reads=[txn, self.tident], writes=[tpst])
            P.op("dve", lambda e, t=t, pst=pst, xnT=xnT: e.tensor_copy(out=xnT[:, :, t * 128:(t + 1) * 128],
                                                                       in_=pst[:].rearrange("p (c k) -> p c k", k=128)),
                 reads=[tpst], writes=[txnT])
        return xt, txt, xnT, txnT


def load_ident(P, ident_ap):
    ident = P.sbuf("ident", [128, 128], BF16)
    t = P.tile()
    P.dma("pool", lambda e: e.dma_start(out=ident[:], in_=ident_ap), writes=[t])
    return ident, t


def stage_ffn(nc, h_in, h_out, g_ap, wup_ap, cw_ap, cb_ap, wdn_ap, ident_ap, n_seq, seq_len):
    P = Prog(nc)
    ident, tident = load_ident(P, ident_ap)
    g_sb, tg = load_gain(P, "g", g_ap)
    wup = P.sbuf("wup", [128, 8, 2 * DFF], BF16)
    twup = P.tile()
    load_weight_bf16(P, wup, twup, wup_ap, 8, g_sb, tg)
    wdn = P.sbuf("wdn", [128, NFT, D], BF16)
    twdn = P.tile()
    load_weight_bf16(P, wdn, twdn, wdn_ap, NFT)
    cw = P.sbuf("cw", [128, NFT, 3], F32)
    cb = P.sbuf("cb", [128, NFT], F32)
    tcw = P.tile()
    for j in range(3):
        P.dma("sp", lambda e, j=j: e.dma_start(out=cw[:, :, j], in_=cw_ap[j].rearrange("(f p) -> p f", p=128), allow_slow_non_contiguous=True), writes=[tcw])
    P.dma("sp", lambda e: e.dma_start(out=cb[:], in_=cb_ap.rearrange("(f p) -> p f", p=128), allow_slow_non_contiguous=True), writes=[tcw])
    nt = NormT(P, ident, tident)
    halo = P.sbuf("halo", [128, NFT, 2], F32)
    thalo = P.tile()
    gbuf = [P.sbuf(f"gbuf{i}", [128, 514], F32) for i in range(2)]
    tgbuf = P.tiles(2)
    gc = [P.sbuf(f"gc{i}", [128, 512], F32) for i in range(2)]
    tgc = P.tiles(2)
    hid = P.sbuf("hid", [128, NFT, 512], BF16)
    thid = P.tile()
    pv = [P.psum(f"pv{i}", [128, 512]) for i in range(2)]
    tpv = P.tiles(2)
    pg = [P.psum(f"pg{i}", [128, 512]) for i in range(2)]
    tpg = P.tiles(2)
    po = [P.psum(f"po{i}", [128, 512]) for i in range(2)]
    tpo = P.tiles(2)
    outs = []
    nsup = seq_len // 512
    k = 0
    ko = 0
    for s in range(n_seq):
        P.op("pool", lambda e: e.memset(halo[:], 0.0), writes=[thalo])
        for u in range(nsup):
            r0 = s * seq_len + u * 512
            xt, txt, xnT, txnT = nt.run(h_in[r0:r0 + 512, :])
            for ft in range(NFT):
                b = k % 2
                k += 1
                for c in range(8):
                    P.op("pe", lambda e, c=c, b=b, ft=ft: e.matmul(out=pv[b][:], lhsT=wup[:, c, ft * 128:(ft + 1) * 128], rhs=xnT[:, c, :],
                                                                   start=(c == 0), stop=(c == 7)), reads=[twup, txnT], writes=[tpv[b]])
                for c in range(8):
                    P.op("pe", lambda e, c=c, b=b, ft=ft: e.matmul(out=pg[b][:], lhsT=wup[:, c, DFF + ft * 128:DFF + (ft + 1) * 128], rhs=xnT[:, c, :],
                                                                   start=(c == 0), stop=(c == 7)), reads=[twup, txnT], writes=[tpg[b]])
                gb, tgb, g2, tg2 = gbuf[b], tgbuf[b], gc[b], tgc[b]
                P.op("pool", lambda e, gb=gb, ft=ft: e.tensor_copy(out=gb[:, 0:2], in_=halo[:, ft, :]), reads=[thalo], writes=[tgb])
                P.op("act", lambda e, gb=gb, b=b: e.activation(out=gb[:, 2:514], in_=pg[b][:], func=AF.Copy), reads=[tpg[b]], writes=[tgb])
                P.op("pool", lambda e, gb=gb, ft=ft: e.tensor_copy(out=halo[:, ft, :], in_=gb[:, 512:514]), reads=[tgb], writes=[thalo])
                P.op("act", lambda e, gb=gb, g2=g2, ft=ft: e.activation(out=g2[:], in_=gb[:, 2:514], func=AF.Identity,
                                                                        scale=cw[:, ft, 2:3], bias=cb[:, ft:ft + 1]),
                     reads=[tgb, tcw], writes=[tg2])
                P.op("dve", lambda e, gb=gb, g2=g2, ft=ft: e.scalar_tensor_tensor(out=g2[:], in0=gb[:, 1:513], scalar=cw[:, ft, 1:2], in1=g2[:],
                                                                                  op0=ALU.mult, op1=ALU.add), reads=[tgb, tcw, tg2], writes=[tg2])
                P.op("dve", lambda e, gb=gb, g2=g2, ft=ft: e.scalar_tensor_tensor(out=g2[:], in0=gb[:, 0:512], scalar=cw[:, ft, 0:1], in1=g2[:],
                                                                                   op0=ALU.mult, op1=ALU.add), reads=[tgb, tcw, tg2], writes=[tg2])
                P.op("act", lambda e, g2=g2: e.activation(out=g2[:], in_=g2[:], func=AF.Silu), reads=[tg2], writes=[tg2])
                P.op("dve", lambda e, g2=g2, b=b, ft=ft: e.tensor_tensor(out=hid[:, ft, :], in0=g2[:], in1=pv[b][:], op=ALU.mult),
                     reads=[tg2, tpv[b]], writes=[thid])
            for t in range(4):
                for nb in range(2):
                    b = ko % 2
                    ko += 1
                    for ft in range(NFT):
                        P.op("pe", lambda e, ft=ft, t=t, nb=nb, b=b: e.matmul(out=po[b][:], lhsT=hid[:, ft, t * 128:(t + 1) * 128],
                                                                              rhs=wdn[:, ft, nb * 512:(nb + 1) * 512],
                                                                              start=(ft == 0), stop=(ft == NFT - 1)),
                             reads=[thid, twdn], writes=[tpo[b]])
                    P.op("dve", lambda e, t=t, nb=nb, b=b, xt=xt: e.tensor_tensor(out=xt[:, t, nb * 512:(nb + 1) * 512], in0=po[b][:],
                                                                                   in1=xt[:, t, nb * 512:(nb + 1) * 512], op=ALU.add),
                         reads=[tpo[b], txt], writes=[txt])
            to = P.tile()
            P.dma("act", lambda e, xt=xt, r0=r0: e.dma_start(out=h_out[r0:r0 + 512, :].rearrange("(t p) d -> p t d", p=128), in_=xt[:]),
                  reads=[txt], writes=[to])
            outs.append(to)
    P.finish(outs)


def stage_kvq(nc, h_in, kT_rev, qT, v_rev, kvn_ap, wkv_ap, kn_ap, bn_ap, wq_ap, qn_ap, ident_ap, bones_ap, n_seq, S):
    P = Prog(nc)
    ident, tident = load_ident(P, ident_ap)
    bones = P.sbuf("bones", [128, 128], BF16)
    tbones = P.tile()
    P.dma("pool", lambda e: e.dma_start(out=bones[:], in_=bones_ap), writes=[tbones])
    gkv, tgkv = load_gain(P, "gkv", kvn_ap)
    gb, tgb = load_gain(P, "gb", bn_ap)
    wkv = P.sbuf("wkv", [128, 8, 2048], BF16)
    twkv = P.tile()
    load_weight_bf16(P, wkv, twkv, wkv_ap, 8, gkv, tgkv)
    wq = P.sbuf("wq", [128, 8, 1024], BF16)
    twq = P.tile()
    load_weight_bf16(P, wq, twq, wq_ap, 8, gb, tgb)
    hg = P.sbuf("hg", [128, 4], F32)
    thg = P.tile()
    for hf in range(2):
        P.dma("sp", lambda e, hf=hf: e.dma_start(out=hg[hf * 64:(hf + 1) * 64, 0:1], in_=kn_ap.rearrange("(p o) -> p o", o=1)), writes=[thg])
        P.dma("sp", lambda e, hf=hf: e.dma_start(out=hg[hf * 64:(hf + 1) * 64, 1:2], in_=qn_ap.rearrange("(p o) -> p o", o=1)), writes=[thg])
    P.op("dve", lambda e: e.tensor_scalar(out=hg[:, 1:2], in0=hg[:, 1:2], scalar1=0.125, scalar2=None, op0=ALU.mult), reads=[thg], writes=[thg])
    P.op("dve", lambda e: e.memset(hg[:, 2:3], EPS), reads=[thg], writes=[thg])
    nt = NormT(P, ident, tident)
    pk = [P.psum(f"pk{i}", [128, 512]) for i in range(2)]
    tpk = P.tiles(2)
    pss = [P.psum(f"pss{i}", [128, 512]) for i in range(2)]
    tpss = P.tiles(2)
    sq = [P.sbuf(f"sq{i}", [128, 512], BF16) for i in range(2)]
    tsq = P.tiles(2)
    rs = [P.sbuf(f"rs{i}", [128, 512], F32) for i in range(2)]
    trs = P.tiles(2)
    kbuf = [P.sbuf(f"kbuf{i}", [128, 8, 512], BF16) for i in range(2)]
    tkbuf = P.tiles(2)
    qbuf = [P.sbuf(f"qbuf{i}", [128, 8, 512], BF16) for i in range(2)]
    tqbuf = P.tiles(2)
    vbuf = [P.sbuf(f"vbuf{i}", [128, 4, 1024], BF16) for i in range(2)]
    tvbuf = P.tiles(2)
    xrev = P.sbuf("xrev", [128, 8, 512], BF16)
    txrev = P.tile()
    outs = []
    nsup = S // 512
    k = 0
    for s in range(n_seq):
        for u in range(nsup):
            r0 = s * S + u * 512
            rr = s * S + S - 512 * (u + 1)
            ob = (s * nsup + u) % 2
            xt, txt, xnT, txnT = nt.run(h_in[r0:r0 + 512, :])
            for which in range(2):
                for ft in range(8):
                    b = k % 2
                    k += 1
                    for c in range(8):
                        if which == 0:
                            P.op("pe", lambda e, c=c, b=b, ft=ft: e.matmul(out=pk[b][:], lhsT=wkv[:, c, ft * 128:(ft + 1) * 128], rhs=xnT[:, c, :],
                                                                           start=(c == 0), stop=(c == 7)), reads=[twkv, txnT], writes=[tpk[b]])
                        else:
                            P.op("pe", lambda e, c=c, b=b, ft=ft: e.matmul(out=pk[b][:], lhsT=wq[:, c, ft * 128:(ft + 1) * 128], rhs=xnT[:, c, :],
                                                                           start=(c == 0), stop=(c == 7)), reads=[twq, txnT], writes=[tpk[b]])
                    P.op("act", lambda e, b=b: e.activation(out=sq[b][:], in_=pk[b][:], func=AF.Square), reads=[tpk[b]], writes=[tsq[b]])
                    P.op("pe", lambda e, b=b: e.matmul(out=pss[b][:], lhsT=bones[:], rhs=sq[b][:], start=True, stop=True),
                         reads=[tbones, tsq[b]], writes=[tpss[b]])
                    P.op("act", lambda e, b=b: e.activation(out=rs[b][:], in_=pss[b][:], func=AF.Sqrt, scale=1.0 / 64, bias=hg[:, 2:3]),
                         reads=[tpss[b], thg], writes=[trs[b]])
                    P.op("dve", lambda e, b=b: e.reciprocal(out=rs[b][:], in_=rs[b][:]), reads=[trs[b]], writes=[trs[b]])
                    if which == 0:
                        P.op("dve", lambda e, b=b, ft=ft, ob=ob: e.scalar_tensor_tensor(out=kbuf[ob][:, ft, ::-1], in0=pk[b][:], scalar=hg[:, 0:1], in1=rs[b][:],
                                                                                        op0=ALU.mult, op1=ALU.mult),
                             reads=[tpk[b], thg, trs[b]], writes=[tkbuf[ob]])
                    else:
                        P.op("dve", lambda e, b=b, ft=ft, ob=ob: e.scalar_tensor_tensor(out=qbuf[ob][:, ft, :], in0=pk[b][:], scalar=hg[:, 1:2], in1=rs[b][:],
                                                                                        op0=ALU.mult, op1=ALU.mult),
                             reads=[tpk[b], thg, trs[b]], writes=[tqbuf[ob]])
            P.op("pool", lambda e, xnT=xnT: e.tensor_copy(out=xrev[:, :, ::-1], in_=xnT[:]), reads=[txnT], writes=[txrev])
            for t in range(4):
                for nb in range(2):
                    b = k % 2
                    k += 1
                    for c in range(8):
                        P.op("pe", lambda e, c=c, b=b, t=t, nb=nb: e.matmul(out=pk[b][:], lhsT=xrev[:, c, t * 128:(t + 1) * 128],
                                                                            rhs=wkv[:, c, 1024 + nb * 512:1024 + (nb + 1) * 512],
                                                                            start=(c == 0), stop=(c == 7)), reads=[twkv, txrev], writes=[tpk[b]])
                    P.op("act", lambda e, b=b, t=t, nb=nb, ob=ob: e.activation(out=vbuf[ob][:, t, nb * 512:(nb + 1) * 512], in_=pk[b][:], func=AF.Copy),
                         reads=[tpk[b]], writes=[tvbuf[ob]])
            t1, t2, t3 = P.tile(), P.tile(), P.tile()
            P.dma("act", lambda e, ob=ob, rr=rr: e.dma_start(out=kT_rev[:, rr:rr + 512].rearrange("(f p) n -> p f n", p=128), in_=kbuf[ob][:]),
                  reads=[tkbuf[ob]], writes=[t1])
            P.dma("act", lambda e, ob=ob, r0=r0: e.dma_start(out=qT[:, r0:r0 + 512].rearrange("(f p) n -> p f n", p=128), in_=qbuf[ob][:]),
                  reads=[tqbuf[ob]], writes=[t2])
            P.dma("act", lambda e, ob=ob, rr=rr: e.dma_start(out=v_rev[rr:rr + 512, :].rearrange("(t p) d -> p t d", p=128), in_=vbuf[ob][:]),
                  reads=[tvbuf[ob]], writes=[t3])
            outs += [t1, t2, t3]
    P.finish(outs)


def stage_att(nc, h_in, h_out, qT, kT_rev, v_rev, wo_ap, ident_ap, maskb_ap, n_seq, S):
    P = Prog(nc)
    ident, tident = load_ident(P, ident_ap)
    maskb = P.sbuf("maskb", [128, 128], BF16)
    tmask = P.tile()
    P.dma("pool", lambda e: e.dma_start(out=maskb[:], in_=maskb_ap), writes=[tmask])
    wo = P.sbuf("wo", [128, 8, 1024], BF16)
    two = P.tile()
    load_weight_bf16(P, wo, two, wo_ap, 8)
    zeros = P.sbuf("zeros", [128, 512], F32)
    tz = P.tile()
    P.op("pool", lambda e: e.memset(zeros[:], 0.0), writes=[tz])
    NB = S // 128
    qsb = P.sbuf("qsb", [128, 8, S], BF16)
    ksb = P.sbuf("ksb", [128, 8, S], BF16)
    vsb = P.sbuf("vsb", [128, NB, 1024], BF16)
    oT = P.sbuf("oT", [128, 8, S], BF16)
    tq, tk, tv, toT = P.tile(), P.tile(), P.tile(), P.tile()
    pz = [P.psum(f"pz{i}", [128, 512]) for i in range(2)]
    tpz = P.tiles(2)
    pwT = [P.psum(f"pwT{i}", [128, 512], BF16) for i in range(2)]
    tpwT = P.tiles(2)
    po = [P.psum(f"po{i}", [128, 128]) for i in range(2)]
    tpo = P.tiles(2)
    pw = [P.psum(f"pw{i}", [128, 512]) for i in range(2)]
    tpw = P.tiles(2)
    ssb = [P.sbuf(f"ssb{i}", [128, 512], F32) for i in range(2)]
    tssb = P.tiles(2)
    bsb = [P.sbuf(f"bsb{i}", [128, 512], F32) for i in range(2)]
    tbsb = P.tiles(2)
    pbuf = [P.sbuf(f"pbuf{i}", [128, 513], F32) for i in range(2)]
    tpbuf = P.tiles(2)
    wsb = [P.sbuf(f"wsb{i}", [128, 512], BF16) for i in range(2)]
    twsb = P.tiles(2)
    wTs = [P.sbuf(f"wTs{i}", [128, 512], BF16) for i in range(2)]
    twTs = P.tiles(2)
    xt = [P.sbuf(f"xt{i}", [128, D], F32) for i in range(2)]
    txt = P.tiles(2)
    outs = []
    k = 0
    ko = 0
    kx = 0
    for s in range(n_seq):
        c0 = s * S
        for f in range(0, 8, 2):
            P.dma("sp", lambda e, f=f, c0=c0: e.dma_start(out=qsb[:, f:f + 2, :], in_=qT[f * 128:(f + 2) * 128, c0:c0 + S].rearrange("(f p) n -> p f n", p=128)), writes=[tq])
            P.dma("sp", lambda e, f=f, c0=c0: e.dma_start(out=ksb[:, f:f + 2, :], in_=kT_rev[f * 128:(f + 2) * 128, c0:c0 + S].rearrange("(f p) n -> p f n", p=128)), writes=[tk])
        for b4 in range(0, NB, 4):
            P.dma("sp", lambda e, b4=b4, c0=c0: e.dma_start(out=vsb[:, b4:b4 + 4, :], in_=v_rev[c0 + b4 * 128:c0 + (b4 + 4) * 128, :].rearrange("(t p) d -> p t d", p=128)), writes=[tv])
        for ft in range(8):
            for i in range(NB):
                q0 = i * 128
                kp0 = S - 128 - q0
                klen = q0 + 128
                ob = ko % 2
                ko += 1
                for hp in range(2):
                    h = ft * 2 + hp
                    ps = slice(hp * 64, (hp + 1) * 64)
                    ntile = (klen + 511) // 512
                    for n in range(ntile):
                        col0 = kp0 + 512 * n
                        wdt = min(512, S - col0)
                        b = k % 2
                        k += 1
                        P.op("pe", lambda e, b=b, ft=ft, ps=ps, q0=q0, col0=col0, wdt=wdt, n=n: e.matmul(
                            out=pz[b][:, 0:wdt], lhsT=qsb[ps, ft, q0:q0 + 128], rhs=ksb[ps, ft, col0:col0 + wdt], start=True, stop=(n != 0)),
                            reads=[tq, tk], writes=[tpz[b]])
                        if n == 0:
                            P.op("pe", lambda e, b=b: e.matmul(out=pz[b][:, 0:128], lhsT=ident[:], rhs=maskb[:], start=False, stop=True),
                                 reads=[tident, tmask], writes=[tpz[b]])
                        P.op("act", lambda e, b=b, wdt=wdt: e.activation(out=ssb[b][:, 0:wdt], in_=pz[b][:, 0:wdt], func=AF.Sigmoid, scale=-1.0),
                             reads=[tpz[b]], writes=[tssb[b]])
                        if n == 0:
                            P.op("pool", lambda e, b=b: e.memset(pbuf[b][:, 0:1], 1.0), writes=[tpbuf[b]])
                        else:
                            P.op("pool", lambda e, b=b: e.tensor_copy(out=pbuf[b][:, 0:1], in_=pbuf[1 - b][:, 512:513]), reads=[tpbuf[1 - b]], writes=[tpbuf[b]])
                        P.op("dve", lambda e, b=b, wdt=wdt: e.tensor_tensor_scan(out=pbuf[b][:, 1:1 + wdt], data0=ssb[b][:, 0:wdt], data1=zeros[:, 0:wdt],
                                                                                 initial=pbuf[b][:, 0:1], op0=ALU.mult, op1=ALU.add),
                             reads=[tssb[b], tz, tpbuf[b]], writes=[tpbuf[b]])
                        P.op("pool", lambda e, b=b, wdt=wdt: e.tensor_scalar(out=bsb[b][:, 0:wdt], in0=ssb[b][:, 0:wdt], scalar1=-1.0, scalar2=1.0,
                                                                             op0=ALU.mult, op1=ALU.add), reads=[tssb[b]], writes=[tbsb[b]])
                        P.op("pool", lambda e, b=b, wdt=wdt: e.tensor_tensor(out=wsb[b][:, 0:wdt], in0=bsb[b][:, 0:wdt], in1=pbuf[b][:, 0:wdt], op=ALU.mult),
                             reads=[tbsb[b], tpbuf[b]], writes=[twsb[b]])
                        nblk = wdt // 128
                        for jb in range(nblk):
                            P.op("pe", lambda e, b=b, jb=jb: e.transpose(out=pwT[b][:, jb * 128:(jb + 1) * 128], in_=wsb[b][:, jb * 128:(jb + 1) * 128], identity=ident[:]),
                                 reads=[twsb[b], tident], writes=[tpwT[b]])
                        P.op("act", lambda e, b=b, wdt=wdt: e.activation(out=wTs[b][:, 0:wdt], in_=pwT[b][:, 0:wdt], func=AF.Copy),
                             reads=[tpwT[b]], writes=[twTs[b]])
                        for jb in range(nblk):
                            blk = col0 // 128 + jb
                            first = (n == 0 and jb == 0)
                            last = (n == ntile - 1 and jb == nblk - 1)
                            P.op("pe", lambda e, b=b, jb=jb, blk=blk, h=h, ps=ps, ob=ob, first=first, last=last: e.matmul(
                                out=po[ob][ps, :], lhsT=vsb[:, blk, h * 64:(h + 1) * 64], rhs=wTs[b][:, jb * 128:(jb + 1) * 128], start=first, stop=last),
                                reads=[tv, twTs[b]], writes=[tpo[ob]])
                        if wdt < 512 and n != ntile - 1:
                            raise AssertionError("partial tile must be last")
                P.op("act", lambda e, ob=ob, ft=ft, q0=q0: e.activation(out=oT[:, ft, q0:q0 + 128], in_=po[ob][:], func=AF.Copy),
                     reads=[tpo[ob]], writes=[toT])
        for t in range(NB):
            r0 = c0 + t * 128
            xb = kx % 2
            kx += 1
            P.dma("sp", lambda e, xb=xb, r0=r0: e.dma_start(out=xt[xb][:], in_=h_in[r0:r0 + 128, :]), writes=[txt[xb]])
            for nb in range(2):
                b = k % 2
                k += 1
                for ft in range(8):
                    P.op("pe", lambda e, b=b, ft=ft, t=t, nb=nb: e.matmul(out=pw[b][:], lhsT=oT[:, ft, t * 128:(t + 1) * 128], rhs=wo[:, ft, nb * 512:(nb + 1) * 512],
                                                                          start=(ft == 0), stop=(ft == 7)), reads=[toT, two], writes=[tpw[b]])
                P.op("dve", lambda e, b=b, xb=xb, nb=nb: e.tensor_tensor(out=xt[xb][:, nb * 512:(nb + 1) * 512], in0=pw[b][:], in1=xt[xb][:, nb * 512:(nb + 1) * 512], op=ALU.add),
                     reads=[tpw[b], txt[xb]], writes=[txt[xb]])
            to = P.tile()
            P.dma("act", lambda e, xb=xb, r0=r0: e.dma_start(out=h_out[r0:r0 + 128, :], in_=xt[xb][:]), reads=[txt[xb]], writes=[to])
            outs.append(to)
    P.finish(outs)


def stage_a1(nc, h_in, uT, g_ap, win_ap, ident_ap, n_tok):
    P = Prog(nc)
    ident, tident = load_ident(P, ident_ap)
    g_sb, tg = load_gain(P, "g", g_ap)
    win = P.sbuf("win", [128, 8, 1024], BF16)
    twin = P.tile()
    load_weight_bf16(P, win, twin, win_ap, 8, g_sb, tg)
    nt = NormT(P, ident, tident)
    pu = [P.psum(f"pu{i}", [128, 512]) for i in range(2)]
    tpu = P.tiles(2)
    ubuf = [P.sbuf(f"ubuf{i}", [128, 8, 512], BF16) for i in range(2)]
    tubuf = P.tiles(2)
    outs = []
    k = 0
    for u in range(n_tok // 512):
        r0 = u * 512
        ob = u % 2
        xt, txt, xnT, txnT = nt.run(h_in[r0:r0 + 512, :])
        for m in range(8):
            b = k % 2
            k += 1
            for c in range(8):
                P.op("pe", lambda e, c=c, b=b, m=m: e.matmul(out=pu[b][:], lhsT=win[:, c, m * 128:(m + 1) * 128], rhs=xnT[:, c, :],
                                                             start=(c == 0), stop=(c == 7)), reads=[twin, txnT], writes=[tpu[b]])
            if m % 2 == 0:
                P.op("act", lambda e, b=b, m=m, ob=ob: e.activation(out=ubuf[ob][:, m, :], in_=pu[b][:], func=AF.Copy), reads=[tpu[b]], writes=[tubuf[ob]])
            else:
                P.op("dve", lambda e, b=b, m=m, ob=ob: e.tensor_copy(out=ubuf[ob][:, m, :], in_=pu[b][:]), reads=[tpu[b]], writes=[tubuf[ob]])
        to = P.tile()
        P.dma("act", lambda e, ob=ob, r0=r0: e.dma_start(out=uT[:, r0:r0 + 512].rearrange("(m p) n -> p m n", p=128), in_=ubuf[ob][:]),
              reads=[tubuf[ob]], writes=[to])
        outs.append(to)
    P.finish(outs)


TWO_PI = 6.283185307179586


def stage_a2(nc, uT, gT, lre_ap, lim_ap, bre_ap, bim_ap, cre_ap, cim_ap, d_ap, ldt_ap, iota_ap, n_seq, S, dbg_pairs=None, dbg=None, dbg_stop=9):
    P = Prog(nc)
    I32 = mybir.dt.int32
    NP = 32
    lr = P.sbuf("lr", [128, NP], F32)
    li = P.sbuf("li", [128, NP], F32)
    dt = P.sbuf("dt", [128, NP], F32)
    tprm = P.tile()
    for e_ in range(2):
        ps = slice(e_ * 64, (e_ + 1) * 64)
        P.dma("sp", lambda e, e_=e_, ps=ps: e.dma_start(out=lr[ps, :], in_=lre_ap.rearrange("(k e) n -> e n k", e=2)[e_], allow_slow_non_contiguous=True), writes=[tprm])
        P.dma("sp", lambda e, e_=e_, ps=ps: e.dma_start(out=li[ps, :], in_=lim_ap.rearrange("(k e) n -> e n k", e=2)[e_], allow_slow_non_contiguous=True), writes=[tprm])
        P.dma("sp", lambda e, e_=e_, ps=ps: e.dma_start(out=dt[ps, :], in_=ldt_ap.rearrange("(k e) -> e k", e=2)[e_].partition_broadcast(64), allow_slow_non_contiguous=True), writes=[tprm])
    cst = P.sbuf("cst", [128, 4], F32)
    tcst = P.tile()
    P.op("dve", lambda e: e.memset(cst[:, 0:1], TWO_PI / 4), writes=[tcst])
    P.op("dve", lambda e: e.memset(cst[:, 1:2], 0.0), reads=[tcst], writes=[tcst])
    sc = {}
    for nm in ["f", "fhi", "flo", "r", "t0", "t1", "t2", "t3", "sn", "cs", "ar", "ai", "qre", "qim", "nqim", "den"]:
        sc[nm] = P.sbuf("p_" + nm, [128, NP], F32)
    fhb = P.sbuf("p_fhb", [128, NP], BF16)
    tiq = P.sbuf("p_ti", [128, NP], I32)
    tsc = P.tile()

    def V(fn, eng="dve", extra=()):
        P.op(eng, fn, reads=[tprm, tsc, tcst] + list(extra), writes=[tsc])

    V(lambda e: e.activation(out=sc["t0"][:], in_=dt[:], func=AF.Exp), "act")
    V(lambda e: e.activation(out=sc["t1"][:], in_=sc["t0"][:], func=AF.Ln), "act")
    V(lambda e: e.tensor_tensor(out=sc["t1"][:], in0=dt[:], in1=sc["t1"][:], op=ALU.subtract))
    V(lambda e: e.tensor_scalar(out=sc["t1"][:], in0=sc["t1"][:], scalar1=1.0, scalar2=None, op0=ALU.add))
    V(lambda e: e.tensor_tensor(out=dt[:], in0=sc["t0"][:], in1=sc["t1"][:], op=ALU.mult))
    V(lambda e: e.tensor_tensor(out=sc["t0"][:], in0=li[:], in1=dt[:], op=ALU.mult))
    V(lambda e: e.tensor_scalar(out=sc["f"][:], in0=sc["t0"][:], scalar1=1.0 / TWO_PI, scalar2=None, op0=ALU.mult))
    V(lambda e: e.tensor_copy(out=fhb[:], in_=sc["f"][:]))
    V(lambda e: e.tensor_copy(out=sc["fhi"][:], in_=fhb[:]))
    V(lambda e: e.tensor_tensor(out=sc["flo"][:], in0=sc["f"][:], in1=sc["fhi"][:], op=ALU.subtract))
    V(lambda e: e.tensor_tensor(out=sc["t1"][:], in0=lr[:], in1=dt[:], op=ALU.mult))
    V(lambda e: e.activation(out=sc["t2"][:], in_=sc["t1"][:], func=AF.Exp), "act")
    V(lambda e: e.activation(out=sc["t3"][:], in_=sc["t2"][:], func=AF.Ln), "act")
    V(lambda e: e.tensor_tensor(out=sc["t3"][:], in0=sc["t1"][:], in1=sc["t3"][:], op=ALU.subtract))
    V(lambda e: e.tensor_scalar(out=sc["t3"][:], in0=sc["t3"][:], scalar1=1.0, scalar2=None, op0=ALU.add))
    V(lambda e: e.tensor_tensor(out=sc["r"][:], in0=sc["t2"][:], in1=sc["t3"][:], op=ALU.mult))
    V(lambda e: e.tensor_copy(out=tiq[:], in_=sc["f"][:]))
    V(lambda e: e.tensor_copy(out=sc["t3"][:], in_=tiq[:]))
    V(lambda e: e.tensor_tensor(out=sc["t2"][:], in0=sc["f"][:], in1=sc["t3"][:], op=ALU.subtract))
    V(lambda e: e.activation(out=sc["sn"][:], in_=sc["t2"][:], func=AF.Sin, scale=TWO_PI), "act")
    V(lambda e: e.tensor_scalar(out=sc["t3"][:], in0=sc["t2"][:], scalar1=-1.0, scalar2=None, op0=ALU.mult))
    V(lambda e: e.tensor_tensor(out=sc["t3"][:], in0=sc["t3"][:], in1=sc["t2"][:], op=ALU.min))
    V(lambda e: e.activation(out=sc["cs"][:], in_=sc["t3"][:], func=AF.Sin, scale=TWO_PI, bias=cst[:, 0:1]), "act")
    V(lambda e: e.tensor_tensor(out=sc["ar"][:], in0=sc["r"][:], in1=sc["cs"][:], op=ALU.mult))
    V(lambda e: e.tensor_tensor(out=sc["ai"][:], in0=sc["r"][:], in1=sc["sn"][:], op=ALU.mult))
    V(lambda e: e.tensor_scalar(out=sc["ar"][:], in0=sc["ar"][:], scalar1=-1.0, scalar2=None, op0=ALU.add))
    V(lambda e: e.tensor_tensor(out=sc["t0"][:], in0=lr[:], in1=lr[:], op=ALU.mult))
    V(lambda e: e.tensor_tensor(out=sc["t1"][:], in0=li[:], in1=li[:], op=ALU.mult))
    V(lambda e: e.tensor_tensor(out=sc["den"][:], in0=sc["t0"][:], in1=sc["t1"][:], op=ALU.add))
    V(lambda e: e.reciprocal(out=sc["den"][:], in_=sc["den"][:]))
    V(lambda e: e.tensor_tensor(out=sc["t0"][:], in0=sc["ar"][:], in1=lr[:], op=ALU.mult))
    V(lambda e: e.tensor_tensor(out=sc["t1"][:], in0=sc["ai"][:], in1=li[:], op=ALU.mult))
    V(lambda e: e.tensor_tensor(out=sc["t0"][:], in0=sc["t0"][:], in1=sc["t1"][:], op=ALU.add))
    V(lambda e: e.tensor_tensor(out=sc["qre"][:], in0=sc["t0"][:], in1=sc["den"][:], op=ALU.mult))
    V(lambda e: e.tensor_tensor(out=sc["t0"][:], in0=sc["ai"][:], in1=lr[:], op=ALU.mult))
    V(lambda e: e.tensor_tensor(out=sc["t1"][:], in0=sc["ar"][:], in1=li[:], op=ALU.mult))
    V(lambda e: e.tensor_tensor(out=sc["t0"][:], in0=sc["t0"][:], in1=sc["t1"][:], op=ALU.subtract))
    V(lambda e: e.tensor_tensor(out=sc["qim"][:], in0=sc["t0"][:], in1=sc["den"][:], op=ALU.mult))
    V(lambda e: e.tensor_scalar(out=sc["nqim"][:], in0=sc["qim"][:], scalar1=-1.0, scalar2=None, op0=ALU.mult))
    craw = P.sbuf("craw", [128, NP, 2, 16], F32)
    tcraw = P.tile()
    for e_ in range(2):
        ps = slice(e_ * 64, (e_ + 1) * 64)
        for k1 in range(NP):
            for ri, ap_ in enumerate((cre_ap, cim_ap)):
                P.dma("sp" if ri == 0 else "act", lambda e, e_=e_, ps=ps, k1=k1, ri=ri, ap_=ap_: e.dma_start(
                    out=craw[ps, k1, ri, :], in_=ap_[2 * k1 + e_].rearrange("h n -> n h"), allow_slow_non_contiguous=True),
                    writes=[tcraw])
    CT = P.sbuf("CT", [128, NP, 2, 128], BF16)
    tCT = P.tile()
    ctmp = P.sbuf("ctmp", [128, NP, 16], F32)
    ctmp2 = P.sbuf("ctmp2", [128, NP, 16], F32)
    tctmp = P.tile()
    P.op("pool", lambda e: e.memset(CT[:], 0.0), writes=[tCT])

    def bq(nm):
        return sc[nm][:].unsqueeze(2).to_broadcast([128, NP, 16])

    rd = [tcraw, tsc]
    P.op("dve", lambda e: e.tensor_tensor(out=ctmp[:], in0=craw[:, :, 0, :], in1=bq("qre"), op=ALU.mult), reads=rd, writes=[tctmp])
    P.op("dve", lambda e: e.tensor_tensor(out=ctmp2[:], in0=craw[:, :, 1, :], in1=bq("qim"), op=ALU.mult), reads=rd, writes=[tctmp])
    for e_ in range(2):
        ps = slice(e_ * 64, (e_ + 1) * 64)
        P.op("dve", lambda e, e_=e_, ps=ps: e.tensor_tensor(out=CT[ps, :, 0, e_ * 16:(e_ + 1) * 16], in0=ctmp[ps], in1=ctmp2[ps], op=ALU.subtract),
             reads=[tctmp], writes=[tCT])
    P.op("dve", lambda e: e.tensor_tensor(out=ctmp[:], in0=craw[:, :, 0, :], in1=bq("nqim"), op=ALU.mult), reads=rd + [tCT], writes=[tctmp])
    P.op("dve", lambda e: e.tensor_tensor(out=ctmp2[:], in0=craw[:, :, 1, :], in1=bq("qre"), op=ALU.mult), reads=rd, writes=[tctmp])
    for e_ in range(2):
        ps = slice(e_ * 64, (e_ + 1) * 64)
        P.op("dve", lambda e, e_=e_, ps=ps: e.tensor_tensor(out=CT[ps, :, 1, e_ * 16:(e_ + 1) * 16], in0=ctmp[ps], in1=ctmp2[ps], op=ALU.subtract),
             reads=[tctmp], writes=[tCT])
    BT = P.sbuf("BT", [32, NP, 2, 128], BF16)
    tBT = P.tile()
    P.op("pool", lambda e: e.memset(BT[:], 0.0), writes=[tBT])
    for e_ in range(2):
        for ri, ap_ in enumerate((bre_ap, bim_ap)):
            for k1 in range(NP):
                P.dma("pool", lambda e, e_=e_, ri=ri, ap_=ap_, k1=k1: e.dma_start(
                    out=BT[e_ * 16:(e_ + 1) * 16, k1, ri, e_ * 64:(e_ + 1) * 64], in_=ap_[2 * k1 + e_].rearrange("n h -> h n"),
                    allow_slow_non_contiguous=True), writes=[tBT])
    dwin = P.sbuf("dwin", [32, NP], F32)
    tdw = P.tile()
    P.dma("sp", lambda e: e.dma_start(out=dwin[:], in_=d_ap.rearrange("(k r) -> r k", r=32), allow_slow_non_contiguous=True), writes=[tdw])
    iota = P.sbuf("iota", [128, S], F32)
    tio = P.tile()
    P.dma("sp", lambda e: e.dma_start(out=iota[:], in_=iota_ap[:, 0:S]), writes=[tio])
    HB = 1024 if S >= 1024 else S
    nhalf = S // HB
    sn = P.sbuf("sn", [128, S], F32)
    cs = P.sbuf("cs", [128, S], F32)
    ttab = P.tile()
    wk1 = P.sbuf("wk1", [128, S], F32)
    wk2 = P.sbuf("wk2", [128, S], F32)
    tiw = P.sbuf("tiw", [128, S], I32)
    twk = P.tile()
    uwin = [P.sbuf(f"uwin{i}", [32, S], BF16) for i in range(2)]
    tuw = P.tiles(2)
    pbr = P.psum("pbr", [128, HB])
    pbi = P.psum("pbi", [128, HB])
    tpb = P.tile()
    py = [P.psum(f"py{i}", [128, HB]) for i in range(2)]
    tpy = P.tiles(2)
    A = [P.sbuf(f"A{i}", [128, HB], F32) for i in range(4)]
    tA = P.tiles(4)
    W = [P.sbuf(f"W{i}", [128, HB], F32) for i in range(2)]
    tW = P.tiles(2)
    Z = [P.sbuf(f"Z{i}", [128, HB], F32) for i in range(2)]
    tZ = P.tiles(2)
    Sb = [P.sbuf(f"Sb{i}", [128, HB], BF16) for i in range(2)]
    tSb = P.tiles(2)
    rtab = P.sbuf("rtab", [128, HB], F32)
    trt = P.tile()
    ones = P.sbuf("ones", [128, HB], F32)
    tones = P.tile()
    P.op("pool", lambda e: e.memset(ones[:], 1.0), writes=[tones])
    carry = P.sbuf("carry", [128, 2], F32)
    tcar = P.tile()
    ytmp = P.sbuf("ytmp", [32, HB], F32)
    tyt = P.tile()
    gout = [P.sbuf(f"gout{i}", [32, S], BF16) for i in range(2)]
    tgo = P.tiles(2)
    outs = []
    nu = 0
    ny = 0
    for k in range(NP if dbg_pairs is None else dbg_pairs):
        fh, fl, rk = sc["fhi"][:, k:k + 1], sc["flo"][:, k:k + 1], sc["r"][:, k:k + 1]
        P.op("dve", lambda e, fh=fh: e.tensor_scalar(out=wk1[:], in0=iota[:], scalar1=fh, scalar2=None, op0=ALU.mult), reads=[tio, tsc], writes=[twk])
        P.op("dve", lambda e: e.tensor_copy(out=tiw[:], in_=wk1[:]), reads=[twk], writes=[twk])
        P.op("pool", lambda e: e.tensor_copy(out=wk2[:], in_=tiw[:]), reads=[twk], writes=[twk])
        P.op("pool", lambda e: e.tensor_tensor(out=wk1[:], in0=wk1[:], in1=wk2[:], op=ALU.subtract), reads=[twk], writes=[twk])
        P.op("dve", lambda e, fl=fl: e.scalar_tensor_tensor(out=wk2[:], in0=iota[:], scalar=fl, in1=wk1[:], op0=ALU.mult, op1=ALU.add), reads=[tio, tsc, twk], writes=[twk])
        P.op("dve", lambda e: e.tensor_copy(out=tiw[:], in_=wk2[:]), reads=[twk], writes=[twk])
        P.op("pool", lambda e: e.tensor_copy(out=wk1[:], in_=tiw[:]), reads=[twk], writes=[twk])
        P.op("pool", lambda e: e.tensor_tensor(out=wk2[:], in0=wk2[:], in1=wk1[:], op=ALU.subtract), reads=[twk], writes=[twk])
        P.op("act", lambda e: e.activation(out=sn[:], in_=wk2[:], func=AF.Sin, scale=TWO_PI), reads=[twk], writes=[ttab])
        P.op("pool", lambda e: e.tensor_scalar(out=wk1[:], in0=wk2[:], scalar1=-1.0, scalar2=None, op0=ALU.mult), reads=[twk], writes=[twk])
        P.op("dve", lambda e: e.tensor_tensor(out=wk1[:], in0=wk1[:], in1=wk2[:], op=ALU.min), reads=[twk], writes=[twk])
        P.op("act", lambda e: e.activation(out=cs[:], in_=wk1[:], func=AF.Sin, scale=TWO_PI, bias=cst[:, 0:1]), reads=[twk, tcst], writes=[ttab])
        P.op("pool", lambda e, rk=rk: e.tensor_scalar(out=rtab[:], in0=ones[:], scalar1=rk, scalar2=None, op0=ALU.mult), reads=[tones, tsc], writes=[trt])
        m, q = k // 4, k % 4
        row0 = m * 128 + q * 32
        for s in range(n_seq if dbg_stop > 0 else 0):
            c0 = s * S
            ub = nu % 2
            nu += 1
            P.dma("sp", lambda e, ub=ub, row0=row0, c0=c0: e.dma_start(out=uwin[ub][:], in_=uT[row0:row0 + 32, c0:c0 + S]), writes=[tuw[ub]])
            P.op("pool", lambda e: e.memset(carry[:], 0.0), writes=[tcar])
            for hb in range(nhalf):
                t0 = hb * HB
                tsl = slice(t0, t0 + HB)
                if dbg_stop < 1.5:
                    continue
                for blk in range(HB // 512):
                    bs = slice(blk * 512, (blk + 1) * 512)
                    us = slice(t0 + blk * 512, t0 + (blk + 1) * 512)
                    P.op("pe", lambda e, bs=bs, us=us, k=k, ub=ub: e.matmul(out=pbr[:, bs], lhsT=BT[:, k, 0, :], rhs=uwin[ub][:, us], start=True, stop=True),
                         reads=[tBT, tuw[ub]], writes=[tpb])
                    P.op("pe", lambda e, bs=bs, us=us, k=k, ub=ub: e.matmul(out=pbi[:, bs], lhsT=BT[:, k, 1, :], rhs=uwin[ub][:, us], start=True, stop=True),
                         reads=[tBT, tuw[ub]], writes=[tpb])
                if dbg_stop < 2:
                    continue
                for blk in range(HB // 512):
                    bs = slice(blk * 512, (blk + 1) * 512)
                    ts2 = slice(t0 + blk * 512, t0 + (blk + 1) * 512)
                    P.op("dve", lambda e, bs=bs, ts2=ts2: e.tensor_tensor(out=A[0][:, bs], in0=cs[:, ts2], in1=pbr[:, bs], op=ALU.mult), reads=[ttab, tpb], writes=[tA[0]])
                    P.op("dve", lambda e, bs=bs, ts2=ts2: e.tensor_tensor(out=A[1][:, bs], in0=sn[:, ts2], in1=pbi[:, bs], op=ALU.mult), reads=[ttab, tpb], writes=[tA[1]])
                    P.op("dve", lambda e, bs=bs, ts2=ts2: e.tensor_tensor(out=A[2][:, bs], in0=cs[:, ts2], in1=pbi[:, bs], op=ALU.mult), reads=[ttab, tpb], writes=[tA[2]])
                    P.op("dve", lambda e, bs=bs, ts2=ts2: e.tensor_tensor(out=A[3][:, bs], in0=sn[:, ts2], in1=pbr[:, bs], op=ALU.mult), reads=[ttab, tpb], writes=[tA[3]])
                P.op("pool", lambda e: e.tensor_tensor(out=W[0][:], in0=A[0][:], in1=A[1][:], op=ALU.add), reads=[tA[0], tA[1]], writes=[tW[0]])
                P.op("pool", lambda e: e.tensor_tensor(out=W[1][:], in0=A[2][:], in1=A[3][:], op=ALU.subtract), reads=[tA[2], tA[3]], writes=[tW[1]])
                if dbg_stop < 3:
                    continue
                for j in range(2):
                    P.op("dve", lambda e, j=j, rk=rk: e.tensor_tensor_scan(out=Z[j][:], data0=rtab[:], data1=W[j][:], initial=carry[:, j:j + 1],
                                                                           op0=ALU.mult, op1=ALU.add), reads=[tW[j], trt, tcar], writes=[tZ[j]])
                for j in range(2):
                    P.op("pool", lambda e, j=j: e.tensor_copy(out=carry[:, j:j + 1], in_=Z[j][:, HB - 1:HB]), reads=[tZ[j]], writes=[tcar])
                if dbg_stop < 4:
                    continue
                P.op("pool", lambda e, tsl=tsl: e.tensor_tensor(out=A[0][:], in0=cs[:, tsl], in1=Z[0][:], op=ALU.mult), reads=[ttab, tZ[0]], writes=[tA[0]])
                P.op("pool", lambda e, tsl=tsl: e.tensor_tensor(out=A[1][:], in0=sn[:, tsl], in1=Z[1][:], op=ALU.mult), reads=[ttab, tZ[1]], writes=[tA[1]])
                P.op("pool", lambda e, tsl=tsl: e.tensor_tensor(out=A[2][:], in0=sn[:, tsl], in1=Z[0][:], op=ALU.mult), reads=[ttab, tZ[0]], writes=[tA[2]])
                P.op("pool", lambda e, tsl=tsl: e.tensor_tensor(out=A[3][:], in0=cs[:, tsl], in1=Z[1][:], op=ALU.mult), reads=[ttab, tZ[1]], writes=[tA[3]])
                P.op("dve", lambda e: e.tensor_tensor(out=Sb[0][:], in0=A[0][:], in1=A[1][:], op=ALU.subtract), reads=[tA[0], tA[1]], writes=[tSb[0]])
                P.op("pool", lambda e: e.tensor_tensor(out=Sb[1][:], in0=A[2][:], in1=A[3][:], op=ALU.add), reads=[tA[2], tA[3]], writes=[tSb[1]])
                if dbg_stop < 5:
                    continue
                yb = ny % 2
                ny += 1
                for blk in range(HB // 512):
                    bs = slice(blk * 512, (blk + 1) * 512)
                    for ri in range(2):
                        P.op("pe", lambda e, bs=bs, ri=ri, k=k, yb=yb: e.matmul(out=py[yb][:, bs], lhsT=CT[:, k, ri, :], rhs=Sb[ri][:, bs],
                                                                                start=(ri == 0), stop=(ri == 1)),
                             reads=[tCT, tSb[ri]], writes=[tpy[yb]])
                for blk in range(HB // 512):
                    bs = slice(blk * 512, (blk + 1) * 512)
                    ts2 = slice(t0 + blk * 512, t0 + (blk + 1) * 512)
                    P.op("dve", lambda e, yb=yb, ub=ub, bs=bs, ts2=ts2, k=k: e.scalar_tensor_tensor(out=ytmp[:, bs], in0=uwin[ub][:, ts2], scalar=dwin[:, k:k + 1], in1=py[yb][0:32, bs],
                                                                                                    op0=ALU.mult, op1=ALU.add), reads=[tuw[ub], tdw, tpy[yb]], writes=[tyt])
                P.op("act", lambda e, ub=ub, tsl=tsl: e.activation(out=gout[ub][:, tsl], in_=ytmp[:], func=AF.Gelu), reads=[tyt], writes=[tgo[ub]])
            if dbg_stop < 6:
                continue
            to = P.tile()
            P.dma("act", lambda e, ub=ub, row0=row0, c0=c0: e.dma_start(out=gT[row0:row0 + 32, c0:c0 + S], in_=gout[ub][:]), reads=[tgo[ub]], writes=[to])
            outs.append(to)
    if dbg is not None:
        for nm, ap_ in dbg.items():
            src = {"CT": CT, "BT": BT, "sn": sn, "cs": cs, "r": sc["r"], "qre": sc["qre"], "qim": sc["qim"], "fhi": sc["fhi"], "flo": sc["flo"]}[nm]
            tl = {"CT": tCT, "BT": tBT, "sn": ttab, "cs": ttab}.get(nm, tsc)
            to = P.tile()
            P.dma("sp", lambda e, ap_=ap_, src=src: e.dma_start(out=ap_, in_=src[:]), reads=[tl], writes=[to])
            outs.append(to)
    P.finish(outs)


def stage_a3(nc, h_in, h_out, gT, wglu_ap, n_tok):
    P = Prog(nc)
    wg = P.sbuf("wg", [128, 8, 2048], BF16)
    twg = P.tile()
    load_weight_bf16(P, wg, twg, wglu_ap, 8)
    gb = [P.sbuf(f"gb{i}", [128, 8, 512], BF16) for i in range(2)]
    tgb = P.tiles(2)
    xt = [P.sbuf(f"xt{i}", [128, 4, D], F32) for i in range(2)]
    txt = P.tiles(2)
    pv = [P.psum(f"pv{i}", [128, 512]) for i in range(2)]
    tpv = P.tiles(2)
    pg = [P.psum(f"pg{i}", [128, 512]) for i in range(2)]
    tpg = P.tiles(2)
    sg = [P.sbuf(f"sg{i}", [128, 512], F32) for i in range(2)]
    tsg = P.tiles(2)
    outs = []
    k = 0
    for u in range(n_tok // 512):
        r0 = u * 512
        ob = u % 2
        P.dma("sp", lambda e, ob=ob, r0=r0: e.dma_start(out=gb[ob][:], in_=gT[:, r0:r0 + 512].rearrange("(c p) n -> p c n", p=128)), writes=[tgb[ob]])
        P.dma("sp", lambda e, ob=ob, r0=r0: e.dma_start(out=xt[ob][:], in_=h_in[r0:r0 + 512, :].rearrange("(t p) d -> p t d", p=128)), writes=[txt[ob]])
        for t in range(4):
            for nb in range(2):
                b = k % 2
                k += 1
                for c in range(8):
                    P.op("pe", lambda e, c=c, b=b, t=t, nb=nb, ob=ob: e.matmul(out=pv[b][:], lhsT=gb[ob][:, c, t * 128:(t + 1) * 128], rhs=wg[:, c, nb * 512:(nb + 1) * 512],
                                                                               start=(c == 0), stop=(c == 7)), reads=[tgb[ob], twg], writes=[tpv[b]])
                for c in range(8):
                    P.op("pe", lambda e, c=c, b=b, t=t, nb=nb, ob=ob: e.matmul(out=pg[b][:], lhsT=gb[ob][:, c, t * 128:(t + 1) * 128], rhs=wg[:, c, 1024 + nb * 512:1024 + (nb + 1) * 512],
                                                                               start=(c == 0), stop=(c == 7)), reads=[tgb[ob], twg], writes=[tpg[b]])
                P.op("act", lambda e, b=b: e.activation(out=sg[b][:], in_=pg[b][:], func=AF.Sigmoid), reads=[tpg[b]], writes=[tsg[b]])
                P.op("dve", lambda e, b=b: e.tensor_tensor(out=sg[b][:], in0=sg[b][:], in1=pv[b][:], op=ALU.mult), reads=[tsg[b], tpv[b]], writes=[tsg[b]])
                P.op("pool", lambda e, b=b, t=t, nb=nb, ob=ob: e.tensor_tensor(out=xt[ob][:, t, nb * 512:(nb + 1) * 512], in0=xt[ob][:, t, nb * 512:(nb + 1) * 512], in1=sg[b][:], op=ALU.add),
                     reads=[tsg[b], txt[ob]], writes=[txt[ob]])
        to = P.tile()
        P.dma("act", lambda e, ob=ob, r0=r0: e.dma_start(out=h_out[r0:r0 + 512, :].rearrange("(t p) d -> p t d", p=128), in_=xt[ob][:]), reads=[txt[ob]], writes=[to])
        outs.append(to)
    P.finish(outs)


N_CORES = 8
SEQ = 2048
N_SEQ = 4
NT = N_SEQ * SEQ

_PARAMS = [
    ("a_norm", [1, 1024]), ("a_w_in", [1, 1024, 1024]), ("a_lam_re", [1, 64, 64]), ("a_lam_im", [1, 64, 64]),
    ("a_b_re", [1, 64, 64, 16]), ("a_b_im", [1, 64, 64, 16]), ("a_c_re", [1, 64, 16, 64]), ("a_c_im", [1, 64, 16, 64]),
    ("a_d", [1, 1024]), ("a_log_dt", [1, 64]), ("a_w_glu", [1, 1024, 2048]), ("kv_norm", [1024]), ("w_kv", [1024, 2048]),
    ("k_norm", [64]), ("b_norm", [1, 1024]), ("b_w_q", [1, 1024, 1024]), ("b_q_norm", [1, 64]), ("b_w_o", [1, 1024, 1024]),
    ("ffn_norm", [2, 1024]), ("ffn_w_up", [2, 1024, 5632]), ("ffn_conv_w", [2, 3, 2816]), ("ffn_conv_b", [2, 2816]),
    ("ffn_w_down", [2, 2816, 1024]),
]


def build_program(N_SEQ=N_SEQ, SEQ=SEQ, debug=False):
    NT = N_SEQ * SEQ
    nc = bass.Bass("TRN2", target_bir_lowering=False)
    x = nc.dram_tensor("x", [NT, D], F32, kind="ExternalInput").ap()
    prm = {n: nc.dram_tensor(n, s, F32, kind="ExternalInput").ap() for n, s in _PARAMS}
    ident = nc.dram_tensor("c_ident", [128, 128], F32, kind="ExternalInput").ap()
    bones = nc.dram_tensor("c_bones", [128, 128], F32, kind="ExternalInput").ap()
    maskb = nc.dram_tensor("c_maskb", [128, 128], F32, kind="ExternalInput").ap()
    iota = nc.dram_tensor("c_iota", [128, 2048], F32, kind="ExternalInput").ap()
    out = nc.dram_tensor("out", [NT, D], F32, kind="ExternalOutput").ap()
    kd = "ExternalOutput" if debug else "Internal"
    h1 = nc.dram_tensor("h1", [NT, D], F32, kind=kd).ap()
    h2 = nc.dram_tensor("h2", [NT, D], F32, kind=kd).ap()
    h3 = nc.dram_tensor("h3", [NT, D], F32, kind=kd).ap()
    uT = nc.dram_tensor("uT", [D, NT], BF16).ap()
    gT = nc.dram_tensor("gT", [D, NT], BF16).ap()
    kT = nc.dram_tensor("kT", [D, NT], BF16).ap()
    qT = nc.dram_tensor("qT", [D, NT], BF16).ap()
    vr = nc.dram_tensor("vr", [NT, D], BF16).ap()
    stage_a1(nc, x, uT, prm["a_norm"][0], prm["a_w_in"][0], ident, NT)
    stage_a2(nc, uT, gT, prm["a_lam_re"][0], prm["a_lam_im"][0], prm["a_b_re"][0], prm["a_b_im"][0], prm["a_c_re"][0], prm["a_c_im"][0],
             prm["a_d"][0], prm["a_log_dt"][0], iota, N_SEQ, SEQ)
    stage_a3(nc, x, h1, gT, prm["a_w_glu"][0], NT)
    stage_ffn(nc, h1, h2, prm["ffn_norm"][0], prm["ffn_w_up"][0], prm["ffn_conv_w"][0], prm["ffn_conv_b"][0],
              prm["ffn_w_down"][0], ident, N_SEQ, SEQ)
    stage_kvq(nc, h2, kT, qT, vr, prm["kv_norm"], prm["w_kv"], prm["k_norm"], prm["b_norm"][0], prm["b_w_q"][0], prm["b_q_norm"][0],
              ident, bones, N_SEQ, SEQ)
    stage_att(nc, h2, h3, qT, kT, vr, prm["b_w_o"][0], ident, maskb, N_SEQ, SEQ)
    stage_ffn(nc, h3, out, prm["ffn_norm"][1], prm["ffn_w_up"][1], prm["ffn_conv_w"][1], prm["ffn_conv_b"][1],
              prm["ffn_w_down"][1], ident, N_SEQ, SEQ)
    return nc


def kernel(**inputs):
    x = np.ascontiguousarray(np.asarray(inputs["x"], dtype=np.float32))
    nc = build_program()
    p = np.arange(128)
    consts = {
        "c_ident": np.eye(128, dtype=np.float32),
        "c_bones": (p[:, None] // 64 == p[None, :] // 64).astype(np.float32),
        "c_maskb": np.where(p[None, :] + p[:, None] >= 128, 0.0, -30000.0).astype(np.float32),
        "c_iota": np.ascontiguousarray(np.tile(np.arange(2048, dtype=np.float32), (128, 1))),
    }
    params = {n: np.ascontiguousarray(np.asarray(inputs[n], dtype=np.float32)).reshape(s) for n, s in _PARAMS}
    in_maps = []
    for c in range(N_CORES):
        m = {"x": x[c * N_SEQ:(c + 1) * N_SEQ].reshape(NT, D)}
        m.update(params)
        m.update(consts)
        in_maps.append(m)
    res = run_bass_kernel_spmd(nc, in_maps, core_ids=list(range(N_CORES)))
    outs = [np.asarray(r["out"], dtype=np.float32).reshape(N_SEQ, SEQ, D) for r in res.results]
    return np.concatenate(outs, axis=0)
```

```python
import contextlib
import numpy as np
import concourse.bass as bass
import concourse.mybir as mybir
from concourse.bass_utils import run_bass_kernel_spmd

F32 = mybir.dt.float32
BF16 = mybir.dt.bfloat16
AF = mybir.ActivationFunctionType
ALU = mybir.AluOpType
AX = mybir.AxisListType

ENGS = ("pe", "act", "dve", "pool", "sp")
N_DMA_SEMS = 4

D = 1024
DFF = 2816
NFT = DFF // 128
EPS = 1e-6


class T:
    __slots__ = ("name", "writes", "reads")

    def __init__(self, name="t"):
        self.name = name
        self.writes = {}
        self.reads = {}


class Prog:
    _stage = 0

    def __init__(self, nc):
        self.nc = nc
        Prog._stage += 1
        self.sid = Prog._stage
        self.stack = contextlib.ExitStack()
        self.ops = {e: [] for e in ENGS}
        self.known = {e: {} for e in ENGS}
        self.ndma = {e: 0 for e in ENGS}
        self.cnt = {e: 0 for e in ENGS}
        self.milestones = {e: set() for e in ENGS}
        self._n = 0

    def sbuf(self, name, shape, dtype):
        return self.stack.enter_context(self.nc.sbuf_tensor(f"s{self.sid}_{name}", list(shape), dtype))

    def psum(self, name, shape, dtype=F32):
        return self.stack.enter_context(self.nc.psum_tensor(f"s{self.sid}_{name}", list(shape), dtype))

    def tile(self, name=None):
        return T(name or "t")

    def tiles(self, n):
        return [T() for _ in range(n)]

    def _collect(self, eng, reads, writes):
        need = {}
        for t in reads:
            for k, v in t.writes.items():
                if need.get(k, 0) < v:
                    need[k] = v
        for t in writes:
            for d in (t.writes, t.reads):
                for k, v in d.items():
                    if need.get(k, 0) < v:
                        need[k] = v
        waits = []
        kn = self.known[eng]
        for k, v in need.items():
            if k == ("e", eng) and eng == "pe":
                continue
            if kn.get(k, 0) >= v:
                continue
            kn[k] = v
            waits.append((k, v))
            if k[0] == "e":
                self.milestones[k[1]].add(v)
        return waits

    def op(self, eng, fn, reads=(), writes=()):
        waits = self._collect(eng, reads, writes)
        self.cnt[eng] += 1
        idx = self.cnt[eng]
        key = ("e", eng)
        self.ops[eng].append(dict(kind="op", fn=fn, waits=waits, idx=idx))
        for t in reads:
            if t.reads.get(key, 0) < idx:
                t.reads[key] = idx
        for t in writes:
            t.writes = {key: idx}
            t.reads = {}

    def dma(self, eng, fn, reads=(), writes=()):
        i = self.ndma[eng]
        self.ndma[eng] += 1
        slot = i % N_DMA_SEMS
        gen = i // N_DMA_SEMS
        key = ("d", eng, slot)
        waits = self._collect(eng, reads, writes)
        kn = self.known[eng]
        if gen > 0 and kn.get(key, 0) < 16 * gen:
            kn[key] = 16 * gen
            waits.append((key, 16 * gen))
        val = 16 * (gen + 1)
        self.ops[eng].append(dict(kind="dma", fn=fn, waits=waits, key=key))
        for t in reads:
            if t.reads.get(key, 0) < val:
                t.reads[key] = val
        for t in writes:
            t.writes = {key: val}
            t.reads = {}

    def wait_all(self, eng, tiles):
        waits = self._collect(eng, tiles, ())
        self.ops[eng].append(dict(kind="wait", waits=waits))

    def finish(self, out_tiles):
        self.wait_all("sp", out_tiles)
        nc = self.nc
        with nc.cleanup_on_exit():
            sems = {}
            for e in ENGS:
                if self.milestones[e]:
                    sems[("e", e)] = nc.alloc_semaphore(f"p{self.sid}_{e}")
                for s in range(min(N_DMA_SEMS, self.ndma[e])):
                    sems[("d", e, s)] = nc.alloc_semaphore(f"d{self.sid}_{e}_{s}")
            mmap = {e: {v: i + 1 for i, v in enumerate(sorted(self.milestones[e]))} for e in ENGS}

            def replay(e):
                def body(engine):
                    for o in self.ops[e]:
                        for k, v in o["waits"]:
                            if k[0] == "e":
                                v = mmap[k[1]][v]
                            engine.wait_ge(sems[k], v)
                        if o["kind"] == "op":
                            ins = o["fn"](engine)
                            if o["idx"] in mmap[e]:
                                ins.then_inc(sems[("e", e)], 1)
                        elif o["kind"] == "dma":
                            ins = o["fn"](engine)
                            ins.then_inc(sems[o["key"]], 16)
                return body

            with nc.Block() as block:
                block.tensor(replay("pe"))
                block.scalar(replay("act"))
                block.vector(replay("dve"))
                block.gpsimd(replay("pool"))
                block.sync(replay("sp"))
            nc.all_engine_barrier()
        self.stack.close()


def run_pipeline(tasks, offsets):
    nph = len(offsets)
    for step in range(len(tasks) + max(offsets) + 1):
        for p in reversed(range(nph)):
            t = step - offsets[p]
            if 0 <= t < len(tasks) and tasks[t][p] is not None:
                tasks[t][p]()


def load_weight_bf16(P, dst, tdst, w_ap, kchunks, gain_sb=None, tgain=None, eng="pool"):
    n = w_ap.shape[-1]
    src = w_ap.rearrange("(c p) n -> p c n", p=128)
    step = max(1, 4096 // n)
    for c0 in range(0, kchunks, step):
        c1 = min(kchunks, c0 + step)
        P.dma("pool", lambda e, c0=c0, c1=c1: e.dma_start(out=dst[:, c0:c1, :], in_=src[:, c0:c1, :]), writes=[tdst])
    if gain_sb is not None:
        for c in range(kchunks):
            P.op(eng, lambda e, c=c: e.tensor_scalar(out=dst[:, c, :], in0=dst[:, c, :], scalar1=gain_sb[:, c:c + 1],
                                                      scalar2=None, op0=ALU.mult), reads=[tdst, tgain], writes=[tdst])


def load_gain(P, name, g_ap):
    g_sb = P.sbuf(name, [128, 8], F32)
    t = P.tile()
    P.dma("sp", lambda e: e.dma_start(out=g_sb[:], in_=g_ap.rearrange("(c p) -> p c", p=128), allow_slow_non_contiguous=True), writes=[t])
    return g_sb, t


class NormT:
    def __init__(self, P, ident, tident, keep_x=True):
        self.P = P
        self.ident, self.tident = ident, tident
        self.xt = [P.sbuf(f"nt_xt{i}", [128, 4, D], F32) for i in range(2)]
        self.txt = P.tiles(2)
        self.junk = P.sbuf("nt_junk", [128, D], BF16)
        self.tjunk = P.tile()
        self.ss = P.sbuf("nt_ss", [128, 8], F32)
        self.tss = P.tile()
        self.xn = [P.sbuf(f"nt_xn{i}", [128, D], BF16) for i in range(1)] * 2
        self.txn = P.tiles(1) * 2
        self.pst = [P.psum(f"nt_pst{i}", [128, D], BF16) for i in range(2)]
        self.tpst = P.tiles(2)
        self.xnT = [P.sbuf(f"nt_xnT{i}", [128, 8, 512], BF16) for i in range(1)] * 2
        self.txnT = P.tiles(1) * 2
        self.k = 0
        self.n = 0

    def run(self, src_rows):
        P = self.P
        b = self.n % 2
        self.n += 1
        xt, txt, xnT, txnT = self.xt[b], self.txt[b], self.xnT[b], self.txnT[b]
        P.dma("sp", lambda e: e.dma_start(out=xt[:], in_=src_rows.rearrange("(t p) d -> p t d", p=128)), writes=[txt])
        ss, tss = self.ss, self.tss
        for t in range(4):
            kk = self.k % 2
            self.k += 1
            xn, txn, pst, tpst = self.xn[kk], self.txn[kk], self.pst[kk], self.tpst[kk]
            col = (self.k % 4) * 2
            P.op("dve", lambda e, col=col: e.memset(ss[:, col:col + 1], 0.0), writes=[tss])
            P.op("act", lambda e, t=t, col=col: e.activation(out=self.junk[:], in_=xt[:, t, :], func=AF.Square,
                                                             accum_out=ss[:, col:col + 1]),
                 reads=[txt], writes=[self.tjunk, tss])
            P.op("dve", lambda e, col=col: e.tensor_scalar(out=ss[:, col + 1:col + 2], in0=ss[:, col:col + 1], scalar1=1.0 / D,
                                                           scalar2=EPS, op0=ALU.mult, op1=ALU.add), reads=[tss], writes=[tss])
            P.op("act", lambda e, col=col: e.activation(out=ss[:, col + 1:col + 2], in_=ss[:, col + 1:col + 2], func=AF.Sqrt),
                 reads=[tss], writes=[tss])
            P.op("dve", lambda e, col=col: e.reciprocal(out=ss[:, col + 1:col + 2], in_=ss[:, col + 1:col + 2]), reads=[tss], writes=[tss])
            P.op("act", lambda e, t=t, col=col, xn=xn: e.activation(out=xn[:], in_=xt[:, t, :], func=AF.Copy,
                                                                     scale=ss[:, col + 1:col + 2]),
                 reads=[txt, tss], writes=[txn])
            for c in range(8):
                P.op("pe", lambda e, c=c, xn=xn, pst=pst: e.transpose(out=pst[:, c * 128:(c + 1) * 128],
                                                                       in_=xn[:, c * 128:(c + 1) * 128], identity=self.ident[:]),
                     reads=[txn, self.tident], writes=[tpst])
            P.op("dve", lambda e, t=t, pst=pst, xnT=xnT: e.tensor_copy(out=xnT[:, :, t * 128:(t + 1) * 128],
                                                                       in_=pst[:].rearrange("p (c k) -> p c k", k=128)),
                 reads=[tpst], writes=[txnT])
        return xt, txt, xnT, txnT


def load_ident(P, ident_ap):
    ident = P.sbuf("ident", [128, 128], BF16)
    t = P.tile()
    P.dma("pool", lambda e: e.dma_start(out=ident[:], in_=ident_ap), writes=[t])
    return ident, t


def stage_ffn(nc, h_in, h_out, g_ap, wup_ap, cw_ap, cb_ap, wdn_ap, ident_ap, n_seq, seq_len):
    P = Prog(nc)
    ident, tident = load_ident(P, ident_ap)
    g_sb, tg = load_gain(P, "g", g_ap)
    wup = P.sbuf("wup", [128, 8, 2 * DFF], BF16)
    twup = P.tile()
    load_weight_bf16(P, wup, twup, wup_ap, 8, g_sb, tg)
    wdn = P.sbuf("wdn", [128, NFT, D], BF16)
    twdn = P.tile()
    load_weight_bf16(P, wdn, twdn, wdn_ap, NFT)
    cw = P.sbuf("cw", [128, NFT, 3], F32)
    cb = P.sbuf("cb", [128, NFT], F32)
    tcw = P.tile()
    for j in range(3):
        P.dma("sp", lambda e, j=j: e.dma_start(out=cw[:, :, j], in_=cw_ap[j].rearrange("(f p) -> p f", p=128), allow_slow_non_contiguous=True), writes=[tcw])
    P.dma("sp", lambda e: e.dma_start(out=cb[:], in_=cb_ap.rearrange("(f p) -> p f", p=128), allow_slow_non_contiguous=True), writes=[tcw])
    nt = NormT(P, ident, tident)
    halo = P.sbuf("halo", [128, NFT, 2], F32)
    thalo = P.tile()
    gbuf = [P.sbuf(f"gbuf{i}", [128, 514], F32) for i in range(2)]
    tgbuf = P.tiles(2)
    gc = [P.sbuf(f"gc{i}", [128, 512], F32) for i in range(2)]
    tgc = P.tiles(2)
    hid = P.sbuf("hid", [128, NFT, 512], BF16)
    thid = P.tile()
    pv = [P.psum(f"pv{i}", [128, 512]) for i in range(2)]
    tpv = P.tiles(2)
    pg = [P.psum(f"pg{i}", [128, 512]) for i in range(2)]
    tpg = P.tiles(2)
    po = [P.psum(f"po{i}", [128, 512]) for i in range(2)]
    tpo = P.tiles(2)
    outs = []
    nsup = seq_len // 512
    k = 0
    ko = 0
    for s in range(n_seq):
        P.op("pool", lambda e: e.memset(halo[:], 0.0), writes=[thalo])
        for u in range(nsup):
            r0 = s * seq_len + u * 512
            xt, txt, xnT, txnT = nt.run(h_in[r0:r0 + 512, :])
            for ft in range(NFT):
                b = k % 2
                k += 1
                for c in range(8):
                    P.op("pe", lambda e, c=c, b=b, ft=ft: e.matmul(out=pv[b][:], lhsT=wup[:, c, ft * 128:(ft + 1) * 128], rhs=xnT[:, c, :],
                                                                   start=(c == 0), stop=(c == 7)), reads=[twup, txnT], writes=[tpv[b]])
                for c in range(8):
                    P.op("pe", lambda e, c=c, b=b, ft=ft: e.matmul(out=pg[b][:], lhsT=wup[:, c, DFF + ft * 128:DFF + (ft + 1) * 128], rhs=xnT[:, c, :],
                                                                   start=(c == 0), stop=(c == 7)), reads=[twup, txnT], writes=[tpg[b]])
                gb, tgb, g2, tg2 = gbuf[b], tgbuf[b], gc[b], tgc[b]
                P.op("pool", lambda e, gb=gb, ft=ft: e.tensor_copy(out=gb[:, 0:2], in_=halo[:, ft, :]), reads=[thalo], writes=[tgb])
                P.op("act", lambda e, gb=gb, b=b: e.activation(out=gb[:, 2:514], in_=pg[b][:], func=AF.Copy), reads=[tpg[b]], writes=[tgb])
                P.op("pool", lambda e, gb=gb, ft=ft: e.tensor_copy(out=halo[:, ft, :], in_=gb[:, 512:514]), reads=[tgb], writes=[thalo])
                P.op("act", lambda e, gb=gb, g2=g2, ft=ft: e.activation(out=g2[:], in_=gb[:, 2:514], func=AF.Identity,
                                                                        scale=cw[:, ft, 2:3], bias=cb[:, ft:ft + 1]),
                     reads=[tgb, tcw], writes=[tg2])
                P.op("dve", lambda e, gb=gb, g2=g2, ft=ft: e.scalar_tensor_tensor(out=g2[:], in0=gb[:, 1:513], scalar=cw[:, ft, 1:2], in1=g2[:],
                                                                                  op0=ALU.mult, op1=ALU.add), reads=[tgb, tcw, tg2], writes=[tg2])
                P.op("dve", lambda e, gb=gb, g2=g2, ft=ft: e.scalar_tensor_tensor(out=g2[:], in0=gb[:, 0:512], scalar=cw[:, ft, 0:1], in1=g2[:],
                                                                                   op0=ALU.mult, op1=ALU.add), reads=[tgb, tcw, tg2], writes=[tg2])
                P.op("act", lambda e, g2=g2: e.activation(out=g2[:], in_=g2[:], func=AF.Silu), reads=[tg2], writes=[tg2])
                P.op("dve", lambda e, g2=g2, b=b, ft=ft: e.tensor_tensor(out=hid[:, ft, :], in0=g2[:], in1=pv[b][:], op=ALU.mult),
                     reads=[tg2, tpv[b]], writes=[thid])
            for t in range(4):
                for nb in range(2):
                    b = ko % 2
                    ko += 1
                    for ft in range(NFT):
                        P.op("pe", lambda e, ft=ft, t=t, nb=nb, b=b: e.matmul(out=po[b][:], lhsT=hid[:, ft, t * 128:(t + 1) * 128],
                                                                              rhs=wdn[:, ft, nb * 512:(nb + 1) * 512],
                                                                              start=(ft == 0), stop=(ft == NFT - 1)),
                             reads=[thid, twdn], writes=[tpo[b]])
                    P.op("dve", lambda e, t=t, nb=nb, b=b, xt=xt: e.tensor_tensor(out=xt[:, t, nb * 512:(nb + 1) * 512], in0=po[b][:],
                                                                                   in1=xt[:, t, nb * 512:(nb + 1) * 512], op=ALU.add),
                         reads=[tpo[b], txt], writes=[txt])
            to = P.tile()
            P.dma("act", lambda e, xt=xt, r0=r0: e.dma_start(out=h_out[r0:r0 + 512, :].rearrange("(t p) d -> p t d", p=128), in_=xt[:]),
                  reads=[txt], writes=[to])
            outs.append(to)
    P.finish(outs)


def stage_kvq(nc, h_in, kT_rev, qT, v_rev, kvn_ap, wkv_ap, kn_ap, bn_ap, wq_ap, qn_ap, ident_ap, bones_ap, n_seq, S):
    P = Prog(nc)
    ident, tident = load_ident(P, ident_ap)
    bones = P.sbuf("bones", [128, 128], BF16)
    tbones = P.tile()
    P.dma("pool", lambda e: e.dma_start(out=bones[:], in_=bones_ap), writes=[tbones])
    gkv, tgkv = load_gain(P, "gkv", kvn_ap)
    gb, tgb = load_gain(P, "gb", bn_ap)
    wkv = P.sbuf("wkv", [128, 8, 2048], BF16)
    twkv = P.tile()
    load_weight_bf16(P, wkv, twkv, wkv_ap, 8, gkv, tgkv)
    wq = P.sbuf("wq", [128, 8, 1024], BF16)
    twq = P.tile()
    load_weight_bf16(P, wq, twq, wq_ap, 8, gb, tgb)
    hg = P.sbuf("hg", [128, 4], F32)
    thg = P.tile()
    for hf in range(2):
        P.dma("sp", lambda e, hf=hf: e.dma_start(out=hg[hf * 64:(hf + 1) * 64, 0:1], in_=kn_ap.rearrange("(p o) -> p o", o=1)), writes=[thg])
        P.dma("sp", lambda e, hf=hf: e.dma_start(out=hg[hf * 64:(hf + 1) * 64, 1:2], in_=qn_ap.rearrange("(p o) -> p o", o=1)), writes=[thg])
    P.op("dve", lambda e: e.tensor_scalar(out=hg[:, 1:2], in0=hg[:, 1:2], scalar1=0.125, scalar2=None, op0=ALU.mult), reads=[thg], writes=[thg])
    P.op("dve", lambda e: e.memset(hg[:, 2:3], EPS), reads=[thg], writes=[thg])
    nt = NormT(P, ident, tident)
    pk = [P.psum(f"pk{i}", [128, 512]) for i in range(2)]
    tpk = P.tiles(2)
    pss = [P.psum(f"pss{i}", [128, 512]) for i in range(2)]
    tpss = P.tiles(2)
    sq = [P.sbuf(f"sq{i}", [128, 512], BF16) for i in range(2)]
    tsq = P.tiles(2)
    rs = [P.sbuf(f"rs{i}", [128, 512], F32) for i in range(2)]
    trs = P.tiles(2)
    kbuf = [P.sbuf(f"kbuf{i}", [128, 8, 512], BF16) for i in range(2)]
    tkbuf = P.tiles(2)
    qbuf = [P.sbuf(f"qbuf{i}", [128, 8, 512], BF16) for i in range(2)]
    tqbuf = P.tiles(2)
    vbuf = [P.sbuf(f"vbuf{i}", [128, 4, 1024], BF16) for i in range(2)]
    tvbuf = P.tiles(2)
    xrev = P.sbuf("xrev", [128, 8, 512], BF16)
    txrev = P.tile()
    outs = []
    nsup = S // 512
    k = 0
    for s in range(n_seq):
        for u in range(nsup):
            r0 = s * S + u * 512
            rr = s * S + S - 512 * (u + 1)
            ob = (s * nsup + u) % 2
            xt, txt, xnT, txnT = nt.run(h_in[r0:r0 + 512, :])
            for which in range(2):
                for ft in range(8):
                    b = k % 2
                    k += 1
                    for c in range(8):
                        if which == 0:
                            P.op("pe", lambda e, c=c, b=b, ft=ft: e.matmul(out=pk[b][:], lhsT=wkv[:, c, ft * 128:(ft + 1) * 128], rhs=xnT[:, c, :],
                                                                           start=(c == 0), stop=(c == 7)), reads=[twkv, txnT], writes=[tpk[b]])
                        else:
                            P.op("pe", lambda e, c=c, b=b, ft=ft: e.matmul(out=pk[b][:], lhsT=wq[:, c, ft * 128:(ft + 1) * 128], rhs=xnT[:, c, :],
                                                                           start=(c == 0), stop=(c == 7)), reads=[twq, txnT], writes=[tpk[b]])
                    P.op("act", lambda e, b=b: e.activation(out=sq[b][:], in_=pk[b][:], func=AF.Square), reads=[tpk[b]], writes=[tsq[b]])
                    P.op("pe", lambda e, b=b: e.matmul(out=pss[b][:], lhsT=bones[:], rhs=sq[b][:], start=True, stop=True),
                         reads=[tbones, tsq[b]], writes=[tpss[b]])
                    P.op("act", lambda e, b=b: e.activation(out=rs[b][:], in_=pss[b][:], func=AF.Sqrt, scale=1.0 / 64, bias=hg[:, 2:3]),
                         reads=[tpss[b], thg], writes=[trs[b]])
                    P.op("dve", lambda e, b=b: e.reciprocal(out=rs[b][:], in_=rs[b][:]), reads=[trs[b]], writes=[trs[b]])
                    if which == 0:
                        P.op("dve", lambda e, b=b, ft=ft, ob=ob: e.scalar_tensor_tensor(out=kbuf[ob][:, ft, ::-1], in0=pk[b][:], scalar=hg[:, 0:1], in1=rs[b][:],
                                                                                        op0=ALU.mult, op1=ALU.mult),
                             reads=[tpk[b], thg, trs[b]], writes=[tkbuf[ob]])
                    else:
                        P.op("dve", lambda e, b=b, ft=ft, ob=ob: e.scalar_tensor_tensor(out=qbuf[ob][:, ft, :], in0=pk[b][:], scalar=hg[:, 1:2], in1=rs[b][:],
                                                                                        op0=ALU.mult, op1=ALU.mult),
                             reads=[tpk[b], thg, trs[b]], writes=[tqbuf[ob]])
            P.op("pool", lambda e, xnT=xnT: e.tensor_copy(out=xrev[:, :, ::-1], in_=xnT[:]), reads=[txnT], writes=[txrev])
            for t in range(4):
                for nb in range(2):
                    b = k % 2
                    k += 1
                    for c in range(8):
                        P.op("pe", lambda e, c=c, b=b, t=t, nb=nb: e.matmul(out=pk[b][:], lhsT=xrev[:, c, t * 128:(t + 1) * 128],
                                                                            rhs=wkv[:, c, 1024 + nb * 512:1024 + (nb + 1) * 512],
                                                                            start=(c == 0), stop=(c == 7)), reads=[twkv, txrev], writes=[tpk[b]])
                    P.op("act", lambda e, b=b, t=t, nb=nb, ob=ob: e.activation(out=vbuf[ob][:, t, nb * 512:(nb + 1) * 512], in_=pk[b][:], func=AF.Copy),
                         reads=[tpk[b]], writes=[tvbuf[ob]])
            t1, t2, t3 = P.tile(), P.tile(), P.tile()
            P.dma("act", lambda e, ob=ob, rr=rr: e.dma_start(out=kT_rev[:, rr:rr + 512].rearrange("(f p) n -> p f n", p=128), in_=kbuf[ob][:]),
                  reads=[tkbuf[ob]], writes=[t1])
            P.dma("act", lambda e, ob=ob, r0=r0: e.dma_start(out=qT[:, r0:r0 + 512].rearrange("(f p) n -> p f n", p=128), in_=qbuf[ob][:]),
                  reads=[tqbuf[ob]], writes=[t2])
            P.dma("act", lambda e, ob=ob, rr=rr: e.dma_start(out=v_rev[rr:rr + 512, :].rearrange("(t p) d -> p t d", p=128), in_=vbuf[ob][:]),
                  reads=[tvbuf[ob]], writes=[t3])
            outs += [t1, t2, t3]
    P.finish(outs)


def stage_att(nc, h_in, h_out, qT, kT_rev, v_rev, wo_ap, ident_ap, maskb_ap, n_seq, S):
    P = Prog(nc)
    ident, tident = load_ident(P, ident_ap)
    maskb = P.sbuf("maskb", [128, 128], BF16)
    tmask = P.tile()
    P.dma("pool", lambda e: e.dma_start(out=maskb[:], in_=maskb_ap), writes=[tmask])
    wo = P.sbuf("wo", [128, 8, 1024], BF16)
    two = P.tile()
    load_weight_bf16(P, wo, two, wo_ap, 8)
    zeros = P.sbuf("zeros", [128, 512], F32)
    onec = P.sbuf("onec", [128, 1], F32)
    tz = P.tile()
    P.op("pool", lambda e: e.memset(zeros[:], 0.0), writes=[tz])
    P.op("pool", lambda e: e.memset(onec[:], 1.0), reads=[tz], writes=[tz])
    NB = S // 128
    qsb = P.sbuf("qsb", [128, 8, S], BF16)
    ksb = P.sbuf("ksb", [128, 8, S], BF16)
    vsb = P.sbuf("vsb", [128, NB, 1024], BF16)
    oT = P.sbuf("oT", [128, 8, S], BF16)
    tq, tk, tv, toT = P.tile(), P.tile(), P.tile(), P.tile()
    NZ, NS, NPB, NW, NWT, NWS, NPO = 3, 4, 5, 4, 2, 4, 3
    pz = [P.psum(f"pz{i}", [128, 512]) for i in range(NZ)]
    tpz = P.tiles(NZ)
    pwT = [P.psum(f"pwT{i}", [128, 512], BF16) for i in range(NWT)]
    tpwT = P.tiles(NWT)
    po = [P.psum(f"po{i}", [128, 128]) for i in range(NPO)]
    tpo = P.tiles(NPO)
    pw = pz[0:2]
    tpw = tpz[0:2]
    ssb = [P.sbuf(f"ssb{i}", [128, 512], F32) for i in range(NS)]
    tssb = P.tiles(NS)
    pbuf = [P.sbuf(f"pbuf{i}", [128, 513], F32) for i in range(NPB)]
    tpbuf = P.tiles(NPB)
    wsb = [P.sbuf(f"wsb{i}", [128, 512], BF16) for i in range(NW)]
    twsb = P.tiles(NW)
    wTs = [P.sbuf(f"wTs{i}", [128, 512], BF16) for i in range(NWS)]
    twTs = P.tiles(NWS)
    xt = [P.sbuf(f"xt{i}", [128, D], F32) for i in range(2)]
    txt = P.tiles(2)
    outs = []
    kk = 0
    kx = 0
    gt = 0
    gpo = 0
    for s in range(n_seq):
        c0 = s * S
        for f in range(0, 8, 2):
            P.dma("sp", lambda e, f=f, c0=c0: e.dma_start(out=qsb[:, f:f + 2, :], in_=qT[f * 128:(f + 2) * 128, c0:c0 + S].rearrange("(f p) n -> p f n", p=128)), writes=[tq])
            P.dma("act", lambda e, f=f, c0=c0: e.dma_start(out=ksb[:, f:f + 2, :], in_=kT_rev[f * 128:(f + 2) * 128, c0:c0 + S].rearrange("(f p) n -> p f n", p=128)), writes=[tk])
        for b4 in range(0, NB, 4):
            P.dma("sp", lambda e, b4=b4, c0=c0: e.dma_start(out=vsb[:, b4:b4 + 4, :], in_=v_rev[c0 + b4 * 128:c0 + (b4 + 4) * 128, :].rearrange("(t p) d -> p t d", p=128)), writes=[tv])
        tasks = []
        for ft in range(8):
            for i in range(NB):
                q0 = i * 128
                kp0 = S - 128 - q0
                klen = q0 + 128
                ob = gpo % NPO
                gpo += 1
                ntile = (klen + 511) // 512
                for hp in range(2):
                    h = ft * 2 + hp
                    ps = slice(hp * 64, (hp + 1) * 64)
                    for n in range(ntile):
                        col0 = kp0 + 512 * n
                        wdt = min(512, S - col0)
                        nblk = wdt // 128
                        g = gt
                        gt += 1
                        bz, bs_, bp, bw, bwt, bws = g % NZ, g % NS, g % NPB, g % NW, g % NWT, g % NWS
                        bpp = (g - 1) % NPB

                        def ph0(bz=bz, ft=ft, ps=ps, q0=q0, col0=col0, wdt=wdt, n=n):
                            P.op("pe", lambda e: e.matmul(out=pz[bz][:, 0:wdt], lhsT=qsb[ps, ft, q0:q0 + 128], rhs=ksb[ps, ft, col0:col0 + wdt],
                                                          start=True, stop=(n != 0)), reads=[tq, tk], writes=[tpz[bz]])
                            if n == 0:
                                P.op("pe", lambda e: e.matmul(out=pz[bz][:, 0:128], lhsT=ident[:], rhs=maskb[:], start=False, stop=True),
                                     reads=[tident, tmask], writes=[tpz[bz]])

                        def ph1(bz=bz, bs_=bs_, wdt=wdt):
                            P.op("act", lambda e: e.activation(out=ssb[bs_][:, 0:wdt], in_=pz[bz][:, 0:wdt], func=AF.Sigmoid, scale=-1.0),
                                 reads=[tpz[bz]], writes=[tssb[bs_]])

                        def ph2(bs_=bs_, bp=bp, bpp=bpp, wdt=wdt, n=n):
                            if n == 0:
                                P.op("pool", lambda e: e.memset(pbuf[bp][:, 0:1], 1.0), writes=[tpbuf[bp]])
                                init = onec[:, 0:1]
                                rd = [tssb[bs_], tz, tpbuf[bp]]
                            else:
                                P.op("pool", lambda e: e.tensor_copy(out=pbuf[bp][:, 0:1], in_=pbuf[bpp][:, 512:513]), reads=[tpbuf[bpp]], writes=[tpbuf[bp]])
                                init = pbuf[bpp][:, 512:513]
                                rd = [tssb[bs_], tz, tpbuf[bp], tpbuf[bpp]]
                            P.op("dve", lambda e: e.tensor_tensor_scan(out=pbuf[bp][:, 1:1 + wdt], data0=ssb[bs_][:, 0:wdt], data1=zeros[:, 0:wdt],
                                                                       initial=init, op0=ALU.mult, op1=ALU.add), reads=rd, writes=[tpbuf[bp]])

                        def ph3(bp=bp, bw=bw, wdt=wdt):
                            P.op("pool", lambda e: e.tensor_tensor(out=wsb[bw][:, 0:wdt], in0=pbuf[bp][:, 0:wdt], in1=pbuf[bp][:, 1:1 + wdt], op=ALU.subtract),
                                 reads=[tpbuf[bp]], writes=[twsb[bw]])

                        def ph4(bw=bw, bwt=bwt, nblk=nblk):
                            for jb in range(nblk):
                                P.op("pe", lambda e, jb=jb: e.transpose(out=pwT[bwt][:, jb * 128:(jb + 1) * 128], in_=wsb[bw][:, jb * 128:(jb + 1) * 128], identity=ident[:]),
                                     reads=[twsb[bw], tident], writes=[tpwT[bwt]])

                        def ph5(bwt=bwt, bws=bws, wdt=wdt, g=g):
                            if g % 4 != 3:
                                P.op("act", lambda e: e.activation(out=wTs[bws][:, 0:wdt], in_=pwT[bwt][:, 0:wdt], func=AF.Copy), reads=[tpwT[bwt]], writes=[twTs[bws]])
                            else:
                                P.op("dve", lambda e: e.tensor_copy(out=wTs[bws][:, 0:wdt], in_=pwT[bwt][:, 0:wdt]), reads=[tpwT[bwt]], writes=[twTs[bws]])

                        def ph6(bws=bws, nblk=nblk, col0=col0, h=h, ps=ps, ob=ob, n=n, ntile=ntile, hp=hp, ft=ft, q0=q0):
                            for jb in range(nblk):
                                blk = col0 // 128 + jb
                                first = (n == 0 and jb == 0)
                                last = (n == ntile - 1 and jb == nblk - 1)
                                P.op("pe", lambda e, jb=jb, blk=blk, first=first, last=last: e.matmul(
                                    out=po[ob][ps, :], lhsT=vsb[:, blk, h * 64:(h + 1) * 64], rhs=wTs[bws][:, jb * 128:(jb + 1) * 128], start=first, stop=last),
                                    reads=[tv, twTs[bws]], writes=[tpo[ob]])
                            if hp == 1 and n == ntile - 1:
                                P.op("act", lambda e: e.activation(out=oT[:, ft, q0:q0 + 128], in_=po[ob][:], func=AF.Copy), reads=[tpo[ob]], writes=[toT])

                        tasks.append([ph0, ph1, ph2, ph3, ph4, ph5, ph6])
        run_pipeline(tasks, ATT_OFFS)
        for t in range(NB):
            r0 = c0 + t * 128
            xb = kx % 2
            kx += 1
            P.dma("sp", lambda e, xb=xb, r0=r0: e.dma_start(out=xt[xb][:], in_=h_in[r0:r0 + 128, :]), writes=[txt[xb]])
            for nb in range(2):
                b = kk % 2
                kk += 1
                for ft in range(8):
                    P.op("pe", lambda e, b=b, ft=ft, t=t, nb=nb: e.matmul(out=pw[b][:], lhsT=oT[:, ft, t * 128:(t + 1) * 128], rhs=wo[:, ft, nb * 512:(nb + 1) * 512],
                                                                          start=(ft == 0), stop=(ft == 7)), reads=[toT, two], writes=[tpw[b]])
                P.op("dve", lambda e, b=b, xb=xb, nb=nb: e.tensor_tensor(out=xt[xb][:, nb * 512:(nb + 1) * 512], in0=pw[b][:], in1=xt[xb][:, nb * 512:(nb + 1) * 512], op=ALU.add),
                     reads=[tpw[b], txt[xb]], writes=[txt[xb]])
            to = P.tile()
            P.dma("act", lambda e, xb=xb, r0=r0: e.dma_start(out=h_out[r0:r0 + 128, :], in_=xt[xb][:]), reads=[txt[xb]], writes=[to])
            outs.append(to)
    P.finish(outs)


def stage_a1(nc, h_in, uT, g_ap, win_ap, ident_ap, n_tok):
    P = Prog(nc)
    ident, tident = load_ident(P, ident_ap)
    g_sb, tg = load_gain(P, "g", g_ap)
    win = P.sbuf("win", [128, 8, 1024], BF16)
    twin = P.tile()
    load_weight_bf16(P, win, twin, win_ap, 8, g_sb, tg)
    nt = NormT(P, ident, tident)
    pu = [P.psum(f"pu{i}", [128, 512]) for i in range(2)]
    tpu = P.tiles(2)
    ubuf = [P.sbuf(f"ubuf{i}", [128, 8, 512], BF16) for i in range(2)]
    tubuf = P.tiles(2)
    outs = []
    k = 0
    for u in range(n_tok // 512):
        r0 = u * 512
        ob = u % 2
        xt, txt, xnT, txnT = nt.run(h_in[r0:r0 + 512, :])
        for m in range(8):
            b = k % 2
            k += 1
            for c in range(8):
                P.op("pe", lambda e, c=c, b=b, m=m: e.matmul(out=pu[b][:], lhsT=win[:, c, m * 128:(m + 1) * 128], rhs=xnT[:, c, :],
                                                             start=(c == 0), stop=(c == 7)), reads=[twin, txnT], writes=[tpu[b]])
            if m % 2 == 0:
                P.op("act", lambda e, b=b, m=m, ob=ob: e.activation(out=ubuf[ob][:, m, :], in_=pu[b][:], func=AF.Copy), reads=[tpu[b]], writes=[tubuf[ob]])
            else:
                P.op("dve", lambda e, b=b, m=m, ob=ob: e.tensor_copy(out=ubuf[ob][:, m, :], in_=pu[b][:]), reads=[tpu[b]], writes=[tubuf[ob]])
        to = P.tile()
        P.dma("act", lambda e, ob=ob, r0=r0: e.dma_start(out=uT[:, r0:r0 + 512].rearrange("(m p) n -> p m n", p=128), in_=ubuf[ob][:]),
              reads=[tubuf[ob]], writes=[to])
        outs.append(to)
    P.finish(outs)


TWO_PI = 6.283185307179586
ATT_OFFS = [0, 2, 4, 6, 8, 9, 11]


def stage_a2(nc, uT, gT, lre_ap, lim_ap, bre_ap, bim_ap, cre_ap, cim_ap, d_ap, ldt_ap, iota_ap, n_seq, S, dbg_pairs=None, dbg=None, dbg_stop=9):
    P = Prog(nc)
    I32 = mybir.dt.int32
    NP = 32
    lr = P.sbuf("lr", [128, NP], F32)
    li = P.sbuf("li", [128, NP], F32)
    dt = P.sbuf("dt", [128, NP], F32)
    tprm = P.tile()
    for e_ in range(2):
        ps = slice(e_ * 64, (e_ + 1) * 64)
        P.dma("sp", lambda e, e_=e_, ps=ps: e.dma_start(out=lr[ps, :], in_=lre_ap.rearrange("(k e) n -> e n k", e=2)[e_], allow_slow_non_contiguous=True), writes=[tprm])
        P.dma("sp", lambda e, e_=e_, ps=ps: e.dma_start(out=li[ps, :], in_=lim_ap.rearrange("(k e) n -> e n k", e=2)[e_], allow_slow_non_contiguous=True), writes=[tprm])
        P.dma("sp", lambda e, e_=e_, ps=ps: e.dma_start(out=dt[ps, :], in_=ldt_ap.rearrange("(k e) -> e k", e=2)[e_].partition_broadcast(64), allow_slow_non_contiguous=True), writes=[tprm])
    cst = P.sbuf("cst", [128, 4], F32)
    tcst = P.tile()
    P.op("dve", lambda e: e.memset(cst[:, 0:1], TWO_PI / 4), writes=[tcst])
    P.op("dve", lambda e: e.memset(cst[:, 1:2], 0.0), reads=[tcst], writes=[tcst])
    sc = {}
    for nm in ["f", "fhi", "flo", "r", "t0", "t1", "t2", "t3", "sn", "cs", "ar", "ai", "qre", "qim", "nqim", "den"]:
        sc[nm] = P.sbuf("p_" + nm, [128, NP], F32)
    fhb = P.sbuf("p_fhb", [128, NP], BF16)
    tiq = P.sbuf("p_ti", [128, NP], I32)
    tsc = P.tile()

    def V(fn, eng="dve", extra=()):
        P.op(eng, fn, reads=[tprm, tsc, tcst] + list(extra), writes=[tsc])

    V(lambda e: e.activation(out=sc["t0"][:], in_=dt[:], func=AF.Exp), "act")
    V(lambda e: e.activation(out=sc["t1"][:], in_=sc["t0"][:], func=AF.Ln), "act")
    V(lambda e: e.tensor_tensor(out=sc["t1"][:], in0=dt[:], in1=sc["t1"][:], op=ALU.subtract))
    V(lambda e: e.tensor_scalar(out=sc["t1"][:], in0=sc["t1"][:], scalar1=1.0, scalar2=None, op0=ALU.add))
    V(lambda e: e.tensor_tensor(out=dt[:], in0=sc["t0"][:], in1=sc["t1"][:], op=ALU.mult))
    V(lambda e: e.tensor_tensor(out=sc["t0"][:], in0=li[:], in1=dt[:], op=ALU.mult))
    V(lambda e: e.tensor_scalar(out=sc["f"][:], in0=sc["t0"][:], scalar1=1.0 / TWO_PI, scalar2=None, op0=ALU.mult))
    V(lambda e: e.tensor_copy(out=fhb[:], in_=sc["f"][:]))
    V(lambda e: e.tensor_copy(out=sc["fhi"][:], in_=fhb[:]))
    V(lambda e: e.tensor_tensor(out=sc["flo"][:], in0=sc["f"][:], in1=sc["fhi"][:], op=ALU.subtract))
    V(lambda e: e.tensor_tensor(out=sc["t1"][:], in0=lr[:], in1=dt[:], op=ALU.mult))
    V(lambda e: e.activation(out=sc["t2"][:], in_=sc["t1"][:], func=AF.Exp), "act")
    V(lambda e: e.activation(out=sc["t3"][:], in_=sc["t2"][:], func=AF.Ln), "act")
    V(lambda e: e.tensor_tensor(out=sc["t3"][:], in0=sc["t1"][:], in1=sc["t3"][:], op=ALU.subtract))
    V(lambda e: e.tensor_scalar(out=sc["t3"][:], in0=sc["t3"][:], scalar1=1.0, scalar2=None, op0=ALU.add))
    V(lambda e: e.tensor_tensor(out=sc["r"][:], in0=sc["t2"][:], in1=sc["t3"][:], op=ALU.mult))
    V(lambda e: e.tensor_copy(out=tiq[:], in_=sc["f"][:]))
    V(lambda e: e.tensor_copy(out=sc["t3"][:], in_=tiq[:]))
    V(lambda e: e.tensor_tensor(out=sc["t2"][:], in0=sc["f"][:], in1=sc["t3"][:], op=ALU.subtract))
    V(lambda e: e.activation(out=sc["sn"][:], in_=sc["t2"][:], func=AF.Sin, scale=TWO_PI), "act")
    V(lambda e: e.tensor_scalar(out=sc["t3"][:], in0=sc["t2"][:], scalar1=-1.0, scalar2=None, op0=ALU.mult))
    V(lambda e: e.tensor_tensor(out=sc["t3"][:], in0=sc["t3"][:], in1=sc["t2"][:], op=ALU.min))
    V(lambda e: e.activation(out=sc["cs"][:], in_=sc["t3"][:], func=AF.Sin, scale=TWO_PI, bias=cst[:, 0:1]), "act")
    V(lambda e: e.tensor_tensor(out=sc["ar"][:], in0=sc["r"][:], in1=sc["cs"][:], op=ALU.mult))
    V(lambda e: e.tensor_tensor(out=sc["ai"][:], in0=sc["r"][:], in1=sc["sn"][:], op=ALU.mult))
    V(lambda e: e.tensor_scalar(out=sc["ar"][:], in0=sc["ar"][:], scalar1=-1.0, scalar2=None, op0=ALU.add))
    V(lambda e: e.tensor_tensor(out=sc["t0"][:], in0=lr[:], in1=lr[:], op=ALU.mult))
    V(lambda e: e.tensor_tensor(out=sc["t1"][:], in0=li[:], in1=li[:], op=ALU.mult))
    V(lambda e: e.tensor_tensor(out=sc["den"][:], in0=sc["t0"][:], in1=sc["t1"][:], op=ALU.add))
    V(lambda e: e.reciprocal(out=sc["den"][:], in_=sc["den"][:]))
    V(lambda e: e.tensor_tensor(out=sc["t0"][:], in0=sc["ar"][:], in1=lr[:], op=ALU.mult))
    V(lambda e: e.tensor_tensor(out=sc["t1"][:], in0=sc["ai"][:], in1=li[:], op=ALU.mult))
    V(lambda e: e.tensor_tensor(out=sc["t0"][:], in0=sc["t0"][:], in1=sc["t1"][:], op=ALU.add))
    V(lambda e: e.tensor_tensor(out=sc["qre"][:], in0=sc["t0"][:], in1=sc["den"][:], op=ALU.mult))
    V(lambda e: e.tensor_tensor(out=sc["t0"][:], in0=sc["ai"][:], in1=lr[:], op=ALU.mult))
    V(lambda e: e.tensor_tensor(out=sc["t1"][:], in0=sc["ar"][:], in1=li[:], op=ALU.mult))
    V(lambda e: e.tensor_tensor(out=sc["t0"][:], in0=sc["t0"][:], in1=sc["t1"][:], op=ALU.subtract))
    V(lambda e: e.tensor_tensor(out=sc["qim"][:], in0=sc["t0"][:], in1=sc["den"][:], op=ALU.mult))
    V(lambda e: e.tensor_scalar(out=sc["nqim"][:], in0=sc["qim"][:], scalar1=-1.0, scalar2=None, op0=ALU.mult))
    craw = P.sbuf("craw", [128, NP, 2, 16], F32)
    tcraw = P.tile()
    for e_ in range(2):
        ps = slice(e_ * 64, (e_ + 1) * 64)
        for k1 in range(NP):
            for ri, ap_ in enumerate((cre_ap, cim_ap)):
                P.dma("sp" if ri == 0 else "act", lambda e, e_=e_, ps=ps, k1=k1, ri=ri, ap_=ap_: e.dma_start(
                    out=craw[ps, k1, ri, :], in_=ap_[2 * k1 + e_].rearrange("h n -> n h"), allow_slow_non_contiguous=True),
                    writes=[tcraw])
    CT = P.sbuf("CT", [128, NP, 2, 128], BF16)
    tCT = P.tile()
    ctmp = P.sbuf("ctmp", [128, NP, 16], F32)
    ctmp2 = P.sbuf("ctmp2", [128, NP, 16], F32)
    tctmp = P.tile()
    P.op("pool", lambda e: e.memset(CT[:], 0.0), writes=[tCT])

    def bq(nm):
        return sc[nm][:].unsqueeze(2).to_broadcast([128, NP, 16])

    rd = [tcraw, tsc]
    P.op("dve", lambda e: e.tensor_tensor(out=ctmp[:], in0=craw[:, :, 0, :], in1=bq("qre"), op=ALU.mult), reads=rd, writes=[tctmp])
    P.op("dve", lambda e: e.tensor_tensor(out=ctmp2[:], in0=craw[:, :, 1, :], in1=bq("qim"), op=ALU.mult), reads=rd, writes=[tctmp])
    for e_ in range(2):
        ps = slice(e_ * 64, (e_ + 1) * 64)
        P.op("dve", lambda e, e_=e_, ps=ps: e.tensor_tensor(out=CT[ps, :, 0, e_ * 16:(e_ + 1) * 16], in0=ctmp[ps], in1=ctmp2[ps], op=ALU.subtract),
             reads=[tctmp], writes=[tCT])
    P.op("dve", lambda e: e.tensor_tensor(out=ctmp[:], in0=craw[:, :, 0, :], in1=bq("nqim"), op=ALU.mult), reads=rd + [tCT], writes=[tctmp])
    P.op("dve", lambda e: e.tensor_tensor(out=ctmp2[:], in0=craw[:, :, 1, :], in1=bq("qre"), op=ALU.mult), reads=rd, writes=[tctmp])
    for e_ in range(2):
        ps = slice(e_ * 64, (e_ + 1) * 64)
        P.op("dve", lambda e, e_=e_, ps=ps: e.tensor_tensor(out=CT[ps, :, 1, e_ * 16:(e_ + 1) * 16], in0=ctmp[ps], in1=ctmp2[ps], op=ALU.subtract),
             reads=[tctmp], writes=[tCT])
    BT = P.sbuf("BT", [32, NP, 2, 128], BF16)
    tBT = P.tile()
    P.op("pool", lambda e: e.memset(BT[:], 0.0), writes=[tBT])
    for e_ in range(2):
        for ri, ap_ in enumerate((bre_ap, bim_ap)):
            for k1 in range(NP):
                P.dma("pool", lambda e, e_=e_, ri=ri, ap_=ap_, k1=k1: e.dma_start(
                    out=BT[e_ * 16:(e_ + 1) * 16, k1, ri, e_ * 64:(e_ + 1) * 64], in_=ap_[2 * k1 + e_].rearrange("n h -> h n"),
                    allow_slow_non_contiguous=True), writes=[tBT])
    dwin = P.sbuf("dwin", [32, NP], F32)
    tdw = P.tile()
    P.dma("sp", lambda e: e.dma_start(out=dwin[:], in_=d_ap.rearrange("(k r) -> r k", r=32), allow_slow_non_contiguous=True), writes=[tdw])
    iota = P.sbuf("iota", [128, S], F32)
    tio = P.tile()
    P.dma("sp", lambda e: e.dma_start(out=iota[:], in_=iota_ap[:, 0:S]), writes=[tio])
    HB = 1024 if S >= 1024 else S
    nhalf = S // HB
    sn = P.sbuf("sn", [128, S], F32)
    cs = P.sbuf("cs", [128, S], F32)
    ttab = P.tile()
    wk1 = P.sbuf("wk1", [128, S], F32)
    wk2 = P.sbuf("wk2", [128, S], F32)
    tiw = P.sbuf("tiw", [128, S], I32)
    twk = P.tile()
    uwin = [P.sbuf(f"uwin{i}", [32, S], BF16) for i in range(2)]
    tuw = P.tiles(2)
    pbr = P.psum("pbr", [128, HB])
    pbi = P.psum("pbi", [128, HB])
    tpb = P.tile()
    py = [P.psum(f"py{i}", [128, HB]) for i in range(2)]
    tpy = P.tiles(2)
    A = [P.sbuf(f"A{i}", [128, HB], F32) for i in range(4)]
    tA = P.tiles(4)
    W = [P.sbuf(f"W{i}", [128, HB], F32) for i in range(2)]
    tW = P.tiles(2)
    Z = [P.sbuf(f"Z{i}", [128, HB], F32) for i in range(2)]
    tZ = P.tiles(2)
    Sb = [P.sbuf(f"Sb{i}", [128, HB], BF16) for i in range(2)]
    tSb = P.tiles(2)
    rtab = P.sbuf("rtab", [128, HB], F32)
    trt = P.tile()
    ones = P.sbuf("ones", [128, HB], F32)
    tones = P.tile()
    P.op("pool", lambda e: e.memset(ones[:], 1.0), writes=[tones])
    carry = P.sbuf("carry", [128, 2], F32)
    tcar = P.tile()
    ytmp = P.sbuf("ytmp", [32, HB], F32)
    tyt = P.tile()
    gout = [P.sbuf(f"gout{i}", [32, S], BF16) for i in range(2)]
    tgo = P.tiles(2)
    outs = []
    nu = 0
    ny = 0
    for k in range(NP if dbg_pairs is None else dbg_pairs):
        fh, fl, rk = sc["fhi"][:, k:k + 1], sc["flo"][:, k:k + 1], sc["r"][:, k:k + 1]
        P.op("dve", lambda e, fh=fh: e.tensor_scalar(out=wk1[:], in0=iota[:], scalar1=fh, scalar2=None, op0=ALU.mult), reads=[tio, tsc], writes=[twk])
        P.op("dve", lambda e: e.tensor_copy(out=tiw[:], in_=wk1[:]), reads=[twk], writes=[twk])
        P.op("pool", lambda e: e.tensor_copy(out=wk2[:], in_=tiw[:]), reads=[twk], writes=[twk])
        P.op("pool", lambda e: e.tensor_tensor(out=wk1[:], in0=wk1[:], in1=wk2[:], op=ALU.subtract), reads=[twk], writes=[twk])
        P.op("dve", lambda e, fl=fl: e.scalar_tensor_tensor(out=wk2[:], in0=iota[:], scalar=fl, in1=wk1[:], op0=ALU.mult, op1=ALU.add), reads=[tio, tsc, twk], writes=[twk])
        P.op("dve", lambda e: e.tensor_copy(out=tiw[:], in_=wk2[:]), reads=[twk], writes=[twk])
        P.op("pool", lambda e: e.tensor_copy(out=wk1[:], in_=tiw[:]), reads=[twk], writes=[twk])
        P.op("pool", lambda e: e.tensor_tensor(out=wk2[:], in0=wk2[:], in1=wk1[:], op=ALU.subtract), reads=[twk], writes=[twk])
        P.op("act", lambda e: e.activation(out=sn[:], in_=wk2[:], func=AF.Sin, scale=TWO_PI), reads=[twk], writes=[ttab])
        P.op("pool", lambda e: e.tensor_scalar(out=wk1[:], in0=wk2[:], scalar1=-1.0, scalar2=None, op0=ALU.mult), reads=[twk], writes=[twk])
        P.op("dve", lambda e: e.tensor_tensor(out=wk1[:], in0=wk1[:], in1=wk2[:], op=ALU.min), reads=[twk], writes=[twk])
        P.op("act", lambda e: e.activation(out=cs[:], in_=wk1[:], func=AF.Sin, scale=TWO_PI, bias=cst[:, 0:1]), reads=[twk, tcst], writes=[ttab])
        P.op("pool", lambda e, rk=rk: e.tensor_scalar(out=rtab[:], in0=ones[:], scalar1=rk, scalar2=None, op0=ALU.mult), reads=[tones, tsc], writes=[trt])
        m, q = k // 4, k % 4
        row0 = m * 128 + q * 32
        for s in range(n_seq if dbg_stop > 0 else 0):
            c0 = s * S
            ub = nu % 2
            nu += 1
            P.dma("sp", lambda e, ub=ub, row0=row0, c0=c0: e.dma_start(out=uwin[ub][:], in_=uT[row0:row0 + 32, c0:c0 + S]), writes=[tuw[ub]])
            P.op("pool", lambda e: e.memset(carry[:], 0.0), writes=[tcar])
            for hb in range(nhalf):
                t0 = hb * HB
                tsl = slice(t0, t0 + HB)
                if dbg_stop < 1.5:
                    continue
                for blk in range(HB // 512):
                    bs = slice(blk * 512, (blk + 1) * 512)
                    us = slice(t0 + blk * 512, t0 + (blk + 1) * 512)
                    P.op("pe", lambda e, bs=bs, us=us, k=k, ub=ub: e.matmul(out=pbr[:, bs], lhsT=BT[:, k, 0, :], rhs=uwin[ub][:, us], start=True, stop=True),
                         reads=[tBT, tuw[ub]], writes=[tpb])
                    P.op("pe", lambda e, bs=bs, us=us, k=k, ub=ub: e.matmul(out=pbi[:, bs], lhsT=BT[:, k, 1, :], rhs=uwin[ub][:, us], start=True, stop=True),
                         reads=[tBT, tuw[ub]], writes=[tpb])
                if dbg_stop < 2:
                    continue
                for blk in range(HB // 512):
                    bs = slice(blk * 512, (blk + 1) * 512)
                    ts2 = slice(t0 + blk * 512, t0 + (blk + 1) * 512)
                    P.op("dve", lambda e, bs=bs, ts2=ts2: e.tensor_tensor(out=A[0][:, bs], in0=cs[:, ts2], in1=pbr[:, bs], op=ALU.mult), reads=[ttab, tpb], writes=[tA[0]])
                    P.op("dve", lambda e, bs=bs, ts2=ts2: e.tensor_tensor(out=A[1][:, bs], in0=sn[:, ts2], in1=pbi[:, bs], op=ALU.mult), reads=[ttab, tpb], writes=[tA[1]])
                    P.op("dve", lambda e, bs=bs, ts2=ts2: e.tensor_tensor(out=A[2][:, bs], in0=cs[:, ts2], in1=pbi[:, bs], op=ALU.mult), reads=[ttab, tpb], writes=[tA[2]])
                    P.op("dve", lambda e, bs=bs, ts2=ts2: e.tensor_tensor(out=A[3][:, bs], in0=sn[:, ts2], in1=pbr[:, bs], op=ALU.mult), reads=[ttab, tpb], writes=[tA[3]])
                P.op("pool", lambda e: e.tensor_tensor(out=W[0][:], in0=A[0][:], in1=A[1][:], op=ALU.add), reads=[tA[0], tA[1]], writes=[tW[0]])
                P.op("pool", lambda e: e.tensor_tensor(out=W[1][:], in0=A[2][:], in1=A[3][:], op=ALU.subtract), reads=[tA[2], tA[3]], writes=[tW[1]])
                if dbg_stop < 3:
                    continue
                for j in range(2):
                    P.op("dve", lambda e, j=j, rk=rk: e.tensor_tensor_scan(out=Z[j][:], data0=rtab[:], data1=W[j][:], initial=carry[:, j:j + 1],
                                                                           op0=ALU.mult, op1=ALU.add), reads=[tW[j], trt, tcar], writes=[tZ[j]])
                for j in range(2):
                    P.op("pool", lambda e, j=j: e.tensor_copy(out=carry[:, j:j + 1], in_=Z[j][:, HB - 1:HB]), reads=[tZ[j]], writes=[tcar])
                if dbg_stop < 4:
                    continue
                P.op("pool", lambda e, tsl=tsl: e.tensor_tensor(out=A[0][:], in0=cs[:, tsl], in1=Z[0][:], op=ALU.mult), reads=[ttab, tZ[0]], writes=[tA[0]])
                P.op("pool", lambda e, tsl=tsl: e.tensor_tensor(out=A[1][:], in0=sn[:, tsl], in1=Z[1][:], op=ALU.mult), reads=[ttab, tZ[1]], writes=[tA[1]])
                P.op("pool", lambda e, tsl=tsl: e.tensor_tensor(out=A[2][:], in0=sn[:, tsl], in1=Z[0][:], op=ALU.mult), reads=[ttab, tZ[0]], writes=[tA[2]])
                P.op("pool", lambda e, tsl=tsl: e.tensor_tensor(out=A[3][:], in0=cs[:, tsl], in1=Z[1][:], op=ALU.mult), reads=[ttab, tZ[1]], writes=[tA[3]])
                P.op("dve", lambda e: e.tensor_tensor(out=Sb[0][:], in0=A[0][:], in1=A[1][:], op=ALU.subtract), reads=[tA[0], tA[1]], writes=[tSb[0]])
                P.op("pool", lambda e: e.tensor_tensor(out=Sb[1][:], in0=A[2][:], in1=A[3][:], op=ALU.add), reads=[tA[2], tA[3]], writes=[tSb[1]])
                if dbg_stop < 5:
                    continue
                yb = ny % 2
                ny += 1
                for blk in range(HB // 512):
                    bs = slice(blk * 512, (blk + 1) * 512)
                    for ri in range(2):
                        P.op("pe", lambda e, bs=bs, ri=ri, k=k, yb=yb: e.matmul(out=py[yb][:, bs], lhsT=CT[:, k, ri, :], rhs=Sb[ri][:, bs],
                                                                                start=(ri == 0), stop=(ri == 1)),
                             reads=[tCT, tSb[ri]], writes=[tpy[yb]])
                for blk in range(HB // 512):
                    bs = slice(blk * 512, (blk + 1) * 512)
                    ts2 = slice(t0 + blk * 512, t0 + (blk + 1) * 512)
                    P.op("dve", lambda e, yb=yb, ub=ub, bs=bs, ts2=ts2, k=k: e.scalar_tensor_tensor(out=ytmp[:, bs], in0=uwin[ub][:, ts2], scalar=dwin[:, k:k + 1], in1=py[yb][0:32, bs],
                                                                                                    op0=ALU.mult, op1=ALU.add), reads=[tuw[ub], tdw, tpy[yb]], writes=[tyt])
                P.op("act", lambda e, ub=ub, tsl=tsl: e.activation(out=gout[ub][:, tsl], in_=ytmp[:], func=AF.Gelu), reads=[tyt], writes=[tgo[ub]])
            if dbg_stop < 6:
                continue
            to = P.tile()
            P.dma("act", lambda e, ub=ub, row0=row0, c0=c0: e.dma_start(out=gT[row0:row0 + 32, c0:c0 + S], in_=gout[ub][:]), reads=[tgo[ub]], writes=[to])
            outs.append(to)
    if dbg is not None:
        for nm, ap_ in dbg.items():
            src = {"CT": CT, "BT": BT, "sn": sn, "cs": cs, "r": sc["r"], "qre": sc["qre"], "qim": sc["qim"], "fhi": sc["fhi"], "flo": sc["flo"]}[nm]
            tl = {"CT": tCT, "BT": tBT, "sn": ttab, "cs": ttab}.get(nm, tsc)
            to = P.tile()
            P.dma("sp", lambda e, ap_=ap_, src=src: e.dma_start(out=ap_, in_=src[:]), reads=[tl], writes=[to])
            outs.append(to)
    P.finish(outs)


def stage_a3(nc, h_in, h_out, gT, wglu_ap, n_tok):
    P = Prog(nc)
    wg = P.sbuf("wg", [128, 8, 2048], BF16)
    twg = P.tile()
    load_weight_bf16(P, wg, twg, wglu_ap, 8)
    gb = [P.sbuf(f"gb{i}", [128, 8, 512], BF16) for i in range(2)]
    tgb = P.tiles(2)
    xt = [P.sbuf(f"xt{i}", [128, 4, D], F32) for i in range(2)]
    txt = P.tiles(2)
    pv = [P.psum(f"pv{i}", [128, 512]) for i in range(2)]
    tpv = P.tiles(2)
    pg = [P.psum(f"pg{i}", [128, 512]) for i in range(2)]
    tpg = P.tiles(2)
    sg = [P.sbuf(f"sg{i}", [128, 512], F32) for i in range(2)]
    tsg = P.tiles(2)
    outs = []
    k = 0
    for u in range(n_tok // 512):
        r0 = u * 512
        ob = u % 2
        P.dma("sp", lambda e, ob=ob, r0=r0: e.dma_start(out=gb[ob][:], in_=gT[:, r0:r0 + 512].rearrange("(c p) n -> p c n", p=128)), writes=[tgb[ob]])
        P.dma("sp", lambda e, ob=ob, r0=r0: e.dma_start(out=xt[ob][:], in_=h_in[r0:r0 + 512, :].rearrange("(t p) d -> p t d", p=128)), writes=[txt[ob]])
        for t in range(4):
            for nb in range(2):
                b = k % 2
                k += 1
                for c in range(8):
                    P.op("pe", lambda e, c=c, b=b, t=t, nb=nb, ob=ob: e.matmul(out=pv[b][:], lhsT=gb[ob][:, c, t * 128:(t + 1) * 128], rhs=wg[:, c, nb * 512:(nb + 1) * 512],
                                                                               start=(c == 0), stop=(c == 7)), reads=[tgb[ob], twg], writes=[tpv[b]])
                for c in range(8):
                    P.op("pe", lambda e, c=c, b=b, t=t, nb=nb, ob=ob: e.matmul(out=pg[b][:], lhsT=gb[ob][:, c, t * 128:(t + 1) * 128], rhs=wg[:, c, 1024 + nb * 512:1024 + (nb + 1) * 512],
                                                                               start=(c == 0), stop=(c == 7)), reads=[tgb[ob], twg], writes=[tpg[b]])
                P.op("act", lambda e, b=b: e.activation(out=sg[b][:], in_=pg[b][:], func=AF.Sigmoid), reads=[tpg[b]], writes=[tsg[b]])
                P.op("dve", lambda e, b=b: e.tensor_tensor(out=sg[b][:], in0=sg[b][:], in1=pv[b][:], op=ALU.mult), reads=[tsg[b], tpv[b]], writes=[tsg[b]])
                P.op("pool", lambda e, b=b, t=t, nb=nb, ob=ob: e.tensor_tensor(out=xt[ob][:, t, nb * 512:(nb + 1) * 512], in0=xt[ob][:, t, nb * 512:(nb + 1) * 512], in1=sg[b][:], op=ALU.add),
                     reads=[tsg[b], txt[ob]], writes=[txt[ob]])
        to = P.tile()
        P.dma("act", lambda e, ob=ob, r0=r0: e.dma_start(out=h_out[r0:r0 + 512, :].rearrange("(t p) d -> p t d", p=128), in_=xt[ob][:]), reads=[txt[ob]], writes=[to])
        outs.append(to)
    P.finish(outs)


N_CORES = 8
SEQ = 2048
N_SEQ = 4
NT = N_SEQ * SEQ

_PARAMS = [
    ("a_norm", [1, 1024]), ("a_w_in", [1, 1024, 1024]), ("a_lam_re", [1, 64, 64]), ("a_lam_im", [1, 64, 64]),
    ("a_b_re", [1, 64, 64, 16]), ("a_b_im", [1, 64, 64, 16]), ("a_c_re", [1, 64, 16, 64]), ("a_c_im", [1, 64, 16, 64]),
    ("a_d", [1, 1024]), ("a_log_dt", [1, 64]), ("a_w_glu", [1, 1024, 2048]), ("kv_norm", [1024]), ("w_kv", [1024, 2048]),
    ("k_norm", [64]), ("b_norm", [1, 1024]), ("b_w_q", [1, 1024, 1024]), ("b_q_norm", [1, 64]), ("b_w_o", [1, 1024, 1024]),
    ("ffn_norm", [2, 1024]), ("ffn_w_up", [2, 1024, 5632]), ("ffn_conv_w", [2, 3, 2816]), ("ffn_conv_b", [2, 2816]),
    ("ffn_w_down", [2, 2816, 1024]),
]


def build_program(N_SEQ=N_SEQ, SEQ=SEQ, debug=False):
    NT = N_SEQ * SEQ
    nc = bass.Bass("TRN2", target_bir_lowering=False)
    x = nc.dram_tensor("x", [NT, D], F32, kind="ExternalInput").ap()
    prm = {n: nc.dram_tensor(n, s, F32, kind="ExternalInput").ap() for n, s in _PARAMS}
    ident = nc.dram_tensor("c_ident", [128, 128], F32, kind="ExternalInput").ap()
    bones = nc.dram_tensor("c_bones", [128, 128], F32, kind="ExternalInput").ap()
    maskb = nc.dram_tensor("c_maskb", [128, 128], F32, kind="ExternalInput").ap()
    iota = nc.dram_tensor("c_iota", [128, 2048], F32, kind="ExternalInput").ap()
    out = nc.dram_tensor("out", [NT, D], F32, kind="ExternalOutput").ap()
    kd = "ExternalOutput" if debug else "Internal"
    h1 = nc.dram_tensor("h1", [NT, D], F32, kind=kd).ap()
    h2 = nc.dram_tensor("h2", [NT, D], F32, kind=kd).ap()
    h3 = nc.dram_tensor("h3", [NT, D], F32, kind=kd).ap()
    uT = nc.dram_tensor("uT", [D, NT], BF16).ap()
    gT = nc.dram_tensor("gT", [D, NT], BF16).ap()
    kT = nc.dram_tensor("kT", [D, NT], BF16).ap()
    qT = nc.dram_tensor("qT", [D, NT], BF16).ap()
    vr = nc.dram_tensor("vr", [NT, D], BF16).ap()
    stage_a1(nc, x, uT, prm["a_norm"][0], prm["a_w_in"][0], ident, NT)
    stage_a2(nc, uT, gT, prm["a_lam_re"][0], prm["a_lam_im"][0], prm["a_b_re"][0], prm["a_b_im"][0], prm["a_c_re"][0], prm["a_c_im"][0],
             prm["a_d"][0], prm["a_log_dt"][0], iota, N_SEQ, SEQ)
    stage_a3(nc, x, h1, gT, prm["a_w_glu"][0], NT)
    stage_ffn(nc, h1, h2, prm["ffn_norm"][0], prm["ffn_w_up"][0], prm["ffn_conv_w"][0], prm["ffn_conv_b"][0],
              prm["ffn_w_down"][0], ident, N_SEQ, SEQ)
    stage_kvq(nc, h2, kT, qT, vr, prm["kv_norm"], prm["w_kv"], prm["k_norm"], prm["b_norm"][0], prm["b_w_q"][0], prm["b_q_norm"][0],
              ident, bones, N_SEQ, SEQ)
    stage_att(nc, h2, h3, qT, kT, vr, prm["b_w_o"][0], ident, maskb, N_SEQ, SEQ)
    stage_ffn(nc, h3, out, prm["ffn_norm"][1], prm["ffn_w_up"][1], prm["ffn_conv_w"][1], prm["ffn_conv_b"][1],
              prm["ffn_w_down"][1], ident, N_SEQ, SEQ)
    return nc


def kernel(**inputs):
    x = np.ascontiguousarray(np.asarray(inputs["x"], dtype=np.float32))
    nc = build_program()
    p = np.arange(128)
    consts = {
        "c_ident": np.eye(128, dtype=np.float32),
        "c_bones": (p[:, None] // 64 == p[None, :] // 64).astype(np.float32),
        "c_maskb": np.where(p[None, :] + p[:, None] >= 128, 0.0, -30000.0).astype(np.float32),
        "c_iota": np.ascontiguousarray(np.tile(np.arange(2048, dtype=np.float32), (128, 1))),
    }
    params = {n: np.ascontiguousarray(np.asarray(inputs[n], dtype=np.float32)).reshape(s) for n, s in _PARAMS}
    in_maps = []
    for c in range(N_CORES):
        m = {"x": x[c * N_SEQ:(c + 1) * N_SEQ].reshape(NT, D)}
        m.update(params)
        m.update(consts)
        in_maps.append(m)
    res = run_bass_kernel_spmd(nc, in_maps, core_ids=list(range(N_CORES)))
    outs = [np.asarray(r["out"], dtype=np.float32).reshape(N_SEQ, SEQ, D) for r in res.results]
    return np.concatenate(outs, axis=0)
```

```python
import contextlib
import numpy as np
import concourse.bass as bass
import concourse.mybir as mybir
from concourse.bass_utils import run_bass_kernel_spmd

F32 = mybir.dt.float32
BF16 = mybir.dt.bfloat16
AF = mybir.ActivationFunctionType
ALU = mybir.AluOpType
AX = mybir.AxisListType

ENGS = ("pe", "act", "dve", "pool", "sp")
N_DMA_SEMS = 4

D = 1024
DFF = 2816
NFT = DFF // 128
EPS = 1e-6


class T:
    __slots__ = ("name", "writes", "reads")

    def __init__(self, name="t"):
        self.name = name
        self.writes = {}
        self.reads = {}


class Prog:
    _stage = 0

    def __init__(self, nc):
        self.nc = nc
        Prog._stage += 1
        self.sid = Prog._stage
        self.stack = contextlib.ExitStack()
        self.ops = {e: [] for e in ENGS}
        self.known = {e: {} for e in ENGS}
        self.ndma = {e: 0 for e in ENGS}
        self.cnt = {e: 0 for e in ENGS}
        self.milestones = {e: set() for e in ENGS}
        self._n = 0

    def sbuf(self, name, shape, dtype):
        return self.stack.enter_context(self.nc.sbuf_tensor(f"s{self.sid}_{name}", list(shape), dtype))

    def psum(self, name, shape, dtype=F32):
        return self.stack.enter_context(self.nc.psum_tensor(f"s{self.sid}_{name}", list(shape), dtype))

    def tile(self, name=None):
        return T(name or "t")

    def tiles(self, n):
        return [T() for _ in range(n)]

    def _collect(self, eng, reads, writes):
        need = {}
        for t in reads:
            for k, v in t.writes.items():
                if need.get(k, 0) < v:
                    need[k] = v
        for t in writes:
            for d in (t.writes, t.reads):
                for k, v in d.items():
                    if need.get(k, 0) < v:
                        need[k] = v
        waits = []
        kn = self.known[eng]
        for k, v in need.items():
            if k == ("e", eng) and eng == "pe":
                continue
            if kn.get(k, 0) >= v:
                continue
            kn[k] = v
            waits.append((k, v))
            if k[0] == "e":
                self.milestones[k[1]].add(v)
        return waits

    def op(self, eng, fn, reads=(), writes=()):
        waits = self._collect(eng, reads, writes)
        self.cnt[eng] += 1
        idx = self.cnt[eng]
        key = ("e", eng)
        self.ops[eng].append(dict(kind="op", fn=fn, waits=waits, idx=idx))
        for t in reads:
            if t.reads.get(key, 0) < idx:
                t.reads[key] = idx
        for t in writes:
            t.writes = {key: idx}
            t.reads = {}

    def dma(self, eng, fn, reads=(), writes=()):
        i = self.ndma[eng]
        self.ndma[eng] += 1
        slot = i % N_DMA_SEMS
        gen = i // N_DMA_SEMS
        key = ("d", eng, slot)
        waits = self._collect(eng, reads, writes)
        kn = self.known[eng]
        if gen > 0 and kn.get(key, 0) < 16 * gen:
            kn[key] = 16 * gen
            waits.append((key, 16 * gen))
        val = 16 * (gen + 1)
        self.ops[eng].append(dict(kind="dma", fn=fn, waits=waits, key=key))
        for t in reads:
            if t.reads.get(key, 0) < val:
                t.reads[key] = val
        for t in writes:
            t.writes = {key: val}
            t.reads = {}

    def wait_all(self, eng, tiles):
        waits = self._collect(eng, tiles, ())
        self.ops[eng].append(dict(kind="wait", waits=waits))

    def finish(self, out_tiles):
        self.wait_all("sp", out_tiles)
        nc = self.nc
        with nc.cleanup_on_exit():
            sems = {}
            for e in ENGS:
                if self.milestones[e]:
                    sems[("e", e)] = nc.alloc_semaphore(f"p{self.sid}_{e}")
                for s in range(min(N_DMA_SEMS, self.ndma[e])):
                    sems[("d", e, s)] = nc.alloc_semaphore(f"d{self.sid}_{e}_{s}")
            mmap = {e: {v: i + 1 for i, v in enumerate(sorted(self.milestones[e]))} for e in ENGS}

            def replay(e):
                def body(engine):
                    for o in self.ops[e]:
                        for k, v in o["waits"]:
                            if k[0] == "e":
                                v = mmap[k[1]][v]
                            engine.wait_ge(sems[k], v)
                        if o["kind"] == "op":
                            ins = o["fn"](engine)
                            if o["idx"] in mmap[e]:
                                ins.then_inc(sems[("e", e)], 1)
                        elif o["kind"] == "dma":
                            ins = o["fn"](engine)
                            ins.then_inc(sems[o["key"]], 16)
                return body

            with nc.Block() as block:
                block.tensor(replay("pe"))
                block.scalar(replay("act"))
                block.vector(replay("dve"))
                block.gpsimd(replay("pool"))
                block.sync(replay("sp"))
            nc.all_engine_barrier()
        self.stack.close()


def run_pipeline(tasks, offsets):
    nph = len(offsets)
    for step in range(len(tasks) + max(offsets) + 1):
        for p in reversed(range(nph)):
            t = step - offsets[p]
            if 0 <= t < len(tasks) and tasks[t][p] is not None:
                tasks[t][p]()


def load_weight_bf16(P, dst, tdst, w_ap, kchunks, gain_sb=None, tgain=None, eng="dve"):
    n = w_ap.shape[-1]
    src = w_ap.rearrange("(c p) n -> p c n", p=128)
    step = max(1, 4096 // n)
    for c0 in range(0, kchunks, step):
        c1 = min(kchunks, c0 + step)
        P.dma("pool", lambda e, c0=c0, c1=c1: e.dma_start(out=dst[:, c0:c1, :], in_=src[:, c0:c1, :]), writes=[tdst])
    if gain_sb is not None:
        for c in range(kchunks):
            P.op(eng, lambda e, c=c: e.tensor_scalar(out=dst[:, c, :], in0=dst[:, c, :], scalar1=gain_sb[:, c:c + 1],
                                                      scalar2=None, op0=ALU.mult), reads=[tdst, tgain], writes=[tdst])


def load_gain(P, name, g_ap):
    g_sb = P.sbuf(name, [128, 8], F32)
    t = P.tile()
    P.dma("sp", lambda e: e.dma_start(out=g_sb[:], in_=g_ap.rearrange("(c p) -> p c", p=128), allow_slow_non_contiguous=True), writes=[t])
    return g_sb, t


class NormT:
    def __init__(self, P, ident, tident, n_xt=2, n_xnT=1, junk=None, tjunk=None):
        self.P = P
        self.ident, self.tident = ident, tident
        self.n_xt, self.n_xnT = n_xt, n_xnT
        self.xt = [P.sbuf(f"nt_xt{i}", [128, 4, D], F32) for i in range(n_xt)]
        self.txt = P.tiles(n_xt)
        if junk is None:
            self.junk = P.sbuf("nt_junk", [128, D], BF16)[:]
            self.tjunk = P.tile()
        else:
            self.junk, self.tjunk = junk, tjunk
        self.ss = P.sbuf("nt_ss", [128, 8], F32)
        self.tss = P.tile()
        self.xn = [P.sbuf(f"nt_xn{i}", [128, D], BF16) for i in range(4)]
        self.txn = P.tiles(4)
        self.pst = [P.psum(f"nt_pst{i}", [128, D], BF16) for i in range(2)]
        self.tpst = P.tiles(2)
        self.xnT = [P.sbuf(f"nt_xnT{i}", [128, 8, 512], BF16) for i in range(n_xnT)]
        self.txnT = P.tiles(n_xnT)
        self.k = 0
        self.n = 0
        self.n2 = 0
        self.kp = 0

    def part1(self, src_rows):
        P = self.P
        b = self.n % self.n_xt
        self.n += 1
        xt, txt = self.xt[b], self.txt[b]
        P.dma("sp", lambda e: e.dma_start(out=xt[:], in_=src_rows.rearrange("(t p) d -> p t d", p=128)), writes=[txt])
        ss, tss = self.ss, self.tss
        for t in range(4):
            xn, txn = self.xn[t], self.txn[t]
            self.k += 1
            col = (self.k % 4) * 2
            P.op("dve", lambda e, col=col: e.memset(ss[:, col:col + 1], 0.0), writes=[tss])
            P.op("act", lambda e, t=t, col=col: e.activation(out=self.junk, in_=xt[:, t, :], func=AF.Square,
                                                             accum_out=ss[:, col:col + 1]),
                 reads=[txt], writes=[self.tjunk, tss])
            P.op("dve", lambda e, col=col: e.tensor_scalar(out=ss[:, col + 1:col + 2], in0=ss[:, col:col + 1], scalar1=1.0 / D,
                                                           scalar2=EPS, op0=ALU.mult, op1=ALU.add), reads=[tss], writes=[tss])
            P.op("act", lambda e, col=col: e.activation(out=ss[:, col + 1:col + 2], in_=ss[:, col + 1:col + 2], func=AF.Sqrt),
                 reads=[tss], writes=[tss])
            P.op("dve", lambda e, col=col: e.reciprocal(out=ss[:, col + 1:col + 2], in_=ss[:, col + 1:col + 2]), reads=[tss], writes=[tss])
            P.op("act", lambda e, t=t, col=col, xn=xn: e.activation(out=xn[:], in_=xt[:, t, :], func=AF.Copy,
                                                                     scale=ss[:, col + 1:col + 2]),
                 reads=[txt, tss], writes=[txn])
        return xt, txt

    def part2(self):
        P = self.P
        b2 = self.n2 % self.n_xnT
        self.n2 += 1
        xnT, txnT = self.xnT[b2], self.txnT[b2]
        for t in range(4):
            xn, txn = self.xn[t], self.txn[t]
            kk = self.kp % 2
            self.kp += 1
            pst, tpst = self.pst[kk], self.tpst[kk]
            for c in range(8):
                P.op("pe", lambda e, c=c, xn=xn, pst=pst: e.transpose(out=pst[:, c * 128:(c + 1) * 128],
                                                                       in_=xn[:, c * 128:(c + 1) * 128], identity=self.ident[:]),
                     reads=[txn, self.tident], writes=[tpst])
            P.op("dve", lambda e, t=t, pst=pst, xnT=xnT: e.tensor_copy(out=xnT[:, :, t * 128:(t + 1) * 128],
                                                                       in_=pst[:].rearrange("p (c k) -> p c k", k=128)),
                 reads=[tpst], writes=[txnT])
        return xnT, txnT

    def run(self, src_rows):
        xt, txt = self.part1(src_rows)
        xnT, txnT = self.part2()
        return xt, txt, xnT, txnT


def load_ident(P, ident_ap):
    ident = P.sbuf("ident", [128, 128], BF16)
    t = P.tile()
    P.dma("pool", lambda e: e.dma_start(out=ident[:], in_=ident_ap), writes=[t])
    return ident, t


def stage_ffn(nc, h_in, h_out, g_ap, wup_ap, cw_ap, cb_ap, wdn_ap, ident_ap, n_seq, seq_len):
    P = Prog(nc)
    ident, tident = load_ident(P, ident_ap)
    g_sb, tg = load_gain(P, "g", g_ap)
    wup = P.sbuf("wup", [128, 8, 2 * DFF], BF16)
    twup = P.tile()
    load_weight_bf16(P, wup, twup, wup_ap, 8, g_sb, tg)
    wdn = P.sbuf("wdn", [128, NFT, D], BF16)
    twdn = P.tile()
    load_weight_bf16(P, wdn, twdn, wdn_ap, NFT)
    cw = P.sbuf("cw", [128, NFT, 3], F32)
    cb = P.sbuf("cb", [128, NFT], F32)
    tcw = P.tile()
    for j in range(3):
        P.dma("sp", lambda e, j=j: e.dma_start(out=cw[:, :, j], in_=cw_ap[j].rearrange("(f p) -> p f", p=128), allow_slow_non_contiguous=True), writes=[tcw])
    P.dma("sp", lambda e: e.dma_start(out=cb[:], in_=cb_ap.rearrange("(f p) -> p f", p=128), allow_slow_non_contiguous=True), writes=[tcw])
    gc = [P.sbuf(f"gc{i}", [128, 512], F32) for i in range(2)]
    tgc = P.tiles(2)
    nt = NormT(P, ident, tident, n_xt=1, n_xnT=2, junk=gc[1][:].bitcast(BF16), tjunk=tgc[1])
    halo = P.sbuf("halo", [128, NFT, 2], F32)
    thalo = P.tile()
    gbuf = [P.sbuf(f"gbuf{i}", [128, 514], F32) for i in range(2)]
    tgbuf = P.tiles(2)
    hid = P.sbuf("hid", [128, NFT, 512], BF16)
    thid = P.tile()
    pv = [P.psum(f"pv{i}", [128, 512]) for i in range(2)]
    tpv = P.tiles(2)
    pg = [P.psum(f"pg{i}", [128, 512]) for i in range(2)]
    tpg = P.tiles(2)
    po = [P.psum(f"po{i}", [128, 512]) for i in range(2)]
    tpo = P.tiles(2)
    xr = [P.sbuf(f"xr{i}", [128, D], F32) for i in range(1)] * 2
    txr = P.tiles(1) * 2
    outs = []
    nsup = seq_len // 512
    nblk = n_seq * nsup
    cnt = {"k": 0, "ko": 0, "kx": 0}
    xn_of = {}

    def phA1(u):
        r0 = u * 512
        nt.part1(h_in[r0:r0 + 512, :])

    def phA2(u):
        xn_of[u] = nt.part2()

    def phB(u):
        xnT, txnT = xn_of[u]
        if u % nsup == 0:
            P.op("pool", lambda e: e.memset(halo[:], 0.0), writes=[thalo])
        for ft in range(NFT):
            if ft == 2 and u + 1 < nblk:
                phA1(u + 1)
            b = cnt["k"] % 2
            cnt["k"] += 1
            for c in range(8):
                P.op("pe", lambda e, c=c, b=b, ft=ft: e.matmul(out=pv[b][:], lhsT=wup[:, c, ft * 128:(ft + 1) * 128], rhs=xnT[:, c, :],
                                                               start=(c == 0), stop=(c == 7)), reads=[twup, txnT], writes=[tpv[b]])
            for c in range(8):
                P.op("pe", lambda e, c=c, b=b, ft=ft: e.matmul(out=pg[b][:], lhsT=wup[:, c, DFF + ft * 128:DFF + (ft + 1) * 128], rhs=xnT[:, c, :],
                                                               start=(c == 0), stop=(c == 7)), reads=[twup, txnT], writes=[tpg[b]])
            gb, tgb, g2, tg2 = gbuf[b], tgbuf[b], gc[b], tgc[b]
            P.op("pool", lambda e, gb=gb, ft=ft: e.tensor_copy(out=gb[:, 0:2], in_=halo[:, ft, :]), reads=[thalo], writes=[tgb])
            P.op("act", lambda e, gb=gb, b=b: e.activation(out=gb[:, 2:514], in_=pg[b][:], func=AF.Copy), reads=[tpg[b]], writes=[tgb])
            P.op("pool", lambda e, gb=gb, ft=ft: e.tensor_copy(out=halo[:, ft, :], in_=gb[:, 512:514]), reads=[tgb], writes=[thalo])
            P.op("act", lambda e, gb=gb, g2=g2, ft=ft: e.activation(out=g2[:], in_=gb[:, 2:514], func=AF.Identity,
                                                                    scale=cw[:, ft, 2:3], bias=cb[:, ft:ft + 1]),
                 reads=[tgb, tcw], writes=[tg2])
            P.op("dve", lambda e, gb=gb, g2=g2, ft=ft: e.scalar_tensor_tensor(out=g2[:], in0=gb[:, 1:513], scalar=cw[:, ft, 1:2], in1=g2[:],
                                                                              op0=ALU.mult, op1=ALU.add), reads=[tgb, tcw, tg2], writes=[tg2])
            P.op("dve", lambda e, gb=gb, g2=g2, ft=ft: e.scalar_tensor_tensor(out=g2[:], in0=gb[:, 0:512], scalar=cw[:, ft, 0:1], in1=g2[:],
                                                                              op0=ALU.mult, op1=ALU.add), reads=[tgb, tcw, tg2], writes=[tg2])
            P.op("act", lambda e, g2=g2: e.activation(out=g2[:], in_=g2[:], func=AF.Silu), reads=[tg2], writes=[tg2])
            P.op("dve", lambda e, g2=g2, b=b, ft=ft: e.tensor_tensor(out=hid[:, ft, :], in0=g2[:], in1=pv[b][:], op=ALU.mult),
                 reads=[tg2, tpv[b]], writes=[thid])

    def phC(u):
        r0 = u * 512
        for t in range(4):
            xb = cnt["kx"] % 2
            cnt["kx"] += 1
            rr = r0 + t * 128
            P.dma("sp", lambda e, xb=xb, rr=rr: e.dma_start(out=xr[xb][:], in_=h_in[rr:rr + 128, :]), writes=[txr[xb]])
            for nb in range(2):
                b = cnt["ko"] % 2
                cnt["ko"] += 1
                for ft in range(NFT):
                    P.op("pe", lambda e, ft=ft, t=t, nb=nb, b=b: e.matmul(out=po[b][:], lhsT=hid[:, ft, t * 128:(t + 1) * 128],
                                                                          rhs=wdn[:, ft, nb * 512:(nb + 1) * 512],
                                                                          start=(ft == 0), stop=(ft == NFT - 1)),
                         reads=[thid, twdn], writes=[tpo[b]])
                P.op("dve", lambda e, nb=nb, b=b, xb=xb: e.tensor_tensor(out=xr[xb][:, nb * 512:(nb + 1) * 512], in0=po[b][:],
                                                                         in1=xr[xb][:, nb * 512:(nb + 1) * 512], op=ALU.add),
                     reads=[tpo[b], txr[xb]], writes=[txr[xb]])
            to = P.tile()
            P.dma("act", lambda e, xb=xb, rr=rr: e.dma_start(out=h_out[rr:rr + 128, :], in_=xr[xb][:]), reads=[txr[xb]], writes=[to])
            outs.append(to)

    phA1(0)
    phA2(0)
    for u in range(nblk):
        phB(u)
        if u + 1 < nblk:
            phA2(u + 1)
        phC(u)
    P.finish(outs)


def stage_kvq(nc, h_in, kT_rev, qT, v_rev, kvn_ap, wkv_ap, kn_ap, bn_ap, wq_ap, qn_ap, ident_ap, bones_ap, n_seq, S):
    P = Prog(nc)
    ident, tident = load_ident(P, ident_ap)
    bones = P.sbuf("bones", [128, 128], BF16)
    tbones = P.tile()
    P.dma("pool", lambda e: e.dma_start(out=bones[:], in_=bones_ap), writes=[tbones])
    gkv, tgkv = load_gain(P, "gkv", kvn_ap)
    gb, tgb = load_gain(P, "gb", bn_ap)
    wkv = P.sbuf("wkv", [128, 8, 2048], BF16)
    twkv = P.tile()
    load_weight_bf16(P, wkv, twkv, wkv_ap, 8, gkv, tgkv)
    wq = P.sbuf("wq", [128, 8, 1024], BF16)
    twq = P.tile()
    load_weight_bf16(P, wq, twq, wq_ap, 8, gb, tgb)
    hg = P.sbuf("hg", [128, 4], F32)
    thg = P.tile()
    for hf in range(2):
        P.dma("sp", lambda e, hf=hf: e.dma_start(out=hg[hf * 64:(hf + 1) * 64, 0:1], in_=kn_ap.rearrange("(p o) -> p o", o=1)), writes=[thg])
        P.dma("sp", lambda e, hf=hf: e.dma_start(out=hg[hf * 64:(hf + 1) * 64, 1:2], in_=qn_ap.rearrange("(p o) -> p o", o=1)), writes=[thg])
    P.op("dve", lambda e: e.tensor_scalar(out=hg[:, 1:2], in0=hg[:, 1:2], scalar1=0.125, scalar2=None, op0=ALU.mult), reads=[thg], writes=[thg])
    P.op("dve", lambda e: e.memset(hg[:, 2:3], EPS), reads=[thg], writes=[thg])
    nt = NormT(P, ident, tident)
    pk = [P.psum(f"pk{i}", [128, 512]) for i in range(2)]
    tpk = P.tiles(2)
    pss = [P.psum(f"pss{i}", [128, 512]) for i in range(2)]
    tpss = P.tiles(2)
    sq = [P.sbuf(f"sq{i}", [128, 512], BF16) for i in range(2)]
    tsq = P.tiles(2)
    rs = [P.sbuf(f"rs{i}", [128, 512], F32) for i in range(2)]
    trs = P.tiles(2)
    kbuf = [P.sbuf(f"kbuf{i}", [128, 8, 512], BF16) for i in range(2)]
    tkbuf = P.tiles(2)
    qbuf = [P.sbuf(f"qbuf{i}", [128, 8, 512], BF16) for i in range(2)]
    tqbuf = P.tiles(2)
    vbuf = [P.sbuf(f"vbuf{i}", [128, 4, 1024], BF16) for i in range(2)]
    tvbuf = P.tiles(2)
    xrev = P.sbuf("xrev", [128, 8, 512], BF16)
    txrev = P.tile()
    outs = []
    nsup = S // 512
    k = 0
    for s in range(n_seq):
        for u in range(nsup):
            r0 = s * S + u * 512
            rr = s * S + S - 512 * (u + 1)
            ob = (s * nsup + u) % 2
            xt, txt, xnT, txnT = nt.run(h_in[r0:r0 + 512, :])
            for which in range(2):
                for ft in range(8):
                    b = k % 2
                    k += 1
                    for c in range(8):
                        if which == 0:
                            P.op("pe", lambda e, c=c, b=b, ft=ft: e.matmul(out=pk[b][:], lhsT=wkv[:, c, ft * 128:(ft + 1) * 128], rhs=xnT[:, c, :],
                                                                           start=(c == 0), stop=(c == 7)), reads=[twkv, txnT], writes=[tpk[b]])
                        else:
                            P.op("pe", lambda e, c=c, b=b, ft=ft: e.matmul(out=pk[b][:], lhsT=wq[:, c, ft * 128:(ft + 1) * 128], rhs=xnT[:, c, :],
                                                                           start=(c == 0), stop=(c == 7)), reads=[twq, txnT], writes=[tpk[b]])
                    P.op("act", lambda e, b=b: e.activation(out=sq[b][:], in_=pk[b][:], func=AF.Square), reads=[tpk[b]], writes=[tsq[b]])
                    P.op("pe", lambda e, b=b: e.matmul(out=pss[b][:], lhsT=bones[:], rhs=sq[b][:], start=True, stop=True),
                         reads=[tbones, tsq[b]], writes=[tpss[b]])
                    P.op("act", lambda e, b=b: e.activation(out=rs[b][:], in_=pss[b][:], func=AF.Sqrt, scale=1.0 / 64, bias=hg[:, 2:3]),
                         reads=[tpss[b], thg], writes=[trs[b]])
                    P.op("dve", lambda e, b=b: e.reciprocal(out=rs[b][:], in_=rs[b][:]), reads=[trs[b]], writes=[trs[b]])
                    if which == 0:
                        P.op("dve", lambda e, b=b, ft=ft, ob=ob: e.scalar_tensor_tensor(out=kbuf[ob][:, ft, ::-1], in0=pk[b][:], scalar=hg[:, 0:1], in1=rs[b][:],
                                                                                        op0=ALU.mult, op1=ALU.mult),
                             reads=[tpk[b], thg, trs[b]], writes=[tkbuf[ob]])
                    else:
                        P.op("dve", lambda e, b=b, ft=ft, ob=ob: e.scalar_tensor_tensor(out=qbuf[ob][:, ft, :], in0=pk[b][:], scalar=hg[:, 1:2], in1=rs[b][:],
                                                                                        op0=ALU.mult, op1=ALU.mult),
                             reads=[tpk[b], thg, trs[b]], writes=[tqbuf[ob]])
            P.op("pool", lambda e, xnT=xnT: e.tensor_copy(out=xrev[:, :, ::-1], in_=xnT[:]), reads=[txnT], writes=[txrev])
            for t in range(4):
                for nb in range(2):
                    b = k % 2
                    k += 1
                    for c in range(8):
                        P.op("pe", lambda e, c=c, b=b, t=t, nb=nb: e.matmul(out=pk[b][:], lhsT=xrev[:, c, t * 128:(t + 1) * 128],
                                                                            rhs=wkv[:, c, 1024 + nb * 512:1024 + (nb + 1) * 512],
                                                                            start=(c == 0), stop=(c == 7)), reads=[twkv, txrev], writes=[tpk[b]])
                    P.op("act", lambda e, b=b, t=t, nb=nb, ob=ob: e.activation(out=vbuf[ob][:, t, nb * 512:(nb + 1) * 512], in_=pk[b][:], func=AF.Copy),
                         reads=[tpk[b]], writes=[tvbuf[ob]])
            t1, t2, t3 = P.tile(), P.tile(), P.tile()
            P.dma("act", lambda e, ob=ob, rr=rr: e.dma_start(out=kT_rev[:, rr:rr + 512].rearrange("(f p) n -> p f n", p=128), in_=kbuf[ob][:]),
                  reads=[tkbuf[ob]], writes=[t1])
            P.dma("act", lambda e, ob=ob, r0=r0: e.dma_start(out=qT[:, r0:r0 + 512].rearrange("(f p) n -> p f n", p=128), in_=qbuf[ob][:]),
                  reads=[tqbuf[ob]], writes=[t2])
            P.dma("act", lambda e, ob=ob, rr=rr: e.dma_start(out=v_rev[rr:rr + 512, :].rearrange("(t p) d -> p t d", p=128), in_=vbuf[ob][:]),
                  reads=[tvbuf[ob]], writes=[t3])
            outs += [t1, t2, t3]
    P.finish(outs)


def stage_att(nc, h_in, h_out, qT, kT_rev, v_rev, wo_ap, ident_ap, maskb_ap, n_seq, S):
    P = Prog(nc)
    ident, tident = load_ident(P, ident_ap)
    maskb = P.sbuf("maskb", [128, 128], BF16)
    tmask = P.tile()
    P.dma("pool", lambda e: e.dma_start(out=maskb[:], in_=maskb_ap), writes=[tmask])
    wo = P.sbuf("wo", [128, 8, 1024], BF16)
    two = P.tile()
    load_weight_bf16(P, wo, two, wo_ap, 8)
    zeros = P.sbuf("zeros", [128, 512], F32)
    onec = P.sbuf("onec", [128, 1], F32)
    tz = P.tile()
    P.op("pool", lambda e: e.memset(zeros[:], 0.0), writes=[tz])
    P.op("pool", lambda e: e.memset(onec[:], 1.0), reads=[tz], writes=[tz])
    NB = S // 128
    qsb = P.sbuf("qsb", [128, 8, S], BF16)
    ksb = P.sbuf("ksb", [128, 8, S], BF16)
    vsb = P.sbuf("vsb", [128, NB, 1024], BF16)
    oT = P.sbuf("oT", [128, 8, S], BF16)
    tq, tk, tv, toT = P.tile(), P.tile(), P.tile(), P.tile()
    NZ, NS, NPB, NW, NWT, NWS, NPO = 3, 4, 5, 4, 2, 4, 3
    pz = [P.psum(f"pz{i}", [128, 512]) for i in range(NZ)]
    tpz = P.tiles(NZ)
    pwT = [P.psum(f"pwT{i}", [128, 1024], BF16) for i in range(NWT)]
    tpwT = P.tiles(NWT)
    po = [P.psum(f"po{i}", [128, 512]) for i in range(NPO)]
    tpo = P.tiles(NPO)
    pw = pz[0:2]
    tpw = tpz[0:2]
    ssb = [P.sbuf(f"ssb{i}", [128, 512], F32) for i in range(NS)]
    tssb = P.tiles(NS)
    pbuf = [P.sbuf(f"pbuf{i}", [128, 513], F32) for i in range(NPB)]
    tpbuf = P.tiles(NPB)
    wsb = [P.sbuf(f"wsb{i}", [128, 512], BF16) for i in range(NW)]
    twsb = P.tiles(NW)
    wTs = [P.sbuf(f"wTs{i}", [128, 512], BF16) for i in range(NWS)]
    twTs = P.tiles(NWS)
    xt = [P.sbuf(f"xt{i}", [128, D], F32) for i in range(2)]
    txt = P.tiles(2)
    outs = []
    kk = 0
    kx = 0
    gt = 0
    gpo = 0
    for s in range(n_seq):
        c0 = s * S
        for f in range(0, 8, 2):
            P.dma("sp", lambda e, f=f, c0=c0: e.dma_start(out=qsb[:, f:f + 2, :], in_=qT[f * 128:(f + 2) * 128, c0:c0 + S].rearrange("(f p) n -> p f n", p=128)), writes=[tq])
            P.dma("act", lambda e, f=f, c0=c0: e.dma_start(out=ksb[:, f:f + 2, :], in_=kT_rev[f * 128:(f + 2) * 128, c0:c0 + S].rearrange("(f p) n -> p f n", p=128)), writes=[tk])
        for b4 in range(0, NB, 4):
            P.dma("sp", lambda e, b4=b4, c0=c0: e.dma_start(out=vsb[:, b4:b4 + 4, :], in_=v_rev[c0 + b4 * 128:c0 + (b4 + 4) * 128, :].rearrange("(t p) d -> p t d", p=128)), writes=[tv])
        tasks = []
        for ft in range(8):
            for i in range(NB):
                q0 = i * 128
                kp0 = S - 128 - q0
                klen = q0 + 128
                ob = gpo % NPO
                gpo += 1
                ntile = (klen + 511) // 512
                for hp in range(2):
                    h = ft * 2 + hp
                    ps = slice(hp * 64, (hp + 1) * 64)
                    for n in range(ntile):
                        col0 = kp0 + 512 * n
                        wdt = min(512, S - col0)
                        nblk = wdt // 128
                        g = gt
                        gt += 1
                        bz, bs_, bp, bw, bwt, bws = g % NZ, g % NS, g % NPB, g % NW, g % NWT, g % NWS
                        bpp = (g - 1) % NPB

                        def ph0(bz=bz, ft=ft, ps=ps, q0=q0, col0=col0, wdt=wdt, n=n):
                            P.op("pe", lambda e: e.matmul(out=pz[bz][:, 0:wdt], lhsT=qsb[ps, ft, q0:q0 + 128], rhs=ksb[ps, ft, col0:col0 + wdt],
                                                          start=True, stop=(n != 0)), reads=[tq, tk], writes=[tpz[bz]])
                            if n == 0:
                                P.op("pe", lambda e: e.matmul(out=pz[bz][:, 0:128], lhsT=ident[:], rhs=maskb[:], start=False, stop=True),
                                     reads=[tident, tmask], writes=[tpz[bz]])

                        def ph1(bz=bz, bs_=bs_, wdt=wdt):
                            P.op("act", lambda e: e.activation(out=ssb[bs_][:, 0:wdt], in_=pz[bz][:, 0:wdt], func=AF.Sigmoid, scale=-1.0),
                                 reads=[tpz[bz]], writes=[tssb[bs_]])

                        def ph2(bs_=bs_, bp=bp, bpp=bpp, wdt=wdt, n=n):
                            if n == 0:
                                P.op("pool", lambda e: e.memset(pbuf[bp][:, 0:1], 1.0), writes=[tpbuf[bp]])
                                init = onec[:, 0:1]
                                rd = [tssb[bs_], tz, tpbuf[bp]]
                            else:
                                P.op("pool", lambda e: e.tensor_copy(out=pbuf[bp][:, 0:1], in_=pbuf[bpp][:, 512:513]), reads=[tpbuf[bpp]], writes=[tpbuf[bp]])
                                init = pbuf[bpp][:, 512:513]
                                rd = [tssb[bs_], tz, tpbuf[bp], tpbuf[bpp]]
                            P.op("dve", lambda e: e.tensor_tensor_scan(out=pbuf[bp][:, 1:1 + wdt], data0=ssb[bs_][:, 0:wdt], data1=zeros[:, 0:wdt],
                                                                       initial=init, op0=ALU.mult, op1=ALU.add), reads=rd, writes=[tpbuf[bp]])

                        def ph3(bp=bp, bw=bw, wdt=wdt):
                            P.op("pool", lambda e: e.tensor_tensor(out=wsb[bw][:, 0:wdt], in0=pbuf[bp][:, 0:wdt], in1=pbuf[bp][:, 1:1 + wdt], op=ALU.subtract),
                                 reads=[tpbuf[bp]], writes=[twsb[bw]])

                        def ph4(bw=bw, bwt=bwt, nblk=nblk):
                            for jb in range(nblk):
                                P.op("pe", lambda e, jb=jb: e.transpose(out=pwT[bwt][:, jb * 128:(jb + 1) * 128], in_=wsb[bw][:, jb * 128:(jb + 1) * 128], identity=ident[:]),
                                     reads=[twsb[bw], tident], writes=[tpwT[bwt]])

                        def ph5(bwt=bwt, bws=bws, wdt=wdt, g=g):
                            if g % 4 != 3:
                                P.op("act", lambda e: e.activation(out=wTs[bws][:, 0:wdt], in_=pwT[bwt][:, 0:wdt], func=AF.Copy), reads=[tpwT[bwt]], writes=[twTs[bws]])
                            else:
                                P.op("dve", lambda e: e.tensor_copy(out=wTs[bws][:, 0:wdt], in_=pwT[bwt][:, 0:wdt]), reads=[tpwT[bwt]], writes=[twTs[bws]])

                        def ph6(bws=bws, nblk=nblk, col0=col0, h=h, ps=ps, ob=ob, n=n, ntile=ntile, hp=hp, ft=ft, q0=q0):
                            for jb in range(nblk):
                                blk = col0 // 128 + jb
                                first = (n == 0 and jb == 0)
                                last = (n == ntile - 1 and jb == nblk - 1)
                                P.op("pe", lambda e, jb=jb, blk=blk, first=first, last=last: e.matmul(
                                    out=po[ob][ps, 0:128], lhsT=vsb[:, blk, h * 64:(h + 1) * 64], rhs=wTs[bws][:, jb * 128:(jb + 1) * 128], start=first, stop=last),
                                    reads=[tv, twTs[bws]], writes=[tpo[ob]])
                            if hp == 1 and n == ntile - 1:
                                P.op("act", lambda e: e.activation(out=oT[:, ft, q0:q0 + 128], in_=po[ob][:, 0:128], func=AF.Copy), reads=[tpo[ob]], writes=[toT])

                        tasks.append([ph0, ph1, ph2, ph3, ph4, ph5, ph6])
        run_pipeline(tasks, ATT_OFFS)
        for t in range(NB):
            r0 = c0 + t * 128
            xb = kx % 2
            kx += 1
            P.dma("sp", lambda e, xb=xb, r0=r0: e.dma_start(out=xt[xb][:], in_=h_in[r0:r0 + 128, :]), writes=[txt[xb]])
            for nb in range(2):
                b = kk % 2
                kk += 1
                for ft in range(8):
                    P.op("pe", lambda e, b=b, ft=ft, t=t, nb=nb: e.matmul(out=pw[b][:], lhsT=oT[:, ft, t * 128:(t + 1) * 128], rhs=wo[:, ft, nb * 512:(nb + 1) * 512],
                                                                          start=(ft == 0), stop=(ft == 7)), reads=[toT, two], writes=[tpw[b]])
                P.op("dve", lambda e, b=b, xb=xb, nb=nb: e.tensor_tensor(out=xt[xb][:, nb * 512:(nb + 1) * 512], in0=pw[b][:], in1=xt[xb][:, nb * 512:(nb + 1) * 512], op=ALU.add),
                     reads=[tpw[b], txt[xb]], writes=[txt[xb]])
            to = P.tile()
            P.dma("act", lambda e, xb=xb, r0=r0: e.dma_start(out=h_out[r0:r0 + 128, :], in_=xt[xb][:]), reads=[txt[xb]], writes=[to])
            outs.append(to)
    P.finish(outs)


def stage_a1(nc, h_in, uT, g_ap, win_ap, ident_ap, n_tok):
    P = Prog(nc)
    ident, tident = load_ident(P, ident_ap)
    g_sb, tg = load_gain(P, "g", g_ap)
    win = P.sbuf("win", [128, 8, 1024], BF16)
    twin = P.tile()
    load_weight_bf16(P, win, twin, win_ap, 8, g_sb, tg)
    nt = NormT(P, ident, tident)
    pu = [P.psum(f"pu{i}", [128, 512]) for i in range(2)]
    tpu = P.tiles(2)
    ubuf = [P.sbuf(f"ubuf{i}", [128, 8, 512], BF16) for i in range(2)]
    tubuf = P.tiles(2)
    outs = []
    k = 0
    for u in range(n_tok // 512):
        r0 = u * 512
        ob = u % 2
        xt, txt, xnT, txnT = nt.run(h_in[r0:r0 + 512, :])
        for m in range(8):
            b = k % 2
            k += 1
            for c in range(8):
                P.op("pe", lambda e, c=c, b=b, m=m: e.matmul(out=pu[b][:], lhsT=win[:, c, m * 128:(m + 1) * 128], rhs=xnT[:, c, :],
                                                             start=(c == 0), stop=(c == 7)), reads=[twin, txnT], writes=[tpu[b]])
            if m % 2 == 0:
                P.op("act", lambda e, b=b, m=m, ob=ob: e.activation(out=ubuf[ob][:, m, :], in_=pu[b][:], func=AF.Copy), reads=[tpu[b]], writes=[tubuf[ob]])
            else:
                P.op("dve", lambda e, b=b, m=m, ob=ob: e.tensor_copy(out=ubuf[ob][:, m, :], in_=pu[b][:]), reads=[tpu[b]], writes=[tubuf[ob]])
        to = P.tile()
        P.dma("act", lambda e, ob=ob, r0=r0: e.dma_start(out=uT[:, r0:r0 + 512].rearrange("(m p) n -> p m n", p=128), in_=ubuf[ob][:]),
              reads=[tubuf[ob]], writes=[to])
        outs.append(to)
    P.finish(outs)


TWO_PI = 6.283185307179586
ATT_OFFS = [0, 2, 4, 6, 8, 9, 11]


def stage_a2(nc, uT, gT, lre_ap, lim_ap, bre_ap, bim_ap, cre_ap, cim_ap, d_ap, ldt_ap, iota_ap, n_seq, S, dbg_pairs=None, dbg=None, dbg_stop=9):
    P = Prog(nc)
    I32 = mybir.dt.int32
    NP = 32
    lr = P.sbuf("lr", [128, NP], F32)
    li = P.sbuf("li", [128, NP], F32)
    dt = P.sbuf("dt", [128, NP], F32)
    tprm = P.tile()
    for e_ in range(2):
        ps = slice(e_ * 64, (e_ + 1) * 64)
        P.dma("sp", lambda e, e_=e_, ps=ps: e.dma_start(out=lr[ps, :], in_=lre_ap.rearrange("(k e) n -> e n k", e=2)[e_], allow_slow_non_contiguous=True), writes=[tprm])
        P.dma("sp", lambda e, e_=e_, ps=ps: e.dma_start(out=li[ps, :], in_=lim_ap.rearrange("(k e) n -> e n k", e=2)[e_], allow_slow_non_contiguous=True), writes=[tprm])
        P.dma("sp", lambda e, e_=e_, ps=ps: e.dma_start(out=dt[ps, :], in_=ldt_ap.rearrange("(k e) -> e k", e=2)[e_].partition_broadcast(64), allow_slow_non_contiguous=True), writes=[tprm])
    cst = P.sbuf("cst", [128, 4], F32)
    tcst = P.tile()
    P.op("dve", lambda e: e.memset(cst[:, 0:1], TWO_PI / 4), writes=[tcst])
    P.op("dve", lambda e: e.memset(cst[:, 1:2], 0.0), reads=[tcst], writes=[tcst])
    sc = {}
    for nm in ["f", "fhi", "flo", "r", "t0", "t1", "t2", "t3", "sn", "cs", "ar", "ai", "qre", "qim", "nqim", "den"]:
        sc[nm] = P.sbuf("p_" + nm, [128, NP], F32)
    fhb = P.sbuf("p_fhb", [128, NP], BF16)
    tiq = P.sbuf("p_ti", [128, NP], I32)
    tsc = P.tile()

    def V(fn, eng="dve", extra=()):
        P.op(eng, fn, reads=[tprm, tsc, tcst] + list(extra), writes=[tsc])

    V(lambda e: e.activation(out=sc["t0"][:], in_=dt[:], func=AF.Exp), "act")
    V(lambda e: e.activation(out=sc["t1"][:], in_=sc["t0"][:], func=AF.Ln), "act")
    V(lambda e: e.tensor_tensor(out=sc["t1"][:], in0=dt[:], in1=sc["t1"][:], op=ALU.subtract))
    V(lambda e: e.tensor_scalar(out=sc["t1"][:], in0=sc["t1"][:], scalar1=1.0, scalar2=None, op0=ALU.add))
    V(lambda e: e.tensor_tensor(out=dt[:], in0=sc["t0"][:], in1=sc["t1"][:], op=ALU.mult))
    V(lambda e: e.tensor_tensor(out=sc["t0"][:], in0=li[:], in1=dt[:], op=ALU.mult))
    V(lambda e: e.tensor_scalar(out=sc["f"][:], in0=sc["t0"][:], scalar1=1.0 / TWO_PI, scalar2=None, op0=ALU.mult))
    V(lambda e: e.tensor_copy(out=fhb[:], in_=sc["f"][:]))
    V(lambda e: e.tensor_copy(out=sc["fhi"][:], in_=fhb[:]))
    V(lambda e: e.tensor_tensor(out=sc["flo"][:], in0=sc["f"][:], in1=sc["fhi"][:], op=ALU.subtract))
    V(lambda e: e.tensor_tensor(out=sc["t1"][:], in0=lr[:], in1=dt[:], op=ALU.mult))
    V(lambda e: e.activation(out=sc["t2"][:], in_=sc["t1"][:], func=AF.Exp), "act")
    V(lambda e: e.activation(out=sc["t3"][:], in_=sc["t2"][:], func=AF.Ln), "act")
    V(lambda e: e.tensor_tensor(out=sc["t3"][:], in0=sc["t1"][:], in1=sc["t3"][:], op=ALU.subtract))
    V(lambda e: e.tensor_scalar(out=sc["t3"][:], in0=sc["t3"][:], scalar1=1.0, scalar2=None, op0=ALU.add))
    V(lambda e: e.tensor_tensor(out=sc["r"][:], in0=sc["t2"][:], in1=sc["t3"][:], op=ALU.mult))
    V(lambda e: e.tensor_copy(out=tiq[:], in_=sc["f"][:]))
    V(lambda e: e.tensor_copy(out=sc["t3"][:], in_=tiq[:]))
    V(lambda e: e.tensor_tensor(out=sc["t2"][:], in0=sc["f"][:], in1=sc["t3"][:], op=ALU.subtract))
    V(lambda e: e.activation(out=sc["sn"][:], in_=sc["t2"][:], func=AF.Sin, scale=TWO_PI), "act")
    V(lambda e: e.tensor_scalar(out=sc["t3"][:], in0=sc["t2"][:], scalar1=-1.0, scalar2=None, op0=ALU.mult))
    V(lambda e: e.tensor_tensor(out=sc["t3"][:], in0=sc["t3"][:], in1=sc["t2"][:], op=ALU.min))
    V(lambda e: e.activation(out=sc["cs"][:], in_=sc["t3"][:], func=AF.Sin, scale=TWO_PI, bias=cst[:, 0:1]), "act")
    V(lambda e: e.tensor_tensor(out=sc["ar"][:], in0=sc["r"][:], in1=sc["cs"][:], op=ALU.mult))
    V(lambda e: e.tensor_tensor(out=sc["ai"][:], in0=sc["r"][:], in1=sc["sn"][:], op=ALU.mult))
    V(lambda e: e.tensor_scalar(out=sc["ar"][:], in0=sc["ar"][:], scalar1=-1.0, scalar2=None, op0=ALU.add))
    V(lambda e: e.tensor_tensor(out=sc["t0"][:], in0=lr[:], in1=lr[:], op=ALU.mult))
    V(lambda e: e.tensor_tensor(out=sc["t1"][:], in0=li[:], in1=li[:], op=ALU.mult))
    V(lambda e: e.tensor_tensor(out=sc["den"][:], in0=sc["t0"][:], in1=sc["t1"][:], op=ALU.add))
    V(lambda e: e.reciprocal(out=sc["den"][:], in_=sc["den"][:]))
    V(lambda e: e.tensor_tensor(out=sc["t0"][:], in0=sc["ar"][:], in1=lr[:], op=ALU.mult))
    V(lambda e: e.tensor_tensor(out=sc["t1"][:], in0=sc["ai"][:], in1=li[:], op=ALU.mult))
    V(lambda e: e.tensor_tensor(out=sc["t0"][:], in0=sc["t0"][:], in1=sc["t1"][:], op=ALU.add))
    V(lambda e: e.tensor_tensor(out=sc["qre"][:], in0=sc["t0"][:], in1=sc["den"][:], op=ALU.mult))
    V(lambda e: e.tensor_tensor(out=sc["t0"][:], in0=sc["ai"][:], in1=lr[:], op=ALU.mult))
    V(lambda e: e.tensor_tensor(out=sc["t1"][:], in0=sc["ar"][:], in1=li[:], op=ALU.mult))
    V(lambda e: e.tensor_tensor(out=sc["t0"][:], in0=sc["t0"][:], in1=sc["t1"][:], op=ALU.subtract))
    V(lambda e: e.tensor_tensor(out=sc["qim"][:], in0=sc["t0"][:], in1=sc["den"][:], op=ALU.mult))
    V(lambda e: e.tensor_scalar(out=sc["nqim"][:], in0=sc["qim"][:], scalar1=-1.0, scalar2=None, op0=ALU.mult))
    craw = P.sbuf("craw", [128, NP, 2, 16], F32)
    tcraw = P.tile()
    for e_ in range(2):
        ps = slice(e_ * 64, (e_ + 1) * 64)
        for k1 in range(NP):
            for ri, ap_ in enumerate((cre_ap, cim_ap)):
                P.dma("sp" if ri == 0 else "act", lambda e, e_=e_, ps=ps, k1=k1, ri=ri, ap_=ap_: e.dma_start(
                    out=craw[ps, k1, ri, :], in_=ap_[2 * k1 + e_].rearrange("h n -> n h"), allow_slow_non_contiguous=True),
                    writes=[tcraw])
    CT = P.sbuf("CT", [128, NP, 3, 128], BF16)
    tCT = P.tile()
    ctmp = P.sbuf("ctmp", [128, NP, 16], F32)
    ctmp2 = P.sbuf("ctmp2", [128, NP, 16], F32)
    tctmp = P.tile()
    P.op("pool", lambda e: e.memset(CT[:], 0.0), writes=[tCT])

    def bq(nm):
        return sc[nm][:].unsqueeze(2).to_broadcast([128, NP, 16])

    rd = [tcraw, tsc]
    P.op("dve", lambda e: e.tensor_tensor(out=ctmp[:], in0=craw[:, :, 0, :], in1=bq("qre"), op=ALU.mult), reads=rd, writes=[tctmp])
    P.op("dve", lambda e: e.tensor_tensor(out=ctmp2[:], in0=craw[:, :, 1, :], in1=bq("qim"), op=ALU.mult), reads=rd, writes=[tctmp])
    for e_ in range(2):
        ps = slice(e_ * 64, (e_ + 1) * 64)
        P.op("dve", lambda e, e_=e_, ps=ps: e.tensor_tensor(out=CT[ps, :, 0, e_ * 16:(e_ + 1) * 16], in0=ctmp[ps], in1=ctmp2[ps], op=ALU.subtract),
             reads=[tctmp], writes=[tCT])
    P.op("dve", lambda e: e.tensor_tensor(out=ctmp[:], in0=craw[:, :, 0, :], in1=bq("nqim"), op=ALU.mult), reads=rd + [tCT], writes=[tctmp])
    P.op("dve", lambda e: e.tensor_tensor(out=ctmp2[:], in0=craw[:, :, 1, :], in1=bq("qre"), op=ALU.mult), reads=rd, writes=[tctmp])
    for e_ in range(2):
        ps = slice(e_ * 64, (e_ + 1) * 64)
        P.op("dve", lambda e, e_=e_, ps=ps: e.tensor_tensor(out=CT[ps, :, 1, e_ * 16:(e_ + 1) * 16], in0=ctmp[ps], in1=ctmp2[ps], op=ALU.subtract),
             reads=[tctmp], writes=[tCT])
    BT = P.sbuf("BT", [32, NP, 2, 128], BF16)
    tBT = P.tile()
    P.op("pool", lambda e: e.memset(BT[:], 0.0), writes=[tBT])
    for e_ in range(2):
        for ri, ap_ in enumerate((bre_ap, bim_ap)):
            for k1 in range(NP):
                P.dma("pool", lambda e, e_=e_, ri=ri, ap_=ap_, k1=k1: e.dma_start(
                    out=BT[e_ * 16:(e_ + 1) * 16, k1, ri, e_ * 64:(e_ + 1) * 64], in_=ap_[2 * k1 + e_].rearrange("n h -> h n"),
                    allow_slow_non_contiguous=True), writes=[tBT])
    dwin = P.sbuf("dwin", [32, NP], F32)
    tdw = P.tile()
    P.dma("sp", lambda e: e.dma_start(out=dwin[:], in_=d_ap.rearrange("(k r) -> r k", r=32), allow_slow_non_contiguous=True), writes=[tdw])
    iota = P.sbuf("iota", [128, S], F32)
    tio = P.tile()
    P.dma("sp", lambda e: e.dma_start(out=iota[:], in_=iota_ap[:, 0:S]), writes=[tio])
    HB = 512
    nq = S // HB
    sn = [P.sbuf(f"sn{i}", [128, S], F32) for i in range(2)]
    cs = [P.sbuf(f"cs{i}", [128, S], F32) for i in range(2)]
    snb = [P.sbuf(f"snb{i}", [128, S], BF16) for i in range(2)]
    csb = [P.sbuf(f"csb{i}", [128, S], BF16) for i in range(2)]
    rtab = [P.sbuf(f"rtab{i}", [128, HB], F32) for i in range(2)]
    ttab = P.tiles(2)
    SH = S // 2
    wk1 = P.sbuf("wk1", [128, SH], F32)
    wk2 = P.sbuf("wk2", [128, SH], F32)
    tiw = P.sbuf("tiw", [128, SH], I32)
    twk = P.tile()
    ones = P.sbuf("ones", [128, HB], F32)
    zc = P.sbuf("zc", [128, 1], F32)
    tones = P.tile()
    P.op("dve", lambda e: e.memset(ones[:], 1.0), writes=[tones])
    P.op("dve", lambda e: e.memset(zc[:], 0.0), reads=[tones], writes=[tones])
    P.op("dve", lambda e: e.tensor_scalar(out=CT[:, :, 2, :], in0=CT[:, :, 0, :], scalar1=-1.0, scalar2=None, op0=ALU.mult), reads=[tCT], writes=[tCT])
    NU = 3
    uwin = [P.sbuf(f"uwin{i}", [32, S], BF16) for i in range(NU)]
    tuw = P.tiles(NU)
    pbr = [P.psum(f"pbr{i}", [128, HB]) for i in range(2)]
    pbi = [P.psum(f"pbi{i}", [128, HB]) for i in range(2)]
    tpb = P.tiles(2)
    py = [P.psum(f"py{i}", [128, HB]) for i in range(2)]
    tpy = P.tiles(2)
    bsb = [[P.sbuf(f"bsb{i}_{j}", [128, HB], F32) for j in range(2)] for i in range(2)]
    tbsb = P.tiles(2)
    A = [[P.sbuf(f"A{i}_{j}", [128, HB], F32) for j in range(4)] for i in range(2)]
    tA01 = P.tiles(2)
    tA23 = P.tiles(2)
    W = [[P.sbuf(f"W{i}_{j}", [128, HB], F32) for j in range(2)] for i in range(2)]
    tW = P.tiles(2)
    NZb = 3
    Z = [[P.sbuf(f"Z{i}_{j}", [128, HB], F32) for j in range(2)] for i in range(NZb)]
    tZ = P.tiles(NZb)
    Zb = [[P.sbuf(f"Zb{i}_{j}", [128, HB], BF16) for j in range(2)] for i in range(2)]
    tZb = P.tiles(2)
    Bq = [[P.sbuf(f"B{i}_{j}", [128, HB], BF16) for j in range(4)] for i in range(2)]
    tB = P.tiles(2)
    tB2 = P.tiles(2)
    ytmp = [P.sbuf(f"ytmp{i}", [32, HB], F32) for i in range(2)]
    tyt = P.tiles(2)
    gout = [P.sbuf(f"gout{i}", [32, S], BF16) for i in range(2)]
    tgo = P.tiles(2)
    outs = []
    npairs = NP if dbg_pairs is None else dbg_pairs

    def gen_tables(k):
        tb = k % 2
        fh, fl, rk = sc["fhi"][:, k:k + 1], sc["flo"][:, k:k + 1], sc["r"][:, k:k + 1]
        for hh in range(2):
            hs = slice(hh * SH, (hh + 1) * SH)
            P.op("dve", lambda e, hs=hs: e.tensor_scalar(out=wk1[:], in0=iota[:, hs], scalar1=fh, scalar2=None, op0=ALU.mult), reads=[tio, tsc], writes=[twk])
            P.op("dve", lambda e: e.tensor_copy(out=tiw[:], in_=wk1[:]), reads=[twk], writes=[twk])
            P.op("dve", lambda e: e.tensor_copy(out=wk2[:], in_=tiw[:]), reads=[twk], writes=[twk])
            P.op("dve", lambda e: e.tensor_tensor(out=wk1[:], in0=wk1[:], in1=wk2[:], op=ALU.subtract), reads=[twk], writes=[twk])
            P.op("dve", lambda e, hs=hs: e.scalar_tensor_tensor(out=wk2[:], in0=iota[:, hs], scalar=fl, in1=wk1[:], op0=ALU.mult, op1=ALU.add), reads=[tio, tsc, twk], writes=[twk])
            P.op("dve", lambda e: e.tensor_copy(out=tiw[:], in_=wk2[:]), reads=[twk], writes=[twk])
            P.op("dve", lambda e: e.tensor_copy(out=wk1[:], in_=tiw[:]), reads=[twk], writes=[twk])
            P.op("dve", lambda e: e.tensor_tensor(out=wk2[:], in0=wk2[:], in1=wk1[:], op=ALU.subtract), reads=[twk], writes=[twk])
            P.op("act", lambda e, hs=hs: e.activation(out=sn[tb][:, hs], in_=wk2[:], func=AF.Sin, scale=TWO_PI), reads=[twk], writes=[ttab[tb]])
            P.op("act", lambda e: e.activation(out=wk1[:], in_=wk2[:], func=AF.Copy, scale=-1.0), reads=[twk], writes=[twk])
            P.op("dve", lambda e: e.tensor_tensor(out=wk1[:], in0=wk1[:], in1=wk2[:], op=ALU.min), reads=[twk], writes=[twk])
            P.op("act", lambda e, hs=hs: e.activation(out=cs[tb][:, hs], in_=wk1[:], func=AF.Sin, scale=TWO_PI, bias=cst[:, 0:1]), reads=[twk, tcst], writes=[ttab[tb]])
            P.op("act", lambda e, hs=hs: e.activation(out=snb[tb][:, hs], in_=sn[tb][:, hs], func=AF.Copy), reads=[ttab[tb]], writes=[ttab[tb]])
            P.op("act", lambda e, hs=hs: e.activation(out=csb[tb][:, hs], in_=cs[tb][:, hs], func=AF.Copy), reads=[ttab[tb]], writes=[ttab[tb]])
        P.op("dve", lambda e: e.tensor_scalar(out=rtab[tb][:], in0=ones[:], scalar1=rk, scalar2=None, op0=ALU.mult), reads=[tones, tsc, ttab[tb]], writes=[ttab[tb]])

    if npairs > 0:
        gen_tables(0)
    tasks = []
    g = 0
    for k in range(npairs):
        row0 = k * 32
        tb = k % 2
        for s in range(n_seq):
            c0 = s * S
            ub = (k * n_seq + s) % NU
            gb = (k * n_seq + s) % 2
            for qi in range(nq):
                t0 = qi * HB
                tsl = slice(t0, t0 + HB)
                b2 = g % 2
                bz = g % NZb
                bzp = (g - 1) % NZb
                g += 1

                def p0(k=k, s=s, qi=qi, ub=ub, row0=row0, c0=c0):
                    if qi == 0:
                        P.dma("sp", lambda e: e.dma_start(out=uwin[ub][:], in_=uT[row0:row0 + 32, c0:c0 + S]), writes=[tuw[ub]])
                    tpp = n_seq * nq
                    ti = s * nq + qi
                    assert tpp >= 8, "table double-buffering needs >= 8 tasks per pair"
                    if ti == 7 and k + 1 < npairs:
                        gen_tables(k + 1)

                def p1(k=k, ub=ub, b2=b2, tsl=tsl):
                    P.op("pe", lambda e: e.matmul(out=pbr[b2][:], lhsT=BT[:, k, 0, :], rhs=uwin[ub][:, tsl], start=True, stop=True),
                         reads=[tBT, tuw[ub]], writes=[tpb[b2]])
                    P.op("pe", lambda e: e.matmul(out=pbi[b2][:], lhsT=BT[:, k, 1, :], rhs=uwin[ub][:, tsl], start=True, stop=True),
                         reads=[tBT, tuw[ub]], writes=[tpb[b2]])

                def p2(b2=b2):
                    P.op("act", lambda e: e.activation(out=bsb[b2][0][:], in_=pbr[b2][:], func=AF.Copy), reads=[tpb[b2]], writes=[tbsb[b2]])
                    P.op("act", lambda e: e.activation(out=bsb[b2][1][:], in_=pbi[b2][:], func=AF.Copy), reads=[tpb[b2]], writes=[tbsb[b2]])

                def p3(b2=b2, tb=tb, tsl=tsl):
                    rd = [ttab[tb], tbsb[b2]]
                    P.op("dve", lambda e: e.tensor_tensor(out=A[b2][0][:], in0=cs[tb][:, tsl], in1=bsb[b2][0][:], op=ALU.mult), reads=rd, writes=[tA01[b2]])
                    P.op("dve", lambda e: e.tensor_tensor(out=A[b2][1][:], in0=sn[tb][:, tsl], in1=bsb[b2][1][:], op=ALU.mult), reads=rd, writes=[tA01[b2]])
                    P.op("pool", lambda e: e.tensor_tensor(out=A[b2][2][:], in0=cs[tb][:, tsl], in1=bsb[b2][1][:], op=ALU.mult), reads=rd, writes=[tA23[b2]])
                    P.op("pool", lambda e: e.tensor_tensor(out=A[b2][3][:], in0=sn[tb][:, tsl], in1=bsb[b2][0][:], op=ALU.mult), reads=rd, writes=[tA23[b2]])

                def p4(b2=b2):
                    P.op("pool", lambda e: e.tensor_tensor(out=W[b2][0][:], in0=A[b2][0][:], in1=A[b2][1][:], op=ALU.add), reads=[tA01[b2]], writes=[tW[b2]])
                    P.op("pool", lambda e: e.tensor_tensor(out=W[b2][1][:], in0=A[b2][2][:], in1=A[b2][3][:], op=ALU.subtract), reads=[tA23[b2]], writes=[tW[b2]])

                def p5(b2=b2, bz=bz, bzp=bzp, tb=tb, qi=qi):
                    for j in range(2):
                        if qi == 0:
                            init, rd = zc[:, 0:1], [tW[b2], ttab[tb], tones]
                        else:
                            init, rd = Z[bzp][j][:, HB - 1:HB], [tW[b2], ttab[tb], tZ[bzp]]
                        P.op("dve", lambda e, j=j, init=init: e.tensor_tensor_scan(out=Z[bz][j][:], data0=rtab[tb][:], data1=W[b2][j][:], initial=init,
                                                                                   op0=ALU.mult, op1=ALU.add), reads=rd, writes=[tZ[bz]])

                def p6(b2=b2, bz=bz):
                    for j in range(2):
                        P.op("act", lambda e, j=j: e.activation(out=Zb[b2][j][:], in_=Z[bz][j][:], func=AF.Copy), reads=[tZ[bz]], writes=[tZb[b2]])

                def p7(b2=b2, tb=tb, tsl=tsl):
                    rd = [ttab[tb], tZb[b2]]
                    P.op("dve", lambda e: e.tensor_tensor(out=Bq[b2][0][:], in0=csb[tb][:, tsl], in1=Zb[b2][0][:], op=ALU.mult), reads=rd, writes=[tB[b2]])
                    P.op("dve", lambda e: e.tensor_tensor(out=Bq[b2][1][:], in0=snb[tb][:, tsl], in1=Zb[b2][1][:], op=ALU.mult), reads=rd, writes=[tB[b2]])
                    P.op("pool", lambda e: e.tensor_tensor(out=Bq[b2][2][:], in0=snb[tb][:, tsl], in1=Zb[b2][0][:], op=ALU.mult), reads=rd, writes=[tB2[b2]])
                    P.op("pool", lambda e: e.tensor_tensor(out=Bq[b2][3][:], in0=csb[tb][:, tsl], in1=Zb[b2][1][:], op=ALU.mult), reads=rd, writes=[tB2[b2]])

                def p8(b2=b2, k=k):
                    for i4, ci in enumerate((0, 2, 1, 1)):
                        P.op("pe", lambda e, i4=i4, ci=ci: e.matmul(out=py[b2][:], lhsT=CT[:, k, ci, :], rhs=Bq[b2][i4][:], start=(i4 == 0), stop=(i4 == 3)),
                             reads=[tCT, tB[b2], tB2[b2]], writes=[tpy[b2]])

                def p9(b2=b2, ub=ub, gb=gb, tsl=tsl, k=k, qi=qi, row0=row0, c0=c0):
                    P.op("dve", lambda e: e.scalar_tensor_tensor(out=ytmp[b2][:], in0=uwin[ub][:, tsl], scalar=dwin[:, k:k + 1], in1=py[b2][0:32, :],
                                                                 op0=ALU.mult, op1=ALU.add), reads=[tuw[ub], tdw, tpy[b2]], writes=[tyt[b2]])
                    P.op("act", lambda e: e.activation(out=gout[gb][:, tsl], in_=ytmp[b2][:], func=AF.Gelu), reads=[tyt[b2]], writes=[tgo[gb]])
                    if qi == nq - 1:
                        to = P.tile()
                        P.dma("act", lambda e: e.dma_start(out=gT[row0:row0 + 32, c0:c0 + S], in_=gout[gb][:]), reads=[tgo[gb]], writes=[to])
                        outs.append(to)

                tasks.append([p0, p1, p2, p3, p4, p5, p6, p7, p8, p9])
    run_pipeline(tasks, [0, 1, 2, 3, 4, 5, 6, 7, 8, 9])
    if dbg is not None:
        for nm, ap_ in dbg.items():
            src = {"CT": CT, "BT": BT, "sn": sn[(npairs - 1) % 2], "cs": cs[(npairs - 1) % 2], "r": sc["r"], "qre": sc["qre"], "qim": sc["qim"], "fhi": sc["fhi"], "flo": sc["flo"]}[nm]
            tl = {"CT": tCT, "BT": tBT, "sn": ttab[(npairs - 1) % 2], "cs": ttab[(npairs - 1) % 2]}.get(nm, tsc)
            to = P.tile()
            P.dma("sp", lambda e, ap_=ap_, src=src: e.dma_start(out=ap_, in_=src[:]), reads=[tl], writes=[to])
            outs.append(to)
    P.finish(outs)


def stage_a3(nc, h_in, h_out, gT, wglu_ap, n_tok):
    P = Prog(nc)
    wg = P.sbuf("wg", [128, 8, 2048], BF16)
    twg = P.tile()
    load_weight_bf16(P, wg, twg, wglu_ap, 8)
    gb = [P.sbuf(f"gb{i}", [128, 8, 512], BF16) for i in range(2)]
    tgb = P.tiles(2)
    xt = [P.sbuf(f"xt{i}", [128, 4, D], F32) for i in range(2)]
    txt = P.tiles(2)
    pv = [P.psum(f"pv{i}", [128, 512]) for i in range(2)]
    tpv = P.tiles(2)
    pg = [P.psum(f"pg{i}", [128, 512]) for i in range(2)]
    tpg = P.tiles(2)
    sg = [P.sbuf(f"sg{i}", [128, 512], F32) for i in range(2)]
    tsg = P.tiles(2)
    outs = []
    k = 0
    for u in range(n_tok // 512):
        r0 = u * 512
        ob = u % 2
        P.dma("sp", lambda e, ob=ob, r0=r0: e.dma_start(out=gb[ob][:], in_=gT[:, r0:r0 + 512].rearrange("(c p) n -> p c n", p=128)), writes=[tgb[ob]])
        P.dma("sp", lambda e, ob=ob, r0=r0: e.dma_start(out=xt[ob][:], in_=h_in[r0:r0 + 512, :].rearrange("(t p) d -> p t d", p=128)), writes=[txt[ob]])
        for t in range(4):
            for nb in range(2):
                b = k % 2
                k += 1
                for c in range(8):
                    P.op("pe", lambda e, c=c, b=b, t=t, nb=nb, ob=ob: e.matmul(out=pv[b][:], lhsT=gb[ob][:, c, t * 128:(t + 1) * 128], rhs=wg[:, c, nb * 512:(nb + 1) * 512],
                                                                               start=(c == 0), stop=(c == 7)), reads=[tgb[ob], twg], writes=[tpv[b]])
                for c in range(8):
                    P.op("pe", lambda e, c=c, b=b, t=t, nb=nb, ob=ob: e.matmul(out=pg[b][:], lhsT=gb[ob][:, c, t * 128:(t + 1) * 128], rhs=wg[:, c, 1024 + nb * 512:1024 + (nb + 1) * 512],
                                                                               start=(c == 0), stop=(c == 7)), reads=[tgb[ob], twg], writes=[tpg[b]])
                P.op("act", lambda e, b=b: e.activation(out=sg[b][:], in_=pg[b][:], func=AF.Sigmoid), reads=[tpg[b]], writes=[tsg[b]])
                P.op("dve", lambda e, b=b: e.tensor_tensor(out=sg[b][:], in0=sg[b][:], in1=pv[b][:], op=ALU.mult), reads=[tsg[b], tpv[b]], writes=[tsg[b]])
                P.op("pool", lambda e, b=b, t=t, nb=nb, ob=ob: e.tensor_tensor(out=xt[ob][:, t, nb * 512:(nb + 1) * 512], in0=xt[ob][:, t, nb * 512:(nb + 1) * 512], in1=sg[b][:], op=ALU.add),
                     reads=[tsg[b], txt[ob]], writes=[txt[ob]])
        to = P.tile()
        P.dma("act", lambda e, ob=ob, r0=r0: e.dma_start(out=h_out[r0:r0 + 512, :].rearrange("(t p) d -> p t d", p=128), in_=xt[ob][:]), reads=[txt[ob]], writes=[to])
        outs.append(to)
    P.finish(outs)


N_CORES = 8
SEQ = 2048
N_SEQ = 4
NT = N_SEQ * SEQ

_PARAMS = [
    ("a_norm", [1, 1024]), ("a_w_in", [1, 1024, 1024]), ("a_lam_re", [1, 64, 64]), ("a_lam_im", [1, 64, 64]),
    ("a_b_re", [1, 64, 64, 16]), ("a_b_im", [1, 64, 64, 16]), ("a_c_re", [1, 64, 16, 64]), ("a_c_im", [1, 64, 16, 64]),
    ("a_d", [1, 1024]), ("a_log_dt", [1, 64]), ("a_w_glu", [1, 1024, 2048]), ("kv_norm", [1024]), ("w_kv", [1024, 2048]),
    ("k_norm", [64]), ("b_norm", [1, 1024]), ("b_w_q", [1, 1024, 1024]), ("b_q_norm", [1, 64]), ("b_w_o", [1, 1024, 1024]),
    ("ffn_norm", [2, 1024]), ("ffn_w_up", [2, 1024, 5632]), ("ffn_conv_w", [2, 3, 2816]), ("ffn_conv_b", [2, 2816]),
    ("ffn_w_down", [2, 2816, 1024]),
]


def build_program(N_SEQ=N_SEQ, SEQ=SEQ, debug=False):
    NT = N_SEQ * SEQ
    nc = bass.Bass("TRN2", target_bir_lowering=False)
    x = nc.dram_tensor("x", [NT, D], F32, kind="ExternalInput").ap()
    prm = {n: nc.dram_tensor(n, s, F32, kind="ExternalInput").ap() for n, s in _PARAMS}
    ident = nc.dram_tensor("c_ident", [128, 128], F32, kind="ExternalInput").ap()
    bones = nc.dram_tensor("c_bones", [128, 128], F32, kind="ExternalInput").ap()
    maskb = nc.dram_tensor("c_maskb", [128, 128], F32, kind="ExternalInput").ap()
    iota = nc.dram_tensor("c_iota", [128, 2048], F32, kind="ExternalInput").ap()
    out = nc.dram_tensor("out", [NT, D], F32, kind="ExternalOutput").ap()
    kd = "ExternalOutput" if debug else "Internal"
    h1 = nc.dram_tensor("h1", [NT, D], F32, kind=kd).ap()
    h2 = nc.dram_tensor("h2", [NT, D], F32, kind=kd).ap()
    h3 = nc.dram_tensor("h3", [NT, D], F32, kind=kd).ap()
    uT = nc.dram_tensor("uT", [D, NT], BF16).ap()
    gT = nc.dram_tensor("gT", [D, NT], BF16).ap()
    kT = nc.dram_tensor("kT", [D, NT], BF16).ap()
    qT = nc.dram_tensor("qT", [D, NT], BF16).ap()
    vr = nc.dram_tensor("vr", [NT, D], BF16).ap()
    stage_a1(nc, x, uT, prm["a_norm"][0], prm["a_w_in"][0], ident, NT)
    stage_a2(nc, uT, gT, prm["a_lam_re"][0], prm["a_lam_im"][0], prm["a_b_re"][0], prm["a_b_im"][0], prm["a_c_re"][0], prm["a_c_im"][0],
             prm["a_d"][0], prm["a_log_dt"][0], iota, N_SEQ, SEQ)
    stage_a3(nc, x, h1, gT, prm["a_w_glu"][0], NT)
    stage_ffn(nc, h1, h2, prm["ffn_norm"][0], prm["ffn_w_up"][0], prm["ffn_conv_w"][0], prm["ffn_conv_b"][0],
              prm["ffn_w_down"][0], ident, N_SEQ, SEQ)
    stage_kvq(nc, h2, kT, qT, vr, prm["kv_norm"], prm["w_kv"], prm["k_norm"], prm["b_norm"][0], prm["b_w_q"][0], prm["b_q_norm"][0],
              ident, bones, N_SEQ, SEQ)
    stage_att(nc, h2, h3, qT, kT, vr, prm["b_w_o"][0], ident, maskb, N_SEQ, SEQ)
    stage_ffn(nc, h3, out, prm["ffn_norm"][1], prm["ffn_w_up"][1], prm["ffn_conv_w"][1], prm["ffn_conv_b"][1],
              prm["ffn_w_down"][1], ident, N_SEQ, SEQ)
    return nc


def kernel(**inputs):
    x = np.ascontiguousarray(np.asarray(inputs["x"], dtype=np.float32))
    nc = build_program()
    p = np.arange(128)
    consts = {
        "c_ident": np.eye(128, dtype=np.float32),
        "c_bones": (p[:, None] // 64 == p[None, :] // 64).astype(np.float32),
        "c_maskb": np.where(p[None, :] + p[:, None] >= 128, 0.0, -30000.0).astype(np.float32),
        "c_iota": np.ascontiguousarray(np.tile(np.arange(2048, dtype=np.float32), (128, 1))),
    }
    params = {n: np.ascontiguousarray(np.asarray(inputs[n], dtype=np.float32)).reshape(s) for n, s in _PARAMS}
    in_maps = []
    for c in range(N_CORES):
        m = {"x": x[c * N_SEQ:(c + 1) * N_SEQ].reshape(NT, D)}
        m.update(params)
        m.update(consts)
        in_maps.append(m)
    res = run_bass_kernel_spmd(nc, in_maps, core_ids=list(range(N_CORES)))
    outs = [np.asarray(r["out"], dtype=np.float32).reshape(N_SEQ, SEQ, D) for r in res.results]
    return np.concatenate(outs, axis=0)
```

```python
import contextlib
import numpy as np
import concourse.bass as bass
import concourse.mybir as mybir
from concourse.bass_utils import run_bass_kernel_spmd

F32 = mybir.dt.float32
BF16 = mybir.dt.bfloat16
AF = mybir.ActivationFunctionType
ALU = mybir.AluOpType
AX = mybir.AxisListType

ENGS = ("pe", "act", "dve", "pool", "sp")
N_DMA_SEMS = 4

D = 1024
DFF = 2816
NFT = DFF // 128
EPS = 1e-6


class T:
    __slots__ = ("name", "writes", "reads")

    def __init__(self, name="t"):
        self.name = name
        self.writes = {}
        self.reads = {}


class Prog:
    _stage = 0

    def __init__(self, nc):
        self.nc = nc
        Prog._stage += 1
        self.sid = Prog._stage
        self.stack = contextlib.ExitStack()
        self.ops = {e: [] for e in ENGS}
        self.known = {e: {} for e in ENGS}
        self.ndma = {e: 0 for e in ENGS}
        self.cnt = {e: 0 for e in ENGS}
        self.milestones = {e: set() for e in ENGS}
        self._n = 0

    def sbuf(self, name, shape, dtype):
        return self.stack.enter_context(self.nc.sbuf_tensor(f"s{self.sid}_{name}", list(shape), dtype))

    def psum(self, name, shape, dtype=F32):
        return self.stack.enter_context(self.nc.psum_tensor(f"s{self.sid}_{name}", list(shape), dtype))

    def tile(self, name=None):
        return T(name or "t")

    def tiles(self, n):
        return [T() for _ in range(n)]

    def _collect(self, eng, reads, writes):
        need = {}
        for t in reads:
            for k, v in t.writes.items():
                if need.get(k, 0) < v:
                    need[k] = v
        for t in writes:
            for d in (t.writes, t.reads):
                for k, v in d.items():
                    if need.get(k, 0) < v:
                        need[k] = v
        waits = []
        kn = self.known[eng]
        for k, v in need.items():
            if k == ("e", eng) and eng == "pe":
                continue
            if kn.get(k, 0) >= v:
                continue
            kn[k] = v
            waits.append((k, v))
            if k[0] == "e":
                self.milestones[k[1]].add(v)
        return waits

    def op(self, eng, fn, reads=(), writes=()):
        waits = self._collect(eng, reads, writes)
        self.cnt[eng] += 1
        idx = self.cnt[eng]
        key = ("e", eng)
        self.ops[eng].append(dict(kind="op", fn=fn, waits=waits, idx=idx))
        for t in reads:
            if t.reads.get(key, 0) < idx:
                t.reads[key] = idx
        for t in writes:
            t.writes = {key: idx}
            t.reads = {}

    def dma(self, eng, fn, reads=(), writes=()):
        i = self.ndma[eng]
        self.ndma[eng] += 1
        slot = i % N_DMA_SEMS
        gen = i // N_DMA_SEMS
        key = ("d", eng, slot)
        waits = self._collect(eng, reads, writes)
        kn = self.known[eng]
        if gen > 0 and kn.get(key, 0) < 16 * gen:
            kn[key] = 16 * gen
            waits.append((key, 16 * gen))
        val = 16 * (gen + 1)
        self.ops[eng].append(dict(kind="dma", fn=fn, waits=waits, key=key))
        for t in reads:
            if t.reads.get(key, 0) < val:
                t.reads[key] = val
        for t in writes:
            t.writes = {key: val}
            t.reads = {}

    def wait_all(self, eng, tiles):
        waits = self._collect(eng, tiles, ())
        self.ops[eng].append(dict(kind="wait", waits=waits))

    def finish(self, out_tiles):
        self.wait_all("sp", out_tiles)
        nc = self.nc
        with nc.cleanup_on_exit():
            sems = {}
            for e in ENGS:
                if self.milestones[e]:
                    sems[("e", e)] = nc.alloc_semaphore(f"p{self.sid}_{e}")
                for s in range(min(N_DMA_SEMS, self.ndma[e])):
                    sems[("d", e, s)] = nc.alloc_semaphore(f"d{self.sid}_{e}_{s}")
            mmap = {e: {v: i + 1 for i, v in enumerate(sorted(self.milestones[e]))} for e in ENGS}

            def replay(e):
                def body(engine):
                    for o in self.ops[e]:
                        for k, v in o["waits"]:
                            if k[0] == "e":
                                v = mmap[k[1]][v]
                            engine.wait_ge(sems[k], v)
                        if o["kind"] == "op":
                            ins = o["fn"](engine)
                            if o["idx"] in mmap[e]:
                                ins.then_inc(sems[("e", e)], 1)
                        elif o["kind"] == "dma":
                            ins = o["fn"](engine)
                            ins.then_inc(sems[o["key"]], 16)
                return body

            with nc.Block() as block:
                block.tensor(replay("pe"))
                block.scalar(replay("act"))
                block.vector(replay("dve"))
                block.gpsimd(replay("pool"))
                block.sync(replay("sp"))
            nc.all_engine_barrier()
        self.stack.close()


def run_pipeline(tasks, offsets):
    nph = len(offsets)
    for step in range(len(tasks) + max(offsets) + 1):
        for p in reversed(range(nph)):
            t = step - offsets[p]
            if 0 <= t < len(tasks) and tasks[t][p] is not None:
                tasks[t][p]()


def load_weight_bf16(P, dst, tdst, w_ap, kchunks, gain_sb=None, tgain=None, eng="dve"):
    n = w_ap.shape[-1]
    src = w_ap.rearrange("(c p) n -> p c n", p=128)
    step = max(1, 4096 // n)
    for c0 in range(0, kchunks, step):
        c1 = min(kchunks, c0 + step)
        P.dma("pool", lambda e, c0=c0, c1=c1: e.dma_start(out=dst[:, c0:c1, :], in_=src[:, c0:c1, :]), writes=[tdst])
    if gain_sb is not None:
        for c in range(kchunks):
            P.op(eng, lambda e, c=c: e.tensor_scalar(out=dst[:, c, :], in0=dst[:, c, :], scalar1=gain_sb[:, c:c + 1],
                                                      scalar2=None, op0=ALU.mult), reads=[tdst, tgain], writes=[tdst])


def load_gain(P, name, g_ap):
    g_sb = P.sbuf(name, [128, 8], F32)
    t = P.tile()
    P.dma("sp", lambda e: e.dma_start(out=g_sb[:], in_=g_ap.rearrange("(c p) -> p c", p=128), allow_slow_non_contiguous=True), writes=[t])
    return g_sb, t


class NormT:
    def __init__(self, P, ident, tident, n_xt=2, n_xnT=1, junk=None, tjunk=None):
        self.P = P
        self.ident, self.tident = ident, tident
        self.n_xt, self.n_xnT = n_xt, n_xnT
        self.xt = [P.sbuf(f"nt_xt{i}", [128, 4, D], F32) for i in range(n_xt)]
        self.txt = P.tiles(n_xt)
        if junk is None:
            self.junk = P.sbuf("nt_junk", [128, D], BF16)[:]
            self.tjunk = P.tile()
        else:
            self.junk, self.tjunk = junk, tjunk
        self.ss = P.sbuf("nt_ss", [128, 8], F32)
        self.tss = P.tile()
        self.xn = [P.sbuf(f"nt_xn{i}", [128, D], BF16) for i in range(4)]
        self.txn = P.tiles(4)
        self.pst = [P.psum(f"nt_pst{i}", [128, D], BF16) for i in range(2)]
        self.tpst = P.tiles(2)
        self.xnT = [P.sbuf(f"nt_xnT{i}", [128, 8, 512], BF16) for i in range(n_xnT)]
        self.txnT = P.tiles(n_xnT)
        self.k = 0
        self.n = 0
        self.n2 = 0
        self.kp = 0

    def part1(self, src_rows):
        P = self.P
        b = self.n % self.n_xt
        self.n += 1
        xt, txt = self.xt[b], self.txt[b]
        P.dma("sp", lambda e: e.dma_start(out=xt[:], in_=src_rows.rearrange("(t p) d -> p t d", p=128)), writes=[txt])
        ss, tss = self.ss, self.tss
        for t in range(4):
            xn, txn = self.xn[t], self.txn[t]
            self.k += 1
            col = (self.k % 4) * 2
            P.op("dve", lambda e, col=col: e.memset(ss[:, col:col + 1], 0.0), writes=[tss])
            P.op("act", lambda e, t=t, col=col: e.activation(out=self.junk, in_=xt[:, t, :], func=AF.Square,
                                                             accum_out=ss[:, col:col + 1]),
                 reads=[txt], writes=[self.tjunk, tss])
            P.op("dve", lambda e, col=col: e.tensor_scalar(out=ss[:, col + 1:col + 2], in0=ss[:, col:col + 1], scalar1=1.0 / D,
                                                           scalar2=EPS, op0=ALU.mult, op1=ALU.add), reads=[tss], writes=[tss])
            P.op("act", lambda e, col=col: e.activation(out=ss[:, col + 1:col + 2], in_=ss[:, col + 1:col + 2], func=AF.Sqrt),
                 reads=[tss], writes=[tss])
            P.op("dve", lambda e, col=col: e.reciprocal(out=ss[:, col + 1:col + 2], in_=ss[:, col + 1:col + 2]), reads=[tss], writes=[tss])
            P.op("act", lambda e, t=t, col=col, xn=xn: e.activation(out=xn[:], in_=xt[:, t, :], func=AF.Copy,
                                                                     scale=ss[:, col + 1:col + 2]),
                 reads=[txt, tss], writes=[txn])
        return xt, txt

    def part2(self):
        P = self.P
        b2 = self.n2 % self.n_xnT
        self.n2 += 1
        xnT, txnT = self.xnT[b2], self.txnT[b2]
        for t in range(4):
            xn, txn = self.xn[t], self.txn[t]
            kk = self.kp % 2
            self.kp += 1
            pst, tpst = self.pst[kk], self.tpst[kk]
            for c in range(8):
                P.op("pe", lambda e, c=c, xn=xn, pst=pst: e.transpose(out=pst[:, c * 128:(c + 1) * 128],
                                                                       in_=xn[:, c * 128:(c + 1) * 128], identity=self.ident[:]),
                     reads=[txn, self.tident], writes=[tpst])
            P.op("dve", lambda e, t=t, pst=pst, xnT=xnT: e.tensor_copy(out=xnT[:, :, t * 128:(t + 1) * 128],
                                                                       in_=pst[:].rearrange("p (c k) -> p c k", k=128)),
                 reads=[tpst], writes=[txnT])
        return xnT, txnT

    def run(self, src_rows):
        xt, txt = self.part1(src_rows)
        xnT, txnT = self.part2()
        return xt, txt, xnT, txnT


def load_ident(P, ident_ap):
    ident = P.sbuf("ident", [128, 128], BF16)
    t = P.tile()
    P.dma("pool", lambda e: e.dma_start(out=ident[:], in_=ident_ap), writes=[t])
    return ident, t


def stage_ffn(nc, h_in, h_out, g_ap, wup_ap, cw_ap, cb_ap, wdn_ap, ident_ap, n_seq, seq_len):
    P = Prog(nc)
    ident, tident = load_ident(P, ident_ap)
    g_sb, tg = load_gain(P, "g", g_ap)
    wup = P.sbuf("wup", [128, 8, 2 * DFF], BF16)
    twup = P.tile()
    load_weight_bf16(P, wup, twup, wup_ap, 8, g_sb, tg)
    wdn = P.sbuf("wdn", [128, NFT, D], BF16)
    twdn = P.tile()
    load_weight_bf16(P, wdn, twdn, wdn_ap, NFT)
    cw = P.sbuf("cw", [128, NFT, 3], F32)
    cb = P.sbuf("cb", [128, NFT], F32)
    tcw = P.tile()
    for j in range(3):
        P.dma("sp", lambda e, j=j: e.dma_start(out=cw[:, :, j], in_=cw_ap[j].rearrange("(f p) -> p f", p=128), allow_slow_non_contiguous=True), writes=[tcw])
    P.dma("sp", lambda e: e.dma_start(out=cb[:], in_=cb_ap.rearrange("(f p) -> p f", p=128), allow_slow_non_contiguous=True), writes=[tcw])
    gc = [P.sbuf(f"gc{i}", [128, 512], F32) for i in range(2)]
    tgc = P.tiles(2)
    nt = NormT(P, ident, tident, n_xt=1, n_xnT=2, junk=gc[1][:].bitcast(BF16), tjunk=tgc[1])
    halo = P.sbuf("halo", [128, NFT, 2], F32)
    thalo = P.tile()
    gbuf = [P.sbuf(f"gbuf{i}", [128, 514], F32) for i in range(2)]
    tgbuf = P.tiles(2)
    hid = P.sbuf("hid", [128, NFT, 512], BF16)
    thid = P.tile()
    pv = [P.psum(f"pv{i}", [128, 512]) for i in range(2)]
    tpv = P.tiles(2)
    pg = [P.psum(f"pg{i}", [128, 512]) for i in range(2)]
    tpg = P.tiles(2)
    po = [P.psum(f"po{i}", [128, 512]) for i in range(2)]
    tpo = P.tiles(2)
    xr = [P.sbuf(f"xr{i}", [128, D], F32) for i in range(1)] * 2
    txr = P.tiles(1) * 2
    outs = []
    nsup = seq_len // 512
    nblk = n_seq * nsup
    cnt = {"k": 0, "ko": 0, "kx": 0}
    xn_of = {}

    def phA1(u):
        r0 = u * 512
        nt.part1(h_in[r0:r0 + 512, :])

    def phA2(u):
        xn_of[u] = nt.part2()

    def phB(u):
        xnT, txnT = xn_of[u]
        if u % nsup == 0:
            P.op("pool", lambda e: e.memset(halo[:], 0.0), writes=[thalo])
        for ft in range(NFT):
            if ft == 2 and u + 1 < nblk:
                phA1(u + 1)
            b = cnt["k"] % 2
            cnt["k"] += 1
            for c in range(8):
                P.op("pe", lambda e, c=c, b=b, ft=ft: e.matmul(out=pv[b][:], lhsT=wup[:, c, ft * 128:(ft + 1) * 128], rhs=xnT[:, c, :],
                                                               start=(c == 0), stop=(c == 7)), reads=[twup, txnT], writes=[tpv[b]])
            for c in range(8):
                P.op("pe", lambda e, c=c, b=b, ft=ft: e.matmul(out=pg[b][:], lhsT=wup[:, c, DFF + ft * 128:DFF + (ft + 1) * 128], rhs=xnT[:, c, :],
                                                               start=(c == 0), stop=(c == 7)), reads=[twup, txnT], writes=[tpg[b]])
            gb, tgb, g2, tg2 = gbuf[b], tgbuf[b], gc[b], tgc[b]
            P.op("pool", lambda e, gb=gb, ft=ft: e.tensor_copy(out=gb[:, 0:2], in_=halo[:, ft, :]), reads=[thalo], writes=[tgb])
            P.op("act", lambda e, gb=gb, b=b: e.activation(out=gb[:, 2:514], in_=pg[b][:], func=AF.Copy), reads=[tpg[b]], writes=[tgb])
            P.op("pool", lambda e, gb=gb, ft=ft: e.tensor_copy(out=halo[:, ft, :], in_=gb[:, 512:514]), reads=[tgb], writes=[thalo])
            P.op("act", lambda e, gb=gb, g2=g2, ft=ft: e.activation(out=g2[:], in_=gb[:, 2:514], func=AF.Identity,
                                                                    scale=cw[:, ft, 2:3], bias=cb[:, ft:ft + 1]),
                 reads=[tgb, tcw], writes=[tg2])
            P.op("dve", lambda e, gb=gb, g2=g2, ft=ft: e.scalar_tensor_tensor(out=g2[:], in0=gb[:, 1:513], scalar=cw[:, ft, 1:2], in1=g2[:],
                                                                              op0=ALU.mult, op1=ALU.add), reads=[tgb, tcw, tg2], writes=[tg2])
            P.op("dve", lambda e, gb=gb, g2=g2, ft=ft: e.scalar_tensor_tensor(out=g2[:], in0=gb[:, 0:512], scalar=cw[:, ft, 0:1], in1=g2[:],
                                                                              op0=ALU.mult, op1=ALU.add), reads=[tgb, tcw, tg2], writes=[tg2])
            P.op("act", lambda e, g2=g2: e.activation(out=g2[:], in_=g2[:], func=AF.Silu), reads=[tg2], writes=[tg2])
            P.op("dve", lambda e, g2=g2, b=b, ft=ft: e.tensor_tensor(out=hid[:, ft, :], in0=g2[:], in1=pv[b][:], op=ALU.mult),
                 reads=[tg2, tpv[b]], writes=[thid])

    def phC(u):
        r0 = u * 512
        for t in range(4):
            xb = cnt["kx"] % 2
            cnt["kx"] += 1
            rr = r0 + t * 128
            P.dma("sp", lambda e, xb=xb, rr=rr: e.dma_start(out=xr[xb][:], in_=h_in[rr:rr + 128, :]), writes=[txr[xb]])
            for nb in range(2):
                b = cnt["ko"] % 2
                cnt["ko"] += 1
                for ft in range(NFT):
                    P.op("pe", lambda e, ft=ft, t=t, nb=nb, b=b: e.matmul(out=po[b][:], lhsT=hid[:, ft, t * 128:(t + 1) * 128],
                                                                          rhs=wdn[:, ft, nb * 512:(nb + 1) * 512],
                                                                          start=(ft == 0), stop=(ft == NFT - 1)),
                         reads=[thid, twdn], writes=[tpo[b]])
                P.op("dve", lambda e, nb=nb, b=b, xb=xb: e.tensor_tensor(out=xr[xb][:, nb * 512:(nb + 1) * 512], in0=po[b][:],
                                                                         in1=xr[xb][:, nb * 512:(nb + 1) * 512], op=ALU.add),
                     reads=[tpo[b], txr[xb]], writes=[txr[xb]])
            to = P.tile()
            P.dma("act", lambda e, xb=xb, rr=rr: e.dma_start(out=h_out[rr:rr + 128, :], in_=xr[xb][:]), reads=[txr[xb]], writes=[to])
            outs.append(to)

    phA1(0)
    phA2(0)
    for u in range(nblk):
        phB(u)
        if u + 1 < nblk:
            phA2(u + 1)
        phC(u)
    P.finish(outs)


def stage_kvq(nc, h_in, kT_rev, qT, v_rev, kvn_ap, wkv_ap, kn_ap, bn_ap, wq_ap, qn_ap, ident_ap, bones_ap, n_seq, S):
    P = Prog(nc)
    ident, tident = load_ident(P, ident_ap)
    bones = P.sbuf("bones", [128, 128], BF16)
    tbones = P.tile()
    P.dma("pool", lambda e: e.dma_start(out=bones[:], in_=bones_ap), writes=[tbones])
    gkv, tgkv = load_gain(P, "gkv", kvn_ap)
    gb, tgb = load_gain(P, "gb", bn_ap)
    wkv = P.sbuf("wkv", [128, 8, 2048], BF16)
    twkv = P.tile()
    load_weight_bf16(P, wkv, twkv, wkv_ap, 8, gkv, tgkv)
    wq = P.sbuf("wq", [128, 8, 1024], BF16)
    twq = P.tile()
    load_weight_bf16(P, wq, twq, wq_ap, 8, gb, tgb)
    hg = P.sbuf("hg", [128, 4], F32)
    thg = P.tile()
    for hf in range(2):
        P.dma("sp", lambda e, hf=hf: e.dma_start(out=hg[hf * 64:(hf + 1) * 64, 0:1], in_=kn_ap.rearrange("(p o) -> p o", o=1)), writes=[thg])
        P.dma("sp", lambda e, hf=hf: e.dma_start(out=hg[hf * 64:(hf + 1) * 64, 1:2], in_=qn_ap.rearrange("(p o) -> p o", o=1)), writes=[thg])
    P.op("dve", lambda e: e.tensor_scalar(out=hg[:, 1:2], in0=hg[:, 1:2], scalar1=0.125, scalar2=None, op0=ALU.mult), reads=[thg], writes=[thg])
    P.op("dve", lambda e: e.memset(hg[:, 2:3], EPS), reads=[thg], writes=[thg])
    nt = NormT(P, ident, tident)
    pk = [P.psum(f"pk{i}", [128, 512]) for i in range(2)]
    tpk = P.tiles(2)
    pss = [P.psum(f"pss{i}", [128, 512]) for i in range(2)]
    tpss = P.tiles(2)
    sq = [P.sbuf(f"sq{i}", [128, 512], BF16) for i in range(2)]
    tsq = P.tiles(2)
    rs = [P.sbuf(f"rs{i}", [128, 512], F32) for i in range(2)]
    trs = P.tiles(2)
    kbuf = [P.sbuf(f"kbuf{i}", [128, 8, 512], BF16) for i in range(2)]
    tkbuf = P.tiles(2)
    qbuf = [P.sbuf(f"qbuf{i}", [128, 8, 512], BF16) for i in range(2)]
    tqbuf = P.tiles(2)
    vbuf = [P.sbuf(f"vbuf{i}", [128, 4, 1024], BF16) for i in range(2)]
    tvbuf = P.tiles(2)
    xrev = P.sbuf("xrev", [128, 8, 512], BF16)
    txrev = P.tile()
    outs = []
    nsup = S // 512
    k = 0
    for s in range(n_seq):
        for u in range(nsup):
            r0 = s * S + u * 512
            rr = s * S + S - 512 * (u + 1)
            ob = (s * nsup + u) % 2
            xt, txt, xnT, txnT = nt.run(h_in[r0:r0 + 512, :])
            for which in range(2):
                for ft in range(8):
                    b = k % 2
                    k += 1
                    for c in range(8):
                        if which == 0:
                            P.op("pe", lambda e, c=c, b=b, ft=ft: e.matmul(out=pk[b][:], lhsT=wkv[:, c, ft * 128:(ft + 1) * 128], rhs=xnT[:, c, :],
                                                                           start=(c == 0), stop=(c == 7)), reads=[twkv, txnT], writes=[tpk[b]])
                        else:
                            P.op("pe", lambda e, c=c, b=b, ft=ft: e.matmul(out=pk[b][:], lhsT=wq[:, c, ft * 128:(ft + 1) * 128], rhs=xnT[:, c, :],
                                                                           start=(c == 0), stop=(c == 7)), reads=[twq, txnT], writes=[tpk[b]])
                    P.op("act", lambda e, b=b: e.activation(out=sq[b][:], in_=pk[b][:], func=AF.Square), reads=[tpk[b]], writes=[tsq[b]])
                    P.op("pe", lambda e, b=b: e.matmul(out=pss[b][:], lhsT=bones[:], rhs=sq[b][:], start=True, stop=True),
                         reads=[tbones, tsq[b]], writes=[tpss[b]])
                    P.op("act", lambda e, b=b: e.activation(out=rs[b][:], in_=pss[b][:], func=AF.Sqrt, scale=1.0 / 64, bias=hg[:, 2:3]),
                         reads=[tpss[b], thg], writes=[trs[b]])
                    P.op("dve", lambda e, b=b: e.reciprocal(out=rs[b][:], in_=rs[b][:]), reads=[trs[b]], writes=[trs[b]])
                    if which == 0:
                        P.op("dve", lambda e, b=b, ft=ft, ob=ob: e.scalar_tensor_tensor(out=kbuf[ob][:, ft, ::-1], in0=pk[b][:], scalar=hg[:, 0:1], in1=rs[b][:],
                                                                                        op0=ALU.mult, op1=ALU.mult),
                             reads=[tpk[b], thg, trs[b]], writes=[tkbuf[ob]])
                    else:
                        P.op("dve", lambda e, b=b, ft=ft, ob=ob: e.scalar_tensor_tensor(out=qbuf[ob][:, ft, :], in0=pk[b][:], scalar=hg[:, 1:2], in1=rs[b][:],
                                                                                        op0=ALU.mult, op1=ALU.mult),
                             reads=[tpk[b], thg, trs[b]], writes=[tqbuf[ob]])
            P.op("pool", lambda e, xnT=xnT: e.tensor_copy(out=xrev[:, :, ::-1], in_=xnT[:]), reads=[txnT], writes=[txrev])
            for t in range(4):
                for nb in range(2):
                    b = k % 2
                    k += 1
                    for c in range(8):
                        P.op("pe", lambda e, c=c, b=b, t=t, nb=nb: e.matmul(out=pk[b][:], lhsT=xrev[:, c, t * 128:(t + 1) * 128],
                                                                            rhs=wkv[:, c, 1024 + nb * 512:1024 + (nb + 1) * 512],
                                                                            start=(c == 0), stop=(c == 7)), reads=[twkv, txrev], writes=[tpk[b]])
                    P.op("act", lambda e, b=b, t=t, nb=nb, ob=ob: e.activation(out=vbuf[ob][:, t, nb * 512:(nb + 1) * 512], in_=pk[b][:], func=AF.Copy),
                         reads=[tpk[b]], writes=[tvbuf[ob]])
            t1, t2, t3 = P.tile(), P.tile(), P.tile()
            P.dma("act", lambda e, ob=ob, rr=rr: e.dma_start(out=kT_rev[:, rr:rr + 512].rearrange("(f p) n -> p f n", p=128), in_=kbuf[ob][:]),
                  reads=[tkbuf[ob]], writes=[t1])
            P.dma("act", lambda e, ob=ob, r0=r0: e.dma_start(out=qT[:, r0:r0 + 512].rearrange("(f p) n -> p f n", p=128), in_=qbuf[ob][:]),
                  reads=[tqbuf[ob]], writes=[t2])
            P.dma("act", lambda e, ob=ob, rr=rr: e.dma_start(out=v_rev[rr:rr + 512, :].rearrange("(t p) d -> p t d", p=128), in_=vbuf[ob][:]),
                  reads=[tvbuf[ob]], writes=[t3])
            outs += [t1, t2, t3]
    P.finish(outs)


def stage_att(nc, h_in, h_out, qT, kT_rev, v_rev, wo_ap, ident_ap, maskb_ap, n_seq, S):
    P = Prog(nc)
    ident, tident = load_ident(P, ident_ap)
    maskb = P.sbuf("maskb", [128, 128], BF16)
    tmask = P.tile()
    P.dma("pool", lambda e: e.dma_start(out=maskb[:], in_=maskb_ap), writes=[tmask])
    wo = P.sbuf("wo", [128, 8, 1024], BF16)
    two = P.tile()
    load_weight_bf16(P, wo, two, wo_ap, 8)
    zeros = P.sbuf("zeros", [128, 512], F32)
    onec = P.sbuf("onec", [128, 1], F32)
    tz = P.tile()
    P.op("pool", lambda e: e.memset(zeros[:], 0.0), writes=[tz])
    P.op("pool", lambda e: e.memset(onec[:], 1.0), reads=[tz], writes=[tz])
    NB = S // 128
    qsb = P.sbuf("qsb", [128, 8, S], BF16)
    ksb = P.sbuf("ksb", [128, 8, S], BF16)
    vsb = P.sbuf("vsb", [128, NB, 1024], BF16)
    oT = P.sbuf("oT", [128, 8, S], BF16)
    tq, tk, tv, toT = P.tile(), P.tile(), P.tile(), P.tile()
    NZ, NS, NPB, NW, NWT, NWS, NPO = 3, 4, 5, 4, 2, 4, 3
    pz = [P.psum(f"pz{i}", [128, 512]) for i in range(NZ)]
    tpz = P.tiles(NZ)
    pwT = [P.psum(f"pwT{i}", [128, 1024], BF16) for i in range(NWT)]
    tpwT = P.tiles(NWT)
    po = [P.psum(f"po{i}", [128, 512]) for i in range(NPO)]
    tpo = P.tiles(NPO)
    pw = pz[0:2]
    tpw = tpz[0:2]
    ssb = [P.sbuf(f"ssb{i}", [128, 512], F32) for i in range(NS)]
    tssb = P.tiles(NS)
    pbuf = [P.sbuf(f"pbuf{i}", [128, 513], F32) for i in range(NPB)]
    tpbuf = P.tiles(NPB)
    wsb = [P.sbuf(f"wsb{i}", [128, 512], BF16) for i in range(NW)]
    twsb = P.tiles(NW)
    wTs = [P.sbuf(f"wTs{i}", [128, 512], BF16) for i in range(NWS)]
    twTs = P.tiles(NWS)
    xt = [P.sbuf(f"xt{i}", [128, D], F32) for i in range(2)]
    txt = P.tiles(2)
    outs = []
    kk = 0
    kx = 0
    gt = 0
    gpo = 0
    for s in range(n_seq):
        c0 = s * S
        for f in range(0, 8, 2):
            P.dma("sp", lambda e, f=f, c0=c0: e.dma_start(out=qsb[:, f:f + 2, :], in_=qT[f * 128:(f + 2) * 128, c0:c0 + S].rearrange("(f p) n -> p f n", p=128)), writes=[tq])
            P.dma("act", lambda e, f=f, c0=c0: e.dma_start(out=ksb[:, f:f + 2, :], in_=kT_rev[f * 128:(f + 2) * 128, c0:c0 + S].rearrange("(f p) n -> p f n", p=128)), writes=[tk])
        for b4 in range(0, NB, 4):
            P.dma("sp", lambda e, b4=b4, c0=c0: e.dma_start(out=vsb[:, b4:b4 + 4, :], in_=v_rev[c0 + b4 * 128:c0 + (b4 + 4) * 128, :].rearrange("(t p) d -> p t d", p=128)), writes=[tv])
        tasks = []
        for ft in range(8):
            for i in range(NB):
                q0 = i * 128
                kp0 = S - 128 - q0
                klen = q0 + 128
                ob = gpo % NPO
                gpo += 1
                ntile = (klen + 511) // 512
                for hp in range(2):
                    h = ft * 2 + hp
                    ps = slice(hp * 64, (hp + 1) * 64)
                    for n in range(ntile):
                        col0 = kp0 + 512 * n
                        wdt = min(512, S - col0)
                        nblk = wdt // 128
                        g = gt
                        gt += 1
                        bz, bs_, bp, bw, bwt, bws = g % NZ, g % NS, g % NPB, g % NW, g % NWT, g % NWS
                        bpp = (g - 1) % NPB

                        def ph0(bz=bz, ft=ft, ps=ps, q0=q0, col0=col0, wdt=wdt, n=n):
                            P.op("pe", lambda e: e.matmul(out=pz[bz][:, 0:wdt], lhsT=qsb[ps, ft, q0:q0 + 128], rhs=ksb[ps, ft, col0:col0 + wdt],
                                                          start=True, stop=(n != 0)), reads=[tq, tk], writes=[tpz[bz]])
                            if n == 0:
                                P.op("pe", lambda e: e.matmul(out=pz[bz][:, 0:128], lhsT=ident[:], rhs=maskb[:], start=False, stop=True),
                                     reads=[tident, tmask], writes=[tpz[bz]])

                        def ph1(bz=bz, bs_=bs_, wdt=wdt):
                            P.op("act", lambda e: e.activation(out=ssb[bs_][:, 0:wdt], in_=pz[bz][:, 0:wdt], func=AF.Sigmoid, scale=-1.0),
                                 reads=[tpz[bz]], writes=[tssb[bs_]])

                        def ph2(bs_=bs_, bp=bp, bpp=bpp, wdt=wdt, n=n):
                            if n == 0:
                                P.op("pool", lambda e: e.memset(pbuf[bp][:, 0:1], 1.0), writes=[tpbuf[bp]])
                                init = onec[:, 0:1]
                                rd = [tssb[bs_], tz, tpbuf[bp]]
                            else:
                                P.op("pool", lambda e: e.tensor_copy(out=pbuf[bp][:, 0:1], in_=pbuf[bpp][:, 512:513]), reads=[tpbuf[bpp]], writes=[tpbuf[bp]])
                                init = pbuf[bpp][:, 512:513]
                                rd = [tssb[bs_], tz, tpbuf[bp], tpbuf[bpp]]
                            P.op("dve", lambda e: e.tensor_tensor_scan(out=pbuf[bp][:, 1:1 + wdt], data0=ssb[bs_][:, 0:wdt], data1=zeros[:, 0:wdt],
                                                                       initial=init, op0=ALU.mult, op1=ALU.add), reads=rd, writes=[tpbuf[bp]])

                        def ph3(bp=bp, bw=bw, wdt=wdt):
                            P.op("dve", lambda e: e.tensor_tensor(out=wsb[bw][:, 0:wdt], in0=pbuf[bp][:, 0:wdt], in1=pbuf[bp][:, 1:1 + wdt], op=ALU.subtract),
                                 reads=[tpbuf[bp]], writes=[twsb[bw]])

                        def ph4(bw=bw, bwt=bwt, nblk=nblk):
                            for jb in range(nblk):
                                P.op("pe", lambda e, jb=jb: e.transpose(out=pwT[bwt][:, jb * 128:(jb + 1) * 128], in_=wsb[bw][:, jb * 128:(jb + 1) * 128], identity=ident[:]),
                                     reads=[twsb[bw], tident], writes=[tpwT[bwt]])

                        def ph5(bwt=bwt, bws=bws, wdt=wdt, g=g):
                            if True:
                                P.op("act", lambda e: e.activation(out=wTs[bws][:, 0:wdt], in_=pwT[bwt][:, 0:wdt], func=AF.Copy), reads=[tpwT[bwt]], writes=[twTs[bws]])
                            else:
                                P.op("dve", lambda e: e.tensor_copy(out=wTs[bws][:, 0:wdt], in_=pwT[bwt][:, 0:wdt]), reads=[tpwT[bwt]], writes=[twTs[bws]])

                        def ph6(bws=bws, nblk=nblk, col0=col0, h=h, ps=ps, ob=ob, n=n, ntile=ntile, hp=hp, ft=ft, q0=q0):
                            for jb in range(nblk):
                                blk = col0 // 128 + jb
                                first = (n == 0 and jb == 0)
                                last = (n == ntile - 1 and jb == nblk - 1)
                                P.op("pe", lambda e, jb=jb, blk=blk, first=first, last=last: e.matmul(
                                    out=po[ob][ps, 0:128], lhsT=vsb[:, blk, h * 64:(h + 1) * 64], rhs=wTs[bws][:, jb * 128:(jb + 1) * 128], start=first, stop=last),
                                    reads=[tv, twTs[bws]], writes=[tpo[ob]])
                            if hp == 1 and n == ntile - 1:
                                P.op("act", lambda e: e.activation(out=oT[:, ft, q0:q0 + 128], in_=po[ob][:, 0:128], func=AF.Copy), reads=[tpo[ob]], writes=[toT])

                        tasks.append([ph0, ph1, ph2, ph3, ph4, ph5, ph6])
        run_pipeline(tasks, ATT_OFFS)
        for t in range(NB):
            r0 = c0 + t * 128
            xb = kx % 2
            kx += 1
            P.dma("sp", lambda e, xb=xb, r0=r0: e.dma_start(out=xt[xb][:], in_=h_in[r0:r0 + 128, :]), writes=[txt[xb]])
            for nb in range(2):
                b = kk % 2
                kk += 1
                for ft in range(8):
                    P.op("pe", lambda e, b=b, ft=ft, t=t, nb=nb: e.matmul(out=pw[b][:], lhsT=oT[:, ft, t * 128:(t + 1) * 128], rhs=wo[:, ft, nb * 512:(nb + 1) * 512],
                                                                          start=(ft == 0), stop=(ft == 7)), reads=[toT, two], writes=[tpw[b]])
                P.op("dve", lambda e, b=b, xb=xb, nb=nb: e.tensor_tensor(out=xt[xb][:, nb * 512:(nb + 1) * 512], in0=pw[b][:], in1=xt[xb][:, nb * 512:(nb + 1) * 512], op=ALU.add),
                     reads=[tpw[b], txt[xb]], writes=[txt[xb]])
            to = P.tile()
            P.dma("act", lambda e, xb=xb, r0=r0: e.dma_start(out=h_out[r0:r0 + 128, :], in_=xt[xb][:]), reads=[txt[xb]], writes=[to])
            outs.append(to)
    P.finish(outs)


def stage_a1(nc, h_in, uT, g_ap, win_ap, ident_ap, n_tok):
    P = Prog(nc)
    ident, tident = load_ident(P, ident_ap)
    g_sb, tg = load_gain(P, "g", g_ap)
    win = P.sbuf("win", [128, 8, 1024], BF16)
    twin = P.tile()
    load_weight_bf16(P, win, twin, win_ap, 8, g_sb, tg)
    nt = NormT(P, ident, tident)
    pu = [P.psum(f"pu{i}", [128, 512]) for i in range(2)]
    tpu = P.tiles(2)
    ubuf = [P.sbuf(f"ubuf{i}", [128, 8, 512], BF16) for i in range(2)]
    tubuf = P.tiles(2)
    outs = []
    k = 0
    for u in range(n_tok // 512):
        r0 = u * 512
        ob = u % 2
        xt, txt, xnT, txnT = nt.run(h_in[r0:r0 + 512, :])
        for m in range(8):
            b = k % 2
            k += 1
            for c in range(8):
                P.op("pe", lambda e, c=c, b=b, m=m: e.matmul(out=pu[b][:], lhsT=win[:, c, m * 128:(m + 1) * 128], rhs=xnT[:, c, :],
                                                             start=(c == 0), stop=(c == 7)), reads=[twin, txnT], writes=[tpu[b]])
            if m % 2 == 0:
                P.op("act", lambda e, b=b, m=m, ob=ob: e.activation(out=ubuf[ob][:, m, :], in_=pu[b][:], func=AF.Copy), reads=[tpu[b]], writes=[tubuf[ob]])
            else:
                P.op("dve", lambda e, b=b, m=m, ob=ob: e.tensor_copy(out=ubuf[ob][:, m, :], in_=pu[b][:]), reads=[tpu[b]], writes=[tubuf[ob]])
        to = P.tile()
        P.dma("act", lambda e, ob=ob, r0=r0: e.dma_start(out=uT[:, r0:r0 + 512].rearrange("(m p) n -> p m n", p=128), in_=ubuf[ob][:]),
              reads=[tubuf[ob]], writes=[to])
        outs.append(to)
    P.finish(outs)


TWO_PI = 6.283185307179586
ATT_OFFS = [0, 2, 4, 6, 8, 9, 11]


def stage_a2(nc, uT, gT, lre_ap, lim_ap, bre_ap, bim_ap, cre_ap, cim_ap, d_ap, ldt_ap, iota_ap, n_seq, S, dbg_pairs=None, dbg=None, dbg_stop=9):
    P = Prog(nc)
    I32 = mybir.dt.int32
    NP = 32
    lr = P.sbuf("lr", [128, NP], F32)
    li = P.sbuf("li", [128, NP], F32)
    dt = P.sbuf("dt", [128, NP], F32)
    tprm = P.tile()
    for e_ in range(2):
        ps = slice(e_ * 64, (e_ + 1) * 64)
        P.dma("sp", lambda e, e_=e_, ps=ps: e.dma_start(out=lr[ps, :], in_=lre_ap.rearrange("(k e) n -> e n k", e=2)[e_], allow_slow_non_contiguous=True), writes=[tprm])
        P.dma("sp", lambda e, e_=e_, ps=ps: e.dma_start(out=li[ps, :], in_=lim_ap.rearrange("(k e) n -> e n k", e=2)[e_], allow_slow_non_contiguous=True), writes=[tprm])
        P.dma("sp", lambda e, e_=e_, ps=ps: e.dma_start(out=dt[ps, :], in_=ldt_ap.rearrange("(k e) -> e k", e=2)[e_].partition_broadcast(64), allow_slow_non_contiguous=True), writes=[tprm])
    cst = P.sbuf("cst", [128, 4], F32)
    tcst = P.tile()
    P.op("dve", lambda e: e.memset(cst[:, 0:1], TWO_PI / 4), writes=[tcst])
    P.op("dve", lambda e: e.memset(cst[:, 1:2], 0.0), reads=[tcst], writes=[tcst])
    sc = {}
    for nm in ["f", "fhi", "flo", "r", "t0", "t1", "t2", "t3", "sn", "cs", "ar", "ai", "qre", "qim", "nqim", "den"]:
        sc[nm] = P.sbuf("p_" + nm, [128, NP], F32)
    fhb = P.sbuf("p_fhb", [128, NP], BF16)
    tiq = P.sbuf("p_ti", [128, NP], I32)
    tsc = P.tile()

    def V(fn, eng="dve", extra=()):
        P.op(eng, fn, reads=[tprm, tsc, tcst] + list(extra), writes=[tsc])

    V(lambda e: e.activation(out=sc["t0"][:], in_=dt[:], func=AF.Exp), "act")
    V(lambda e: e.activation(out=sc["t1"][:], in_=sc["t0"][:], func=AF.Ln), "act")
    V(lambda e: e.tensor_tensor(out=sc["t1"][:], in0=dt[:], in1=sc["t1"][:], op=ALU.subtract))
    V(lambda e: e.tensor_scalar(out=sc["t1"][:], in0=sc["t1"][:], scalar1=1.0, scalar2=None, op0=ALU.add))
    V(lambda e: e.tensor_tensor(out=dt[:], in0=sc["t0"][:], in1=sc["t1"][:], op=ALU.mult))
    V(lambda e: e.tensor_tensor(out=sc["t0"][:], in0=li[:], in1=dt[:], op=ALU.mult))
    V(lambda e: e.tensor_scalar(out=sc["f"][:], in0=sc["t0"][:], scalar1=1.0 / TWO_PI, scalar2=None, op0=ALU.mult))
    V(lambda e: e.tensor_copy(out=fhb[:], in_=sc["f"][:]))
    V(lambda e: e.tensor_copy(out=sc["fhi"][:], in_=fhb[:]))
    V(lambda e: e.tensor_tensor(out=sc["flo"][:], in0=sc["f"][:], in1=sc["fhi"][:], op=ALU.subtract))
    V(lambda e: e.tensor_tensor(out=sc["t1"][:], in0=lr[:], in1=dt[:], op=ALU.mult))
    V(lambda e: e.activation(out=sc["t2"][:], in_=sc["t1"][:], func=AF.Exp), "act")
    V(lambda e: e.activation(out=sc["t3"][:], in_=sc["t2"][:], func=AF.Ln), "act")
    V(lambda e: e.tensor_tensor(out=sc["t3"][:], in0=sc["t1"][:], in1=sc["t3"][:], op=ALU.subtract))
    V(lambda e: e.tensor_scalar(out=sc["t3"][:], in0=sc["t3"][:], scalar1=1.0, scalar2=None, op0=ALU.add))
    V(lambda e: e.tensor_tensor(out=sc["r"][:], in0=sc["t2"][:], in1=sc["t3"][:], op=ALU.mult))
    V(lambda e: e.tensor_copy(out=tiq[:], in_=sc["f"][:]))
    V(lambda e: e.tensor_copy(out=sc["t3"][:], in_=tiq[:]))
    V(lambda e: e.tensor_tensor(out=sc["t2"][:], in0=sc["f"][:], in1=sc["t3"][:], op=ALU.subtract))
    V(lambda e: e.activation(out=sc["sn"][:], in_=sc["t2"][:], func=AF.Sin, scale=TWO_PI), "act")
    V(lambda e: e.tensor_scalar(out=sc["t3"][:], in0=sc["t2"][:], scalar1=-1.0, scalar2=None, op0=ALU.mult))
    V(lambda e: e.tensor_tensor(out=sc["t3"][:], in0=sc["t3"][:], in1=sc["t2"][:], op=ALU.min))
    V(lambda e: e.activation(out=sc["cs"][:], in_=sc["t3"][:], func=AF.Sin, scale=TWO_PI, bias=cst[:, 0:1]), "act")
    V(lambda e: e.tensor_tensor(out=sc["ar"][:], in0=sc["r"][:], in1=sc["cs"][:], op=ALU.mult))
    V(lambda e: e.tensor_tensor(out=sc["ai"][:], in0=sc["r"][:], in1=sc["sn"][:], op=ALU.mult))
    V(lambda e: e.tensor_scalar(out=sc["ar"][:], in0=sc["ar"][:], scalar1=-1.0, scalar2=None, op0=ALU.add))
    V(lambda e: e.tensor_tensor(out=sc["t0"][:], in0=lr[:], in1=lr[:], op=ALU.mult))
    V(lambda e: e.tensor_tensor(out=sc["t1"][:], in0=li[:], in1=li[:], op=ALU.mult))
    V(lambda e: e.tensor_tensor(out=sc["den"][:], in0=sc["t0"][:], in1=sc["t1"][:], op=ALU.add))
    V(lambda e: e.reciprocal(out=sc["den"][:], in_=sc["den"][:]))
    V(lambda e: e.tensor_tensor(out=sc["t0"][:], in0=sc["ar"][:], in1=lr[:], op=ALU.mult))
    V(lambda e: e.tensor_tensor(out=sc["t1"][:], in0=sc["ai"][:], in1=li[:], op=ALU.mult))
    V(lambda e: e.tensor_tensor(out=sc["t0"][:], in0=sc["t0"][:], in1=sc["t1"][:], op=ALU.add))
    V(lambda e: e.tensor_tensor(out=sc["qre"][:], in0=sc["t0"][:], in1=sc["den"][:], op=ALU.mult))
    V(lambda e: e.tensor_tensor(out=sc["t0"][:], in0=sc["ai"][:], in1=lr[:], op=ALU.mult))
    V(lambda e: e.tensor_tensor(out=sc["t1"][:], in0=sc["ar"][:], in1=li[:], op=ALU.mult))
    V(lambda e: e.tensor_tensor(out=sc["t0"][:], in0=sc["t0"][:], in1=sc["t1"][:], op=ALU.subtract))
    V(lambda e: e.tensor_tensor(out=sc["qim"][:], in0=sc["t0"][:], in1=sc["den"][:], op=ALU.mult))
    V(lambda e: e.tensor_scalar(out=sc["nqim"][:], in0=sc["qim"][:], scalar1=-1.0, scalar2=None, op0=ALU.mult))
    craw = P.sbuf("craw", [128, NP, 2, 16], F32)
    tcraw = P.tile()
    for e_ in range(2):
        ps = slice(e_ * 64, (e_ + 1) * 64)
        for k1 in range(NP):
            for ri, ap_ in enumerate((cre_ap, cim_ap)):
                P.dma("sp" if ri == 0 else "act", lambda e, e_=e_, ps=ps, k1=k1, ri=ri, ap_=ap_: e.dma_start(
                    out=craw[ps, k1, ri, :], in_=ap_[2 * k1 + e_].rearrange("h n -> n h"), allow_slow_non_contiguous=True),
                    writes=[tcraw])
    CT = P.sbuf("CT", [128, NP, 3, 128], BF16)
    tCT = P.tile()
    ctmp = P.sbuf("ctmp", [128, NP, 16], F32)
    ctmp2 = P.sbuf("ctmp2", [128, NP, 16], F32)
    tctmp = P.tile()
    P.op("pool", lambda e: e.memset(CT[:], 0.0), writes=[tCT])

    def bq(nm):
        return sc[nm][:].unsqueeze(2).to_broadcast([128, NP, 16])

    rd = [tcraw, tsc]
    P.op("dve", lambda e: e.tensor_tensor(out=ctmp[:], in0=craw[:, :, 0, :], in1=bq("qre"), op=ALU.mult), reads=rd, writes=[tctmp])
    P.op("dve", lambda e: e.tensor_tensor(out=ctmp2[:], in0=craw[:, :, 1, :], in1=bq("qim"), op=ALU.mult), reads=rd, writes=[tctmp])
    for e_ in range(2):
        ps = slice(e_ * 64, (e_ + 1) * 64)
        P.op("dve", lambda e, e_=e_, ps=ps: e.tensor_tensor(out=CT[ps, :, 0, e_ * 16:(e_ + 1) * 16], in0=ctmp[ps], in1=ctmp2[ps], op=ALU.subtract),
             reads=[tctmp], writes=[tCT])
    P.op("dve", lambda e: e.tensor_tensor(out=ctmp[:], in0=craw[:, :, 0, :], in1=bq("nqim"), op=ALU.mult), reads=rd + [tCT], writes=[tctmp])
    P.op("dve", lambda e: e.tensor_tensor(out=ctmp2[:], in0=craw[:, :, 1, :], in1=bq("qre"), op=ALU.mult), reads=rd, writes=[tctmp])
    for e_ in range(2):
        ps = slice(e_ * 64, (e_ + 1) * 64)
        P.op("dve", lambda e, e_=e_, ps=ps: e.tensor_tensor(out=CT[ps, :, 1, e_ * 16:(e_ + 1) * 16], in0=ctmp[ps], in1=ctmp2[ps], op=ALU.subtract),
             reads=[tctmp], writes=[tCT])
    BT = P.sbuf("BT", [32, NP, 2, 128], BF16)
    tBT = P.tile()
    P.op("pool", lambda e: e.memset(BT[:], 0.0), writes=[tBT])
    for e_ in range(2):
        for ri, ap_ in enumerate((bre_ap, bim_ap)):
            for k1 in range(NP):
                P.dma("pool", lambda e, e_=e_, ri=ri, ap_=ap_, k1=k1: e.dma_start(
                    out=BT[e_ * 16:(e_ + 1) * 16, k1, ri, e_ * 64:(e_ + 1) * 64], in_=ap_[2 * k1 + e_].rearrange("n h -> h n"),
                    allow_slow_non_contiguous=True), writes=[tBT])
    dwin = P.sbuf("dwin", [32, NP], F32)
    tdw = P.tile()
    P.dma("sp", lambda e: e.dma_start(out=dwin[:], in_=d_ap.rearrange("(k r) -> r k", r=32), allow_slow_non_contiguous=True), writes=[tdw])
    iota = P.sbuf("iota", [128, S], F32)
    tio = P.tile()
    P.dma("sp", lambda e: e.dma_start(out=iota[:], in_=iota_ap[:, 0:S]), writes=[tio])
    HB = 512
    nq = S // HB
    sn = [P.sbuf(f"sn{i}", [128, S], F32) for i in range(2)]
    cs = [P.sbuf(f"cs{i}", [128, S], F32) for i in range(2)]
    snb = [P.sbuf(f"snb{i}", [128, S], BF16) for i in range(2)]
    csb = [P.sbuf(f"csb{i}", [128, S], BF16) for i in range(2)]
    rtab = [P.sbuf(f"rtab{i}", [128, HB], F32) for i in range(2)]
    ttab = P.tiles(2)
    SH = S // 2
    wk1 = P.sbuf("wk1", [128, SH], F32)
    wk2 = P.sbuf("wk2", [128, SH], F32)
    tiw = P.sbuf("tiw", [128, SH], I32)
    twk = P.tile()
    ones = P.sbuf("ones", [128, HB], F32)
    zc = P.sbuf("zc", [128, 1], F32)
    tones = P.tile()
    P.op("dve", lambda e: e.memset(ones[:], 1.0), writes=[tones])
    P.op("dve", lambda e: e.memset(zc[:], 0.0), reads=[tones], writes=[tones])
    P.op("dve", lambda e: e.tensor_scalar(out=CT[:, :, 2, :], in0=CT[:, :, 0, :], scalar1=-1.0, scalar2=None, op0=ALU.mult), reads=[tCT], writes=[tCT])
    NU = 3
    uwin = [P.sbuf(f"uwin{i}", [32, S], BF16) for i in range(NU)]
    tuw = P.tiles(NU)
    pbr = [P.psum(f"pbr{i}", [128, HB]) for i in range(2)]
    pbi = [P.psum(f"pbi{i}", [128, HB]) for i in range(2)]
    tpb = P.tiles(2)
    py = [P.psum(f"py{i}", [128, HB]) for i in range(2)]
    tpy = P.tiles(2)
    bsb = [[P.sbuf(f"bsb{i}_{j}", [128, HB], F32) for j in range(2)] for i in range(2)]
    tbsb = P.tiles(2)
    A = [[P.sbuf(f"A{i}_{j}", [128, HB], F32) for j in range(4)] for i in range(2)]
    tA01 = P.tiles(2)
    tA23 = P.tiles(2)
    W = [[P.sbuf(f"W{i}_{j}", [128, HB], F32) for j in range(2)] for i in range(2)]
    tW = P.tiles(2)
    NZb = 3
    Z = [[P.sbuf(f"Z{i}_{j}", [128, HB], F32) for j in range(2)] for i in range(NZb)]
    tZ = P.tiles(NZb)
    Zb = [[P.sbuf(f"Zb{i}_{j}", [128, HB], BF16) for j in range(2)] for i in range(2)]
    tZb = P.tiles(2)
    Bq = [[P.sbuf(f"B{i}_{j}", [128, HB], BF16) for j in range(4)] for i in range(2)]
    tB = P.tiles(2)
    tB2 = P.tiles(2)
    ytmp = [P.sbuf(f"ytmp{i}", [32, HB], F32) for i in range(2)]
    tyt = P.tiles(2)
    gout = [P.sbuf(f"gout{i}", [32, S], BF16) for i in range(2)]
    tgo = P.tiles(2)
    outs = []
    npairs = NP if dbg_pairs is None else dbg_pairs

    def gen_tables(k):
        tb = k % 2
        fh, fl, rk = sc["fhi"][:, k:k + 1], sc["flo"][:, k:k + 1], sc["r"][:, k:k + 1]
        for hh in range(2):
            hs = slice(hh * SH, (hh + 1) * SH)
            P.op("dve", lambda e, hs=hs: e.tensor_scalar(out=wk1[:], in0=iota[:, hs], scalar1=fh, scalar2=None, op0=ALU.mult), reads=[tio, tsc], writes=[twk])
            P.op("dve", lambda e: e.tensor_copy(out=tiw[:], in_=wk1[:]), reads=[twk], writes=[twk])
            P.op("dve", lambda e: e.tensor_copy(out=wk2[:], in_=tiw[:]), reads=[twk], writes=[twk])
            P.op("dve", lambda e: e.tensor_tensor(out=wk1[:], in0=wk1[:], in1=wk2[:], op=ALU.subtract), reads=[twk], writes=[twk])
            P.op("dve", lambda e, hs=hs: e.scalar_tensor_tensor(out=wk2[:], in0=iota[:, hs], scalar=fl, in1=wk1[:], op0=ALU.mult, op1=ALU.add), reads=[tio, tsc, twk], writes=[twk])
            P.op("dve", lambda e: e.tensor_copy(out=tiw[:], in_=wk2[:]), reads=[twk], writes=[twk])
            P.op("dve", lambda e: e.tensor_copy(out=wk1[:], in_=tiw[:]), reads=[twk], writes=[twk])
            P.op("dve", lambda e: e.tensor_tensor(out=wk2[:], in0=wk2[:], in1=wk1[:], op=ALU.subtract), reads=[twk], writes=[twk])
            P.op("act", lambda e, hs=hs: e.activation(out=sn[tb][:, hs], in_=wk2[:], func=AF.Sin, scale=TWO_PI), reads=[twk], writes=[ttab[tb]])
            P.op("act", lambda e: e.activation(out=wk1[:], in_=wk2[:], func=AF.Copy, scale=-1.0), reads=[twk], writes=[twk])
            P.op("dve", lambda e: e.tensor_tensor(out=wk1[:], in0=wk1[:], in1=wk2[:], op=ALU.min), reads=[twk], writes=[twk])
            P.op("act", lambda e, hs=hs: e.activation(out=cs[tb][:, hs], in_=wk1[:], func=AF.Sin, scale=TWO_PI, bias=cst[:, 0:1]), reads=[twk, tcst], writes=[ttab[tb]])
            P.op("act", lambda e, hs=hs: e.activation(out=snb[tb][:, hs], in_=sn[tb][:, hs], func=AF.Copy), reads=[ttab[tb]], writes=[ttab[tb]])
            P.op("act", lambda e, hs=hs: e.activation(out=csb[tb][:, hs], in_=cs[tb][:, hs], func=AF.Copy), reads=[ttab[tb]], writes=[ttab[tb]])
        P.op("dve", lambda e: e.tensor_scalar(out=rtab[tb][:], in0=ones[:], scalar1=rk, scalar2=None, op0=ALU.mult), reads=[tones, tsc, ttab[tb]], writes=[ttab[tb]])

    if npairs > 0:
        gen_tables(0)
    tasks = []
    g = 0
    for k in range(npairs):
        row0 = k * 32
        tb = k % 2
        for s in range(n_seq):
            c0 = s * S
            ub = (k * n_seq + s) % NU
            gb = (k * n_seq + s) % 2
            for qi in range(nq):
                t0 = qi * HB
                tsl = slice(t0, t0 + HB)
                b2 = g % 2
                bz = g % NZb
                bzp = (g - 1) % NZb
                g += 1

                def p0(k=k, s=s, qi=qi, ub=ub, row0=row0, c0=c0):
                    if qi == 0:
                        P.dma("sp", lambda e: e.dma_start(out=uwin[ub][:], in_=uT[row0:row0 + 32, c0:c0 + S]), writes=[tuw[ub]])
                    tpp = n_seq * nq
                    ti = s * nq + qi
                    assert tpp >= 8, "table double-buffering needs >= 8 tasks per pair"
                    if ti == 7 and k + 1 < npairs:
                        gen_tables(k + 1)

                def p1(k=k, ub=ub, b2=b2, tsl=tsl):
                    P.op("pe", lambda e: e.matmul(out=pbr[b2][:], lhsT=BT[:, k, 0, :], rhs=uwin[ub][:, tsl], start=True, stop=True),
                         reads=[tBT, tuw[ub]], writes=[tpb[b2]])
                    P.op("pe", lambda e: e.matmul(out=pbi[b2][:], lhsT=BT[:, k, 1, :], rhs=uwin[ub][:, tsl], start=True, stop=True),
                         reads=[tBT, tuw[ub]], writes=[tpb[b2]])

                def p2(b2=b2):
                    P.op("act", lambda e: e.activation(out=bsb[b2][0][:], in_=pbr[b2][:], func=AF.Copy), reads=[tpb[b2]], writes=[tbsb[b2]])
                    P.op("act", lambda e: e.activation(out=bsb[b2][1][:], in_=pbi[b2][:], func=AF.Copy), reads=[tpb[b2]], writes=[tbsb[b2]])

                def p3(b2=b2, tb=tb, tsl=tsl):
                    rd = [ttab[tb], tbsb[b2]]
                    P.op("dve", lambda e: e.tensor_tensor(out=A[b2][0][:], in0=cs[tb][:, tsl], in1=bsb[b2][0][:], op=ALU.mult), reads=rd, writes=[tA01[b2]])
                    P.op("dve", lambda e: e.tensor_tensor(out=A[b2][1][:], in0=sn[tb][:, tsl], in1=bsb[b2][1][:], op=ALU.mult), reads=rd, writes=[tA01[b2]])
                    P.op("dve", lambda e: e.tensor_tensor(out=A[b2][2][:], in0=cs[tb][:, tsl], in1=bsb[b2][1][:], op=ALU.mult), reads=rd, writes=[tA23[b2]])
                    P.op("dve", lambda e: e.tensor_tensor(out=A[b2][3][:], in0=sn[tb][:, tsl], in1=bsb[b2][0][:], op=ALU.mult), reads=rd, writes=[tA23[b2]])

                def p4(b2=b2):
                    P.op("dve", lambda e: e.tensor_tensor(out=W[b2][0][:], in0=A[b2][0][:], in1=A[b2][1][:], op=ALU.add), reads=[tA01[b2]], writes=[tW[b2]])
                    P.op("dve", lambda e: e.tensor_tensor(out=W[b2][1][:], in0=A[b2][2][:], in1=A[b2][3][:], op=ALU.subtract), reads=[tA23[b2]], writes=[tW[b2]])

                def p5(b2=b2, bz=bz, bzp=bzp, tb=tb, qi=qi):
                    for j in range(2):
                        if qi == 0:
                            init, rd = zc[:, 0:1], [tW[b2], ttab[tb], tones]
                        else:
                            init, rd = Z[bzp][j][:, HB - 1:HB], [tW[b2], ttab[tb], tZ[bzp]]
                        P.op("dve", lambda e, j=j, init=init: e.tensor_tensor_scan(out=Z[bz][j][:], data0=rtab[tb][:], data1=W[b2][j][:], initial=init,
                                                                                   op0=ALU.mult, op1=ALU.add), reads=rd, writes=[tZ[bz]])

                def p6(b2=b2, bz=bz):
                    for j in range(2):
                        P.op("act", lambda e, j=j: e.activation(out=Zb[b2][j][:], in_=Z[bz][j][:], func=AF.Copy), reads=[tZ[bz]], writes=[tZb[b2]])

                def p7(b2=b2, tb=tb, tsl=tsl):
                    rd = [ttab[tb], tZb[b2]]
                    P.op("dve", lambda e: e.tensor_tensor(out=Bq[b2][0][:], in0=csb[tb][:, tsl], in1=Zb[b2][0][:], op=ALU.mult), reads=rd, writes=[tB[b2]])
                    P.op("dve", lambda e: e.tensor_tensor(out=Bq[b2][1][:], in0=snb[tb][:, tsl], in1=Zb[b2][1][:], op=ALU.mult), reads=rd, writes=[tB[b2]])
                    P.op("dve", lambda e: e.tensor_tensor(out=Bq[b2][2][:], in0=snb[tb][:, tsl], in1=Zb[b2][0][:], op=ALU.mult), reads=rd, writes=[tB2[b2]])
                    P.op("dve", lambda e: e.tensor_tensor(out=Bq[b2][3][:], in0=csb[tb][:, tsl], in1=Zb[b2][1][:], op=ALU.mult), reads=rd, writes=[tB2[b2]])

                def p8(b2=b2, k=k):
                    for i4, ci in enumerate((0, 2, 1, 1)):
                        P.op("pe", lambda e, i4=i4, ci=ci: e.matmul(out=py[b2][:], lhsT=CT[:, k, ci, :], rhs=Bq[b2][i4][:], start=(i4 == 0), stop=(i4 == 3)),
                             reads=[tCT, tB[b2], tB2[b2]], writes=[tpy[b2]])

                def p9(b2=b2, ub=ub, gb=gb, tsl=tsl, k=k, qi=qi, row0=row0, c0=c0):
                    P.op("dve", lambda e: e.scalar_tensor_tensor(out=ytmp[b2][:], in0=uwin[ub][:, tsl], scalar=dwin[:, k:k + 1], in1=py[b2][0:32, :],
                                                                 op0=ALU.mult, op1=ALU.add), reads=[tuw[ub], tdw, tpy[b2]], writes=[tyt[b2]])
                    P.op("act", lambda e: e.activation(out=gout[gb][:, tsl], in_=ytmp[b2][:], func=AF.Gelu), reads=[tyt[b2]], writes=[tgo[gb]])
                    if qi == nq - 1:
                        to = P.tile()
                        P.dma("act", lambda e: e.dma_start(out=gT[row0:row0 + 32, c0:c0 + S], in_=gout[gb][:]), reads=[tgo[gb]], writes=[to])
                        outs.append(to)

                tasks.append([p0, p1, p2, p3, p4, p5, p6, p7, p8, p9])
    run_pipeline(tasks, [0, 1, 2, 3, 4, 5, 6, 7, 8, 9])
    if dbg is not None:
        for nm, ap_ in dbg.items():
            src = {"CT": CT, "BT": BT, "sn": sn[(npairs - 1) % 2], "cs": cs[(npairs - 1) % 2], "r": sc["r"], "qre": sc["qre"], "qim": sc["qim"], "fhi": sc["fhi"], "flo": sc["flo"]}[nm]
            tl = {"CT": tCT, "BT": tBT, "sn": ttab[(npairs - 1) % 2], "cs": ttab[(npairs - 1) % 2]}.get(nm, tsc)
            to = P.tile()
            P.dma("sp", lambda e, ap_=ap_, src=src: e.dma_start(out=ap_, in_=src[:]), reads=[tl], writes=[to])
            outs.append(to)
    P.finish(outs)


def stage_a3(nc, h_in, h_out, gT, wglu_ap, n_tok):
    P = Prog(nc)
    wg = P.sbuf("wg", [128, 8, 2048], BF16)
    twg = P.tile()
    load_weight_bf16(P, wg, twg, wglu_ap, 8)
    gb = [P.sbuf(f"gb{i}", [128, 8, 512], BF16) for i in range(2)]
    tgb = P.tiles(2)
    xt = [P.sbuf(f"xt{i}", [128, 4, D], F32) for i in range(2)]
    txt = P.tiles(2)
    pv = [P.psum(f"pv{i}", [128, 512]) for i in range(2)]
    tpv = P.tiles(2)
    pg = [P.psum(f"pg{i}", [128, 512]) for i in range(2)]
    tpg = P.tiles(2)
    sg = [P.sbuf(f"sg{i}", [128, 512], F32) for i in range(2)]
    tsg = P.tiles(2)
    outs = []
    k = 0
    for u in range(n_tok // 512):
        r0 = u * 512
        ob = u % 2
        P.dma("sp", lambda e, ob=ob, r0=r0: e.dma_start(out=gb[ob][:], in_=gT[:, r0:r0 + 512].rearrange("(c p) n -> p c n", p=128)), writes=[tgb[ob]])
        P.dma("sp", lambda e, ob=ob, r0=r0: e.dma_start(out=xt[ob][:], in_=h_in[r0:r0 + 512, :].rearrange("(t p) d -> p t d", p=128)), writes=[txt[ob]])
        for t in range(4):
            for nb in range(2):
                b = k % 2
                k += 1
                for c in range(8):
                    P.op("pe", lambda e, c=c, b=b, t=t, nb=nb, ob=ob: e.matmul(out=pv[b][:], lhsT=gb[ob][:, c, t * 128:(t + 1) * 128], rhs=wg[:, c, nb * 512:(nb + 1) * 512],
                                                                               start=(c == 0), stop=(c == 7)), reads=[tgb[ob], twg], writes=[tpv[b]])
                for c in range(8):
                    P.op("pe", lambda e, c=c, b=b, t=t, nb=nb, ob=ob: e.matmul(out=pg[b][:], lhsT=gb[ob][:, c, t * 128:(t + 1) * 128], rhs=wg[:, c, 1024 + nb * 512:1024 + (nb + 1) * 512],
                                                                               start=(c == 0), stop=(c == 7)), reads=[tgb[ob], twg], writes=[tpg[b]])
                P.op("act", lambda e, b=b: e.activation(out=sg[b][:], in_=pg[b][:], func=AF.Sigmoid), reads=[tpg[b]], writes=[tsg[b]])
                P.op("dve", lambda e, b=b: e.tensor_tensor(out=sg[b][:], in0=sg[b][:], in1=pv[b][:], op=ALU.mult), reads=[tsg[b], tpv[b]], writes=[tsg[b]])
                P.op("pool", lambda e, b=b, t=t, nb=nb, ob=ob: e.tensor_tensor(out=xt[ob][:, t, nb * 512:(nb + 1) * 512], in0=xt[ob][:, t, nb * 512:(nb + 1) * 512], in1=sg[b][:], op=ALU.add),
                     reads=[tsg[b], txt[ob]], writes=[txt[ob]])
        to = P.tile()
        P.dma("act", lambda e, ob=ob, r0=r0: e.dma_start(out=h_out[r0:r0 + 512, :].rearrange("(t p) d -> p t d", p=128), in_=xt[ob][:]), reads=[txt[ob]], writes=[to])
        outs.append(to)
    P.finish(outs)


N_CORES = 8
SEQ = 2048
N_SEQ = 4
NT = N_SEQ * SEQ

_PARAMS = [
    ("a_norm", [1, 1024]), ("a_w_in", [1, 1024, 1024]), ("a_lam_re", [1, 64, 64]), ("a_lam_im", [1, 64, 64]),
    ("a_b_re", [1, 64, 64, 16]), ("a_b_im", [1, 64, 64, 16]), ("a_c_re", [1, 64, 16, 64]), ("a_c_im", [1, 64, 16, 64]),
    ("a_d", [1, 1024]), ("a_log_dt", [1, 64]), ("a_w_glu", [1, 1024, 2048]), ("kv_norm", [1024]), ("w_kv", [1024, 2048]),
    ("k_norm", [64]), ("b_norm", [1, 1024]), ("b_w_q", [1, 1024, 1024]), ("b_q_norm", [1, 64]), ("b_w_o", [1, 1024, 1024]),
    ("ffn_norm", [2, 1024]), ("ffn_w_up", [2, 1024, 5632]), ("ffn_conv_w", [2, 3, 2816]), ("ffn_conv_b", [2, 2816]),
    ("ffn_w_down", [2, 2816, 1024]),
]


def build_program(N_SEQ=N_SEQ, SEQ=SEQ, debug=False):
    NT = N_SEQ * SEQ
    nc = bass.Bass("TRN2", target_bir_lowering=False)
    x = nc.dram_tensor("x", [NT, D], F32, kind="ExternalInput").ap()
    prm = {n: nc.dram_tensor(n, s, F32, kind="ExternalInput").ap() for n, s in _PARAMS}
    ident = nc.dram_tensor("c_ident", [128, 128], F32, kind="ExternalInput").ap()
    bones = nc.dram_tensor("c_bones", [128, 128], F32, kind="ExternalInput").ap()
    maskb = nc.dram_tensor("c_maskb", [128, 128], F32, kind="ExternalInput").ap()
    iota = nc.dram_tensor("c_iota", [128, 2048], F32, kind="ExternalInput").ap()
    out = nc.dram_tensor("out", [NT, D], F32, kind="ExternalOutput").ap()
    kd = "ExternalOutput" if debug else "Internal"
    h1 = nc.dram_tensor("h1", [NT, D], F32, kind=kd).ap()
    h2 = nc.dram_tensor("h2", [NT, D], F32, kind=kd).ap()
    h3 = nc.dram_tensor("h3", [NT, D], F32, kind=kd).ap()
    uT = nc.dram_tensor("uT", [D, NT], BF16).ap()
    gT = nc.dram_tensor("gT", [D, NT], BF16).ap()
    kT = nc.dram_tensor("kT", [D, NT], BF16).ap()
    qT = nc.dram_tensor("qT", [D, NT], BF16).ap()
    vr = nc.dram_tensor("vr", [NT, D], BF16).ap()
    stage_a1(nc, x, uT, prm["a_norm"][0], prm["a_w_in"][0], ident, NT)
    stage_a2(nc, uT, gT, prm["a_lam_re"][0], prm["a_lam_im"][0], prm["a_b_re"][0], prm["a_b_im"][0], prm["a_c_re"][0], prm["a_c_im"][0],
             prm["a_d"][0], prm["a_log_dt"][0], iota, N_SEQ, SEQ)
    stage_a3(nc, x, h1, gT, prm["a_w_glu"][0], NT)
    stage_ffn(nc, h1, h2, prm["ffn_norm"][0], prm["ffn_w_up"][0], prm["ffn_conv_w"][0], prm["ffn_conv_b"][0],
              prm["ffn_w_down"][0], ident, N_SEQ, SEQ)
    stage_kvq(nc, h2, kT, qT, vr, prm["kv_norm"], prm["w_kv"], prm["k_norm"], prm["b_norm"][0], prm["b_w_q"][0], prm["b_q_norm"][0],
              ident, bones, N_SEQ, SEQ)
    stage_att(nc, h2, h3, qT, kT, vr, prm["b_w_o"][0], ident, maskb, N_SEQ, SEQ)
    stage_ffn(nc, h3, out, prm["ffn_norm"][1], prm["ffn_w_up"][1], prm["ffn_conv_w"][1], prm["ffn_conv_b"][1],
              prm["ffn_w_down"][1], ident, N_SEQ, SEQ)
    return nc


def kernel(**inputs):
    x = np.ascontiguousarray(np.asarray(inputs["x"], dtype=np.float32))
    nc = build_program()
    p = np.arange(128)
    consts = {
        "c_ident": np.eye(128, dtype=np.float32),
        "c_bones": (p[:, None] // 64 == p[None, :] // 64).astype(np.float32),
        "c_maskb": np.where(p[None, :] + p[:, None] >= 128, 0.0, -30000.0).astype(np.float32),
        "c_iota": np.ascontiguousarray(np.tile(np.arange(2048, dtype=np.float32), (128, 1))),
    }
    params = {n: np.ascontiguousarray(np.asarray(inputs[n], dtype=np.float32)).reshape(s) for n, s in _PARAMS}
    in_maps = []
    for c in range(N_CORES):
        m = {"x": x[c * N_SEQ:(c + 1) * N_SEQ].reshape(NT, D)}
        m.update(params)
        m.update(consts)
        in_maps.append(m)
    res = run_bass_kernel_spmd(nc, in_maps, core_ids=list(range(N_CORES)))
    outs = [np.asarray(r["out"], dtype=np.float32).reshape(N_SEQ, SEQ, D) for r in res.results]
    return np.concatenate(outs, axis=0)
```

```python
import contextlib
import numpy as np
import concourse.bass as bass
import concourse.mybir as mybir
from concourse.bass_utils import run_bass_kernel_spmd

F32 = mybir.dt.float32
BF16 = mybir.dt.bfloat16
AF = mybir.ActivationFunctionType
ALU = mybir.AluOpType
AX = mybir.AxisListType

ENGS = ("pe", "act", "dve", "pool", "sp")
N_DMA_SEMS = 4

D = 1024
DFF = 2816
NFT = DFF // 128
EPS = 1e-6


class T:
    __slots__ = ("name", "writes", "reads")

    def __init__(self, name="t"):
        self.name = name
        self.writes = {}
        self.reads = {}


class Prog:
    _stage = 0

    def __init__(self, nc):
        self.nc = nc
        Prog._stage += 1
        self.sid = Prog._stage
        self.stack = contextlib.ExitStack()
        self.ops = {e: [] for e in ENGS}
        self.known = {e: {} for e in ENGS}
        self.ndma = {e: 0 for e in ENGS}
        self.cnt = {e: 0 for e in ENGS}
        self.milestones = {e: set() for e in ENGS}
        self._n = 0

    def sbuf(self, name, shape, dtype):
        return self.stack.enter_context(self.nc.sbuf_tensor(f"s{self.sid}_{name}", list(shape), dtype))

    def psum(self, name, shape, dtype=F32):
        return self.stack.enter_context(self.nc.psum_tensor(f"s{self.sid}_{name}", list(shape), dtype))

    def tile(self, name=None):
        return T(name or "t")

    def tiles(self, n):
        return [T() for _ in range(n)]

    def _collect(self, eng, reads, writes):
        need = {}
        for t in reads:
            for k, v in t.writes.items():
                if need.get(k, 0) < v:
                    need[k] = v
        for t in writes:
            for d in (t.writes, t.reads):
                for k, v in d.items():
                    if need.get(k, 0) < v:
                        need[k] = v
        waits = []
        kn = self.known[eng]
        for k, v in need.items():
            if k == ("e", eng) and eng == "pe":
                continue
            if kn.get(k, 0) >= v:
                continue
            kn[k] = v
            waits.append((k, v))
            if k[0] == "e":
                self.milestones[k[1]].add(v)
        return waits

    def op(self, eng, fn, reads=(), writes=()):
        waits = self._collect(eng, reads, writes)
        self.cnt[eng] += 1
        idx = self.cnt[eng]
        key = ("e", eng)
        self.ops[eng].append(dict(kind="op", fn=fn, waits=waits, idx=idx))
        for t in reads:
            if t.reads.get(key, 0) < idx:
                t.reads[key] = idx
        for t in writes:
            t.writes = {key: idx}
            t.reads = {}

    def dma(self, eng, fn, reads=(), writes=()):
        i = self.ndma[eng]
        self.ndma[eng] += 1
        slot = i % N_DMA_SEMS
        gen = i // N_DMA_SEMS
        key = ("d", eng, slot)
        waits = self._collect(eng, reads, writes)
        kn = self.known[eng]
        if gen > 0 and kn.get(key, 0) < 16 * gen:
            kn[key] = 16 * gen
            waits.append((key, 16 * gen))
        val = 16 * (gen + 1)
        self.ops[eng].append(dict(kind="dma", fn=fn, waits=waits, key=key))
        for t in reads:
            if t.reads.get(key, 0) < val:
                t.reads[key] = val
        for t in writes:
            t.writes = {key: val}
            t.reads = {}

    def wait_all(self, eng, tiles):
        waits = self._collect(eng, tiles, ())
        self.ops[eng].append(dict(kind="wait", waits=waits))

    def finish(self, out_tiles):
        self.wait_all("sp", out_tiles)
        nc = self.nc
        with nc.cleanup_on_exit():
            sems = {}
            for e in ENGS:
                if self.milestones[e]:
                    sems[("e", e)] = nc.alloc_semaphore(f"p{self.sid}_{e}")
                for s in range(min(N_DMA_SEMS, self.ndma[e])):
                    sems[("d", e, s)] = nc.alloc_semaphore(f"d{self.sid}_{e}_{s}")
            mmap = {e: {v: i + 1 for i, v in enumerate(sorted(self.milestones[e]))} for e in ENGS}

            def replay(e):
                def body(engine):
                    for o in self.ops[e]:
                        for k, v in o["waits"]:
                            if k[0] == "e":
                                v = mmap[k[1]][v]
                            engine.wait_ge(sems[k], v)
                        if o["kind"] == "op":
                            ins = o["fn"](engine)
                            if o["idx"] in mmap[e]:
                                ins.then_inc(sems[("e", e)], 1)
                        elif o["kind"] == "dma":
                            ins = o["fn"](engine)
                            ins.then_inc(sems[o["key"]], 16)
                return body

            with nc.Block() as block:
                block.tensor(replay("pe"))
                block.scalar(replay("act"))
                block.vector(replay("dve"))
                block.gpsimd(replay("pool"))
                block.sync(replay("sp"))
            nc.all_engine_barrier()
        self.stack.close()


def run_pipeline(tasks, offsets):
    nph = len(offsets)
    for step in range(len(tasks) + max(offsets) + 1):
        for p in reversed(range(nph)):
            t = step - offsets[p]
            if 0 <= t < len(tasks) and tasks[t][p] is not None:
                tasks[t][p]()


def load_weight_bf16(P, dst, tdst, w_ap, kchunks, gain_sb=None, tgain=None, eng="dve"):
    n = w_ap.shape[-1]
    src = w_ap.rearrange("(c p) n -> p c n", p=128)
    step = max(1, 4096 // n)
    for c0 in range(0, kchunks, step):
        c1 = min(kchunks, c0 + step)
        P.dma("pool", lambda e, c0=c0, c1=c1: e.dma_start(out=dst[:, c0:c1, :], in_=src[:, c0:c1, :]), writes=[tdst])
    if gain_sb is not None:
        for c in range(kchunks):
            P.op(eng, lambda e, c=c: e.tensor_scalar(out=dst[:, c, :], in0=dst[:, c, :], scalar1=gain_sb[:, c:c + 1],
                                                      scalar2=None, op0=ALU.mult), reads=[tdst, tgain], writes=[tdst])


def load_gain(P, name, g_ap):
    g_sb = P.sbuf(name, [128, 8], F32)
    t = P.tile()
    P.dma("sp", lambda e: e.dma_start(out=g_sb[:], in_=g_ap.rearrange("(c p) -> p c", p=128), allow_slow_non_contiguous=True), writes=[t])
    return g_sb, t


class NormT:
    def __init__(self, P, ident, tident, n_xt=2, n_xnT=1, junk=None, tjunk=None):
        self.P = P
        self.ident, self.tident = ident, tident
        self.n_xt, self.n_xnT = n_xt, n_xnT
        self.xt = [P.sbuf(f"nt_xt{i}", [128, 4, D], F32) for i in range(n_xt)]
        self.txt = P.tiles(n_xt)
        if junk is None:
            self.junk = P.sbuf("nt_junk", [128, D], BF16)[:]
            self.tjunk = P.tile()
        else:
            self.junk, self.tjunk = junk, tjunk
        self.ss = P.sbuf("nt_ss", [128, 8], F32)
        self.tss = P.tile()
        self.xn = [P.sbuf(f"nt_xn{i}", [128, D], BF16) for i in range(4)]
        self.txn = P.tiles(4)
        self.pst = [P.psum(f"nt_pst{i}", [128, D], BF16) for i in range(2)]
        self.tpst = P.tiles(2)
        self.xnT = [P.sbuf(f"nt_xnT{i}", [128, 8, 512], BF16) for i in range(n_xnT)]
        self.txnT = P.tiles(n_xnT)
        self.k = 0
        self.n = 0
        self.n2 = 0
        self.kp = 0

    def part1(self, src_rows):
        P = self.P
        b = self.n % self.n_xt
        self.n += 1
        xt, txt = self.xt[b], self.txt[b]
        P.dma("sp", lambda e: e.dma_start(out=xt[:], in_=src_rows.rearrange("(t p) d -> p t d", p=128)), writes=[txt])
        ss, tss = self.ss, self.tss
        for t in range(4):
            xn, txn = self.xn[t], self.txn[t]
            self.k += 1
            col = (self.k % 4) * 2
            P.op("dve", lambda e, col=col: e.memset(ss[:, col:col + 1], 0.0), writes=[tss])
            P.op("act", lambda e, t=t, col=col: e.activation(out=self.junk, in_=xt[:, t, :], func=AF.Square,
                                                             accum_out=ss[:, col:col + 1]),
                 reads=[txt], writes=[self.tjunk, tss])
            P.op("dve", lambda e, col=col: e.tensor_scalar(out=ss[:, col + 1:col + 2], in0=ss[:, col:col + 1], scalar1=1.0 / D,
                                                           scalar2=EPS, op0=ALU.mult, op1=ALU.add), reads=[tss], writes=[tss])
            P.op("act", lambda e, col=col: e.activation(out=ss[:, col + 1:col + 2], in_=ss[:, col + 1:col + 2], func=AF.Sqrt),
                 reads=[tss], writes=[tss])
            P.op("dve", lambda e, col=col: e.reciprocal(out=ss[:, col + 1:col + 2], in_=ss[:, col + 1:col + 2]), reads=[tss], writes=[tss])
            P.op("act", lambda e, t=t, col=col, xn=xn: e.activation(out=xn[:], in_=xt[:, t, :], func=AF.Copy,
                                                                     scale=ss[:, col + 1:col + 2]),
                 reads=[txt, tss], writes=[txn])
        return xt, txt

    def part2(self):
        P = self.P
        b2 = self.n2 % self.n_xnT
        self.n2 += 1
        xnT, txnT = self.xnT[b2], self.txnT[b2]
        for t in range(4):
            xn, txn = self.xn[t], self.txn[t]
            kk = self.kp % 2
            self.kp += 1
            pst, tpst = self.pst[kk], self.tpst[kk]
            for c in range(8):
                P.op("pe", lambda e, c=c, xn=xn, pst=pst: e.transpose(out=pst[:, c * 128:(c + 1) * 128],
                                                                       in_=xn[:, c * 128:(c + 1) * 128], identity=self.ident[:]),
                     reads=[txn, self.tident], writes=[tpst])
            P.op("dve", lambda e, t=t, pst=pst, xnT=xnT: e.tensor_copy(out=xnT[:, :, t * 128:(t + 1) * 128],
                                                                       in_=pst[:].rearrange("p (c k) -> p c k", k=128)),
                 reads=[tpst], writes=[txnT])
        return xnT, txnT

    def run(self, src_rows):
        xt, txt = self.part1(src_rows)
        xnT, txnT = self.part2()
        return xt, txt, xnT, txnT


def load_ident(P, ident_ap):
    ident = P.sbuf("ident", [128, 128], BF16)
    t = P.tile()
    P.dma("pool", lambda e: e.dma_start(out=ident[:], in_=ident_ap), writes=[t])
    return ident, t


def stage_ffn(nc, h_in, h_out, g_ap, wup_ap, cw_ap, cb_ap, wdn_ap, ident_ap, n_seq, seq_len):
    P = Prog(nc)
    ident, tident = load_ident(P, ident_ap)
    g_sb, tg = load_gain(P, "g", g_ap)
    wup = P.sbuf("wup", [128, 8, 2 * DFF], BF16)
    twup = P.tile()
    load_weight_bf16(P, wup, twup, wup_ap, 8, g_sb, tg)
    wdn = P.sbuf("wdn", [128, NFT, D], BF16)
    twdn = P.tile()
    load_weight_bf16(P, wdn, twdn, wdn_ap, NFT)
    cw = P.sbuf("cw", [128, NFT, 3], F32)
    cb = P.sbuf("cb", [128, NFT], F32)
    tcw = P.tile()
    for j in range(3):
        P.dma("sp", lambda e, j=j: e.dma_start(out=cw[:, :, j], in_=cw_ap[j].rearrange("(f p) -> p f", p=128), allow_slow_non_contiguous=True), writes=[tcw])
    P.dma("sp", lambda e: e.dma_start(out=cb[:], in_=cb_ap.rearrange("(f p) -> p f", p=128), allow_slow_non_contiguous=True), writes=[tcw])
    gc = [P.sbuf(f"gc{i}", [128, 512], F32) for i in range(2)]
    tgc = P.tiles(2)
    nt = NormT(P, ident, tident, n_xt=1, n_xnT=2, junk=gc[1][:].bitcast(BF16), tjunk=tgc[1])
    halo = P.sbuf("halo", [128, NFT, 2], F32)
    thalo = P.tile()
    gbuf = [P.sbuf(f"gbuf{i}", [128, 514], F32) for i in range(2)]
    tgbuf = P.tiles(2)
    hid = P.sbuf("hid", [128, NFT, 512], BF16)
    thid = P.tile()
    pv = [P.psum(f"pv{i}", [128, 512]) for i in range(2)]
    tpv = P.tiles(2)
    pg = [P.psum(f"pg{i}", [128, 512]) for i in range(2)]
    tpg = P.tiles(2)
    po = [P.psum(f"po{i}", [128, 512]) for i in range(2)]
    tpo = P.tiles(2)
    xr = [P.sbuf(f"xr{i}", [128, D], F32) for i in range(1)] * 2
    txr = P.tiles(1) * 2
    outs = []
    nsup = seq_len // 512
    nblk = n_seq * nsup
    cnt = {"k": 0, "ko": 0, "kx": 0}
    xn_of = {}

    def phA1(u):
        r0 = u * 512
        nt.part1(h_in[r0:r0 + 512, :])

    def phA2(u):
        xn_of[u] = nt.part2()

    def phB(u):
        xnT, txnT = xn_of[u]
        if u % nsup == 0:
            P.op("pool", lambda e: e.memset(halo[:], 0.0), writes=[thalo])
        for ft in range(NFT):
            if ft == 2 and u + 1 < nblk:
                phA1(u + 1)
            b = cnt["k"] % 2
            cnt["k"] += 1
            for c in range(8):
                P.op("pe", lambda e, c=c, b=b, ft=ft: e.matmul(out=pv[b][:], lhsT=wup[:, c, ft * 128:(ft + 1) * 128], rhs=xnT[:, c, :],
                                                               start=(c == 0), stop=(c == 7)), reads=[twup, txnT], writes=[tpv[b]])
            for c in range(8):
                P.op("pe", lambda e, c=c, b=b, ft=ft: e.matmul(out=pg[b][:], lhsT=wup[:, c, DFF + ft * 128:DFF + (ft + 1) * 128], rhs=xnT[:, c, :],
                                                               start=(c == 0), stop=(c == 7)), reads=[twup, txnT], writes=[tpg[b]])
            gb, tgb, g2, tg2 = gbuf[b], tgbuf[b], gc[b], tgc[b]
            P.op("pool", lambda e, gb=gb, ft=ft: e.tensor_copy(out=gb[:, 0:2], in_=halo[:, ft, :]), reads=[thalo], writes=[tgb])
            P.op("act", lambda e, gb=gb, b=b: e.activation(out=gb[:, 2:514], in_=pg[b][:], func=AF.Copy), reads=[tpg[b]], writes=[tgb])
            P.op("pool", lambda e, gb=gb, ft=ft: e.tensor_copy(out=halo[:, ft, :], in_=gb[:, 512:514]), reads=[tgb], writes=[thalo])
            P.op("act", lambda e, gb=gb, g2=g2, ft=ft: e.activation(out=g2[:], in_=gb[:, 2:514], func=AF.Identity,
                                                                    scale=cw[:, ft, 2:3], bias=cb[:, ft:ft + 1]),
                 reads=[tgb, tcw], writes=[tg2])
            P.op("dve", lambda e, gb=gb, g2=g2, ft=ft: e.scalar_tensor_tensor(out=g2[:], in0=gb[:, 1:513], scalar=cw[:, ft, 1:2], in1=g2[:],
                                                                              op0=ALU.mult, op1=ALU.add), reads=[tgb, tcw, tg2], writes=[tg2])
            P.op("dve", lambda e, gb=gb, g2=g2, ft=ft: e.scalar_tensor_tensor(out=g2[:], in0=gb[:, 0:512], scalar=cw[:, ft, 0:1], in1=g2[:],
                                                                              op0=ALU.mult, op1=ALU.add), reads=[tgb, tcw, tg2], writes=[tg2])
            P.op("act", lambda e, g2=g2: e.activation(out=g2[:], in_=g2[:], func=AF.Silu), reads=[tg2], writes=[tg2])
            P.op("dve", lambda e, g2=g2, b=b, ft=ft: e.tensor_tensor(out=hid[:, ft, :], in0=g2[:], in1=pv[b][:], op=ALU.mult),
                 reads=[tg2, tpv[b]], writes=[thid])

    def phC(u):
        r0 = u * 512
        for t in range(4):
            xb = cnt["kx"] % 2
            cnt["kx"] += 1
            rr = r0 + t * 128
            P.dma("sp", lambda e, xb=xb, rr=rr: e.dma_start(out=xr[xb][:], in_=h_in[rr:rr + 128, :]), writes=[txr[xb]])
            for nb in range(2):
                b = cnt["ko"] % 2
                cnt["ko"] += 1
                for ft in range(NFT):
                    P.op("pe", lambda e, ft=ft, t=t, nb=nb, b=b: e.matmul(out=po[b][:], lhsT=hid[:, ft, t * 128:(t + 1) * 128],
                                                                          rhs=wdn[:, ft, nb * 512:(nb + 1) * 512],
                                                                          start=(ft == 0), stop=(ft == NFT - 1)),
                         reads=[thid, twdn], writes=[tpo[b]])
                P.op("dve", lambda e, nb=nb, b=b, xb=xb: e.tensor_tensor(out=xr[xb][:, nb * 512:(nb + 1) * 512], in0=po[b][:],
                                                                         in1=xr[xb][:, nb * 512:(nb + 1) * 512], op=ALU.add),
                     reads=[tpo[b], txr[xb]], writes=[txr[xb]])
            to = P.tile()
            P.dma("act", lambda e, xb=xb, rr=rr: e.dma_start(out=h_out[rr:rr + 128, :], in_=xr[xb][:]), reads=[txr[xb]], writes=[to])
            outs.append(to)

    phA1(0)
    phA2(0)
    for u in range(nblk):
        phB(u)
        if u + 1 < nblk:
            phA2(u + 1)
        phC(u)
    P.finish(outs)


def stage_kvq(nc, h_in, kT_rev, qT, v_rev, kvn_ap, wkv_ap, kn_ap, bn_ap, wq_ap, qn_ap, ident_ap, bones_ap, n_seq, S):
    P = Prog(nc)
    ident, tident = load_ident(P, ident_ap)
    bones = P.sbuf("bones", [128, 128], BF16)
    tbones = P.tile()
    P.dma("pool", lambda e: e.dma_start(out=bones[:], in_=bones_ap), writes=[tbones])
    gkv, tgkv = load_gain(P, "gkv", kvn_ap)
    gb, tgb = load_gain(P, "gb", bn_ap)
    wkv = P.sbuf("wkv", [128, 8, 2048], BF16)
    twkv = P.tile()
    load_weight_bf16(P, wkv, twkv, wkv_ap, 8, gkv, tgkv)
    wq = P.sbuf("wq", [128, 8, 1024], BF16)
    twq = P.tile()
    load_weight_bf16(P, wq, twq, wq_ap, 8, gb, tgb)
    hg = P.sbuf("hg", [128, 4], F32)
    thg = P.tile()
    for hf in range(2):
        P.dma("sp", lambda e, hf=hf: e.dma_start(out=hg[hf * 64:(hf + 1) * 64, 0:1], in_=kn_ap.rearrange("(p o) -> p o", o=1)), writes=[thg])
        P.dma("sp", lambda e, hf=hf: e.dma_start(out=hg[hf * 64:(hf + 1) * 64, 1:2], in_=qn_ap.rearrange("(p o) -> p o", o=1)), writes=[thg])
    P.op("dve", lambda e: e.tensor_scalar(out=hg[:, 1:2], in0=hg[:, 1:2], scalar1=0.125, scalar2=None, op0=ALU.mult), reads=[thg], writes=[thg])
    P.op("dve", lambda e: e.memset(hg[:, 2:3], EPS), reads=[thg], writes=[thg])
    nt = NormT(P, ident, tident)
    pk = [P.psum(f"pk{i}", [128, 512]) for i in range(2)]
    tpk = P.tiles(2)
    pss = [P.psum(f"pss{i}", [128, 512]) for i in range(2)]
    tpss = P.tiles(2)
    sq = [P.sbuf(f"sq{i}", [128, 512], BF16) for i in range(2)]
    tsq = P.tiles(2)
    rs = [P.sbuf(f"rs{i}", [128, 512], F32) for i in range(2)]
    trs = P.tiles(2)
    kbuf = [P.sbuf(f"kbuf{i}", [128, 8, 512], BF16) for i in range(2)]
    tkbuf = P.tiles(2)
    qbuf = [P.sbuf(f"qbuf{i}", [128, 8, 512], BF16) for i in range(2)]
    tqbuf = P.tiles(2)
    vbuf = [P.sbuf(f"vbuf{i}", [128, 4, 1024], BF16) for i in range(2)]
    tvbuf = P.tiles(2)
    xrev = P.sbuf("xrev", [128, 8, 512], BF16)
    txrev = P.tile()
    outs = []
    nsup = S // 512
    k = 0
    for s in range(n_seq):
        for u in range(nsup):
            r0 = s * S + u * 512
            rr = s * S + S - 512 * (u + 1)
            ob = (s * nsup + u) % 2
            xt, txt, xnT, txnT = nt.run(h_in[r0:r0 + 512, :])
            for which in range(2):
                for ft in range(8):
                    b = k % 2
                    k += 1
                    for c in range(8):
                        if which == 0:
                            P.op("pe", lambda e, c=c, b=b, ft=ft: e.matmul(out=pk[b][:], lhsT=wkv[:, c, ft * 128:(ft + 1) * 128], rhs=xnT[:, c, :],
                                                                           start=(c == 0), stop=(c == 7)), reads=[twkv, txnT], writes=[tpk[b]])
                        else:
                            P.op("pe", lambda e, c=c, b=b, ft=ft: e.matmul(out=pk[b][:], lhsT=wq[:, c, ft * 128:(ft + 1) * 128], rhs=xnT[:, c, :],
                                                                           start=(c == 0), stop=(c == 7)), reads=[twq, txnT], writes=[tpk[b]])
                    P.op("act", lambda e, b=b: e.activation(out=sq[b][:], in_=pk[b][:], func=AF.Square), reads=[tpk[b]], writes=[tsq[b]])
                    P.op("pe", lambda e, b=b: e.matmul(out=pss[b][:], lhsT=bones[:], rhs=sq[b][:], start=True, stop=True),
                         reads=[tbones, tsq[b]], writes=[tpss[b]])
                    P.op("act", lambda e, b=b: e.activation(out=rs[b][:], in_=pss[b][:], func=AF.Ln, scale=1.0 / 64, bias=hg[:, 2:3]),
                         reads=[tpss[b], thg], writes=[trs[b]])
                    P.op("act", lambda e, b=b: e.activation(out=rs[b][:], in_=rs[b][:], func=AF.Exp, scale=-0.5), reads=[trs[b]], writes=[trs[b]])
                    if which == 0:
                        P.op("dve", lambda e, b=b, ft=ft, ob=ob: e.scalar_tensor_tensor(out=kbuf[ob][:, ft, ::-1], in0=pk[b][:], scalar=hg[:, 0:1], in1=rs[b][:],
                                                                                        op0=ALU.mult, op1=ALU.mult),
                             reads=[tpk[b], thg, trs[b]], writes=[tkbuf[ob]])
                    else:
                        P.op("dve", lambda e, b=b, ft=ft, ob=ob: e.scalar_tensor_tensor(out=qbuf[ob][:, ft, :], in0=pk[b][:], scalar=hg[:, 1:2], in1=rs[b][:],
                                                                                        op0=ALU.mult, op1=ALU.mult),
                             reads=[tpk[b], thg, trs[b]], writes=[tqbuf[ob]])
            P.op("dve", lambda e, xnT=xnT: e.tensor_copy(out=xrev[:, :, ::-1], in_=xnT[:]), reads=[txnT], writes=[txrev])
            for t in range(4):
                for nb in range(2):
                    b = k % 2
                    k += 1
                    for c in range(8):
                        P.op("pe", lambda e, c=c, b=b, t=t, nb=nb: e.matmul(out=pk[b][:], lhsT=xrev[:, c, t * 128:(t + 1) * 128],
                                                                            rhs=wkv[:, c, 1024 + nb * 512:1024 + (nb + 1) * 512],
                                                                            start=(c == 0), stop=(c == 7)), reads=[twkv, txrev], writes=[tpk[b]])
                    P.op("act", lambda e, b=b, t=t, nb=nb, ob=ob: e.activation(out=vbuf[ob][:, t, nb * 512:(nb + 1) * 512], in_=pk[b][:], func=AF.Copy),
                         reads=[tpk[b]], writes=[tvbuf[ob]])
            t1, t2, t3 = P.tile(), P.tile(), P.tile()
            P.dma("act", lambda e, ob=ob, rr=rr: e.dma_start(out=kT_rev[:, rr:rr + 512].rearrange("(f p) n -> p f n", p=128), in_=kbuf[ob][:]),
                  reads=[tkbuf[ob]], writes=[t1])
            P.dma("act", lambda e, ob=ob, r0=r0: e.dma_start(out=qT[:, r0:r0 + 512].rearrange("(f p) n -> p f n", p=128), in_=qbuf[ob][:]),
                  reads=[tqbuf[ob]], writes=[t2])
            P.dma("act", lambda e, ob=ob, rr=rr: e.dma_start(out=v_rev[rr:rr + 512, :].rearrange("(t p) d -> p t d", p=128), in_=vbuf[ob][:]),
                  reads=[tvbuf[ob]], writes=[t3])
            outs += [t1, t2, t3]
    P.finish(outs)


def stage_att(nc, h_in, h_out, qT, kT_rev, v_rev, wo_ap, ident_ap, maskb_ap, n_seq, S):
    P = Prog(nc)
    ident, tident = load_ident(P, ident_ap)
    maskb = P.sbuf("maskb", [128, 128], BF16)
    tmask = P.tile()
    P.dma("pool", lambda e: e.dma_start(out=maskb[:], in_=maskb_ap), writes=[tmask])
    wo = P.sbuf("wo", [128, 8, 1024], BF16)
    two = P.tile()
    load_weight_bf16(P, wo, two, wo_ap, 8)
    zeros = P.sbuf("zeros", [128, 512], F32)
    onec = P.sbuf("onec", [128, 1], F32)
    tz = P.tile()
    P.op("pool", lambda e: e.memset(zeros[:], 0.0), writes=[tz])
    P.op("pool", lambda e: e.memset(onec[:], 1.0), reads=[tz], writes=[tz])
    NB = S // 128
    qsb = P.sbuf("qsb", [128, 8, S], BF16)
    ksb = P.sbuf("ksb", [128, 8, S], BF16)
    vsb = P.sbuf("vsb", [128, NB, 1024], BF16)
    oT = P.sbuf("oT", [128, 8, S], BF16)
    tq, tk, tv, toT = P.tile(), P.tile(), P.tile(), P.tile()
    NZ, NS, NPB, NW, NWT, NWS, NPO = 3, 4, 5, 4, 2, 4, 3
    pz = [P.psum(f"pz{i}", [128, 512]) for i in range(NZ)]
    tpz = P.tiles(NZ)
    pwT = [P.psum(f"pwT{i}", [128, 1024], BF16) for i in range(NWT)]
    tpwT = P.tiles(NWT)
    po = [P.psum(f"po{i}", [128, 512]) for i in range(NPO)]
    tpo = P.tiles(NPO)
    pw = pz[0:2]
    tpw = tpz[0:2]
    ssb = [P.sbuf(f"ssb{i}", [128, 512], F32) for i in range(NS)]
    tssb = P.tiles(NS)
    pbuf = [P.sbuf(f"pbuf{i}", [128, 513], F32) for i in range(NPB)]
    tpbuf = P.tiles(NPB)
    wsb = [P.sbuf(f"wsb{i}", [128, 512], BF16) for i in range(NW)]
    twsb = P.tiles(NW)
    wTs = [P.sbuf(f"wTs{i}", [128, 512], BF16) for i in range(NWS)]
    twTs = P.tiles(NWS)
    xt = [P.sbuf(f"xt{i}", [128, D], F32) for i in range(2)]
    txt = P.tiles(2)
    outs = []
    kk = 0
    kx = 0
    gt = 0
    gpo = 0
    for s in range(n_seq):
        c0 = s * S
        for f in range(0, 8, 2):
            P.dma("sp", lambda e, f=f, c0=c0: e.dma_start(out=qsb[:, f:f + 2, :], in_=qT[f * 128:(f + 2) * 128, c0:c0 + S].rearrange("(f p) n -> p f n", p=128)), writes=[tq])
            P.dma("act", lambda e, f=f, c0=c0: e.dma_start(out=ksb[:, f:f + 2, :], in_=kT_rev[f * 128:(f + 2) * 128, c0:c0 + S].rearrange("(f p) n -> p f n", p=128)), writes=[tk])
        for b4 in range(0, NB, 4):
            P.dma("sp", lambda e, b4=b4, c0=c0: e.dma_start(out=vsb[:, b4:b4 + 4, :], in_=v_rev[c0 + b4 * 128:c0 + (b4 + 4) * 128, :].rearrange("(t p) d -> p t d", p=128)), writes=[tv])
        tasks = []
        for ft in range(8):
            for i in range(NB):
                q0 = i * 128
                kp0 = S - 128 - q0
                klen = q0 + 128
                ob = gpo % NPO
                gpo += 1
                ntile = (klen + 511) // 512
                for hp in range(2):
                    h = ft * 2 + hp
                    ps = slice(hp * 64, (hp + 1) * 64)
                    for n in range(ntile):
                        col0 = kp0 + 512 * n
                        wdt = min(512, S - col0)
                        nblk = wdt // 128
                        g = gt
                        gt += 1
                        bz, bs_, bp, bw, bwt, bws = g % NZ, g % NS, g % NPB, g % NW, g % NWT, g % NWS
                        bpp = (g - 1) % NPB

                        def ph0(bz=bz, ft=ft, ps=ps, q0=q0, col0=col0, wdt=wdt, n=n):
                            P.op("pe", lambda e: e.matmul(out=pz[bz][:, 0:wdt], lhsT=qsb[ps, ft, q0:q0 + 128], rhs=ksb[ps, ft, col0:col0 + wdt],
                                                          start=True, stop=(n != 0)), reads=[tq, tk], writes=[tpz[bz]])
                            if n == 0:
                                P.op("pe", lambda e: e.matmul(out=pz[bz][:, 0:128], lhsT=ident[:], rhs=maskb[:], start=False, stop=True),
                                     reads=[tident, tmask], writes=[tpz[bz]])

                        def ph1(bz=bz, bs_=bs_, wdt=wdt):
                            P.op("act", lambda e: e.activation(out=ssb[bs_][:, 0:wdt], in_=pz[bz][:, 0:wdt], func=AF.Sigmoid, scale=-1.0),
                                 reads=[tpz[bz]], writes=[tssb[bs_]])

                        def ph2(bs_=bs_, bp=bp, bpp=bpp, wdt=wdt, n=n):
                            if n == 0:
                                P.op("pool", lambda e: e.memset(pbuf[bp][:, 0:1], 1.0), writes=[tpbuf[bp]])
                                init = onec[:, 0:1]
                                rd = [tssb[bs_], tz, tpbuf[bp]]
                            else:
                                P.op("pool", lambda e: e.tensor_copy(out=pbuf[bp][:, 0:1], in_=pbuf[bpp][:, 512:513]), reads=[tpbuf[bpp]], writes=[tpbuf[bp]])
                                init = pbuf[bpp][:, 512:513]
                                rd = [tssb[bs_], tz, tpbuf[bp], tpbuf[bpp]]
                            P.op("dve", lambda e: e.tensor_tensor_scan(out=pbuf[bp][:, 1:1 + wdt], data0=ssb[bs_][:, 0:wdt], data1=zeros[:, 0:wdt],
                                                                       initial=init, op0=ALU.mult, op1=ALU.add), reads=rd, writes=[tpbuf[bp]])

                        def ph3(bp=bp, bw=bw, wdt=wdt):
                            P.op("dve", lambda e: e.tensor_tensor(out=wsb[bw][:, 0:wdt], in0=pbuf[bp][:, 0:wdt], in1=pbuf[bp][:, 1:1 + wdt], op=ALU.subtract),
                                 reads=[tpbuf[bp]], writes=[twsb[bw]])

                        def ph4(bw=bw, bwt=bwt, nblk=nblk):
                            for jb in range(nblk):
                                P.op("pe", lambda e, jb=jb: e.transpose(out=pwT[bwt][:, jb * 128:(jb + 1) * 128], in_=wsb[bw][:, jb * 128:(jb + 1) * 128], identity=ident[:]),
                                     reads=[twsb[bw], tident], writes=[tpwT[bwt]])

                        def ph5(bwt=bwt, bws=bws, wdt=wdt, g=g):
                            if True:
                                P.op("act", lambda e: e.activation(out=wTs[bws][:, 0:wdt], in_=pwT[bwt][:, 0:wdt], func=AF.Copy), reads=[tpwT[bwt]], writes=[twTs[bws]])
                            else:
                                P.op("dve", lambda e: e.tensor_copy(out=wTs[bws][:, 0:wdt], in_=pwT[bwt][:, 0:wdt]), reads=[tpwT[bwt]], writes=[twTs[bws]])

                        def ph6(bws=bws, nblk=nblk, col0=col0, h=h, ps=ps, ob=ob, n=n, ntile=ntile, hp=hp, ft=ft, q0=q0):
                            for jb in range(nblk):
                                blk = col0 // 128 + jb
                                first = (n == 0 and jb == 0)
                                last = (n == ntile - 1 and jb == nblk - 1)
                                P.op("pe", lambda e, jb=jb, blk=blk, first=first, last=last: e.matmul(
                                    out=po[ob][ps, 0:128], lhsT=vsb[:, blk, h * 64:(h + 1) * 64], rhs=wTs[bws][:, jb * 128:(jb + 1) * 128], start=first, stop=last),
                                    reads=[tv, twTs[bws]], writes=[tpo[ob]])
                            if hp == 1 and n == ntile - 1:
                                P.op("act", lambda e: e.activation(out=oT[:, ft, q0:q0 + 128], in_=po[ob][:, 0:128], func=AF.Copy), reads=[tpo[ob]], writes=[toT])

                        tasks.append([ph0, ph1, ph2, ph3, ph4, ph5, ph6])
        run_pipeline(tasks, ATT_OFFS)
        for t in range(NB):
            r0 = c0 + t * 128
            xb = kx % 2
            kx += 1
            P.dma("sp", lambda e, xb=xb, r0=r0: e.dma_start(out=xt[xb][:], in_=h_in[r0:r0 + 128, :]), writes=[txt[xb]])
            for nb in range(2):
                b = kk % 2
                kk += 1
                for ft in range(8):
                    P.op("pe", lambda e, b=b, ft=ft, t=t, nb=nb: e.matmul(out=pw[b][:], lhsT=oT[:, ft, t * 128:(t + 1) * 128], rhs=wo[:, ft, nb * 512:(nb + 1) * 512],
                                                                          start=(ft == 0), stop=(ft == 7)), reads=[toT, two], writes=[tpw[b]])
                P.op("dve", lambda e, b=b, xb=xb, nb=nb: e.tensor_tensor(out=xt[xb][:, nb * 512:(nb + 1) * 512], in0=pw[b][:], in1=xt[xb][:, nb * 512:(nb + 1) * 512], op=ALU.add),
                     reads=[tpw[b], txt[xb]], writes=[txt[xb]])
            to = P.tile()
            P.dma("act", lambda e, xb=xb, r0=r0: e.dma_start(out=h_out[r0:r0 + 128, :], in_=xt[xb][:]), reads=[txt[xb]], writes=[to])
            outs.append(to)
    P.finish(outs)


def stage_a1(nc, h_in, uT, g_ap, win_ap, ident_ap, n_tok, prep=None):
    P = Prog(nc)
    if prep is not None:
        prep(P)
    ident, tident = load_ident(P, ident_ap)
    g_sb, tg = load_gain(P, "g", g_ap)
    win = P.sbuf("win", [128, 8, 1024], BF16)
    twin = P.tile()
    load_weight_bf16(P, win, twin, win_ap, 8, g_sb, tg)
    nt = NormT(P, ident, tident)
    pu = [P.psum(f"pu{i}", [128, 512]) for i in range(2)]
    tpu = P.tiles(2)
    ubuf = [P.sbuf(f"ubuf{i}", [128, 8, 512], BF16) for i in range(2)]
    tubuf = P.tiles(2)
    outs = []
    k = 0
    for u in range(n_tok // 512):
        r0 = u * 512
        ob = u % 2
        xt, txt, xnT, txnT = nt.run(h_in[r0:r0 + 512, :])
        for m in range(8):
            b = k % 2
            k += 1
            for c in range(8):
                P.op("pe", lambda e, c=c, b=b, m=m: e.matmul(out=pu[b][:], lhsT=win[:, c, m * 128:(m + 1) * 128], rhs=xnT[:, c, :],
                                                             start=(c == 0), stop=(c == 7)), reads=[twin, txnT], writes=[tpu[b]])
            if m % 2 == 0:
                P.op("act", lambda e, b=b, m=m, ob=ob: e.activation(out=ubuf[ob][:, m, :], in_=pu[b][:], func=AF.Copy), reads=[tpu[b]], writes=[tubuf[ob]])
            else:
                P.op("dve", lambda e, b=b, m=m, ob=ob: e.tensor_copy(out=ubuf[ob][:, m, :], in_=pu[b][:]), reads=[tpu[b]], writes=[tubuf[ob]])
        to = P.tile()
        P.dma("act", lambda e, ob=ob, r0=r0: e.dma_start(out=uT[:, r0:r0 + 512].rearrange("(m p) n -> p m n", p=128), in_=ubuf[ob][:]),
              reads=[tubuf[ob]], writes=[to])
        outs.append(to)
    P.finish(outs)


TWO_PI = 6.283185307179586
ATT_OFFS = [0, 2, 4, 6, 8, 9, 11]


def a2_prep(P, alloc, lre_ap, lim_ap, bre_ap, bim_ap, cre_ap, cim_ap, d_ap, ldt_ap, iota_ap, S):
    I32 = mybir.dt.int32
    NP = 32
    lr = alloc("lr", [128, NP], F32)
    li = alloc("li", [128, NP], F32)
    dt = alloc("dt", [128, NP], F32)
    tprm = P.tile()
    for e_ in range(2):
        ps = slice(e_ * 64, (e_ + 1) * 64)
        P.dma("sp", lambda e, e_=e_, ps=ps: e.dma_start(out=lr[ps, :], in_=lre_ap.rearrange("(k e) n -> e n k", e=2)[e_], allow_slow_non_contiguous=True), writes=[tprm])
        P.dma("sp", lambda e, e_=e_, ps=ps: e.dma_start(out=li[ps, :], in_=lim_ap.rearrange("(k e) n -> e n k", e=2)[e_], allow_slow_non_contiguous=True), writes=[tprm])
        P.dma("sp", lambda e, e_=e_, ps=ps: e.dma_start(out=dt[ps, :], in_=ldt_ap.rearrange("(k e) -> e k", e=2)[e_].partition_broadcast(64), allow_slow_non_contiguous=True), writes=[tprm])
    cst = alloc("cst", [128, 4], F32)
    tcst = P.tile()
    P.op("dve", lambda e: e.memset(cst[:, 0:1], TWO_PI / 4), writes=[tcst])
    P.op("dve", lambda e: e.memset(cst[:, 1:2], 0.0), reads=[tcst], writes=[tcst])
    sc = {}
    for nm in ["f", "fhi", "flo", "r", "t0", "t1", "t2", "t3", "sn", "cs", "ar", "ai", "qre", "qim", "nqim", "den"]:
        sc[nm] = alloc("p_" + nm, [128, NP], F32)
    fhb = alloc("p_fhb", [128, NP], BF16)
    tiq = alloc("p_ti", [128, NP], I32)
    tsc = P.tile()

    def V(fn, eng="dve", extra=()):
        P.op(eng, fn, reads=[tprm, tsc, tcst] + list(extra), writes=[tsc])

    V(lambda e: e.activation(out=sc["t0"][:], in_=dt[:], func=AF.Exp), "act")
    V(lambda e: e.activation(out=sc["t1"][:], in_=sc["t0"][:], func=AF.Ln), "act")
    V(lambda e: e.tensor_tensor(out=sc["t1"][:], in0=dt[:], in1=sc["t1"][:], op=ALU.subtract))
    V(lambda e: e.tensor_scalar(out=sc["t1"][:], in0=sc["t1"][:], scalar1=1.0, scalar2=None, op0=ALU.add))
    V(lambda e: e.tensor_tensor(out=dt[:], in0=sc["t0"][:], in1=sc["t1"][:], op=ALU.mult))
    V(lambda e: e.tensor_tensor(out=sc["t0"][:], in0=li[:], in1=dt[:], op=ALU.mult))
    V(lambda e: e.tensor_scalar(out=sc["f"][:], in0=sc["t0"][:], scalar1=1.0 / TWO_PI, scalar2=None, op0=ALU.mult))
    V(lambda e: e.tensor_copy(out=fhb[:], in_=sc["f"][:]))
    V(lambda e: e.tensor_copy(out=sc["fhi"][:], in_=fhb[:]))
    V(lambda e: e.tensor_tensor(out=sc["flo"][:], in0=sc["f"][:], in1=sc["fhi"][:], op=ALU.subtract))
    V(lambda e: e.tensor_tensor(out=sc["t1"][:], in0=lr[:], in1=dt[:], op=ALU.mult))
    V(lambda e: e.activation(out=sc["t2"][:], in_=sc["t1"][:], func=AF.Exp), "act")
    V(lambda e: e.activation(out=sc["t3"][:], in_=sc["t2"][:], func=AF.Ln), "act")
    V(lambda e: e.tensor_tensor(out=sc["t3"][:], in0=sc["t1"][:], in1=sc["t3"][:], op=ALU.subtract))
    V(lambda e: e.tensor_scalar(out=sc["t3"][:], in0=sc["t3"][:], scalar1=1.0, scalar2=None, op0=ALU.add))
    V(lambda e: e.tensor_tensor(out=sc["r"][:], in0=sc["t2"][:], in1=sc["t3"][:], op=ALU.mult))
    V(lambda e: e.tensor_copy(out=tiq[:], in_=sc["f"][:]))
    V(lambda e: e.tensor_copy(out=sc["t3"][:], in_=tiq[:]))
    V(lambda e: e.tensor_tensor(out=sc["t2"][:], in0=sc["f"][:], in1=sc["t3"][:], op=ALU.subtract))
    V(lambda e: e.activation(out=sc["sn"][:], in_=sc["t2"][:], func=AF.Sin, scale=TWO_PI), "act")
    V(lambda e: e.tensor_scalar(out=sc["t3"][:], in0=sc["t2"][:], scalar1=-1.0, scalar2=None, op0=ALU.mult))
    V(lambda e: e.tensor_tensor(out=sc["t3"][:], in0=sc["t3"][:], in1=sc["t2"][:], op=ALU.min))
    V(lambda e: e.activation(out=sc["cs"][:], in_=sc["t3"][:], func=AF.Sin, scale=TWO_PI, bias=cst[:, 0:1]), "act")
    V(lambda e: e.tensor_tensor(out=sc["ar"][:], in0=sc["r"][:], in1=sc["cs"][:], op=ALU.mult))
    V(lambda e: e.tensor_tensor(out=sc["ai"][:], in0=sc["r"][:], in1=sc["sn"][:], op=ALU.mult))
    V(lambda e: e.tensor_scalar(out=sc["ar"][:], in0=sc["ar"][:], scalar1=-1.0, scalar2=None, op0=ALU.add))
    V(lambda e: e.tensor_tensor(out=sc["t0"][:], in0=lr[:], in1=lr[:], op=ALU.mult))
    V(lambda e: e.tensor_tensor(out=sc["t1"][:], in0=li[:], in1=li[:], op=ALU.mult))
    V(lambda e: e.tensor_tensor(out=sc["den"][:], in0=sc["t0"][:], in1=sc["t1"][:], op=ALU.add))
    V(lambda e: e.reciprocal(out=sc["den"][:], in_=sc["den"][:]))
    V(lambda e: e.tensor_tensor(out=sc["t0"][:], in0=sc["ar"][:], in1=lr[:], op=ALU.mult))
    V(lambda e: e.tensor_tensor(out=sc["t1"][:], in0=sc["ai"][:], in1=li[:], op=ALU.mult))
    V(lambda e: e.tensor_tensor(out=sc["t0"][:], in0=sc["t0"][:], in1=sc["t1"][:], op=ALU.add))
    V(lambda e: e.tensor_tensor(out=sc["qre"][:], in0=sc["t0"][:], in1=sc["den"][:], op=ALU.mult))
    V(lambda e: e.tensor_tensor(out=sc["t0"][:], in0=sc["ai"][:], in1=lr[:], op=ALU.mult))
    V(lambda e: e.tensor_tensor(out=sc["t1"][:], in0=sc["ar"][:], in1=li[:], op=ALU.mult))
    V(lambda e: e.tensor_tensor(out=sc["t0"][:], in0=sc["t0"][:], in1=sc["t1"][:], op=ALU.subtract))
    V(lambda e: e.tensor_tensor(out=sc["qim"][:], in0=sc["t0"][:], in1=sc["den"][:], op=ALU.mult))
    V(lambda e: e.tensor_scalar(out=sc["nqim"][:], in0=sc["qim"][:], scalar1=-1.0, scalar2=None, op0=ALU.mult))
    craw = alloc("craw", [128, NP, 2, 16], F32)
    tcraw = P.tile()
    for e_ in range(2):
        ps = slice(e_ * 64, (e_ + 1) * 64)
        for k1 in range(NP):
            for ri, ap_ in enumerate((cre_ap, cim_ap)):
                P.dma("sp" if ri == 0 else "act", lambda e, e_=e_, ps=ps, k1=k1, ri=ri, ap_=ap_: e.dma_start(
                    out=craw[ps, k1, ri, :], in_=ap_[2 * k1 + e_].rearrange("h n -> n h"), allow_slow_non_contiguous=True),
                    writes=[tcraw])
    CT = alloc("CT", [128, NP, 3, 128], BF16)
    tCT = P.tile()
    ctmp = alloc("ctmp", [128, NP, 16], F32)
    ctmp2 = alloc("ctmp2", [128, NP, 16], F32)
    tctmp = P.tile()
    P.op("pool", lambda e: e.memset(CT[:], 0.0), writes=[tCT])

    def bq(nm):
        return sc[nm][:].unsqueeze(2).to_broadcast([128, NP, 16])

    rd = [tcraw, tsc]
    P.op("dve", lambda e: e.tensor_tensor(out=ctmp[:], in0=craw[:, :, 0, :], in1=bq("qre"), op=ALU.mult), reads=rd, writes=[tctmp])
    P.op("dve", lambda e: e.tensor_tensor(out=ctmp2[:], in0=craw[:, :, 1, :], in1=bq("qim"), op=ALU.mult), reads=rd, writes=[tctmp])
    for e_ in range(2):
        ps = slice(e_ * 64, (e_ + 1) * 64)
        P.op("dve", lambda e, e_=e_, ps=ps: e.tensor_tensor(out=CT[ps, :, 0, e_ * 16:(e_ + 1) * 16], in0=ctmp[ps], in1=ctmp2[ps], op=ALU.subtract),
             reads=[tctmp], writes=[tCT])
    P.op("dve", lambda e: e.tensor_tensor(out=ctmp[:], in0=craw[:, :, 0, :], in1=bq("nqim"), op=ALU.mult), reads=rd + [tCT], writes=[tctmp])
    P.op("dve", lambda e: e.tensor_tensor(out=ctmp2[:], in0=craw[:, :, 1, :], in1=bq("qre"), op=ALU.mult), reads=rd, writes=[tctmp])
    for e_ in range(2):
        ps = slice(e_ * 64, (e_ + 1) * 64)
        P.op("dve", lambda e, e_=e_, ps=ps: e.tensor_tensor(out=CT[ps, :, 1, e_ * 16:(e_ + 1) * 16], in0=ctmp[ps], in1=ctmp2[ps], op=ALU.subtract),
             reads=[tctmp], writes=[tCT])
    BT = alloc("BT", [32, NP, 2, 128], BF16)
    tBT = P.tile()
    P.op("pool", lambda e: e.memset(BT[:], 0.0), writes=[tBT])
    for e_ in range(2):
        for ri, ap_ in enumerate((bre_ap, bim_ap)):
            for k1 in range(NP):
                P.dma("pool", lambda e, e_=e_, ri=ri, ap_=ap_, k1=k1: e.dma_start(
                    out=BT[e_ * 16:(e_ + 1) * 16, k1, ri, e_ * 64:(e_ + 1) * 64], in_=ap_[2 * k1 + e_].rearrange("n h -> h n"),
                    allow_slow_non_contiguous=True), writes=[tBT])
    dwin = alloc("dwin", [32, NP], F32)
    tdw = P.tile()
    P.dma("sp", lambda e: e.dma_start(out=dwin[:], in_=d_ap.rearrange("(k r) -> r k", r=32), allow_slow_non_contiguous=True), writes=[tdw])
    iota = alloc("iota", [128, S], F32)
    tio = P.tile()
    P.dma("sp", lambda e: e.dma_start(out=iota[:], in_=iota_ap[:, 0:S]), writes=[tio])
    return dict(sc=sc, tsc=tsc, CT=CT, tCT=tCT, BT=BT, tBT=tBT, dwin=dwin, tdw=tdw, iota=iota, tio=tio, cst=cst, tcst=tcst)


def stage_a2(nc, uT, gT, lre_ap, lim_ap, bre_ap, bim_ap, cre_ap, cim_ap, d_ap, ldt_ap, iota_ap, n_seq, S, dbg_pairs=None, dbg=None, dbg_stop=9, ctx=None):
    P = Prog(nc)
    I32 = mybir.dt.int32
    NP = 32
    if ctx is None:
        ctx = a2_prep(P, P.sbuf, lre_ap, lim_ap, bre_ap, bim_ap, cre_ap, cim_ap, d_ap, ldt_ap, iota_ap, S)
    else:
        ctx = dict(ctx)
        for tn in ("tsc", "tCT", "tBT", "tdw", "tio", "tcst"):
            ctx[tn] = P.tile()
    sc, tsc, CT, tCT, BT, tBT = ctx["sc"], ctx["tsc"], ctx["CT"], ctx["tCT"], ctx["BT"], ctx["tBT"]
    dwin, tdw, iota, tio, cst, tcst = ctx["dwin"], ctx["tdw"], ctx["iota"], ctx["tio"], ctx["cst"], ctx["tcst"]
    HB = 512
    nq = S // HB
    sn = [P.sbuf(f"sn{i}", [128, S], F32) for i in range(2)]
    cs = [P.sbuf(f"cs{i}", [128, S], F32) for i in range(2)]
    snb = [P.sbuf(f"snb{i}", [128, S], BF16) for i in range(2)]
    csb = [P.sbuf(f"csb{i}", [128, S], BF16) for i in range(2)]
    rtab = [P.sbuf(f"rtab{i}", [128, HB], F32) for i in range(2)]
    ttab = P.tiles(2)
    SH = S // 2
    wk1 = P.sbuf("wk1", [128, SH], F32)
    wk2 = P.sbuf("wk2", [128, SH], F32)
    tiw = P.sbuf("tiw", [128, SH], I32)
    twk = P.tile()
    ones = P.sbuf("ones", [128, HB], F32)
    zc = P.sbuf("zc", [128, 1], F32)
    tones = P.tile()
    P.op("dve", lambda e: e.memset(ones[:], 1.0), writes=[tones])
    P.op("dve", lambda e: e.memset(zc[:], 0.0), reads=[tones], writes=[tones])
    P.op("dve", lambda e: e.tensor_scalar(out=CT[:, :, 2, :], in0=CT[:, :, 0, :], scalar1=-1.0, scalar2=None, op0=ALU.mult), reads=[tCT], writes=[tCT])
    NU = 3
    uwin = [P.sbuf(f"uwin{i}", [32, S], BF16) for i in range(NU)]
    tuw = P.tiles(NU)
    pbr = [P.psum(f"pbr{i}", [128, HB]) for i in range(2)]
    pbi = [P.psum(f"pbi{i}", [128, HB]) for i in range(2)]
    tpb = P.tiles(2)
    py = [P.psum(f"py{i}", [128, HB]) for i in range(2)]
    tpy = P.tiles(2)
    bsb = [[P.sbuf(f"bsb{i}_{j}", [128, HB], F32) for j in range(2)] for i in range(2)]
    tbsb = P.tiles(2)
    A = [[P.sbuf(f"A{i}_{j}", [128, HB], F32) for j in range(4)] for i in range(2)]
    tA01 = P.tiles(2)
    tA23 = P.tiles(2)
    W = [[P.sbuf(f"W{i}_{j}", [128, HB], F32) for j in range(2)] for i in range(2)]
    tW = P.tiles(2)
    NZb = 3
    Z = [[P.sbuf(f"Z{i}_{j}", [128, HB], F32) for j in range(2)] for i in range(NZb)]
    tZ = P.tiles(NZb)
    Zb = [[P.sbuf(f"Zb{i}_{j}", [128, HB], BF16) for j in range(2)] for i in range(2)]
    tZb = P.tiles(2)
    Bq = [[P.sbuf(f"B{i}_{j}", [128, HB], BF16) for j in range(4)] for i in range(2)]
    tB = P.tiles(2)
    tB2 = P.tiles(2)
    ytmp = [P.sbuf(f"ytmp{i}", [32, HB], F32) for i in range(2)]
    tyt = P.tiles(2)
    gout = [P.sbuf(f"gout{i}", [32, S], BF16) for i in range(2)]
    tgo = P.tiles(2)
    outs = []
    npairs = NP if dbg_pairs is None else dbg_pairs

    def gen_tables(k):
        tb = k % 2
        fh, fl, rk = sc["fhi"][:, k:k + 1], sc["flo"][:, k:k + 1], sc["r"][:, k:k + 1]
        for hh in range(2):
            hs = slice(hh * SH, (hh + 1) * SH)
            P.op("dve", lambda e, hs=hs: e.tensor_scalar(out=wk1[:], in0=iota[:, hs], scalar1=fh, scalar2=None, op0=ALU.mult), reads=[tio, tsc], writes=[twk])
            P.op("dve", lambda e: e.tensor_copy(out=tiw[:], in_=wk1[:]), reads=[twk], writes=[twk])
            P.op("dve", lambda e: e.tensor_copy(out=wk2[:], in_=tiw[:]), reads=[twk], writes=[twk])
            P.op("dve", lambda e: e.tensor_tensor(out=wk1[:], in0=wk1[:], in1=wk2[:], op=ALU.subtract), reads=[twk], writes=[twk])
            P.op("dve", lambda e, hs=hs: e.scalar_tensor_tensor(out=wk2[:], in0=iota[:, hs], scalar=fl, in1=wk1[:], op0=ALU.mult, op1=ALU.add), reads=[tio, tsc, twk], writes=[twk])
            P.op("dve", lambda e: e.tensor_copy(out=tiw[:], in_=wk2[:]), reads=[twk], writes=[twk])
            P.op("dve", lambda e: e.tensor_copy(out=wk1[:], in_=tiw[:]), reads=[twk], writes=[twk])
            P.op("dve", lambda e: e.tensor_tensor(out=wk2[:], in0=wk2[:], in1=wk1[:], op=ALU.subtract), reads=[twk], writes=[twk])
            P.op("act", lambda e, hs=hs: e.activation(out=sn[tb][:, hs], in_=wk2[:], func=AF.Sin, scale=TWO_PI), reads=[twk], writes=[ttab[tb]])
            P.op("act", lambda e: e.activation(out=wk1[:], in_=wk2[:], func=AF.Copy, scale=-1.0), reads=[twk], writes=[twk])
            P.op("dve", lambda e: e.tensor_tensor(out=wk1[:], in0=wk1[:], in1=wk2[:], op=ALU.min), reads=[twk], writes=[twk])
            P.op("act", lambda e, hs=hs: e.activation(out=cs[tb][:, hs], in_=wk1[:], func=AF.Sin, scale=TWO_PI, bias=cst[:, 0:1]), reads=[twk, tcst], writes=[ttab[tb]])
            P.op("act", lambda e, hs=hs: e.activation(out=snb[tb][:, hs], in_=sn[tb][:, hs], func=AF.Copy), reads=[ttab[tb]], writes=[ttab[tb]])
            P.op("act", lambda e, hs=hs: e.activation(out=csb[tb][:, hs], in_=cs[tb][:, hs], func=AF.Copy), reads=[ttab[tb]], writes=[ttab[tb]])
        P.op("dve", lambda e: e.tensor_scalar(out=rtab[tb][:], in0=ones[:], scalar1=rk, scalar2=None, op0=ALU.mult), reads=[tones, tsc, ttab[tb]], writes=[ttab[tb]])

    if npairs > 0:
        gen_tables(0)
    tasks = []
    g = 0
    for k in range(npairs):
        row0 = k * 32
        tb = k % 2
        for s in range(n_seq):
            c0 = s * S
            ub = (k * n_seq + s) % NU
            gb = (k * n_seq + s) % 2
            for qi in range(nq):
                t0 = qi * HB
                tsl = slice(t0, t0 + HB)
                b2 = g % 2
                bz = g % NZb
                bzp = (g - 1) % NZb
                g += 1

                def p0(k=k, s=s, qi=qi, ub=ub, row0=row0, c0=c0):
                    if qi == 0:
                        P.dma("sp", lambda e: e.dma_start(out=uwin[ub][:], in_=uT[row0:row0 + 32, c0:c0 + S]), writes=[tuw[ub]])
                    tpp = n_seq * nq
                    ti = s * nq + qi
                    assert tpp >= 8, "table double-buffering needs >= 8 tasks per pair"
                    if ti == 7 and k + 1 < npairs:
                        gen_tables(k + 1)

                def p1(k=k, ub=ub, b2=b2, tsl=tsl):
                    P.op("pe", lambda e: e.matmul(out=pbr[b2][:], lhsT=BT[:, k, 0, :], rhs=uwin[ub][:, tsl], start=True, stop=True),
                         reads=[tBT, tuw[ub]], writes=[tpb[b2]])
                    P.op("pe", lambda e: e.matmul(out=pbi[b2][:], lhsT=BT[:, k, 1, :], rhs=uwin[ub][:, tsl], start=True, stop=True),
                         reads=[tBT, tuw[ub]], writes=[tpb[b2]])

                def p2(b2=b2):
                    P.op("act", lambda e: e.activation(out=bsb[b2][0][:], in_=pbr[b2][:], func=AF.Copy), reads=[tpb[b2]], writes=[tbsb[b2]])
                    P.op("act", lambda e: e.activation(out=bsb[b2][1][:], in_=pbi[b2][:], func=AF.Copy), reads=[tpb[b2]], writes=[tbsb[b2]])

                def p3(b2=b2, tb=tb, tsl=tsl):
                    rd = [ttab[tb], tbsb[b2]]
                    P.op("dve", lambda e: e.tensor_tensor(out=A[b2][0][:], in0=cs[tb][:, tsl], in1=bsb[b2][0][:], op=ALU.mult), reads=rd, writes=[tA01[b2]])
                    P.op("dve", lambda e: e.tensor_tensor(out=A[b2][1][:], in0=sn[tb][:, tsl], in1=bsb[b2][1][:], op=ALU.mult), reads=rd, writes=[tA01[b2]])
                    P.op("dve", lambda e: e.tensor_tensor(out=A[b2][2][:], in0=cs[tb][:, tsl], in1=bsb[b2][1][:], op=ALU.mult), reads=rd, writes=[tA23[b2]])
                    P.op("dve", lambda e: e.tensor_tensor(out=A[b2][3][:], in0=sn[tb][:, tsl], in1=bsb[b2][0][:], op=ALU.mult), reads=rd, writes=[tA23[b2]])

                def p4(b2=b2):
                    P.op("dve", lambda e: e.tensor_tensor(out=W[b2][0][:], in0=A[b2][0][:], in1=A[b2][1][:], op=ALU.add), reads=[tA01[b2]], writes=[tW[b2]])
                    P.op("dve", lambda e: e.tensor_tensor(out=W[b2][1][:], in0=A[b2][2][:], in1=A[b2][3][:], op=ALU.subtract), reads=[tA23[b2]], writes=[tW[b2]])

                def p5(b2=b2, bz=bz, bzp=bzp, tb=tb, qi=qi):
                    for j in range(2):
                        if qi == 0:
                            init, rd = zc[:, 0:1], [tW[b2], ttab[tb], tones]
                        else:
                            init, rd = Z[bzp][j][:, HB - 1:HB], [tW[b2], ttab[tb], tZ[bzp]]
                        P.op("dve", lambda e, j=j, init=init: e.tensor_tensor_scan(out=Z[bz][j][:], data0=rtab[tb][:], data1=W[b2][j][:], initial=init,
                                                                                   op0=ALU.mult, op1=ALU.add), reads=rd, writes=[tZ[bz]])

                def p6(b2=b2, bz=bz):
                    for j in range(2):
                        P.op("act", lambda e, j=j: e.activation(out=Zb[b2][j][:], in_=Z[bz][j][:], func=AF.Copy), reads=[tZ[bz]], writes=[tZb[b2]])

                def p7(b2=b2, tb=tb, tsl=tsl):
                    rd = [ttab[tb], tZb[b2]]
                    P.op("dve", lambda e: e.tensor_tensor(out=Bq[b2][0][:], in0=csb[tb][:, tsl], in1=Zb[b2][0][:], op=ALU.mult), reads=rd, writes=[tB[b2]])
                    P.op("dve", lambda e: e.tensor_tensor(out=Bq[b2][1][:], in0=snb[tb][:, tsl], in1=Zb[b2][1][:], op=ALU.mult), reads=rd, writes=[tB[b2]])
                    P.op("dve", lambda e: e.tensor_tensor(out=Bq[b2][2][:], in0=snb[tb][:, tsl], in1=Zb[b2][0][:], op=ALU.mult), reads=rd, writes=[tB2[b2]])
                    P.op("dve", lambda e: e.tensor_tensor(out=Bq[b2][3][:], in0=csb[tb][:, tsl], in1=Zb[b2][1][:], op=ALU.mult), reads=rd, writes=[tB2[b2]])

                def p8(b2=b2, k=k):
                    for i4, ci in enumerate((0, 2, 1, 1)):
                        P.op("pe", lambda e, i4=i4, ci=ci: e.matmul(out=py[b2][:], lhsT=CT[:, k, ci, :], rhs=Bq[b2][i4][:], start=(i4 == 0), stop=(i4 == 3)),
                             reads=[tCT, tB[b2], tB2[b2]], writes=[tpy[b2]])

                def p9(b2=b2, ub=ub, gb=gb, tsl=tsl, k=k, qi=qi, row0=row0, c0=c0):
                    P.op("dve", lambda e: e.scalar_tensor_tensor(out=ytmp[b2][:], in0=uwin[ub][:, tsl], scalar=dwin[:, k:k + 1], in1=py[b2][0:32, :],
                                                                 op0=ALU.mult, op1=ALU.add), reads=[tuw[ub], tdw, tpy[b2]], writes=[tyt[b2]])
                    P.op("act", lambda e: e.activation(out=gout[gb][:, tsl], in_=ytmp[b2][:], func=AF.Gelu), reads=[tyt[b2]], writes=[tgo[gb]])
                    if qi == nq - 1:
                        to = P.tile()
                        P.dma("act", lambda e: e.dma_start(out=gT[row0:row0 + 32, c0:c0 + S], in_=gout[gb][:]), reads=[tgo[gb]], writes=[to])
                        outs.append(to)

                tasks.append([p0, p1, p2, p3, p4, p5, p6, p7, p8, p9])
    run_pipeline(tasks, [0, 1, 2, 3, 4, 5, 6, 7, 8, 9])
    if dbg is not None:
        for nm, ap_ in dbg.items():
            src = {"CT": CT, "BT": BT, "sn": sn[(npairs - 1) % 2], "cs": cs[(npairs - 1) % 2], "r": sc["r"], "qre": sc["qre"], "qim": sc["qim"], "fhi": sc["fhi"], "flo": sc["flo"]}[nm]
            tl = {"CT": tCT, "BT": tBT, "sn": ttab[(npairs - 1) % 2], "cs": ttab[(npairs - 1) % 2]}.get(nm, tsc)
            to = P.tile()
            P.dma("sp", lambda e, ap_=ap_, src=src: e.dma_start(out=ap_, in_=src[:]), reads=[tl], writes=[to])
            outs.append(to)
    P.finish(outs)


def stage_a3(nc, h_in, h_out, gT, wglu_ap, n_tok):
    P = Prog(nc)
    wg = P.sbuf("wg", [128, 8, 2048], BF16)
    twg = P.tile()
    load_weight_bf16(P, wg, twg, wglu_ap, 8)
    gb = [P.sbuf(f"gb{i}", [128, 8, 512], BF16) for i in range(2)]
    tgb = P.tiles(2)
    xt = [P.sbuf(f"xt{i}", [128, 4, D], F32) for i in range(2)]
    txt = P.tiles(2)
    pv = [P.psum(f"pv{i}", [128, 512]) for i in range(2)]
    tpv = P.tiles(2)
    pg = [P.psum(f"pg{i}", [128, 512]) for i in range(2)]
    tpg = P.tiles(2)
    sg = [P.sbuf(f"sg{i}", [128, 512], F32) for i in range(2)]
    tsg = P.tiles(2)
    outs = []
    k = 0
    for u in range(n_tok // 512):
        r0 = u * 512
        ob = u % 2
        P.dma("sp", lambda e, ob=ob, r0=r0: e.dma_start(out=gb[ob][:], in_=gT[:, r0:r0 + 512].rearrange("(c p) n -> p c n", p=128)), writes=[tgb[ob]])
        P.dma("sp", lambda e, ob=ob, r0=r0: e.dma_start(out=xt[ob][:], in_=h_in[r0:r0 + 512, :].rearrange("(t p) d -> p t d", p=128)), writes=[txt[ob]])
        for t in range(4):
            for nb in range(2):
                b = k % 2
                k += 1
                for c in range(8):
                    P.op("pe", lambda e, c=c, b=b, t=t, nb=nb, ob=ob: e.matmul(out=pv[b][:], lhsT=gb[ob][:, c, t * 128:(t + 1) * 128], rhs=wg[:, c, nb * 512:(nb + 1) * 512],
                                                                               start=(c == 0), stop=(c == 7)), reads=[tgb[ob], twg], writes=[tpv[b]])
                for c in range(8):
                    P.op("pe", lambda e, c=c, b=b, t=t, nb=nb, ob=ob: e.matmul(out=pg[b][:], lhsT=gb[ob][:, c, t * 128:(t + 1) * 128], rhs=wg[:, c, 1024 + nb * 512:1024 + (nb + 1) * 512],
                                                                               start=(c == 0), stop=(c == 7)), reads=[tgb[ob], twg], writes=[tpg[b]])
                P.op("act", lambda e, b=b: e.activation(out=sg[b][:], in_=pg[b][:], func=AF.Sigmoid), reads=[tpg[b]], writes=[tsg[b]])
                P.op("dve", lambda e, b=b: e.tensor_tensor(out=sg[b][:], in0=sg[b][:], in1=pv[b][:], op=ALU.mult), reads=[tsg[b], tpv[b]], writes=[tsg[b]])
                P.op("pool", lambda e, b=b, t=t, nb=nb, ob=ob: e.tensor_tensor(out=xt[ob][:, t, nb * 512:(nb + 1) * 512], in0=xt[ob][:, t, nb * 512:(nb + 1) * 512], in1=sg[b][:], op=ALU.add),
                     reads=[tsg[b], txt[ob]], writes=[txt[ob]])
        to = P.tile()
        P.dma("act", lambda e, ob=ob, r0=r0: e.dma_start(out=h_out[r0:r0 + 512, :].rearrange("(t p) d -> p t d", p=128), in_=xt[ob][:]), reads=[txt[ob]], writes=[to])
        outs.append(to)
    P.finish(outs)


N_CORES = 8
SEQ = 2048
N_SEQ = 4
NT = N_SEQ * SEQ

_PARAMS = [
    ("a_norm", [1, 1024]), ("a_w_in", [1, 1024, 1024]), ("a_lam_re", [1, 64, 64]), ("a_lam_im", [1, 64, 64]),
    ("a_b_re", [1, 64, 64, 16]), ("a_b_im", [1, 64, 64, 16]), ("a_c_re", [1, 64, 16, 64]), ("a_c_im", [1, 64, 16, 64]),
    ("a_d", [1, 1024]), ("a_log_dt", [1, 64]), ("a_w_glu", [1, 1024, 2048]), ("kv_norm", [1024]), ("w_kv", [1024, 2048]),
    ("k_norm", [64]), ("b_norm", [1, 1024]), ("b_w_q", [1, 1024, 1024]), ("b_q_norm", [1, 64]), ("b_w_o", [1, 1024, 1024]),
    ("ffn_norm", [2, 1024]), ("ffn_w_up", [2, 1024, 5632]), ("ffn_conv_w", [2, 3, 2816]), ("ffn_conv_b", [2, 2816]),
    ("ffn_w_down", [2, 2816, 1024]),
]


def build_program(N_SEQ=N_SEQ, SEQ=SEQ, debug=False):
    NT = N_SEQ * SEQ
    nc = bass.Bass("TRN2", target_bir_lowering=False)
    x = nc.dram_tensor("x", [NT, D], F32, kind="ExternalInput").ap()
    prm = {n: nc.dram_tensor(n, s, F32, kind="ExternalInput").ap() for n, s in _PARAMS}
    ident = nc.dram_tensor("c_ident", [128, 128], F32, kind="ExternalInput").ap()
    bones = nc.dram_tensor("c_bones", [128, 128], F32, kind="ExternalInput").ap()
    maskb = nc.dram_tensor("c_maskb", [128, 128], F32, kind="ExternalInput").ap()
    iota = nc.dram_tensor("c_iota", [128, 2048], F32, kind="ExternalInput").ap()
    out = nc.dram_tensor("out", [NT, D], F32, kind="ExternalOutput").ap()
    kd = "ExternalOutput" if debug else "Internal"
    h1 = nc.dram_tensor("h1", [NT, D], F32, kind=kd).ap()
    h2 = nc.dram_tensor("h2", [NT, D], F32, kind=kd).ap()
    h3 = nc.dram_tensor("h3", [NT, D], F32, kind=kd).ap()
    uT = nc.dram_tensor("uT", [D, NT], BF16).ap()
    gT = nc.dram_tensor("gT", [D, NT], BF16).ap()
    kT = nc.dram_tensor("kT", [D, NT], BF16).ap()
    qT = nc.dram_tensor("qT", [D, NT], BF16).ap()
    vr = nc.dram_tensor("vr", [NT, D], BF16).ap()
    a2args = (prm["a_lam_re"][0], prm["a_lam_im"][0], prm["a_b_re"][0], prm["a_b_im"][0], prm["a_c_re"][0], prm["a_c_im"][0],
              prm["a_d"][0], prm["a_log_dt"][0], iota)
    keep = contextlib.ExitStack()
    box = {}

    def prep(P):
        box["ctx"] = a2_prep(P, lambda n, sh, dt_: keep.enter_context(nc.sbuf_tensor("keep_" + n, list(sh), dt_)), *a2args, SEQ)

    stage_a1(nc, x, uT, prm["a_norm"][0], prm["a_w_in"][0], ident, NT, prep=prep)
    stage_a2(nc, uT, gT, *a2args, N_SEQ, SEQ, ctx=box["ctx"])
    keep.close()
    stage_a3(nc, x, h1, gT, prm["a_w_glu"][0], NT)
    stage_ffn(nc, h1, h2, prm["ffn_norm"][0], prm["ffn_w_up"][0], prm["ffn_conv_w"][0], prm["ffn_conv_b"][0],
              prm["ffn_w_down"][0], ident, N_SEQ, SEQ)
    stage_kvq(nc, h2, kT, qT, vr, prm["kv_norm"], prm["w_kv"], prm["k_norm"], prm["b_norm"][0], prm["b_w_q"][0], prm["b_q_norm"][0],
              ident, bones, N_SEQ, SEQ)
    stage_att(nc, h2, h3, qT, kT, vr, prm["b_w_o"][0], ident, maskb, N_SEQ, SEQ)
    stage_ffn(nc, h3, out, prm["ffn_norm"][1], prm["ffn_w_up"][1], prm["ffn_conv_w"][1], prm["ffn_conv_b"][1],
              prm["ffn_w_down"][1], ident, N_SEQ, SEQ)
    return nc


def kernel(**inputs):
    x = np.ascontiguousarray(np.asarray(inputs["x"], dtype=np.float32))
    nc = build_program()
    p = np.arange(128)
    consts = {
        "c_ident": np.eye(128, dtype=np.float32),
        "c_bones": (p[:, None] // 64 == p[None, :] // 64).astype(np.float32),
        "c_maskb": np.where(p[None, :] + p[:, None] >= 128, 0.0, -30000.0).astype(np.float32),
        "c_iota": np.ascontiguousarray(np.tile(np.arange(2048, dtype=np.float32), (128, 1))),
    }
    params = {n: np.ascontiguousarray(np.asarray(inputs[n], dtype=np.float32)).reshape(s) for n, s in _PARAMS}
    in_maps = []
    for c in range(N_CORES):
        m = {"x": x[c * N_SEQ:(c + 1) * N_SEQ].reshape(NT, D)}
        m.update(params)
        m.update(consts)
        in_maps.append(m)
    res = run_bass_kernel_spmd(nc, in_maps, core_ids=list(range(N_CORES)))
    outs = [np.asarray(r["out"], dtype=np.float32).reshape(N_SEQ, SEQ, D) for r in res.results]
    return np.concatenate(outs, axis=0)
```

```python
import contextlib
import numpy as np
import concourse.bass as bass
import concourse.mybir as mybir
from concourse.bass_utils import run_bass_kernel_spmd

F32 = mybir.dt.float32
BF16 = mybir.dt.bfloat16
AF = mybir.ActivationFunctionType
ALU = mybir.AluOpType
AX = mybir.AxisListType

ENGS = ("pe", "act", "dve", "pool", "sp")
N_DMA_SEMS = 4

D = 1024
DFF = 2816
NFT = DFF // 128
EPS = 1e-6


class T:
    __slots__ = ("name", "writes", "reads")

    def __init__(self, name="t"):
        self.name = name
        self.writes = {}
        self.reads = {}


class Prog:
    _stage = 0

    def __init__(self, nc):
        self.nc = nc
        Prog._stage += 1
        self.sid = Prog._stage
        self.stack = contextlib.ExitStack()
        self.ops = {e: [] for e in ENGS}
        self.known = {e: {} for e in ENGS}
        self.ndma = {e: 0 for e in ENGS}
        self.cnt = {e: 0 for e in ENGS}
        self.milestones = {e: set() for e in ENGS}
        self._n = 0

    def sbuf(self, name, shape, dtype):
        return self.stack.enter_context(self.nc.sbuf_tensor(f"s{self.sid}_{name}", list(shape), dtype))

    def psum(self, name, shape, dtype=F32):
        return self.stack.enter_context(self.nc.psum_tensor(f"s{self.sid}_{name}", list(shape), dtype))

    def tile(self, name=None):
        return T(name or "t")

    def tiles(self, n):
        return [T() for _ in range(n)]

    def _collect(self, eng, reads, writes):
        need = {}
        for t in reads:
            for k, v in t.writes.items():
                if need.get(k, 0) < v:
                    need[k] = v
        for t in writes:
            for d in (t.writes, t.reads):
                for k, v in d.items():
                    if need.get(k, 0) < v:
                        need[k] = v
        waits = []
        kn = self.known[eng]
        for k, v in need.items():
            if k == ("e", eng) and eng == "pe":
                continue
            if kn.get(k, 0) >= v:
                continue
            kn[k] = v
            waits.append((k, v))
            if k[0] == "e":
                self.milestones[k[1]].add(v)
        return waits

    def op(self, eng, fn, reads=(), writes=()):
        waits = self._collect(eng, reads, writes)
        self.cnt[eng] += 1
        idx = self.cnt[eng]
        key = ("e", eng)
        self.ops[eng].append(dict(kind="op", fn=fn, waits=waits, idx=idx))
        for t in reads:
            if t.reads.get(key, 0) < idx:
                t.reads[key] = idx
        for t in writes:
            t.writes = {key: idx}
            t.reads = {}

    def dma(self, eng, fn, reads=(), writes=()):
        i = self.ndma[eng]
        self.ndma[eng] += 1
        slot = i % N_DMA_SEMS
        gen = i // N_DMA_SEMS
        key = ("d", eng, slot)
        waits = self._collect(eng, reads, writes)
        kn = self.known[eng]
        if gen > 0 and kn.get(key, 0) < 16 * gen:
            kn[key] = 16 * gen
            waits.append((key, 16 * gen))
        val = 16 * (gen + 1)
        self.ops[eng].append(dict(kind="dma", fn=fn, waits=waits, key=key))
        for t in reads:
            if t.reads.get(key, 0) < val:
                t.reads[key] = val
        for t in writes:
            t.writes = {key: val}
            t.reads = {}

    def wait_all(self, eng, tiles):
        waits = self._collect(eng, tiles, ())
        self.ops[eng].append(dict(kind="wait", waits=waits))

    def finish(self, out_tiles):
        self.wait_all("sp", out_tiles)
        nc = self.nc
        with nc.cleanup_on_exit():
            sems = {}
            for e in ENGS:
                if self.milestones[e]:
                    sems[("e", e)] = nc.alloc_semaphore(f"p{self.sid}_{e}")
                for s in range(min(N_DMA_SEMS, self.ndma[e])):
                    sems[("d", e, s)] = nc.alloc_semaphore(f"d{self.sid}_{e}_{s}")
            mmap = {e: {v: i + 1 for i, v in enumerate(sorted(self.milestones[e]))} for e in ENGS}

            def replay(e):
                def body(engine):
                    for o in self.ops[e]:
                        for k, v in o["waits"]:
                            if k[0] == "e":
                                v = mmap[k[1]][v]
                            engine.wait_ge(sems[k], v)
                        if o["kind"] == "op":
                            ins = o["fn"](engine)
                            if o["idx"] in mmap[e]:
                                ins.then_inc(sems[("e", e)], 1)
                        elif o["kind"] == "dma":
                            ins = o["fn"](engine)
                            ins.then_inc(sems[o["key"]], 16)
                return body

            with nc.Block() as block:
                block.tensor(replay("pe"))
                block.scalar(replay("act"))
                block.vector(replay("dve"))
                block.gpsimd(replay("pool"))
                block.sync(replay("sp"))
            nc.all_engine_barrier()
        self.stack.close()


def run_pipeline(tasks, offsets):
    nph = len(offsets)
    for step in range(len(tasks) + max(offsets) + 1):
        for p in reversed(range(nph)):
            t = step - offsets[p]
            if 0 <= t < len(tasks) and tasks[t][p] is not None:
                tasks[t][p]()


def load_weight_bf16(P, dst, tdst, w_ap, kchunks, gain_sb=None, tgain=None, eng="dve"):
    n = w_ap.shape[-1]
    src = w_ap.rearrange("(c p) n -> p c n", p=128)
    step = max(1, 4096 // n)
    for c0 in range(0, kchunks, step):
        c1 = min(kchunks, c0 + step)
        P.dma("pool", lambda e, c0=c0, c1=c1: e.dma_start(out=dst[:, c0:c1, :], in_=src[:, c0:c1, :]), writes=[tdst])
    if gain_sb is not None:
        for c in range(kchunks):
            P.op(eng, lambda e, c=c: e.tensor_scalar(out=dst[:, c, :], in0=dst[:, c, :], scalar1=gain_sb[:, c:c + 1],
                                                      scalar2=None, op0=ALU.mult), reads=[tdst, tgain], writes=[tdst])


def load_gain(P, name, g_ap):
    g_sb = P.sbuf(name, [128, 8], F32)
    t = P.tile()
    P.dma("sp", lambda e: e.dma_start(out=g_sb[:], in_=g_ap.rearrange("(c p) -> p c", p=128), allow_slow_non_contiguous=True), writes=[t])
    return g_sb, t


class NormT:
    def __init__(self, P, ident, tident, n_xt=2, n_xnT=1, junk=None, tjunk=None):
        self.P = P
        self.ident, self.tident = ident, tident
        self.n_xt, self.n_xnT = n_xt, n_xnT
        self.xt = [P.sbuf(f"nt_xt{i}", [128, 4, D], F32) for i in range(n_xt)]
        self.txt = P.tiles(n_xt)
        if junk is None:
            self.junk = P.sbuf("nt_junk", [128, D], BF16)[:]
            self.tjunk = P.tile()
        else:
            self.junk, self.tjunk = junk, tjunk
        self.ss = P.sbuf("nt_ss", [128, 8], F32)
        self.tss = P.tile()
        self.xn = [P.sbuf(f"nt_xn{i}", [128, D], BF16) for i in range(4)]
        self.txn = P.tiles(4)
        self.pst = [P.psum(f"nt_pst{i}", [128, D], BF16) for i in range(2)]
        self.tpst = P.tiles(2)
        self.xnT = [P.sbuf(f"nt_xnT{i}", [128, 8, 512], BF16) for i in range(n_xnT)]
        self.txnT = P.tiles(n_xnT)
        self.k = 0
        self.n = 0
        self.n2 = 0
        self.kp = 0

    def part1(self, src_rows):
        P = self.P
        b = self.n % self.n_xt
        self.n += 1
        xt, txt = self.xt[b], self.txt[b]
        P.dma("sp", lambda e: e.dma_start(out=xt[:], in_=src_rows.rearrange("(t p) d -> p t d", p=128)), writes=[txt])
        ss, tss = self.ss, self.tss
        for t in range(4):
            xn, txn = self.xn[t], self.txn[t]
            self.k += 1
            col = (self.k % 4) * 2
            P.op("dve", lambda e, col=col: e.memset(ss[:, col:col + 1], 0.0), writes=[tss])
            P.op("act", lambda e, t=t, col=col: e.activation(out=self.junk, in_=xt[:, t, :], func=AF.Square,
                                                             accum_out=ss[:, col:col + 1]),
                 reads=[txt], writes=[self.tjunk, tss])
            P.op("dve", lambda e, col=col: e.tensor_scalar(out=ss[:, col + 1:col + 2], in0=ss[:, col:col + 1], scalar1=1.0 / D,
                                                           scalar2=EPS, op0=ALU.mult, op1=ALU.add), reads=[tss], writes=[tss])
            P.op("act", lambda e, col=col: e.activation(out=ss[:, col + 1:col + 2], in_=ss[:, col + 1:col + 2], func=AF.Sqrt),
                 reads=[tss], writes=[tss])
            P.op("dve", lambda e, col=col: e.reciprocal(out=ss[:, col + 1:col + 2], in_=ss[:, col + 1:col + 2]), reads=[tss], writes=[tss])
            P.op("act", lambda e, t=t, col=col, xn=xn: e.activation(out=xn[:], in_=xt[:, t, :], func=AF.Copy,
                                                                     scale=ss[:, col + 1:col + 2]),
                 reads=[txt, tss], writes=[txn])
        return xt, txt

    def part2(self):
        P = self.P
        b2 = self.n2 % self.n_xnT
        self.n2 += 1
        xnT, txnT = self.xnT[b2], self.txnT[b2]
        for t in range(4):
            xn, txn = self.xn[t], self.txn[t]
            kk = self.kp % 2
            self.kp += 1
            pst, tpst = self.pst[kk], self.tpst[kk]
            for c in range(8):
                P.op("pe", lambda e, c=c, xn=xn, pst=pst: e.transpose(out=pst[:, c * 128:(c + 1) * 128],
                                                                       in_=xn[:, c * 128:(c + 1) * 128], identity=self.ident[:]),
                     reads=[txn, self.tident], writes=[tpst])
            P.op("dve", lambda e, t=t, pst=pst, xnT=xnT: e.tensor_copy(out=xnT[:, :, t * 128:(t + 1) * 128],
                                                                       in_=pst[:].rearrange("p (c k) -> p c k", k=128)),
                 reads=[tpst], writes=[txnT])
        return xnT, txnT

    def run(self, src_rows):
        xt, txt = self.part1(src_rows)
        xnT, txnT = self.part2()
        return xt, txt, xnT, txnT


def load_ident(P, ident_ap):
    ident = P.sbuf("ident", [128, 128], BF16)
    t = P.tile()
    P.dma("pool", lambda e: e.dma_start(out=ident[:], in_=ident_ap), writes=[t])
    return ident, t


def stage_ffn(nc, h_in, h_out, g_ap, wup_ap, cw_ap, cb_ap, wdn_ap, ident_ap, n_seq, seq_len):
    P = Prog(nc)
    ident, tident = load_ident(P, ident_ap)
    g_sb, tg = load_gain(P, "g", g_ap)
    wup = P.sbuf("wup", [128, 8, 2 * DFF], BF16)
    twup = P.tile()
    load_weight_bf16(P, wup, twup, wup_ap, 8, g_sb, tg)
    wdn = P.sbuf("wdn", [128, NFT, D], BF16)
    twdn = P.tile()
    load_weight_bf16(P, wdn, twdn, wdn_ap, NFT)
    cw = P.sbuf("cw", [128, NFT, 3], F32)
    cb = P.sbuf("cb", [128, NFT], F32)
    tcw = P.tile()
    for j in range(3):
        P.dma("sp", lambda e, j=j: e.dma_start(out=cw[:, :, j], in_=cw_ap[j].rearrange("(f p) -> p f", p=128), allow_slow_non_contiguous=True), writes=[tcw])
    P.dma("sp", lambda e: e.dma_start(out=cb[:], in_=cb_ap.rearrange("(f p) -> p f", p=128), allow_slow_non_contiguous=True), writes=[tcw])
    gc = [P.sbuf(f"gc{i}", [128, 512], F32) for i in range(2)]
    tgc = P.tiles(2)
    nt = NormT(P, ident, tident, n_xt=1, n_xnT=2, junk=gc[1][:].bitcast(BF16), tjunk=tgc[1])
    halo = P.sbuf("halo", [128, NFT, 2], F32)
    thalo = P.tile()
    gbuf = [P.sbuf(f"gbuf{i}", [128, 514], F32) for i in range(2)]
    tgbuf = P.tiles(2)
    hid = P.sbuf("hid", [128, NFT, 512], BF16)
    thid = P.tile()
    pv = [P.psum(f"pv{i}", [128, 512]) for i in range(2)]
    tpv = P.tiles(2)
    pg = [P.psum(f"pg{i}", [128, 512]) for i in range(2)]
    tpg = P.tiles(2)
    po = [P.psum(f"po{i}", [128, 512]) for i in range(2)]
    tpo = P.tiles(2)
    xr = [P.sbuf(f"xr{i}", [128, D], F32) for i in range(1)] * 2
    txr = P.tiles(1) * 2
    outs = []
    nsup = seq_len // 512
    nblk = n_seq * nsup
    cnt = {"k": 0, "ko": 0, "kx": 0}
    xn_of = {}

    def phA1(u):
        r0 = u * 512
        nt.part1(h_in[r0:r0 + 512, :])

    def phA2(u):
        xn_of[u] = nt.part2()

    def phB(u):
        xnT, txnT = xn_of[u]
        if u % nsup == 0:
            P.op("pool", lambda e: e.memset(halo[:], 0.0), writes=[thalo])
        for ft in range(NFT):
            if ft == 2 and u + 1 < nblk:
                phA1(u + 1)
            b = cnt["k"] % 2
            cnt["k"] += 1
            for c in range(8):
                P.op("pe", lambda e, c=c, b=b, ft=ft: e.matmul(out=pv[b][:], lhsT=wup[:, c, ft * 128:(ft + 1) * 128], rhs=xnT[:, c, :],
                                                               start=(c == 0), stop=(c == 7)), reads=[twup, txnT], writes=[tpv[b]])
            for c in range(8):
                P.op("pe", lambda e, c=c, b=b, ft=ft: e.matmul(out=pg[b][:], lhsT=wup[:, c, DFF + ft * 128:DFF + (ft + 1) * 128], rhs=xnT[:, c, :],
                                                               start=(c == 0), stop=(c == 7)), reads=[twup, txnT], writes=[tpg[b]])
            gb, tgb, g2, tg2 = gbuf[b], tgbuf[b], gc[b], tgc[b]
            P.op("pool", lambda e, gb=gb, ft=ft: e.tensor_copy(out=gb[:, 0:2], in_=halo[:, ft, :]), reads=[thalo], writes=[tgb])
            P.op("act", lambda e, gb=gb, b=b: e.activation(out=gb[:, 2:514], in_=pg[b][:], func=AF.Copy), reads=[tpg[b]], writes=[tgb])
            P.op("pool", lambda e, gb=gb, ft=ft: e.tensor_copy(out=halo[:, ft, :], in_=gb[:, 512:514]), reads=[tgb], writes=[thalo])
            P.op("act", lambda e, gb=gb, g2=g2, ft=ft: e.activation(out=g2[:], in_=gb[:, 2:514], func=AF.Identity,
                                                                    scale=cw[:, ft, 2:3], bias=cb[:, ft:ft + 1]),
                 reads=[tgb, tcw], writes=[tg2])
            P.op("dve", lambda e, gb=gb, g2=g2, ft=ft: e.scalar_tensor_tensor(out=g2[:], in0=gb[:, 1:513], scalar=cw[:, ft, 1:2], in1=g2[:],
                                                                              op0=ALU.mult, op1=ALU.add), reads=[tgb, tcw, tg2], writes=[tg2])
            P.op("dve", lambda e, gb=gb, g2=g2, ft=ft: e.scalar_tensor_tensor(out=g2[:], in0=gb[:, 0:512], scalar=cw[:, ft, 0:1], in1=g2[:],
                                                                              op0=ALU.mult, op1=ALU.add), reads=[tgb, tcw, tg2], writes=[tg2])
            P.op("act", lambda e, g2=g2: e.activation(out=g2[:], in_=g2[:], func=AF.Silu), reads=[tg2], writes=[tg2])
            P.op("dve", lambda e, g2=g2, b=b, ft=ft: e.tensor_tensor(out=hid[:, ft, :], in0=g2[:], in1=pv[b][:], op=ALU.mult),
                 reads=[tg2, tpv[b]], writes=[thid])

    def phC(u):
        r0 = u * 512
        for t in range(4):
            xb = cnt["kx"] % 2
            cnt["kx"] += 1
            rr = r0 + t * 128
            P.dma("sp", lambda e, xb=xb, rr=rr: e.dma_start(out=xr[xb][:], in_=h_in[rr:rr + 128, :]), writes=[txr[xb]])
            for nb in range(2):
                b = cnt["ko"] % 2
                cnt["ko"] += 1
                for ft in range(NFT):
                    P.op("pe", lambda e, ft=ft, t=t, nb=nb, b=b: e.matmul(out=po[b][:], lhsT=hid[:, ft, t * 128:(t + 1) * 128],
                                                                          rhs=wdn[:, ft, nb * 512:(nb + 1) * 512],
                                                                          start=(ft == 0), stop=(ft == NFT - 1)),
                         reads=[thid, twdn], writes=[tpo[b]])
                P.op("dve", lambda e, nb=nb, b=b, xb=xb: e.tensor_tensor(out=xr[xb][:, nb * 512:(nb + 1) * 512], in0=po[b][:],
                                                                         in1=xr[xb][:, nb * 512:(nb + 1) * 512], op=ALU.add),
                     reads=[tpo[b], txr[xb]], writes=[txr[xb]])
            to = P.tile()
            P.dma("act", lambda e, xb=xb, rr=rr: e.dma_start(out=h_out[rr:rr + 128, :], in_=xr[xb][:]), reads=[txr[xb]], writes=[to])
            outs.append(to)

    phA1(0)
    phA2(0)
    for u in range(nblk):
        phB(u)
        if u + 1 < nblk:
            phA2(u + 1)
        phC(u)
    P.finish(outs)


def stage_kvq(nc, h_in, kT_rev, qT, v_rev, kvn_ap, wkv_ap, kn_ap, bn_ap, wq_ap, qn_ap, ident_ap, bones_ap, n_seq, S):
    P = Prog(nc)
    ident, tident = load_ident(P, ident_ap)
    bones = P.sbuf("bones", [128, 128], BF16)
    tbones = P.tile()
    P.dma("pool", lambda e: e.dma_start(out=bones[:], in_=bones_ap), writes=[tbones])
    gkv, tgkv = load_gain(P, "gkv", kvn_ap)
    gb, tgb = load_gain(P, "gb", bn_ap)
    wkv = P.sbuf("wkv", [128, 8, 2048], BF16)
    twkv = P.tile()
    load_weight_bf16(P, wkv, twkv, wkv_ap, 8, gkv, tgkv)
    wq = P.sbuf("wq", [128, 8, 1024], BF16)
    twq = P.tile()
    load_weight_bf16(P, wq, twq, wq_ap, 8, gb, tgb)
    hg = P.sbuf("hg", [128, 4], F32)
    thg = P.tile()
    for hf in range(2):
        P.dma("sp", lambda e, hf=hf: e.dma_start(out=hg[hf * 64:(hf + 1) * 64, 0:1], in_=kn_ap.rearrange("(p o) -> p o", o=1)), writes=[thg])
        P.dma("sp", lambda e, hf=hf: e.dma_start(out=hg[hf * 64:(hf + 1) * 64, 1:2], in_=qn_ap.rearrange("(p o) -> p o", o=1)), writes=[thg])
    P.op("dve", lambda e: e.tensor_scalar(out=hg[:, 1:2], in0=hg[:, 1:2], scalar1=0.125, scalar2=None, op0=ALU.mult), reads=[thg], writes=[thg])
    P.op("dve", lambda e: e.memset(hg[:, 2:3], EPS), reads=[thg], writes=[thg])
    nt = NormT(P, ident, tident)
    pk = [P.psum(f"pk{i}", [128, 512]) for i in range(2)]
    tpk = P.tiles(2)
    pss = [P.psum(f"pss{i}", [128, 512]) for i in range(2)]
    tpss = P.tiles(2)
    sq = [P.sbuf(f"sq{i}", [128, 512], BF16) for i in range(2)]
    tsq = P.tiles(2)
    rs = [P.sbuf(f"rs{i}", [128, 512], F32) for i in range(2)]
    trs = P.tiles(2)
    kbuf = [P.sbuf(f"kbuf{i}", [128, 8, 512], BF16) for i in range(2)]
    tkbuf = P.tiles(2)
    qbuf = [P.sbuf(f"qbuf{i}", [128, 8, 512], BF16) for i in range(2)]
    tqbuf = P.tiles(2)
    vbuf = [P.sbuf(f"vbuf{i}", [128, 4, 1024], BF16) for i in range(2)]
    tvbuf = P.tiles(2)
    xrev = P.sbuf("xrev", [128, 8, 512], BF16)
    txrev = P.tile()
    outs = []
    nsup = S // 512
    k = 0
    for s in range(n_seq):
        for u in range(nsup):
            r0 = s * S + u * 512
            rr = s * S + S - 512 * (u + 1)
            ob = (s * nsup + u) % 2
            if s == 0 and u == 0:
                nt.part1(h_in[r0:r0 + 512, :])
            xnT, txnT = nt.part2()
            if r0 + 512 < n_seq * S:
                nt.part1(h_in[r0 + 512:r0 + 1024, :])
            for which in range(2):
                for ft in range(8):
                    b = k % 2
                    k += 1
                    for c in range(8):
                        if which == 0:
                            P.op("pe", lambda e, c=c, b=b, ft=ft: e.matmul(out=pk[b][:], lhsT=wkv[:, c, ft * 128:(ft + 1) * 128], rhs=xnT[:, c, :],
                                                                           start=(c == 0), stop=(c == 7)), reads=[twkv, txnT], writes=[tpk[b]])
                        else:
                            P.op("pe", lambda e, c=c, b=b, ft=ft: e.matmul(out=pk[b][:], lhsT=wq[:, c, ft * 128:(ft + 1) * 128], rhs=xnT[:, c, :],
                                                                           start=(c == 0), stop=(c == 7)), reads=[twq, txnT], writes=[tpk[b]])
                    P.op("act", lambda e, b=b: e.activation(out=sq[b][:], in_=pk[b][:], func=AF.Square), reads=[tpk[b]], writes=[tsq[b]])
                    P.op("pe", lambda e, b=b: e.matmul(out=pss[b][:], lhsT=bones[:], rhs=sq[b][:], start=True, stop=True),
                         reads=[tbones, tsq[b]], writes=[tpss[b]])
                    P.op("act", lambda e, b=b: e.activation(out=rs[b][:], in_=pss[b][:], func=AF.Ln, scale=1.0 / 64, bias=hg[:, 2:3]),
                         reads=[tpss[b], thg], writes=[trs[b]])
                    P.op("act", lambda e, b=b: e.activation(out=rs[b][:], in_=rs[b][:], func=AF.Exp, scale=-0.5), reads=[trs[b]], writes=[trs[b]])
                    if which == 0:
                        P.op("dve", lambda e, b=b, ft=ft, ob=ob: e.scalar_tensor_tensor(out=kbuf[ob][:, ft, ::-1], in0=pk[b][:], scalar=hg[:, 0:1], in1=rs[b][:],
                                                                                        op0=ALU.mult, op1=ALU.mult),
                             reads=[tpk[b], thg, trs[b]], writes=[tkbuf[ob]])
                    else:
                        P.op("dve", lambda e, b=b, ft=ft, ob=ob: e.scalar_tensor_tensor(out=qbuf[ob][:, ft, :], in0=pk[b][:], scalar=hg[:, 1:2], in1=rs[b][:],
                                                                                        op0=ALU.mult, op1=ALU.mult),
                             reads=[tpk[b], thg, trs[b]], writes=[tqbuf[ob]])
            P.op("dve", lambda e, xnT=xnT: e.tensor_copy(out=xrev[:, :, ::-1], in_=xnT[:]), reads=[txnT], writes=[txrev])
            for t in range(4):
                for nb in range(2):
                    b = k % 2
                    k += 1
                    for c in range(8):
                        P.op("pe", lambda e, c=c, b=b, t=t, nb=nb: e.matmul(out=pk[b][:], lhsT=xrev[:, c, t * 128:(t + 1) * 128],
                                                                            rhs=wkv[:, c, 1024 + nb * 512:1024 + (nb + 1) * 512],
                                                                            start=(c == 0), stop=(c == 7)), reads=[twkv, txrev], writes=[tpk[b]])
                    P.op("act", lambda e, b=b, t=t, nb=nb, ob=ob: e.activation(out=vbuf[ob][:, t, nb * 512:(nb + 1) * 512], in_=pk[b][:], func=AF.Copy),
                         reads=[tpk[b]], writes=[tvbuf[ob]])
            t1, t2, t3 = P.tile(), P.tile(), P.tile()
            P.dma("act", lambda e, ob=ob, rr=rr: e.dma_start(out=kT_rev[:, rr:rr + 512].rearrange("(f p) n -> p f n", p=128), in_=kbuf[ob][:]),
                  reads=[tkbuf[ob]], writes=[t1])
            P.dma("act", lambda e, ob=ob, r0=r0: e.dma_start(out=qT[:, r0:r0 + 512].rearrange("(f p) n -> p f n", p=128), in_=qbuf[ob][:]),
                  reads=[tqbuf[ob]], writes=[t2])
            P.dma("act", lambda e, ob=ob, rr=rr: e.dma_start(out=v_rev[rr:rr + 512, :].rearrange("(t p) d -> p t d", p=128), in_=vbuf[ob][:]),
                  reads=[tvbuf[ob]], writes=[t3])
            outs += [t1, t2, t3]
    P.finish(outs)


def stage_att(nc, h_in, h_out, qT, kT_rev, v_rev, wo_ap, ident_ap, maskb_ap, n_seq, S):
    P = Prog(nc)
    ident, tident = load_ident(P, ident_ap)
    maskb = P.sbuf("maskb", [128, 128], BF16)
    tmask = P.tile()
    P.dma("pool", lambda e: e.dma_start(out=maskb[:], in_=maskb_ap), writes=[tmask])
    wo = P.sbuf("wo", [128, 8, 1024], BF16)
    two = P.tile()
    load_weight_bf16(P, wo, two, wo_ap, 8)
    zeros = P.sbuf("zeros", [128, 512], F32)
    onec = P.sbuf("onec", [128, 1], F32)
    tz = P.tile()
    P.op("pool", lambda e: e.memset(zeros[:], 0.0), writes=[tz])
    P.op("pool", lambda e: e.memset(onec[:], 1.0), reads=[tz], writes=[tz])
    NB = S // 128
    qsb = P.sbuf("qsb", [128, 8, S], BF16)
    ksb = P.sbuf("ksb", [128, 8, S], BF16)
    vsb = P.sbuf("vsb", [128, NB, 1024], BF16)
    oT = P.sbuf("oT", [128, 8, S], BF16)
    tq, tk, tv, toT = P.tile(), P.tile(), P.tile(), P.tile()
    NZ, NS, NPB, NW, NWT, NWS, NPO = 3, 4, 5, 4, 2, 4, 3
    pz = [P.psum(f"pz{i}", [128, 512]) for i in range(NZ)]
    tpz = P.tiles(NZ)
    pwT = [P.psum(f"pwT{i}", [128, 1024], BF16) for i in range(NWT)]
    tpwT = P.tiles(NWT)
    po = [P.psum(f"po{i}", [128, 512]) for i in range(NPO)]
    tpo = P.tiles(NPO)
    pw = pz[0:2]
    tpw = tpz[0:2]
    ssb = [P.sbuf(f"ssb{i}", [128, 512], F32) for i in range(NS)]
    tssb = P.tiles(NS)
    pbuf = [P.sbuf(f"pbuf{i}", [128, 513], F32) for i in range(NPB)]
    tpbuf = P.tiles(NPB)
    wsb = [P.sbuf(f"wsb{i}", [128, 512], BF16) for i in range(NW)]
    twsb = P.tiles(NW)
    wTs = [P.sbuf(f"wTs{i}", [128, 512], BF16) for i in range(NWS)]
    twTs = P.tiles(NWS)
    xt = [P.sbuf(f"xt{i}", [128, D], F32) for i in range(2)]
    txt = P.tiles(2)
    outs = []
    kk = 0
    kx = 0
    gt = 0
    gpo = 0
    for s in range(n_seq):
        c0 = s * S
        for f in range(0, 8, 2):
            P.dma("sp", lambda e, f=f, c0=c0: e.dma_start(out=qsb[:, f:f + 2, :], in_=qT[f * 128:(f + 2) * 128, c0:c0 + S].rearrange("(f p) n -> p f n", p=128)), writes=[tq])
            P.dma("act", lambda e, f=f, c0=c0: e.dma_start(out=ksb[:, f:f + 2, :], in_=kT_rev[f * 128:(f + 2) * 128, c0:c0 + S].rearrange("(f p) n -> p f n", p=128)), writes=[tk])
        for b4 in range(0, NB, 4):
            P.dma("sp", lambda e, b4=b4, c0=c0: e.dma_start(out=vsb[:, b4:b4 + 4, :], in_=v_rev[c0 + b4 * 128:c0 + (b4 + 4) * 128, :].rearrange("(t p) d -> p t d", p=128)), writes=[tv])
        tasks = []
        for ft in range(8):
            for i in range(NB):
                q0 = i * 128
                kp0 = S - 128 - q0
                klen = q0 + 128
                ob = gpo % NPO
                gpo += 1
                ntile = (klen + 511) // 512
                for hp in range(2):
                    h = ft * 2 + hp
                    ps = slice(hp * 64, (hp + 1) * 64)
                    for n in range(ntile):
                        col0 = kp0 + 512 * n
                        wdt = min(512, S - col0)
                        nblk = wdt // 128
                        g = gt
                        gt += 1
                        bz, bs_, bp, bw, bwt, bws = g % NZ, g % NS, g % NPB, g % NW, g % NWT, g % NWS
                        bpp = (g - 1) % NPB

                        def ph0(bz=bz, ft=ft, ps=ps, q0=q0, col0=col0, wdt=wdt, n=n):
                            P.op("pe", lambda e: e.matmul(out=pz[bz][:, 0:wdt], lhsT=qsb[ps, ft, q0:q0 + 128], rhs=ksb[ps, ft, col0:col0 + wdt],
                                                          start=True, stop=(n != 0)), reads=[tq, tk], writes=[tpz[bz]])
                            if n == 0:
                                P.op("pe", lambda e: e.matmul(out=pz[bz][:, 0:128], lhsT=ident[:], rhs=maskb[:], start=False, stop=True),
                                     reads=[tident, tmask], writes=[tpz[bz]])

                        def ph1(bz=bz, bs_=bs_, wdt=wdt):
                            P.op("act", lambda e: e.activation(out=ssb[bs_][:, 0:wdt], in_=pz[bz][:, 0:wdt], func=AF.Sigmoid, scale=-1.0),
                                 reads=[tpz[bz]], writes=[tssb[bs_]])

                        def ph2(bs_=bs_, bp=bp, bpp=bpp, wdt=wdt, n=n):
                            if n == 0:
                                P.op("act", lambda e: e.activation(out=pbuf[bp][:, 0:1], in_=onec[:, 0:1], func=AF.Copy), reads=[tz], writes=[tpbuf[bp]])
                                init = onec[:, 0:1]
                                rd = [tssb[bs_], tz, tpbuf[bp]]
                            else:
                                P.op("act", lambda e: e.activation(out=pbuf[bp][:, 0:1], in_=pbuf[bpp][:, 512:513], func=AF.Copy), reads=[tpbuf[bpp]], writes=[tpbuf[bp]])
                                init = pbuf[bpp][:, 512:513]
                                rd = [tssb[bs_], tz, tpbuf[bp], tpbuf[bpp]]
                            P.op("dve", lambda e: e.tensor_tensor_scan(out=pbuf[bp][:, 1:1 + wdt], data0=ssb[bs_][:, 0:wdt], data1=zeros[:, 0:wdt],
                                                                       initial=init, op0=ALU.mult, op1=ALU.add), reads=rd, writes=[tpbuf[bp]])

                        def ph3(bp=bp, bw=bw, wdt=wdt):
                            P.op("dve", lambda e: e.tensor_tensor(out=wsb[bw][:, 0:wdt], in0=pbuf[bp][:, 0:wdt], in1=pbuf[bp][:, 1:1 + wdt], op=ALU.subtract),
                                 reads=[tpbuf[bp]], writes=[twsb[bw]])

                        def ph4(bw=bw, bwt=bwt, nblk=nblk):
                            for jb in range(nblk):
                                P.op("pe", lambda e, jb=jb: e.transpose(out=pwT[bwt][:, jb * 128:(jb + 1) * 128], in_=wsb[bw][:, jb * 128:(jb + 1) * 128], identity=ident[:]),
                                     reads=[twsb[bw], tident], writes=[tpwT[bwt]])

                        def ph5(bwt=bwt, bws=bws, wdt=wdt, g=g):
                            if True:
                                P.op("act", lambda e: e.activation(out=wTs[bws][:, 0:wdt], in_=pwT[bwt][:, 0:wdt], func=AF.Copy), reads=[tpwT[bwt]], writes=[twTs[bws]])
                            else:
                                P.op("dve", lambda e: e.tensor_copy(out=wTs[bws][:, 0:wdt], in_=pwT[bwt][:, 0:wdt]), reads=[tpwT[bwt]], writes=[twTs[bws]])

                        def ph6(bws=bws, nblk=nblk, col0=col0, h=h, ps=ps, ob=ob, n=n, ntile=ntile, hp=hp, ft=ft, q0=q0):
                            for jb in range(nblk):
                                blk = col0 // 128 + jb
                                first = (n == 0 and jb == 0)
                                last = (n == ntile - 1 and jb == nblk - 1)
                                P.op("pe", lambda e, jb=jb, blk=blk, first=first, last=last: e.matmul(
                                    out=po[ob][ps, 0:128], lhsT=vsb[:, blk, h * 64:(h + 1) * 64], rhs=wTs[bws][:, jb * 128:(jb + 1) * 128], start=first, stop=last),
                                    reads=[tv, twTs[bws]], writes=[tpo[ob]])
                            if hp == 1 and n == ntile - 1:
                                P.op("act", lambda e: e.activation(out=oT[:, ft, q0:q0 + 128], in_=po[ob][:, 0:128], func=AF.Copy), reads=[tpo[ob]], writes=[toT])

                        tasks.append([ph0, ph1, ph2, ph3, ph4, ph5, ph6])
        run_pipeline(tasks, ATT_OFFS)
        for t in range(NB):
            r0 = c0 + t * 128
            xb = kx % 2
            kx += 1
            P.dma("sp", lambda e, xb=xb, r0=r0: e.dma_start(out=xt[xb][:], in_=h_in[r0:r0 + 128, :]), writes=[txt[xb]])
            for nb in range(2):
                b = kk % 2
                kk += 1
                for ft in range(8):
                    P.op("pe", lambda e, b=b, ft=ft, t=t, nb=nb: e.matmul(out=pw[b][:], lhsT=oT[:, ft, t * 128:(t + 1) * 128], rhs=wo[:, ft, nb * 512:(nb + 1) * 512],
                                                                          start=(ft == 0), stop=(ft == 7)), reads=[toT, two], writes=[tpw[b]])
                P.op("dve", lambda e, b=b, xb=xb, nb=nb: e.tensor_tensor(out=xt[xb][:, nb * 512:(nb + 1) * 512], in0=pw[b][:], in1=xt[xb][:, nb * 512:(nb + 1) * 512], op=ALU.add),
                     reads=[tpw[b], txt[xb]], writes=[txt[xb]])
            to = P.tile()
            P.dma("act", lambda e, xb=xb, r0=r0: e.dma_start(out=h_out[r0:r0 + 128, :], in_=xt[xb][:]), reads=[txt[xb]], writes=[to])
            outs.append(to)
    P.finish(outs)


def stage_a1(nc, h_in, uT, g_ap, win_ap, ident_ap, n_tok, prep=None):
    P = Prog(nc)
    if prep is not None:
        prep(P)
    ident, tident = load_ident(P, ident_ap)
    g_sb, tg = load_gain(P, "g", g_ap)
    win = P.sbuf("win", [128, 8, 1024], BF16)
    twin = P.tile()
    load_weight_bf16(P, win, twin, win_ap, 8, g_sb, tg)
    nt = NormT(P, ident, tident)
    pu = [P.psum(f"pu{i}", [128, 512]) for i in range(2)]
    tpu = P.tiles(2)
    ubuf = [P.sbuf(f"ubuf{i}", [128, 8, 512], BF16) for i in range(2)]
    tubuf = P.tiles(2)
    outs = []
    k = 0
    nblk = n_tok // 512
    nt.part1(h_in[0:512, :])
    for u in range(nblk):
        r0 = u * 512
        ob = u % 2
        xnT, txnT = nt.part2()
        if u + 1 < nblk:
            nt.part1(h_in[r0 + 512:r0 + 1024, :])
        for m in range(8):
            b = k % 2
            k += 1
            for c in range(8):
                P.op("pe", lambda e, c=c, b=b, m=m: e.matmul(out=pu[b][:], lhsT=win[:, c, m * 128:(m + 1) * 128], rhs=xnT[:, c, :],
                                                             start=(c == 0), stop=(c == 7)), reads=[twin, txnT], writes=[tpu[b]])
            if m % 2 == 0:
                P.op("act", lambda e, b=b, m=m, ob=ob: e.activation(out=ubuf[ob][:, m, :], in_=pu[b][:], func=AF.Copy), reads=[tpu[b]], writes=[tubuf[ob]])
            else:
                P.op("dve", lambda e, b=b, m=m, ob=ob: e.tensor_copy(out=ubuf[ob][:, m, :], in_=pu[b][:]), reads=[tpu[b]], writes=[tubuf[ob]])
        to = P.tile()
        P.dma("act", lambda e, ob=ob, r0=r0: e.dma_start(out=uT[:, r0:r0 + 512].rearrange("(m p) n -> p m n", p=128), in_=ubuf[ob][:]),
              reads=[tubuf[ob]], writes=[to])
        outs.append(to)
    P.finish(outs)


TWO_PI = 6.283185307179586
ATT_OFFS = [0, 2, 4, 6, 8, 9, 11]


def a2_prep(P, alloc, lre_ap, lim_ap, bre_ap, bim_ap, cre_ap, cim_ap, d_ap, ldt_ap, iota_ap, S):
    I32 = mybir.dt.int32
    NP = 32
    lr = alloc("lr", [128, NP], F32)
    li = alloc("li", [128, NP], F32)
    dt = alloc("dt", [128, NP], F32)
    tprm = P.tile()
    for e_ in range(2):
        ps = slice(e_ * 64, (e_ + 1) * 64)
        P.dma("sp", lambda e, e_=e_, ps=ps: e.dma_start(out=lr[ps, :], in_=lre_ap.rearrange("(k e) n -> e n k", e=2)[e_], allow_slow_non_contiguous=True), writes=[tprm])
        P.dma("sp", lambda e, e_=e_, ps=ps: e.dma_start(out=li[ps, :], in_=lim_ap.rearrange("(k e) n -> e n k", e=2)[e_], allow_slow_non_contiguous=True), writes=[tprm])
        P.dma("sp", lambda e, e_=e_, ps=ps: e.dma_start(out=dt[ps, :], in_=ldt_ap.rearrange("(k e) -> e k", e=2)[e_].partition_broadcast(64), allow_slow_non_contiguous=True), writes=[tprm])
    cst = alloc("cst", [128, 4], F32)
    tcst = P.tile()
    P.op("dve", lambda e: e.memset(cst[:, 0:1], TWO_PI / 4), writes=[tcst])
    P.op("dve", lambda e: e.memset(cst[:, 1:2], 0.0), reads=[tcst], writes=[tcst])
    sc = {}
    for nm in ["f", "fhi", "flo", "r", "t0", "t1", "t2", "t3", "sn", "cs", "ar", "ai", "qre", "qim", "nqim", "den"]:
        sc[nm] = alloc("p_" + nm, [128, NP], F32)
    fhb = alloc("p_fhb", [128, NP], BF16)
    tiq = alloc("p_ti", [128, NP], I32)
    tsc = P.tile()

    def V(fn, eng="dve", extra=()):
        P.op(eng, fn, reads=[tprm, tsc, tcst] + list(extra), writes=[tsc])

    V(lambda e: e.activation(out=sc["t0"][:], in_=dt[:], func=AF.Exp), "act")
    V(lambda e: e.activation(out=sc["t1"][:], in_=sc["t0"][:], func=AF.Ln), "act")
    V(lambda e: e.tensor_tensor(out=sc["t1"][:], in0=dt[:], in1=sc["t1"][:], op=ALU.subtract))
    V(lambda e: e.tensor_scalar(out=sc["t1"][:], in0=sc["t1"][:], scalar1=1.0, scalar2=None, op0=ALU.add))
    V(lambda e: e.tensor_tensor(out=dt[:], in0=sc["t0"][:], in1=sc["t1"][:], op=ALU.mult))
    V(lambda e: e.tensor_tensor(out=sc["t0"][:], in0=li[:], in1=dt[:], op=ALU.mult))
    V(lambda e: e.tensor_scalar(out=sc["f"][:], in0=sc["t0"][:], scalar1=1.0 / TWO_PI, scalar2=None, op0=ALU.mult))
    V(lambda e: e.tensor_copy(out=fhb[:], in_=sc["f"][:]))
    V(lambda e: e.tensor_copy(out=sc["fhi"][:], in_=fhb[:]))
    V(lambda e: e.tensor_tensor(out=sc["flo"][:], in0=sc["f"][:], in1=sc["fhi"][:], op=ALU.subtract))
    V(lambda e: e.tensor_tensor(out=sc["t1"][:], in0=lr[:], in1=dt[:], op=ALU.mult))
    V(lambda e: e.activation(out=sc["t2"][:], in_=sc["t1"][:], func=AF.Exp), "act")
    V(lambda e: e.activation(out=sc["t3"][:], in_=sc["t2"][:], func=AF.Ln), "act")
    V(lambda e: e.tensor_tensor(out=sc["t3"][:], in0=sc["t1"][:], in1=sc["t3"][:], op=ALU.subtract))
    V(lambda e: e.tensor_scalar(out=sc["t3"][:], in0=sc["t3"][:], scalar1=1.0, scalar2=None, op0=ALU.add))
    V(lambda e: e.tensor_tensor(out=sc["r"][:], in0=sc["t2"][:], in1=sc["t3"][:], op=ALU.mult))
    V(lambda e: e.tensor_copy(out=tiq[:], in_=sc["f"][:]))
    V(lambda e: e.tensor_copy(out=sc["t3"][:], in_=tiq[:]))
    V(lambda e: e.tensor_tensor(out=sc["t2"][:], in0=sc["f"][:], in1=sc["t3"][:], op=ALU.subtract))
    V(lambda e: e.activation(out=sc["sn"][:], in_=sc["t2"][:], func=AF.Sin, scale=TWO_PI), "act")
    V(lambda e: e.tensor_scalar(out=sc["t3"][:], in0=sc["t2"][:], scalar1=-1.0, scalar2=None, op0=ALU.mult))
    V(lambda e: e.tensor_tensor(out=sc["t3"][:], in0=sc["t3"][:], in1=sc["t2"][:], op=ALU.min))
    V(lambda e: e.activation(out=sc["cs"][:], in_=sc["t3"][:], func=AF.Sin, scale=TWO_PI, bias=cst[:, 0:1]), "act")
    V(lambda e: e.tensor_tensor(out=sc["ar"][:], in0=sc["r"][:], in1=sc["cs"][:], op=ALU.mult))
    V(lambda e: e.tensor_tensor(out=sc["ai"][:], in0=sc["r"][:], in1=sc["sn"][:], op=ALU.mult))
    V(lambda e: e.tensor_scalar(out=sc["ar"][:], in0=sc["ar"][:], scalar1=-1.0, scalar2=None, op0=ALU.add))
    V(lambda e: e.tensor_tensor(out=sc["t0"][:], in0=lr[:], in1=lr[:], op=ALU.mult))
    V(lambda e: e.tensor_tensor(out=sc["t1"][:], in0=li[:], in1=li[:], op=ALU.mult))
    V(lambda e: e.tensor_tensor(out=sc["den"][:], in0=sc["t0"][:], in1=sc["t1"][:], op=ALU.add))
    V(lambda e: e.reciprocal(out=sc["den"][:], in_=sc["den"][:]))
    V(lambda e: e.tensor_tensor(out=sc["t0"][:], in0=sc["ar"][:], in1=lr[:], op=ALU.mult))
    V(lambda e: e.tensor_tensor(out=sc["t1"][:], in0=sc["ai"][:], in1=li[:], op=ALU.mult))
    V(lambda e: e.tensor_tensor(out=sc["t0"][:], in0=sc["t0"][:], in1=sc["t1"][:], op=ALU.add))
    V(lambda e: e.tensor_tensor(out=sc["qre"][:], in0=sc["t0"][:], in1=sc["den"][:], op=ALU.mult))
    V(lambda e: e.tensor_tensor(out=sc["t0"][:], in0=sc["ai"][:], in1=lr[:], op=ALU.mult))
    V(lambda e: e.tensor_tensor(out=sc["t1"][:], in0=sc["ar"][:], in1=li[:], op=ALU.mult))
    V(lambda e: e.tensor_tensor(out=sc["t0"][:], in0=sc["t0"][:], in1=sc["t1"][:], op=ALU.subtract))
    V(lambda e: e.tensor_tensor(out=sc["qim"][:], in0=sc["t0"][:], in1=sc["den"][:], op=ALU.mult))
    V(lambda e: e.tensor_scalar(out=sc["nqim"][:], in0=sc["qim"][:], scalar1=-1.0, scalar2=None, op0=ALU.mult))
    craw = alloc("craw", [128, NP, 2, 16], F32)
    tcraw = P.tile()
    for e_ in range(2):
        ps = slice(e_ * 64, (e_ + 1) * 64)
        for k1 in range(NP):
            for ri, ap_ in enumerate((cre_ap, cim_ap)):
                P.dma("sp" if ri == 0 else "act", lambda e, e_=e_, ps=ps, k1=k1, ri=ri, ap_=ap_: e.dma_start(
                    out=craw[ps, k1, ri, :], in_=ap_[2 * k1 + e_].rearrange("h n -> n h"), allow_slow_non_contiguous=True),
                    writes=[tcraw])
    CT = alloc("CT", [128, NP, 3, 128], BF16)
    tCT = P.tile()
    ctmp = alloc("ctmp", [128, NP, 16], F32)
    ctmp2 = alloc("ctmp2", [128, NP, 16], F32)
    tctmp = P.tile()
    P.op("pool", lambda e: e.memset(CT[:], 0.0), writes=[tCT])

    def bq(nm):
        return sc[nm][:].unsqueeze(2).to_broadcast([128, NP, 16])

    rd = [tcraw, tsc]
    P.op("dve", lambda e: e.tensor_tensor(out=ctmp[:], in0=craw[:, :, 0, :], in1=bq("qre"), op=ALU.mult), reads=rd, writes=[tctmp])
    P.op("dve", lambda e: e.tensor_tensor(out=ctmp2[:], in0=craw[:, :, 1, :], in1=bq("qim"), op=ALU.mult), reads=rd, writes=[tctmp])
    for e_ in range(2):
        ps = slice(e_ * 64, (e_ + 1) * 64)
        P.op("dve", lambda e, e_=e_, ps=ps: e.tensor_tensor(out=CT[ps, :, 0, e_ * 16:(e_ + 1) * 16], in0=ctmp[ps], in1=ctmp2[ps], op=ALU.subtract),
             reads=[tctmp], writes=[tCT])
    P.op("dve", lambda e: e.tensor_tensor(out=ctmp[:], in0=craw[:, :, 0, :], in1=bq("nqim"), op=ALU.mult), reads=rd + [tCT], writes=[tctmp])
    P.op("dve", lambda e: e.tensor_tensor(out=ctmp2[:], in0=craw[:, :, 1, :], in1=bq("qre"), op=ALU.mult), reads=rd, writes=[tctmp])
    for e_ in range(2):
        ps = slice(e_ * 64, (e_ + 1) * 64)
        P.op("dve", lambda e, e_=e_, ps=ps: e.tensor_tensor(out=CT[ps, :, 1, e_ * 16:(e_ + 1) * 16], in0=ctmp[ps], in1=ctmp2[ps], op=ALU.subtract),
             reads=[tctmp], writes=[tCT])
    BT = alloc("BT", [32, NP, 2, 128], BF16)
    tBT = P.tile()
    P.op("pool", lambda e: e.memset(BT[:], 0.0), writes=[tBT])
    for e_ in range(2):
        for ri, ap_ in enumerate((bre_ap, bim_ap)):
            for k1 in range(NP):
                P.dma("pool", lambda e, e_=e_, ri=ri, ap_=ap_, k1=k1: e.dma_start(
                    out=BT[e_ * 16:(e_ + 1) * 16, k1, ri, e_ * 64:(e_ + 1) * 64], in_=ap_[2 * k1 + e_].rearrange("n h -> h n"),
                    allow_slow_non_contiguous=True), writes=[tBT])
    dwin = alloc("dwin", [32, NP], F32)
    tdw = P.tile()
    P.dma("sp", lambda e: e.dma_start(out=dwin[:], in_=d_ap.rearrange("(k r) -> r k", r=32), allow_slow_non_contiguous=True), writes=[tdw])
    iota = alloc("iota", [128, S], F32)
    tio = P.tile()
    P.dma("sp", lambda e: e.dma_start(out=iota[:], in_=iota_ap[:, 0:S]), writes=[tio])
    return dict(sc=sc, tsc=tsc, CT=CT, tCT=tCT, BT=BT, tBT=tBT, dwin=dwin, tdw=tdw, iota=iota, tio=tio, cst=cst, tcst=tcst)


def stage_a2(nc, uT, gT, lre_ap, lim_ap, bre_ap, bim_ap, cre_ap, cim_ap, d_ap, ldt_ap, iota_ap, n_seq, S, dbg_pairs=None, dbg=None, dbg_stop=9, ctx=None):
    P = Prog(nc)
    I32 = mybir.dt.int32
    NP = 32
    if ctx is None:
        ctx = a2_prep(P, P.sbuf, lre_ap, lim_ap, bre_ap, bim_ap, cre_ap, cim_ap, d_ap, ldt_ap, iota_ap, S)
    else:
        ctx = dict(ctx)
        for tn in ("tsc", "tCT", "tBT", "tdw", "tio", "tcst"):
            ctx[tn] = P.tile()
    sc, tsc, CT, tCT, BT, tBT = ctx["sc"], ctx["tsc"], ctx["CT"], ctx["tCT"], ctx["BT"], ctx["tBT"]
    dwin, tdw, iota, tio, cst, tcst = ctx["dwin"], ctx["tdw"], ctx["iota"], ctx["tio"], ctx["cst"], ctx["tcst"]
    HB = 512
    nq = S // HB
    sn = [P.sbuf(f"sn{i}", [128, S], F32) for i in range(2)]
    cs = [P.sbuf(f"cs{i}", [128, S], F32) for i in range(2)]
    snb = [P.sbuf(f"snb{i}", [128, S], BF16) for i in range(2)]
    csb = [P.sbuf(f"csb{i}", [128, S], BF16) for i in range(2)]
    rtab = [P.sbuf(f"rtab{i}", [128, HB], F32) for i in range(2)]
    ttab = P.tiles(2)
    SH = S // 2
    wk1 = P.sbuf("wk1", [128, SH], F32)
    wk2 = P.sbuf("wk2", [128, SH], F32)
    tiw = P.sbuf("tiw", [128, SH], I32)
    twk = P.tile()
    ones = P.sbuf("ones", [128, HB], F32)
    zc = P.sbuf("zc", [128, 1], F32)
    tones = P.tile()
    P.op("dve", lambda e: e.memset(ones[:], 1.0), writes=[tones])
    P.op("dve", lambda e: e.memset(zc[:], 0.0), reads=[tones], writes=[tones])
    P.op("dve", lambda e: e.tensor_scalar(out=CT[:, :, 2, :], in0=CT[:, :, 0, :], scalar1=-1.0, scalar2=None, op0=ALU.mult), reads=[tCT], writes=[tCT])
    NU = 3
    uwin = [P.sbuf(f"uwin{i}", [32, S], BF16) for i in range(NU)]
    tuw = P.tiles(NU)
    pbr = [P.psum(f"pbr{i}", [128, HB]) for i in range(2)]
    pbi = [P.psum(f"pbi{i}", [128, HB]) for i in range(2)]
    tpb = P.tiles(2)
    py = [P.psum(f"py{i}", [128, HB]) for i in range(2)]
    tpy = P.tiles(2)
    bsb = [[P.sbuf(f"bsb{i}_{j}", [128, HB], F32) for j in range(2)] for i in range(2)]
    tbsb = P.tiles(2)
    A = [[P.sbuf(f"A{i}_{j}", [128, HB], F32) for j in range(4)] for i in range(2)]
    tA01 = P.tiles(2)
    tA23 = P.tiles(2)
    W = [[P.sbuf(f"W{i}_{j}", [128, HB], F32) for j in range(2)] for i in range(2)]
    tW = P.tiles(2)
    NZb = 3
    Z = [[P.sbuf(f"Z{i}_{j}", [128, HB], F32) for j in range(2)] for i in range(NZb)]
    tZ = P.tiles(NZb)
    Zb = [[P.sbuf(f"Zb{i}_{j}", [128, HB], BF16) for j in range(2)] for i in range(2)]
    tZb = P.tiles(2)
    Bq = [[P.sbuf(f"B{i}_{j}", [128, HB], BF16) for j in range(4)] for i in range(2)]
    tB = P.tiles(2)
    tB2 = P.tiles(2)
    ytmp = [P.sbuf(f"ytmp{i}", [32, HB], F32) for i in range(2)]
    tyt = P.tiles(2)
    gout = [P.sbuf(f"gout{i}", [32, S], BF16) for i in range(2)]
    tgo = P.tiles(2)
    outs = []
    npairs = NP if dbg_pairs is None else dbg_pairs

    def gen_tables(k):
        tb = k % 2
        fh, fl, rk = sc["fhi"][:, k:k + 1], sc["flo"][:, k:k + 1], sc["r"][:, k:k + 1]
        for hh in range(2):
            hs = slice(hh * SH, (hh + 1) * SH)
            P.op("act", lambda e, hs=hs: e.activation(out=wk1[:], in_=iota[:, hs], func=AF.Copy, scale=fh), reads=[tio, tsc], writes=[twk])
            P.op("act", lambda e: e.activation(out=tiw[:], in_=wk1[:], func=AF.Copy), reads=[twk], writes=[twk])
            P.op("act", lambda e: e.activation(out=wk2[:], in_=tiw[:], func=AF.Copy), reads=[twk], writes=[twk])
            P.op("dve", lambda e: e.tensor_tensor(out=wk1[:], in0=wk1[:], in1=wk2[:], op=ALU.subtract), reads=[twk], writes=[twk])
            P.op("dve", lambda e, hs=hs: e.scalar_tensor_tensor(out=wk2[:], in0=iota[:, hs], scalar=fl, in1=wk1[:], op0=ALU.mult, op1=ALU.add), reads=[tio, tsc, twk], writes=[twk])
            P.op("act", lambda e: e.activation(out=tiw[:], in_=wk2[:], func=AF.Copy), reads=[twk], writes=[twk])
            P.op("act", lambda e: e.activation(out=wk1[:], in_=tiw[:], func=AF.Copy), reads=[twk], writes=[twk])
            P.op("dve", lambda e: e.tensor_tensor(out=wk2[:], in0=wk2[:], in1=wk1[:], op=ALU.subtract), reads=[twk], writes=[twk])
            P.op("act", lambda e, hs=hs: e.activation(out=sn[tb][:, hs], in_=wk2[:], func=AF.Sin, scale=TWO_PI), reads=[twk], writes=[ttab[tb]])
            P.op("act", lambda e: e.activation(out=wk1[:], in_=wk2[:], func=AF.Copy, scale=-1.0), reads=[twk], writes=[twk])
            P.op("dve", lambda e: e.tensor_tensor(out=wk1[:], in0=wk1[:], in1=wk2[:], op=ALU.min), reads=[twk], writes=[twk])
            P.op("act", lambda e, hs=hs: e.activation(out=cs[tb][:, hs], in_=wk1[:], func=AF.Sin, scale=TWO_PI, bias=cst[:, 0:1]), reads=[twk, tcst], writes=[ttab[tb]])
            P.op("act", lambda e, hs=hs: e.activation(out=snb[tb][:, hs], in_=sn[tb][:, hs], func=AF.Copy), reads=[ttab[tb]], writes=[ttab[tb]])
            P.op("act", lambda e, hs=hs: e.activation(out=csb[tb][:, hs], in_=cs[tb][:, hs], func=AF.Copy), reads=[ttab[tb]], writes=[ttab[tb]])
        P.op("dve", lambda e: e.tensor_scalar(out=rtab[tb][:], in0=ones[:], scalar1=rk, scalar2=None, op0=ALU.mult), reads=[tones, tsc, ttab[tb]], writes=[ttab[tb]])

    if npairs > 0:
        gen_tables(0)
    tasks = []
    g = 0
    for k in range(npairs):
        row0 = k * 32
        tb = k % 2
        for s in range(n_seq):
            c0 = s * S
            ub = (k * n_seq + s) % NU
            gb = (k * n_seq + s) % 2
            for qi in range(nq):
                t0 = qi * HB
                tsl = slice(t0, t0 + HB)
                b2 = g % 2
                bz = g % NZb
                bzp = (g - 1) % NZb
                g += 1

                def p0(k=k, s=s, qi=qi, ub=ub, row0=row0, c0=c0):
                    if qi == 0:
                        P.dma("sp", lambda e: e.dma_start(out=uwin[ub][:], in_=uT[row0:row0 + 32, c0:c0 + S]), writes=[tuw[ub]])
                    tpp = n_seq * nq
                    ti = s * nq + qi
                    assert tpp >= 8, "table double-buffering needs >= 8 tasks per pair"
                    if ti == 7 and k + 1 < npairs:
                        gen_tables(k + 1)

                def p1(k=k, ub=ub, b2=b2, tsl=tsl):
                    P.op("pe", lambda e: e.matmul(out=pbr[b2][:], lhsT=BT[:, k, 0, :], rhs=uwin[ub][:, tsl], start=True, stop=True),
                         reads=[tBT, tuw[ub]], writes=[tpb[b2]])
                    P.op("pe", lambda e: e.matmul(out=pbi[b2][:], lhsT=BT[:, k, 1, :], rhs=uwin[ub][:, tsl], start=True, stop=True),
                         reads=[tBT, tuw[ub]], writes=[tpb[b2]])

                def p2(b2=b2):
                    P.op("act", lambda e: e.activation(out=bsb[b2][0][:], in_=pbr[b2][:], func=AF.Copy), reads=[tpb[b2]], writes=[tbsb[b2]])
                    P.op("act", lambda e: e.activation(out=bsb[b2][1][:], in_=pbi[b2][:], func=AF.Copy), reads=[tpb[b2]], writes=[tbsb[b2]])

                def p3(b2=b2, tb=tb, tsl=tsl):
                    rd = [ttab[tb], tbsb[b2]]
                    P.op("dve", lambda e: e.tensor_tensor(out=A[b2][0][:], in0=cs[tb][:, tsl], in1=bsb[b2][0][:], op=ALU.mult), reads=rd, writes=[tA01[b2]])
                    P.op("dve", lambda e: e.tensor_tensor(out=A[b2][1][:], in0=sn[tb][:, tsl], in1=bsb[b2][1][:], op=ALU.mult), reads=rd, writes=[tA01[b2]])
                    P.op("dve", lambda e: e.tensor_tensor(out=A[b2][2][:], in0=cs[tb][:, tsl], in1=bsb[b2][1][:], op=ALU.mult), reads=rd, writes=[tA23[b2]])
                    P.op("dve", lambda e: e.tensor_tensor(out=A[b2][3][:], in0=sn[tb][:, tsl], in1=bsb[b2][0][:], op=ALU.mult), reads=rd, writes=[tA23[b2]])

                def p4(b2=b2):
                    P.op("dve", lambda e: e.tensor_tensor(out=W[b2][0][:], in0=A[b2][0][:], in1=A[b2][1][:], op=ALU.add), reads=[tA01[b2]], writes=[tW[b2]])
                    P.op("dve", lambda e: e.tensor_tensor(out=W[b2][1][:], in0=A[b2][2][:], in1=A[b2][3][:], op=ALU.subtract), reads=[tA23[b2]], writes=[tW[b2]])

                def p5(b2=b2, bz=bz, bzp=bzp, tb=tb, qi=qi):
                    for j in range(2):
                        if qi == 0:
                            init, rd = zc[:, 0:1], [tW[b2], ttab[tb], tones]
                        else:
                            init, rd = Z[bzp][j][:, HB - 1:HB], [tW[b2], ttab[tb], tZ[bzp]]
                        P.op("dve", lambda e, j=j, init=init: e.tensor_tensor_scan(out=Z[bz][j][:], data0=rtab[tb][:], data1=W[b2][j][:], initial=init,
                                                                                   op0=ALU.mult, op1=ALU.add), reads=rd, writes=[tZ[bz]])

                def p6(b2=b2, bz=bz):
                    for j in range(2):
                        P.op("act", lambda e, j=j: e.activation(out=Zb[b2][j][:], in_=Z[bz][j][:], func=AF.Copy), reads=[tZ[bz]], writes=[tZb[b2]])

                def p7(b2=b2, tb=tb, tsl=tsl):
                    rd = [ttab[tb], tZb[b2]]
                    P.op("dve", lambda e: e.tensor_tensor(out=Bq[b2][0][:], in0=csb[tb][:, tsl], in1=Zb[b2][0][:], op=ALU.mult), reads=rd, writes=[tB[b2]])
                    P.op("dve", lambda e: e.tensor_tensor(out=Bq[b2][1][:], in0=snb[tb][:, tsl], in1=Zb[b2][1][:], op=ALU.mult), reads=rd, writes=[tB[b2]])
                    P.op("dve", lambda e: e.tensor_tensor(out=Bq[b2][2][:], in0=snb[tb][:, tsl], in1=Zb[b2][0][:], op=ALU.mult), reads=rd, writes=[tB2[b2]])
                    P.op("dve", lambda e: e.tensor_tensor(out=Bq[b2][3][:], in0=csb[tb][:, tsl], in1=Zb[b2][1][:], op=ALU.mult), reads=rd, writes=[tB2[b2]])

                def p8(b2=b2, k=k):
                    for i4, ci in enumerate((0, 2, 1, 1)):
                        P.op("pe", lambda e, i4=i4, ci=ci: e.matmul(out=py[b2][:], lhsT=CT[:, k, ci, :], rhs=Bq[b2][i4][:], start=(i4 == 0), stop=(i4 == 3)),
                             reads=[tCT, tB[b2], tB2[b2]], writes=[tpy[b2]])

                def p9(b2=b2, ub=ub, gb=gb, tsl=tsl, k=k, qi=qi, row0=row0, c0=c0):
                    P.op("dve", lambda e: e.scalar_tensor_tensor(out=ytmp[b2][:], in0=uwin[ub][:, tsl], scalar=dwin[:, k:k + 1], in1=py[b2][0:32, :],
                                                                 op0=ALU.mult, op1=ALU.add), reads=[tuw[ub], tdw, tpy[b2]], writes=[tyt[b2]])
                    P.op("act", lambda e: e.activation(out=gout[gb][:, tsl], in_=ytmp[b2][:], func=AF.Gelu), reads=[tyt[b2]], writes=[tgo[gb]])
                    if qi == nq - 1:
                        to = P.tile()
                        P.dma("act", lambda e: e.dma_start(out=gT[row0:row0 + 32, c0:c0 + S], in_=gout[gb][:]), reads=[tgo[gb]], writes=[to])
                        outs.append(to)

                tasks.append([p0, p1, p2, p3, p4, p5, p6, p7, p8, p9])
    run_pipeline(tasks, [0, 1, 2, 3, 4, 5, 6, 7, 8, 9])
    if dbg is not None:
        for nm, ap_ in dbg.items():
            src = {"CT": CT, "BT": BT, "sn": sn[(npairs - 1) % 2], "cs": cs[(npairs - 1) % 2], "r": sc["r"], "qre": sc["qre"], "qim": sc["qim"], "fhi": sc["fhi"], "flo": sc["flo"]}[nm]
            tl = {"CT": tCT, "BT": tBT, "sn": ttab[(npairs - 1) % 2], "cs": ttab[(npairs - 1) % 2]}.get(nm, tsc)
            to = P.tile()
            P.dma("sp", lambda e, ap_=ap_, src=src: e.dma_start(out=ap_, in_=src[:]), reads=[tl], writes=[to])
            outs.append(to)
    P.finish(outs)


def stage_a3(nc, h_in, h_out, gT, wglu_ap, n_tok):
    P = Prog(nc)
    wg = P.sbuf("wg", [128, 8, 2048], BF16)
    twg = P.tile()
    load_weight_bf16(P, wg, twg, wglu_ap, 8)
    gb = [P.sbuf(f"gb{i}", [128, 8, 512], BF16) for i in range(2)]
    tgb = P.tiles(2)
    xt = [P.sbuf(f"xt{i}", [128, 4, D], F32) for i in range(2)]
    txt = P.tiles(2)
    pv = [P.psum(f"pv{i}", [128, 512]) for i in range(2)]
    tpv = P.tiles(2)
    pg = [P.psum(f"pg{i}", [128, 512]) for i in range(2)]
    tpg = P.tiles(2)
    sg = [P.sbuf(f"sg{i}", [128, 512], F32) for i in range(2)]
    tsg = P.tiles(2)
    outs = []
    k = 0
    for u in range(n_tok // 512):
        r0 = u * 512
        ob = u % 2
        P.dma("sp", lambda e, ob=ob, r0=r0: e.dma_start(out=gb[ob][:], in_=gT[:, r0:r0 + 512].rearrange("(c p) n -> p c n", p=128)), writes=[tgb[ob]])
        P.dma("sp", lambda e, ob=ob, r0=r0: e.dma_start(out=xt[ob][:], in_=h_in[r0:r0 + 512, :].rearrange("(t p) d -> p t d", p=128)), writes=[txt[ob]])
        for t in range(4):
            for nb in range(2):
                b = k % 2
                k += 1
                for c in range(8):
                    P.op("pe", lambda e, c=c, b=b, t=t, nb=nb, ob=ob: e.matmul(out=pv[b][:], lhsT=gb[ob][:, c, t * 128:(t + 1) * 128], rhs=wg[:, c, nb * 512:(nb + 1) * 512],
                                                                               start=(c == 0), stop=(c == 7)), reads=[tgb[ob], twg], writes=[tpv[b]])
                for c in range(8):
                    P.op("pe", lambda e, c=c, b=b, t=t, nb=nb, ob=ob: e.matmul(out=pg[b][:], lhsT=gb[ob][:, c, t * 128:(t + 1) * 128], rhs=wg[:, c, 1024 + nb * 512:1024 + (nb + 1) * 512],
                                                                               start=(c == 0), stop=(c == 7)), reads=[tgb[ob], twg], writes=[tpg[b]])
                P.op("act", lambda e, b=b: e.activation(out=sg[b][:], in_=pg[b][:], func=AF.Sigmoid), reads=[tpg[b]], writes=[tsg[b]])
                P.op("dve", lambda e, b=b: e.tensor_tensor(out=sg[b][:], in0=sg[b][:], in1=pv[b][:], op=ALU.mult), reads=[tsg[b], tpv[b]], writes=[tsg[b]])
                P.op("pool", lambda e, b=b, t=t, nb=nb, ob=ob: e.tensor_tensor(out=xt[ob][:, t, nb * 512:(nb + 1) * 512], in0=xt[ob][:, t, nb * 512:(nb + 1) * 512], in1=sg[b][:], op=ALU.add),
                     reads=[tsg[b], txt[ob]], writes=[txt[ob]])
        to = P.tile()
        P.dma("act", lambda e, ob=ob, r0=r0: e.dma_start(out=h_out[r0:r0 + 512, :].rearrange("(t p) d -> p t d", p=128), in_=xt[ob][:]), reads=[txt[ob]], writes=[to])
        outs.append(to)
    P.finish(outs)


N_CORES = 8
SEQ = 2048
N_SEQ = 4
NT = N_SEQ * SEQ

_PARAMS = [
    ("a_norm", [1, 1024]), ("a_w_in", [1, 1024, 1024]), ("a_lam_re", [1, 64, 64]), ("a_lam_im", [1, 64, 64]),
    ("a_b_re", [1, 64, 64, 16]), ("a_b_im", [1, 64, 64, 16]), ("a_c_re", [1, 64, 16, 64]), ("a_c_im", [1, 64, 16, 64]),
    ("a_d", [1, 1024]), ("a_log_dt", [1, 64]), ("a_w_glu", [1, 1024, 2048]), ("kv_norm", [1024]), ("w_kv", [1024, 2048]),
    ("k_norm", [64]), ("b_norm", [1, 1024]), ("b_w_q", [1, 1024, 1024]), ("b_q_norm", [1, 64]), ("b_w_o", [1, 1024, 1024]),
    ("ffn_norm", [2, 1024]), ("ffn_w_up", [2, 1024, 5632]), ("ffn_conv_w", [2, 3, 2816]), ("ffn_conv_b", [2, 2816]),
    ("ffn_w_down", [2, 2816, 1024]),
]


def build_program(N_SEQ=N_SEQ, SEQ=SEQ, debug=False):
    NT = N_SEQ * SEQ
    nc = bass.Bass("TRN2", target_bir_lowering=False)
    x = nc.dram_tensor("x", [NT, D], F32, kind="ExternalInput").ap()
    prm = {n: nc.dram_tensor(n, s, F32, kind="ExternalInput").ap() for n, s in _PARAMS}
    ident = nc.dram_tensor("c_ident", [128, 128], F32, kind="ExternalInput").ap()
    bones = nc.dram_tensor("c_bones", [128, 128], F32, kind="ExternalInput").ap()
    maskb = nc.dram_tensor("c_maskb", [128, 128], F32, kind="ExternalInput").ap()
    iota = nc.dram_tensor("c_iota", [128, 2048], F32, kind="ExternalInput").ap()
    out = nc.dram_tensor("out", [NT, D], F32, kind="ExternalOutput").ap()
    kd = "ExternalOutput" if debug else "Internal"
    h1 = nc.dram_tensor("h1", [NT, D], F32, kind=kd).ap()
    h2 = nc.dram_tensor("h2", [NT, D], F32, kind=kd).ap()
    h3 = nc.dram_tensor("h3", [NT, D], F32, kind=kd).ap()
    uT = nc.dram_tensor("uT", [D, NT], BF16).ap()
    gT = nc.dram_tensor("gT", [D, NT], BF16).ap()
    kT = nc.dram_tensor("kT", [D, NT], BF16).ap()
    qT = nc.dram_tensor("qT", [D, NT], BF16).ap()
    vr = nc.dram_tensor("vr", [NT, D], BF16).ap()
    a2args = (prm["a_lam_re"][0], prm["a_lam_im"][0], prm["a_b_re"][0], prm["a_b_im"][0], prm["a_c_re"][0], prm["a_c_im"][0],
              prm["a_d"][0], prm["a_log_dt"][0], iota)
    keep = contextlib.ExitStack()
    box = {}

    def prep(P):
        box["ctx"] = a2_prep(P, lambda n, sh, dt_: keep.enter_context(nc.sbuf_tensor("keep_" + n, list(sh), dt_)), *a2args, SEQ)

    stage_a1(nc, x, uT, prm["a_norm"][0], prm["a_w_in"][0], ident, NT, prep=prep)
    stage_a2(nc, uT, gT, *a2args, N_SEQ, SEQ, ctx=box["ctx"])
    keep.close()
    stage_a3(nc, x, h1, gT, prm["a_w_glu"][0], NT)
    stage_ffn(nc, h1, h2, prm["ffn_norm"][0], prm["ffn_w_up"][0], prm["ffn_conv_w"][0], prm["ffn_conv_b"][0],
              prm["ffn_w_down"][0], ident, N_SEQ, SEQ)
    stage_kvq(nc, h2, kT, qT, vr, prm["kv_norm"], prm["w_kv"], prm["k_norm"], prm["b_norm"][0], prm["b_w_q"][0], prm["b_q_norm"][0],
              ident, bones, N_SEQ, SEQ)
    stage_att(nc, h2, h3, qT, kT, vr, prm["b_w_o"][0], ident, maskb, N_SEQ, SEQ)
    stage_ffn(nc, h3, out, prm["ffn_norm"][1], prm["ffn_w_up"][1], prm["ffn_conv_w"][1], prm["ffn_conv_b"][1],
              prm["ffn_w_down"][1], ident, N_SEQ, SEQ)
    return nc


def kernel(**inputs):
    x = np.ascontiguousarray(np.asarray(inputs["x"], dtype=np.float32))
    nc = build_program()
    p = np.arange(128)
    consts = {
        "c_ident": np.eye(128, dtype=np.float32),
        "c_bones": (p[:, None] // 64 == p[None, :] // 64).astype(np.float32),
        "c_maskb": np.where(p[None, :] + p[:, None] >= 128, 0.0, -30000.0).astype(np.float32),
        "c_iota": np.ascontiguousarray(np.tile(np.arange(2048, dtype=np.float32), (128, 1))),
    }
    params = {n: np.ascontiguousarray(np.asarray(inputs[n], dtype=np.float32)).reshape(s) for n, s in _PARAMS}
    in_maps = []
    for c in range(N_CORES):
        m = {"x": x[c * N_SEQ:(c + 1) * N_SEQ].reshape(NT, D)}
        m.update(params)
        m.update(consts)
        in_maps.append(m)
    res = run_bass_kernel_spmd(nc, in_maps, core_ids=list(range(N_CORES)))
    outs = [np.asarray(r["out"], dtype=np.float32).reshape(N_SEQ, SEQ, D) for r in res.results]
    return np.concatenate(outs, axis=0)
```

```python
import contextlib
import numpy as np
import concourse.bass as bass
import concourse.mybir as mybir
from concourse.bass_utils import run_bass_kernel_spmd

F32 = mybir.dt.float32
BF16 = mybir.dt.bfloat16
AF = mybir.ActivationFunctionType
ALU = mybir.AluOpType
AX = mybir.AxisListType

ENGS = ("pe", "act", "dve", "pool", "sp")
N_DMA_SEMS = 4

D = 1024
DFF = 2816
NFT = DFF // 128
EPS = 1e-6


class T:
    __slots__ = ("name", "writes", "reads")

    def __init__(self, name="t"):
        self.name = name
        self.writes = {}
        self.reads = {}


class Prog:
    _stage = 0

    def __init__(self, nc):
        self.nc = nc
        Prog._stage += 1
        self.sid = Prog._stage
        self.stack = contextlib.ExitStack()
        self.ops = {e: [] for e in ENGS}
        self.known = {e: {} for e in ENGS}
        self.ndma = {e: 0 for e in ENGS}
        self.cnt = {e: 0 for e in ENGS}
        self.milestones = {e: set() for e in ENGS}
        self._n = 0

    def sbuf(self, name, shape, dtype):
        return self.stack.enter_context(self.nc.sbuf_tensor(f"s{self.sid}_{name}", list(shape), dtype))

    def psum(self, name, shape, dtype=F32):
        return self.stack.enter_context(self.nc.psum_tensor(f"s{self.sid}_{name}", list(shape), dtype))

    def tile(self, name=None):
        return T(name or "t")

    def tiles(self, n):
        return [T() for _ in range(n)]

    def _collect(self, eng, reads, writes):
        need = {}
        for t in reads:
            for k, v in t.writes.items():
                if need.get(k, 0) < v:
                    need[k] = v
        for t in writes:
            for d in (t.writes, t.reads):
                for k, v in d.items():
                    if need.get(k, 0) < v:
                        need[k] = v
        waits = []
        kn = self.known[eng]
        for k, v in need.items():
            if k == ("e", eng) and eng == "pe":
                continue
            if kn.get(k, 0) >= v:
                continue
            kn[k] = v
            waits.append((k, v))
            if k[0] == "e":
                self.milestones[k[1]].add(v)
        return waits

    def op(self, eng, fn, reads=(), writes=()):
        waits = self._collect(eng, reads, writes)
        self.cnt[eng] += 1
        idx = self.cnt[eng]
        key = ("e", eng)
        self.ops[eng].append(dict(kind="op", fn=fn, waits=waits, idx=idx))
        for t in reads:
            if t.reads.get(key, 0) < idx:
                t.reads[key] = idx
        for t in writes:
            t.writes = {key: idx}
            t.reads = {}

    def dma(self, eng, fn, reads=(), writes=()):
        i = self.ndma[eng]
        self.ndma[eng] += 1
        slot = i % N_DMA_SEMS
        gen = i // N_DMA_SEMS
        key = ("d", eng, slot)
        waits = self._collect(eng, reads, writes)
        kn = self.known[eng]
        if gen > 0 and kn.get(key, 0) < 16 * gen:
            kn[key] = 16 * gen
            waits.append((key, 16 * gen))
        val = 16 * (gen + 1)
        self.ops[eng].append(dict(kind="dma", fn=fn, waits=waits, key=key))
        for t in reads:
            if t.reads.get(key, 0) < val:
                t.reads[key] = val
        for t in writes:
            t.writes = {key: val}
            t.reads = {}

    def wait_all(self, eng, tiles):
        waits = self._collect(eng, tiles, ())
        self.ops[eng].append(dict(kind="wait", waits=waits))

    def finish(self, out_tiles):
        self.wait_all("sp", out_tiles)
        nc = self.nc
        with nc.cleanup_on_exit():
            sems = {}
            for e in ENGS:
                if self.milestones[e]:
                    sems[("e", e)] = nc.alloc_semaphore(f"p{self.sid}_{e}")
                for s in range(min(N_DMA_SEMS, self.ndma[e])):
                    sems[("d", e, s)] = nc.alloc_semaphore(f"d{self.sid}_{e}_{s}")
            mmap = {e: {v: i + 1 for i, v in enumerate(sorted(self.milestones[e]))} for e in ENGS}

            def replay(e):
                def body(engine):
                    for o in self.ops[e]:
                        for k, v in o["waits"]:
                            if k[0] == "e":
                                v = mmap[k[1]][v]
                            engine.wait_ge(sems[k], v)
                        if o["kind"] == "op":
                            ins = o["fn"](engine)
                            if o["idx"] in mmap[e]:
                                ins.then_inc(sems[("e", e)], 1)
                        elif o["kind"] == "dma":
                            ins = o["fn"](engine)
                            ins.then_inc(sems[o["key"]], 16)
                return body

            with nc.Block() as block:
                block.tensor(replay("pe"))
                block.scalar(replay("act"))
                block.vector(replay("dve"))
                block.gpsimd(replay("pool"))
                block.sync(replay("sp"))
            nc.all_engine_barrier()
        self.stack.close()


def run_pipeline(tasks, offsets):
    nph = len(offsets)
    for step in range(len(tasks) + max(offsets) + 1):
        for p in reversed(range(nph)):
            t = step - offsets[p]
            if 0 <= t < len(tasks) and tasks[t][p] is not None:
                tasks[t][p]()


def load_weight_bf16(P, dst, tdst, w_ap, kchunks, gain_sb=None, tgain=None, eng="dve"):
    n = w_ap.shape[-1]
    src = w_ap.rearrange("(c p) n -> p c n", p=128)
    step = max(1, 4096 // n)
    for c0 in range(0, kchunks, step):
        c1 = min(kchunks, c0 + step)
        P.dma("pool", lambda e, c0=c0, c1=c1: e.dma_start(out=dst[:, c0:c1, :], in_=src[:, c0:c1, :]), writes=[tdst])
    if gain_sb is not None:
        for c in range(kchunks):
            P.op(eng, lambda e, c=c: e.tensor_scalar(out=dst[:, c, :], in0=dst[:, c, :], scalar1=gain_sb[:, c:c + 1],
                                                      scalar2=None, op0=ALU.mult), reads=[tdst, tgain], writes=[tdst])


def load_gain(P, name, g_ap):
    g_sb = P.sbuf(name, [128, 8], F32)
    t = P.tile()
    P.dma("sp", lambda e: e.dma_start(out=g_sb[:], in_=g_ap.rearrange("(c p) -> p c", p=128), allow_slow_non_contiguous=True), writes=[t])
    return g_sb, t


class NormT:
    def __init__(self, P, ident, tident, n_xt=2, n_xnT=1, junk=None, tjunk=None):
        self.P = P
        self.ident, self.tident = ident, tident
        self.n_xt, self.n_xnT = n_xt, n_xnT
        self.xt = [P.sbuf(f"nt_xt{i}", [128, 4, D], F32) for i in range(n_xt)]
        self.txt = P.tiles(n_xt)
        if junk is None:
            self.junk = P.sbuf("nt_junk", [128, D], BF16)[:]
            self.tjunk = P.tile()
        else:
            self.junk, self.tjunk = junk, tjunk
        self.ss = P.sbuf("nt_ss", [128, 8], F32)
        self.tss = P.tile()
        self.xn = [P.sbuf(f"nt_xn{i}", [128, D], BF16) for i in range(4)]
        self.txn = P.tiles(4)
        self.pst = [P.psum(f"nt_pst{i}", [128, D], BF16) for i in range(2)]
        self.tpst = P.tiles(2)
        self.xnT = [P.sbuf(f"nt_xnT{i}", [128, 8, 512], BF16) for i in range(n_xnT)]
        self.txnT = P.tiles(n_xnT)
        self.k = 0
        self.n = 0
        self.n2 = 0
        self.kp = 0

    def part1(self, src_rows):
        P = self.P
        b = self.n % self.n_xt
        self.n += 1
        xt, txt = self.xt[b], self.txt[b]
        P.dma("sp", lambda e: e.dma_start(out=xt[:], in_=src_rows.rearrange("(t p) d -> p t d", p=128)), writes=[txt])
        ss, tss = self.ss, self.tss
        for t in range(4):
            xn, txn = self.xn[t], self.txn[t]
            self.k += 1
            col = (self.k % 4) * 2
            P.op("dve", lambda e, col=col: e.memset(ss[:, col:col + 1], 0.0), writes=[tss])
            P.op("act", lambda e, t=t, col=col: e.activation(out=self.junk, in_=xt[:, t, :], func=AF.Square,
                                                             accum_out=ss[:, col:col + 1]),
                 reads=[txt], writes=[self.tjunk, tss])
            P.op("dve", lambda e, col=col: e.tensor_scalar(out=ss[:, col + 1:col + 2], in0=ss[:, col:col + 1], scalar1=1.0 / D,
                                                           scalar2=EPS, op0=ALU.mult, op1=ALU.add), reads=[tss], writes=[tss])
            P.op("act", lambda e, col=col: e.activation(out=ss[:, col + 1:col + 2], in_=ss[:, col + 1:col + 2], func=AF.Sqrt),
                 reads=[tss], writes=[tss])
            P.op("dve", lambda e, col=col: e.reciprocal(out=ss[:, col + 1:col + 2], in_=ss[:, col + 1:col + 2]), reads=[tss], writes=[tss])
            P.op("act", lambda e, t=t, col=col, xn=xn: e.activation(out=xn[:], in_=xt[:, t, :], func=AF.Copy,
                                                                     scale=ss[:, col + 1:col + 2]),
                 reads=[txt, tss], writes=[txn])
        return xt, txt

    def part2(self):
        P = self.P
        b2 = self.n2 % self.n_xnT
        self.n2 += 1
        xnT, txnT = self.xnT[b2], self.txnT[b2]
        for t in range(4):
            xn, txn = self.xn[t], self.txn[t]
            kk = self.kp % 2
            self.kp += 1
            pst, tpst = self.pst[kk], self.tpst[kk]
            for c in range(8):
                P.op("pe", lambda e, c=c, xn=xn, pst=pst: e.transpose(out=pst[:, c * 128:(c + 1) * 128],
                                                                       in_=xn[:, c * 128:(c + 1) * 128], identity=self.ident[:]),
                     reads=[txn, self.tident], writes=[tpst])
            P.op("dve", lambda e, t=t, pst=pst, xnT=xnT: e.tensor_copy(out=xnT[:, :, t * 128:(t + 1) * 128],
                                                                       in_=pst[:].rearrange("p (c k) -> p c k", k=128)),
                 reads=[tpst], writes=[txnT])
        return xnT, txnT

    def run(self, src_rows):
        xt, txt = self.part1(src_rows)
        xnT, txnT = self.part2()
        return xt, txt, xnT, txnT


def load_ident(P, ident_ap):
    ident = P.sbuf("ident", [128, 128], BF16)
    t = P.tile()
    P.dma("pool", lambda e: e.dma_start(out=ident[:], in_=ident_ap), writes=[t])
    return ident, t


def stage_ffn(nc, h_in, h_out, g_ap, wup_ap, cw_ap, cb_ap, wdn_ap, ident_ap, n_seq, seq_len):
    P = Prog(nc)
    ident, tident = load_ident(P, ident_ap)
    g_sb, tg = load_gain(P, "g", g_ap)
    wup = P.sbuf("wup", [128, 8, 2 * DFF], BF16)
    twup = P.tile()
    load_weight_bf16(P, wup, twup, wup_ap, 8, g_sb, tg)
    wdn = P.sbuf("wdn", [128, NFT, D], BF16)
    twdn = P.tile()
    load_weight_bf16(P, wdn, twdn, wdn_ap, NFT)
    cw = P.sbuf("cw", [128, NFT, 3], F32)
    cb = P.sbuf("cb", [128, NFT], F32)
    tcw = P.tile()
    for j in range(3):
        P.dma("sp", lambda e, j=j: e.dma_start(out=cw[:, :, j], in_=cw_ap[j].rearrange("(f p) -> p f", p=128), allow_slow_non_contiguous=True), writes=[tcw])
    P.dma("sp", lambda e: e.dma_start(out=cb[:], in_=cb_ap.rearrange("(f p) -> p f", p=128), allow_slow_non_contiguous=True), writes=[tcw])
    gc = [P.sbuf(f"gc{i}", [128, 512], F32) for i in range(2)]
    tgc = P.tiles(2)
    nt = NormT(P, ident, tident, n_xt=1, n_xnT=2, junk=gc[1][:].bitcast(BF16), tjunk=tgc[1])
    halo = P.sbuf("halo", [128, NFT, 2], F32)
    thalo = P.tile()
    gbuf = [P.sbuf(f"gbuf{i}", [128, 514], F32) for i in range(2)]
    tgbuf = P.tiles(2)
    hid = P.sbuf("hid", [128, NFT, 512], BF16)
    thid = P.tile()
    pv = [P.psum(f"pv{i}", [128, 512]) for i in range(2)]
    tpv = P.tiles(2)
    pg = [P.psum(f"pg{i}", [128, 512]) for i in range(2)]
    tpg = P.tiles(2)
    po = [P.psum(f"po{i}", [128, 512]) for i in range(2)]
    tpo = P.tiles(2)
    xr = [P.sbuf(f"xr{i}", [128, D], F32) for i in range(1)] * 2
    txr = P.tiles(1) * 2
    outs = []
    nsup = seq_len // 512
    nblk = n_seq * nsup
    cnt = {"k": 0, "ko": 0, "kx": 0}
    xn_of = {}

    def phA1(u):
        r0 = u * 512
        nt.part1(h_in[r0:r0 + 512, :])

    def phA2(u):
        xn_of[u] = nt.part2()

    def phB(u):
        xnT, txnT = xn_of[u]
        if u % nsup == 0:
            P.op("pool", lambda e: e.memset(halo[:], 0.0), writes=[thalo])
        for ft in range(NFT):
            if ft == 2 and u + 1 < nblk:
                phA1(u + 1)
            b = cnt["k"] % 2
            cnt["k"] += 1
            for c in range(8):
                P.op("pe", lambda e, c=c, b=b, ft=ft: e.matmul(out=pv[b][:], lhsT=wup[:, c, ft * 128:(ft + 1) * 128], rhs=xnT[:, c, :],
                                                               start=(c == 0), stop=(c == 7)), reads=[twup, txnT], writes=[tpv[b]])
            for c in range(8):
                P.op("pe", lambda e, c=c, b=b, ft=ft: e.matmul(out=pg[b][:], lhsT=wup[:, c, DFF + ft * 128:DFF + (ft + 1) * 128], rhs=xnT[:, c, :],
                                                               start=(c == 0), stop=(c == 7)), reads=[twup, txnT], writes=[tpg[b]])
            gb, tgb, g2, tg2 = gbuf[b], tgbuf[b], gc[b], tgc[b]
            P.op("pool", lambda e, gb=gb, ft=ft: e.tensor_copy(out=gb[:, 0:2], in_=halo[:, ft, :]), reads=[thalo], writes=[tgb])
            P.op("act", lambda e, gb=gb, b=b: e.activation(out=gb[:, 2:514], in_=pg[b][:], func=AF.Copy), reads=[tpg[b]], writes=[tgb])
            P.op("pool", lambda e, gb=gb, ft=ft: e.tensor_copy(out=halo[:, ft, :], in_=gb[:, 512:514]), reads=[tgb], writes=[thalo])
            P.op("act", lambda e, gb=gb, g2=g2, ft=ft: e.activation(out=g2[:], in_=gb[:, 2:514], func=AF.Identity,
                                                                    scale=cw[:, ft, 2:3], bias=cb[:, ft:ft + 1]),
                 reads=[tgb, tcw], writes=[tg2])
            P.op("dve", lambda e, gb=gb, g2=g2, ft=ft: e.scalar_tensor_tensor(out=g2[:], in0=gb[:, 1:513], scalar=cw[:, ft, 1:2], in1=g2[:],
                                                                              op0=ALU.mult, op1=ALU.add), reads=[tgb, tcw, tg2], writes=[tg2])
            P.op("dve", lambda e, gb=gb, g2=g2, ft=ft: e.scalar_tensor_tensor(out=g2[:], in0=gb[:, 0:512], scalar=cw[:, ft, 0:1], in1=g2[:],
                                                                              op0=ALU.mult, op1=ALU.add), reads=[tgb, tcw, tg2], writes=[tg2])
            P.op("act", lambda e, g2=g2: e.activation(out=g2[:], in_=g2[:], func=AF.Silu), reads=[tg2], writes=[tg2])
            P.op("dve", lambda e, g2=g2, b=b, ft=ft: e.tensor_tensor(out=hid[:, ft, :], in0=g2[:], in1=pv[b][:], op=ALU.mult),
                 reads=[tg2, tpv[b]], writes=[thid])

    def phC(u):
        r0 = u * 512
        for t in range(4):
            xb = cnt["kx"] % 2
            cnt["kx"] += 1
            rr = r0 + t * 128
            P.dma("sp", lambda e, xb=xb, rr=rr: e.dma_start(out=xr[xb][:], in_=h_in[rr:rr + 128, :]), writes=[txr[xb]])
            for nb in range(2):
                b = cnt["ko"] % 2
                cnt["ko"] += 1
                for ft in range(NFT):
                    P.op("pe", lambda e, ft=ft, t=t, nb=nb, b=b: e.matmul(out=po[b][:], lhsT=hid[:, ft, t * 128:(t + 1) * 128],
                                                                          rhs=wdn[:, ft, nb * 512:(nb + 1) * 512],
                                                                          start=(ft == 0), stop=(ft == NFT - 1)),
                         reads=[thid, twdn], writes=[tpo[b]])
                P.op("dve", lambda e, nb=nb, b=b, xb=xb: e.tensor_tensor(out=xr[xb][:, nb * 512:(nb + 1) * 512], in0=po[b][:],
                                                                         in1=xr[xb][:, nb * 512:(nb + 1) * 512], op=ALU.add),
                     reads=[tpo[b], txr[xb]], writes=[txr[xb]])
            to = P.tile()
            P.dma("act", lambda e, xb=xb, rr=rr: e.dma_start(out=h_out[rr:rr + 128, :], in_=xr[xb][:]), reads=[txr[xb]], writes=[to])
            outs.append(to)

    phA1(0)
    phA2(0)
    for u in range(nblk):
        phB(u)
        if u + 1 < nblk:
            phA2(u + 1)
        phC(u)
    P.finish(outs)


def stage_kvq(nc, h_in, kT_rev, qT, v_rev, kvn_ap, wkv_ap, kn_ap, bn_ap, wq_ap, qn_ap, ident_ap, bones_ap, n_seq, S):
    P = Prog(nc)
    ident, tident = load_ident(P, ident_ap)
    bones = P.sbuf("bones", [128, 128], BF16)
    tbones = P.tile()
    P.dma("pool", lambda e: e.dma_start(out=bones[:], in_=bones_ap), writes=[tbones])
    gkv, tgkv = load_gain(P, "gkv", kvn_ap)
    gb, tgb = load_gain(P, "gb", bn_ap)
    wkv = P.sbuf("wkv", [128, 8, 2048], BF16)
    twkv = P.tile()
    load_weight_bf16(P, wkv, twkv, wkv_ap, 8, gkv, tgkv)
    wq = P.sbuf("wq", [128, 8, 1024], BF16)
    twq = P.tile()
    load_weight_bf16(P, wq, twq, wq_ap, 8, gb, tgb)
    hg = P.sbuf("hg", [128, 4], F32)
    thg = P.tile()
    for hf in range(2):
        P.dma("sp", lambda e, hf=hf: e.dma_start(out=hg[hf * 64:(hf + 1) * 64, 0:1], in_=kn_ap.rearrange("(p o) -> p o", o=1)), writes=[thg])
        P.dma("sp", lambda e, hf=hf: e.dma_start(out=hg[hf * 64:(hf + 1) * 64, 1:2], in_=qn_ap.rearrange("(p o) -> p o", o=1)), writes=[thg])
    P.op("dve", lambda e: e.tensor_scalar(out=hg[:, 1:2], in0=hg[:, 1:2], scalar1=0.125, scalar2=None, op0=ALU.mult), reads=[thg], writes=[thg])
    P.op("dve", lambda e: e.memset(hg[:, 2:3], EPS), reads=[thg], writes=[thg])
    nt = NormT(P, ident, tident)
    pk = [P.psum(f"pk{i}", [128, 512]) for i in range(2)]
    tpk = P.tiles(2)
    pss = [P.psum(f"pss{i}", [128, 512]) for i in range(2)]
    tpss = P.tiles(2)
    sq = [P.sbuf(f"sq{i}", [128, 512], BF16) for i in range(2)]
    tsq = P.tiles(2)
    rs = [P.sbuf(f"rs{i}", [128, 512], F32) for i in range(2)]
    trs = P.tiles(2)
    kbuf = [P.sbuf(f"kbuf{i}", [128, 8, 512], BF16) for i in range(2)]
    tkbuf = P.tiles(2)
    qbuf = [P.sbuf(f"qbuf{i}", [128, 8, 512], BF16) for i in range(2)]
    tqbuf = P.tiles(2)
    vbuf = [P.sbuf(f"vbuf{i}", [128, 4, 1024], BF16) for i in range(2)]
    tvbuf = P.tiles(2)
    xrev = P.sbuf("xrev", [128, 8, 512], BF16)
    txrev = P.tile()
    outs = []
    nsup = S // 512
    k = 0
    for s in range(n_seq):
        for u in range(nsup):
            r0 = s * S + u * 512
            rr = s * S + S - 512 * (u + 1)
            ob = (s * nsup + u) % 2
            if s == 0 and u == 0:
                nt.part1(h_in[r0:r0 + 512, :])
            xnT, txnT = nt.part2()
            if r0 + 512 < n_seq * S:
                nt.part1(h_in[r0 + 512:r0 + 1024, :])
            for which in range(2):
                for ft in range(8):
                    b = k % 2
                    k += 1
                    for c in range(8):
                        if which == 0:
                            P.op("pe", lambda e, c=c, b=b, ft=ft: e.matmul(out=pk[b][:], lhsT=wkv[:, c, ft * 128:(ft + 1) * 128], rhs=xnT[:, c, :],
                                                                           start=(c == 0), stop=(c == 7)), reads=[twkv, txnT], writes=[tpk[b]])
                        else:
                            P.op("pe", lambda e, c=c, b=b, ft=ft: e.matmul(out=pk[b][:], lhsT=wq[:, c, ft * 128:(ft + 1) * 128], rhs=xnT[:, c, :],
                                                                           start=(c == 0), stop=(c == 7)), reads=[twq, txnT], writes=[tpk[b]])
                    P.op("act", lambda e, b=b: e.activation(out=sq[b][:], in_=pk[b][:], func=AF.Square), reads=[tpk[b]], writes=[tsq[b]])
                    P.op("pe", lambda e, b=b: e.matmul(out=pss[b][:], lhsT=bones[:], rhs=sq[b][:], start=True, stop=True),
                         reads=[tbones, tsq[b]], writes=[tpss[b]])
                    P.op("act", lambda e, b=b: e.activation(out=rs[b][:], in_=pss[b][:], func=AF.Ln, scale=1.0 / 64, bias=hg[:, 2:3]),
                         reads=[tpss[b], thg], writes=[trs[b]])
                    P.op("act", lambda e, b=b: e.activation(out=rs[b][:], in_=rs[b][:], func=AF.Exp, scale=-0.5), reads=[trs[b]], writes=[trs[b]])
                    if which == 0:
                        P.op("dve", lambda e, b=b, ft=ft, ob=ob: e.scalar_tensor_tensor(out=kbuf[ob][:, ft, ::-1], in0=pk[b][:], scalar=hg[:, 0:1], in1=rs[b][:],
                                                                                        op0=ALU.mult, op1=ALU.mult),
                             reads=[tpk[b], thg, trs[b]], writes=[tkbuf[ob]])
                    else:
                        P.op("dve", lambda e, b=b, ft=ft, ob=ob: e.scalar_tensor_tensor(out=qbuf[ob][:, ft, :], in0=pk[b][:], scalar=hg[:, 1:2], in1=rs[b][:],
                                                                                        op0=ALU.mult, op1=ALU.mult),
                             reads=[tpk[b], thg, trs[b]], writes=[tqbuf[ob]])
            P.op("dve", lambda e, xnT=xnT: e.tensor_copy(out=xrev[:, :, ::-1], in_=xnT[:]), reads=[txnT], writes=[txrev])
            for t in range(4):
                for nb in range(2):
                    b = k % 2
                    k += 1
                    for c in range(8):
                        P.op("pe", lambda e, c=c, b=b, t=t, nb=nb: e.matmul(out=pk[b][:], lhsT=xrev[:, c, t * 128:(t + 1) * 128],
                                                                            rhs=wkv[:, c, 1024 + nb * 512:1024 + (nb + 1) * 512],
                                                                            start=(c == 0), stop=(c == 7)), reads=[twkv, txrev], writes=[tpk[b]])
                    P.op("act", lambda e, b=b, t=t, nb=nb, ob=ob: e.activation(out=vbuf[ob][:, t, nb * 512:(nb + 1) * 512], in_=pk[b][:], func=AF.Copy),
                         reads=[tpk[b]], writes=[tvbuf[ob]])
            t1, t2, t3 = P.tile(), P.tile(), P.tile()
            P.dma("act", lambda e, ob=ob, rr=rr: e.dma_start(out=kT_rev[:, rr:rr + 512].rearrange("(f p) n -> p f n", p=128), in_=kbuf[ob][:]),
                  reads=[tkbuf[ob]], writes=[t1])
            P.dma("act", lambda e, ob=ob, r0=r0: e.dma_start(out=qT[:, r0:r0 + 512].rearrange("(f p) n -> p f n", p=128), in_=qbuf[ob][:]),
                  reads=[tqbuf[ob]], writes=[t2])
            P.dma("act", lambda e, ob=ob, rr=rr: e.dma_start(out=v_rev[rr:rr + 512, :].rearrange("(t p) d -> p t d", p=128), in_=vbuf[ob][:]),
                  reads=[tvbuf[ob]], writes=[t3])
            outs += [t1, t2, t3]
    P.finish(outs)


def stage_att(nc, h_in, h_out, qT, kT_rev, v_rev, wo_ap, ident_ap, maskb_ap, n_seq, S):
    P = Prog(nc)
    ident, tident = load_ident(P, ident_ap)
    maskb = P.sbuf("maskb", [128, 128], BF16)
    tmask = P.tile()
    P.dma("pool", lambda e: e.dma_start(out=maskb[:], in_=maskb_ap), writes=[tmask])
    wo = P.sbuf("wo", [128, 8, 1024], BF16)
    two = P.tile()
    load_weight_bf16(P, wo, two, wo_ap, 8)
    zeros = P.sbuf("zeros", [128, 512], F32)
    onec = P.sbuf("onec", [128, 1], F32)
    tz = P.tile()
    P.op("pool", lambda e: e.memset(zeros[:], 0.0), writes=[tz])
    P.op("pool", lambda e: e.memset(onec[:], 1.0), reads=[tz], writes=[tz])
    NB = S // 128
    qsb = P.sbuf("qsb", [128, 8, S], BF16)
    ksb = P.sbuf("ksb", [128, 8, S], BF16)
    vsb = P.sbuf("vsb", [128, NB, 1024], BF16)
    oT = P.sbuf("oT", [128, 8, S], BF16)
    tq, tk, tv, toT = P.tile(), P.tile(), P.tile(), P.tile()
    NZ, NS, NPB, NW, NWT, NWS, NPO = 3, 4, 5, 4, 2, 4, 3
    pz = [P.psum(f"pz{i}", [128, 512]) for i in range(NZ)]
    tpz = P.tiles(NZ)
    pwT = [P.psum(f"pwT{i}", [128, 1024], BF16) for i in range(NWT)]
    tpwT = P.tiles(NWT)
    po = [P.psum(f"po{i}", [128, 512]) for i in range(NPO)]
    tpo = P.tiles(NPO)
    pw = pz[0:2]
    tpw = tpz[0:2]
    ssb = [P.sbuf(f"ssb{i}", [128, 512], F32) for i in range(NS)]
    tssb = P.tiles(NS)
    pbuf = [P.sbuf(f"pbuf{i}", [128, 513], F32) for i in range(NPB)]
    tpbuf = P.tiles(NPB)
    wsb = [P.sbuf(f"wsb{i}", [128, 512], BF16) for i in range(NW)]
    twsb = P.tiles(NW)
    wTs = [P.sbuf(f"wTs{i}", [128, 512], BF16) for i in range(NWS)]
    twTs = P.tiles(NWS)
    xt = [P.sbuf(f"xt{i}", [128, D], F32) for i in range(2)]
    txt = P.tiles(2)
    outs = []
    kk = 0
    kx = 0
    gt = 0
    gpo = 0
    for s in range(n_seq):
        c0 = s * S
        for f in range(0, 8, 2):
            P.dma("sp", lambda e, f=f, c0=c0: e.dma_start(out=qsb[:, f:f + 2, :], in_=qT[f * 128:(f + 2) * 128, c0:c0 + S].rearrange("(f p) n -> p f n", p=128)), writes=[tq])
            P.dma("act", lambda e, f=f, c0=c0: e.dma_start(out=ksb[:, f:f + 2, :], in_=kT_rev[f * 128:(f + 2) * 128, c0:c0 + S].rearrange("(f p) n -> p f n", p=128)), writes=[tk])
        for b4 in range(0, NB, 4):
            P.dma("sp", lambda e, b4=b4, c0=c0: e.dma_start(out=vsb[:, b4:b4 + 4, :], in_=v_rev[c0 + b4 * 128:c0 + (b4 + 4) * 128, :].rearrange("(t p) d -> p t d", p=128)), writes=[tv])
        tasks = []
        for ft in range(8):
            for i in range(NB):
                q0 = i * 128
                kp0 = S - 128 - q0
                klen = q0 + 128
                ob = gpo % NPO
                gpo += 1
                ntile = (klen + 511) // 512
                for hp in range(2):
                    h = ft * 2 + hp
                    ps = slice(hp * 64, (hp + 1) * 64)
                    for n in range(ntile):
                        col0 = kp0 + 512 * n
                        wdt = min(512, S - col0)
                        nblk = wdt // 128
                        g = gt
                        gt += 1
                        bz, bs_, bp, bw, bwt, bws = g % NZ, g % NS, g % NPB, g % NW, g % NWT, g % NWS
                        bpp = (g - 1) % NPB

                        def ph0(bz=bz, ft=ft, ps=ps, q0=q0, col0=col0, wdt=wdt, n=n):
                            P.op("pe", lambda e: e.matmul(out=pz[bz][:, 0:wdt], lhsT=qsb[ps, ft, q0:q0 + 128], rhs=ksb[ps, ft, col0:col0 + wdt],
                                                          start=True, stop=(n != 0)), reads=[tq, tk], writes=[tpz[bz]])
                            if n == 0:
                                P.op("pe", lambda e: e.matmul(out=pz[bz][:, 0:128], lhsT=ident[:], rhs=maskb[:], start=False, stop=True),
                                     reads=[tident, tmask], writes=[tpz[bz]])

                        def ph1(bz=bz, bs_=bs_, wdt=wdt):
                            P.op("act", lambda e: e.activation(out=ssb[bs_][:, 0:wdt], in_=pz[bz][:, 0:wdt], func=AF.Sigmoid, scale=-1.0),
                                 reads=[tpz[bz]], writes=[tssb[bs_]])

                        def ph2(bs_=bs_, bp=bp, bpp=bpp, wdt=wdt, n=n):
                            if n == 0:
                                P.op("act", lambda e: e.activation(out=pbuf[bp][:, 0:1], in_=onec[:, 0:1], func=AF.Copy), reads=[tz], writes=[tpbuf[bp]])
                                init = onec[:, 0:1]
                                rd = [tssb[bs_], tz, tpbuf[bp]]
                            else:
                                P.op("act", lambda e: e.activation(out=pbuf[bp][:, 0:1], in_=pbuf[bpp][:, 512:513], func=AF.Copy), reads=[tpbuf[bpp]], writes=[tpbuf[bp]])
                                init = pbuf[bpp][:, 512:513]
                                rd = [tssb[bs_], tz, tpbuf[bp], tpbuf[bpp]]
                            P.op("dve", lambda e: e.tensor_tensor_scan(out=pbuf[bp][:, 1:1 + wdt], data0=ssb[bs_][:, 0:wdt], data1=zeros[:, 0:wdt],
                                                                       initial=init, op0=ALU.mult, op1=ALU.add), reads=rd, writes=[tpbuf[bp]])

                        def ph3(bp=bp, bw=bw, wdt=wdt):
                            P.op("dve", lambda e: e.tensor_tensor(out=wsb[bw][:, 0:wdt], in0=pbuf[bp][:, 0:wdt], in1=pbuf[bp][:, 1:1 + wdt], op=ALU.subtract),
                                 reads=[tpbuf[bp]], writes=[twsb[bw]])

                        def ph4(bw=bw, bwt=bwt, nblk=nblk):
                            for jb in range(nblk):
                                P.op("pe", lambda e, jb=jb: e.transpose(out=pwT[bwt][:, jb * 128:(jb + 1) * 128], in_=wsb[bw][:, jb * 128:(jb + 1) * 128], identity=ident[:]),
                                     reads=[twsb[bw], tident], writes=[tpwT[bwt]])

                        def ph5(bwt=bwt, bws=bws, wdt=wdt, g=g):
                            if True:
                                P.op("act", lambda e: e.activation(out=wTs[bws][:, 0:wdt], in_=pwT[bwt][:, 0:wdt], func=AF.Copy), reads=[tpwT[bwt]], writes=[twTs[bws]])
                            else:
                                P.op("dve", lambda e: e.tensor_copy(out=wTs[bws][:, 0:wdt], in_=pwT[bwt][:, 0:wdt]), reads=[tpwT[bwt]], writes=[twTs[bws]])

                        def ph6(bws=bws, nblk=nblk, col0=col0, h=h, ps=ps, ob=ob, n=n, ntile=ntile, hp=hp, ft=ft, q0=q0):
                            for jb in range(nblk):
                                blk = col0 // 128 + jb
                                first = (n == 0 and jb == 0)
                                last = (n == ntile - 1 and jb == nblk - 1)
                                P.op("pe", lambda e, jb=jb, blk=blk, first=first, last=last: e.matmul(
                                    out=po[ob][ps, 0:128], lhsT=vsb[:, blk, h * 64:(h + 1) * 64], rhs=wTs[bws][:, jb * 128:(jb + 1) * 128], start=first, stop=last),
                                    reads=[tv, twTs[bws]], writes=[tpo[ob]])
                            if hp == 1 and n == ntile - 1:
                                P.op("act", lambda e: e.activation(out=oT[:, ft, q0:q0 + 128], in_=po[ob][:, 0:128], func=AF.Copy), reads=[tpo[ob]], writes=[toT])

                        tasks.append([ph0, ph1, ph2, ph3, ph4, ph5, ph6])
        run_pipeline(tasks, ATT_OFFS)
        for t in range(NB):
            r0 = c0 + t * 128
            xb = kx % 2
            kx += 1
            P.dma("sp", lambda e, xb=xb, r0=r0: e.dma_start(out=xt[xb][:], in_=h_in[r0:r0 + 128, :]), writes=[txt[xb]])
            for nb in range(2):
                b = kk % 2
                kk += 1
                for ft in range(8):
                    P.op("pe", lambda e, b=b, ft=ft, t=t, nb=nb: e.matmul(out=pw[b][:], lhsT=oT[:, ft, t * 128:(t + 1) * 128], rhs=wo[:, ft, nb * 512:(nb + 1) * 512],
                                                                          start=(ft == 0), stop=(ft == 7)), reads=[toT, two], writes=[tpw[b]])
                P.op("dve", lambda e, b=b, xb=xb, nb=nb: e.tensor_tensor(out=xt[xb][:, nb * 512:(nb + 1) * 512], in0=pw[b][:], in1=xt[xb][:, nb * 512:(nb + 1) * 512], op=ALU.add),
                     reads=[tpw[b], txt[xb]], writes=[txt[xb]])
            to = P.tile()
            P.dma("act", lambda e, xb=xb, r0=r0: e.dma_start(out=h_out[r0:r0 + 128, :], in_=xt[xb][:]), reads=[txt[xb]], writes=[to])
            outs.append(to)
    P.finish(outs)


def stage_a1(nc, h_in, uT, g_ap, win_ap, ident_ap, n_tok, prep=None):
    P = Prog(nc)
    if prep is not None:
        prep(P)
    ident, tident = load_ident(P, ident_ap)
    g_sb, tg = load_gain(P, "g", g_ap)
    win = P.sbuf("win", [128, 8, 1024], BF16)
    twin = P.tile()
    load_weight_bf16(P, win, twin, win_ap, 8, g_sb, tg)
    nt = NormT(P, ident, tident)
    pu = [P.psum(f"pu{i}", [128, 512]) for i in range(2)]
    tpu = P.tiles(2)
    ubuf = [P.sbuf(f"ubuf{i}", [128, 8, 512], BF16) for i in range(2)]
    tubuf = P.tiles(2)
    outs = []
    k = 0
    nblk = n_tok // 512
    nt.part1(h_in[0:512, :])
    for u in range(nblk):
        r0 = u * 512
        ob = u % 2
        xnT, txnT = nt.part2()
        if u + 1 < nblk:
            nt.part1(h_in[r0 + 512:r0 + 1024, :])
        for m in range(8):
            b = k % 2
            k += 1
            for c in range(8):
                P.op("pe", lambda e, c=c, b=b, m=m: e.matmul(out=pu[b][:], lhsT=win[:, c, m * 128:(m + 1) * 128], rhs=xnT[:, c, :],
                                                             start=(c == 0), stop=(c == 7)), reads=[twin, txnT], writes=[tpu[b]])
            if m % 2 == 0:
                P.op("act", lambda e, b=b, m=m, ob=ob: e.activation(out=ubuf[ob][:, m, :], in_=pu[b][:], func=AF.Copy), reads=[tpu[b]], writes=[tubuf[ob]])
            else:
                P.op("dve", lambda e, b=b, m=m, ob=ob: e.tensor_copy(out=ubuf[ob][:, m, :], in_=pu[b][:]), reads=[tpu[b]], writes=[tubuf[ob]])
        to = P.tile()
        P.dma("act", lambda e, ob=ob, r0=r0: e.dma_start(out=uT[:, r0:r0 + 512].rearrange("(m p) n -> p m n", p=128), in_=ubuf[ob][:]),
              reads=[tubuf[ob]], writes=[to])
        outs.append(to)
    P.finish(outs)


TWO_PI = 6.283185307179586
ATT_OFFS = [0, 2, 4, 6, 8, 9, 11]


def a2_prep(P, alloc, lre_ap, lim_ap, bre_ap, bim_ap, cre_ap, cim_ap, d_ap, ldt_ap, iota_ap, S):
    I32 = mybir.dt.int32
    NP = 32
    lr = alloc("lr", [128, NP], F32)
    li = alloc("li", [128, NP], F32)
    dt = alloc("dt", [128, NP], F32)
    tprm = P.tile()
    for e_ in range(2):
        ps = slice(e_ * 64, (e_ + 1) * 64)
        P.dma("sp", lambda e, e_=e_, ps=ps: e.dma_start(out=lr[ps, :], in_=lre_ap.rearrange("(k e) n -> e n k", e=2)[e_], allow_slow_non_contiguous=True), writes=[tprm])
        P.dma("sp", lambda e, e_=e_, ps=ps: e.dma_start(out=li[ps, :], in_=lim_ap.rearrange("(k e) n -> e n k", e=2)[e_], allow_slow_non_contiguous=True), writes=[tprm])
        P.dma("sp", lambda e, e_=e_, ps=ps: e.dma_start(out=dt[ps, :], in_=ldt_ap.rearrange("(k e) -> e k", e=2)[e_].partition_broadcast(64), allow_slow_non_contiguous=True), writes=[tprm])
    cst = alloc("cst", [128, 4], F32)
    tcst = P.tile()
    P.op("dve", lambda e: e.memset(cst[:, 0:1], TWO_PI / 4), writes=[tcst])
    P.op("dve", lambda e: e.memset(cst[:, 1:2], 0.0), reads=[tcst], writes=[tcst])
    sc = {}
    for nm in ["f", "fhi", "flo", "r", "t0", "t1", "t2", "t3", "sn", "cs", "ar", "ai", "qre", "qim", "nqim", "den"]:
        sc[nm] = alloc("p_" + nm, [128, NP], F32)
    fhb = alloc("p_fhb", [128, NP], BF16)
    tiq = alloc("p_ti", [128, NP], I32)
    tsc = P.tile()

    def V(fn, eng="dve", extra=()):
        P.op(eng, fn, reads=[tprm, tsc, tcst] + list(extra), writes=[tsc])

    V(lambda e: e.activation(out=sc["t0"][:], in_=dt[:], func=AF.Exp), "act")
    V(lambda e: e.activation(out=sc["t1"][:], in_=sc["t0"][:], func=AF.Ln), "act")
    V(lambda e: e.tensor_tensor(out=sc["t1"][:], in0=dt[:], in1=sc["t1"][:], op=ALU.subtract))
    V(lambda e: e.tensor_scalar(out=sc["t1"][:], in0=sc["t1"][:], scalar1=1.0, scalar2=None, op0=ALU.add))
    V(lambda e: e.tensor_tensor(out=dt[:], in0=sc["t0"][:], in1=sc["t1"][:], op=ALU.mult))
    V(lambda e: e.tensor_tensor(out=sc["t0"][:], in0=li[:], in1=dt[:], op=ALU.mult))
    V(lambda e: e.tensor_scalar(out=sc["f"][:], in0=sc["t0"][:], scalar1=1.0 / TWO_PI, scalar2=None, op0=ALU.mult))
    V(lambda e: e.tensor_copy(out=fhb[:], in_=sc["f"][:]))
    V(lambda e: e.tensor_copy(out=sc["fhi"][:], in_=fhb[:]))
    V(lambda e: e.tensor_tensor(out=sc["flo"][:], in0=sc["f"][:], in1=sc["fhi"][:], op=ALU.subtract))
    V(lambda e: e.tensor_tensor(out=sc["t1"][:], in0=lr[:], in1=dt[:], op=ALU.mult))
    V(lambda e: e.activation(out=sc["t2"][:], in_=sc["t1"][:], func=AF.Exp), "act")
    V(lambda e: e.activation(out=sc["t3"][:], in_=sc["t2"][:], func=AF.Ln), "act")
    V(lambda e: e.tensor_tensor(out=sc["t3"][:], in0=sc["t1"][:], in1=sc["t3"][:], op=ALU.subtract))
    V(lambda e: e.tensor_scalar(out=sc["t3"][:], in0=sc["t3"][:], scalar1=1.0, scalar2=None, op0=ALU.add))
    V(lambda e: e.tensor_tensor(out=sc["r"][:], in0=sc["t2"][:], in1=sc["t3"][:], op=ALU.mult))
    V(lambda e: e.tensor_copy(out=tiq[:], in_=sc["f"][:]))
    V(lambda e: e.tensor_copy(out=sc["t3"][:], in_=tiq[:]))
    V(lambda e: e.tensor_tensor(out=sc["t2"][:], in0=sc["f"][:], in1=sc["t3"][:], op=ALU.subtract))
    V(lambda e: e.activation(out=sc["sn"][:], in_=sc["t2"][:], func=AF.Sin, scale=TWO_PI), "act")
    V(lambda e: e.tensor_scalar(out=sc["t3"][:], in0=sc["t2"][:], scalar1=-1.0, scalar2=None, op0=ALU.mult))
    V(lambda e: e.tensor_tensor(out=sc["t3"][:], in0=sc["t3"][:], in1=sc["t2"][:], op=ALU.min))
    V(lambda e: e.activation(out=sc["cs"][:], in_=sc["t3"][:], func=AF.Sin, scale=TWO_PI, bias=cst[:, 0:1]), "act")
    V(lambda e: e.tensor_tensor(out=sc["ar"][:], in0=sc["r"][:], in1=sc["cs"][:], op=ALU.mult))
    V(lambda e: e.tensor_tensor(out=sc["ai"][:], in0=sc["r"][:], in1=sc["sn"][:], op=ALU.mult))
    V(lambda e: e.tensor_scalar(out=sc["ar"][:], in0=sc["ar"][:], scalar1=-1.0, scalar2=None, op0=ALU.add))
    V(lambda e: e.tensor_tensor(out=sc["t0"][:], in0=lr[:], in1=lr[:], op=ALU.mult))
    V(lambda e: e.tensor_tensor(out=sc["t1"][:], in0=li[:], in1=li[:], op=ALU.mult))
    V(lambda e: e.tensor_tensor(out=sc["den"][:], in0=sc["t0"][:], in1=sc["t1"][:], op=ALU.add))
    V(lambda e: e.reciprocal(out=sc["den"][:], in_=sc["den"][:]))
    V(lambda e: e.tensor_tensor(out=sc["t0"][:], in0=sc["ar"][:], in1=lr[:], op=ALU.mult))
    V(lambda e: e.tensor_tensor(out=sc["t1"][:], in0=sc["ai"][:], in1=li[:], op=ALU.mult))
    V(lambda e: e.tensor_tensor(out=sc["t0"][:], in0=sc["t0"][:], in1=sc["t1"][:], op=ALU.add))
    V(lambda e: e.tensor_tensor(out=sc["qre"][:], in0=sc["t0"][:], in1=sc["den"][:], op=ALU.mult))
    V(lambda e: e.tensor_tensor(out=sc["t0"][:], in0=sc["ai"][:], in1=lr[:], op=ALU.mult))
    V(lambda e: e.tensor_tensor(out=sc["t1"][:], in0=sc["ar"][:], in1=li[:], op=ALU.mult))
    V(lambda e: e.tensor_tensor(out=sc["t0"][:], in0=sc["t0"][:], in1=sc["t1"][:], op=ALU.subtract))
    V(lambda e: e.tensor_tensor(out=sc["qim"][:], in0=sc["t0"][:], in1=sc["den"][:], op=ALU.mult))
    V(lambda e: e.tensor_scalar(out=sc["nqim"][:], in0=sc["qim"][:], scalar1=-1.0, scalar2=None, op0=ALU.mult))
    craw = alloc("craw", [128, NP, 2, 16], F32)
    tcraw = P.tile()
    for e_ in range(2):
        ps = slice(e_ * 64, (e_ + 1) * 64)
        for k1 in range(NP):
            for ri, ap_ in enumerate((cre_ap, cim_ap)):
                P.dma("sp" if ri == 0 else "act", lambda e, e_=e_, ps=ps, k1=k1, ri=ri, ap_=ap_: e.dma_start(
                    out=craw[ps, k1, ri, :], in_=ap_[2 * k1 + e_].rearrange("h n -> n h"), allow_slow_non_contiguous=True),
                    writes=[tcraw])
    CT = alloc("CT", [128, NP, 3, 128], BF16)
    tCT = P.tile()
    ctmp = alloc("ctmp", [128, NP, 16], F32)
    ctmp2 = alloc("ctmp2", [128, NP, 16], F32)
    tctmp = P.tile()
    P.op("pool", lambda e: e.memset(CT[:], 0.0), writes=[tCT])

    def bq(nm):
        return sc[nm][:].unsqueeze(2).to_broadcast([128, NP, 16])

    rd = [tcraw, tsc]
    P.op("dve", lambda e: e.tensor_tensor(out=ctmp[:], in0=craw[:, :, 0, :], in1=bq("qre"), op=ALU.mult), reads=rd, writes=[tctmp])
    P.op("dve", lambda e: e.tensor_tensor(out=ctmp2[:], in0=craw[:, :, 1, :], in1=bq("qim"), op=ALU.mult), reads=rd, writes=[tctmp])
    for e_ in range(2):
        ps = slice(e_ * 64, (e_ + 1) * 64)
        P.op("dve", lambda e, e_=e_, ps=ps: e.tensor_tensor(out=CT[ps, :, 0, e_ * 16:(e_ + 1) * 16], in0=ctmp[ps], in1=ctmp2[ps], op=ALU.subtract),
             reads=[tctmp], writes=[tCT])
    P.op("dve", lambda e: e.tensor_tensor(out=ctmp[:], in0=craw[:, :, 0, :], in1=bq("nqim"), op=ALU.mult), reads=rd + [tCT], writes=[tctmp])
    P.op("dve", lambda e: e.tensor_tensor(out=ctmp2[:], in0=craw[:, :, 1, :], in1=bq("qre"), op=ALU.mult), reads=rd, writes=[tctmp])
    for e_ in range(2):
        ps = slice(e_ * 64, (e_ + 1) * 64)
        P.op("dve", lambda e, e_=e_, ps=ps: e.tensor_tensor(out=CT[ps, :, 1, e_ * 16:(e_ + 1) * 16], in0=ctmp[ps], in1=ctmp2[ps], op=ALU.subtract),
             reads=[tctmp], writes=[tCT])
    BT = alloc("BT", [32, NP, 2, 128], BF16)
    tBT = P.tile()
    P.op("pool", lambda e: e.memset(BT[:], 0.0), writes=[tBT])
    for e_ in range(2):
        for ri, ap_ in enumerate((bre_ap, bim_ap)):
            for k1 in range(NP):
                P.dma("pool", lambda e, e_=e_, ri=ri, ap_=ap_, k1=k1: e.dma_start(
                    out=BT[e_ * 16:(e_ + 1) * 16, k1, ri, e_ * 64:(e_ + 1) * 64], in_=ap_[2 * k1 + e_].rearrange("n h -> h n"),
                    allow_slow_non_contiguous=True), writes=[tBT])
    dwin = alloc("dwin", [32, NP], F32)
    tdw = P.tile()
    P.dma("sp", lambda e: e.dma_start(out=dwin[:], in_=d_ap.rearrange("(k r) -> r k", r=32), allow_slow_non_contiguous=True), writes=[tdw])
    iota = alloc("iota", [128, S], F32)
    tio = P.tile()
    P.dma("sp", lambda e: e.dma_start(out=iota[:], in_=iota_ap[:, 0:S]), writes=[tio])
    return dict(sc=sc, tsc=tsc, CT=CT, tCT=tCT, BT=BT, tBT=tBT, dwin=dwin, tdw=tdw, iota=iota, tio=tio, cst=cst, tcst=tcst)


def stage_a2(nc, uT, gT, lre_ap, lim_ap, bre_ap, bim_ap, cre_ap, cim_ap, d_ap, ldt_ap, iota_ap, n_seq, S, dbg_pairs=None, dbg=None, dbg_stop=9, ctx=None):
    P = Prog(nc)
    I32 = mybir.dt.int32
    NP = 32
    if ctx is None:
        ctx = a2_prep(P, P.sbuf, lre_ap, lim_ap, bre_ap, bim_ap, cre_ap, cim_ap, d_ap, ldt_ap, iota_ap, S)
    else:
        ctx = dict(ctx)
        for tn in ("tsc", "tCT", "tBT", "tdw", "tio", "tcst"):
            ctx[tn] = P.tile()
    sc, tsc, CT, tCT, BT, tBT = ctx["sc"], ctx["tsc"], ctx["CT"], ctx["tCT"], ctx["BT"], ctx["tBT"]
    dwin, tdw, iota, tio, cst, tcst = ctx["dwin"], ctx["tdw"], ctx["iota"], ctx["tio"], ctx["cst"], ctx["tcst"]
    HB = 512
    nq = S // HB
    snb = [P.sbuf(f"snb{i}", [128, S], BF16) for i in range(2)]
    csb = [P.sbuf(f"csb{i}", [128, S], BF16) for i in range(2)]
    rtab = [P.sbuf(f"rtab{i}", [128, HB], F32) for i in range(2)]
    ttab = P.tiles(2)
    SH = S // 2
    wk1 = P.sbuf("wk1", [128, SH], F32)
    wk2 = P.sbuf("wk2", [128, SH], F32)
    tiw = P.sbuf("tiw", [128, SH], I32)
    twk = P.tile()
    ones = P.sbuf("ones", [128, HB], F32)
    zc = P.sbuf("zc", [128, 1], F32)
    tones = P.tile()
    P.op("dve", lambda e: e.memset(ones[:], 1.0), writes=[tones])
    P.op("dve", lambda e: e.memset(zc[:], 0.0), reads=[tones], writes=[tones])
    P.op("dve", lambda e: e.tensor_scalar(out=CT[:, :, 2, :], in0=CT[:, :, 0, :], scalar1=-1.0, scalar2=None, op0=ALU.mult), reads=[tCT], writes=[tCT])
    NU = 3
    uwin = [P.sbuf(f"uwin{i}", [32, S], BF16) for i in range(NU)]
    tuw = P.tiles(NU)
    pbr = [P.psum(f"pbr{i}", [128, HB]) for i in range(2)]
    pbi = [P.psum(f"pbi{i}", [128, HB]) for i in range(2)]
    tpb = P.tiles(2)
    py = [P.psum(f"py{i}", [128, HB]) for i in range(2)]
    tpy = P.tiles(2)
    bsb = [[P.sbuf(f"bsb{i}_{j}", [128, HB], BF16) for j in range(2)] for i in range(2)]
    tbsb = P.tiles(2)
    A = [[P.sbuf(f"A{i}_{j}", [128, HB], BF16) for j in range(4)] for i in range(2)]
    tA01 = P.tiles(2)
    tA23 = P.tiles(2)
    W = [[P.sbuf(f"W{i}_{j}", [128, HB], F32) for j in range(2)] for i in range(2)]
    tW = P.tiles(2)
    NZb = 3
    Z = [[P.sbuf(f"Z{i}_{j}", [128, HB], F32) for j in range(2)] for i in range(NZb)]
    tZ = P.tiles(NZb)
    Zb = [[P.sbuf(f"Zb{i}_{j}", [128, HB], BF16) for j in range(2)] for i in range(2)]
    tZb = P.tiles(2)
    Bq = [[P.sbuf(f"B{i}_{j}", [128, HB], BF16) for j in range(4)] for i in range(2)]
    tB = P.tiles(2)
    tB2 = P.tiles(2)
    ytmp = [P.sbuf(f"ytmp{i}", [32, HB], F32) for i in range(2)]
    tyt = P.tiles(2)
    gout = [P.sbuf(f"gout{i}", [32, S], BF16) for i in range(2)]
    tgo = P.tiles(2)
    outs = []
    npairs = NP if dbg_pairs is None else dbg_pairs

    def gen_tables(k):
        tb = k % 2
        fh, fl, rk = sc["fhi"][:, k:k + 1], sc["flo"][:, k:k + 1], sc["r"][:, k:k + 1]
        for hh in range(2):
            hs = slice(hh * SH, (hh + 1) * SH)
            P.op("act", lambda e, hs=hs: e.activation(out=wk1[:], in_=iota[:, hs], func=AF.Copy, scale=fh), reads=[tio, tsc], writes=[twk])
            P.op("act", lambda e: e.activation(out=tiw[:], in_=wk1[:], func=AF.Copy), reads=[twk], writes=[twk])
            P.op("act", lambda e: e.activation(out=wk2[:], in_=tiw[:], func=AF.Copy), reads=[twk], writes=[twk])
            P.op("dve", lambda e: e.tensor_tensor(out=wk1[:], in0=wk1[:], in1=wk2[:], op=ALU.subtract), reads=[twk], writes=[twk])
            P.op("dve", lambda e, hs=hs: e.scalar_tensor_tensor(out=wk2[:], in0=iota[:, hs], scalar=fl, in1=wk1[:], op0=ALU.mult, op1=ALU.add), reads=[tio, tsc, twk], writes=[twk])
            P.op("act", lambda e: e.activation(out=tiw[:], in_=wk2[:], func=AF.Copy), reads=[twk], writes=[twk])
            P.op("act", lambda e: e.activation(out=wk1[:], in_=tiw[:], func=AF.Copy), reads=[twk], writes=[twk])
            P.op("dve", lambda e: e.tensor_tensor(out=wk2[:], in0=wk2[:], in1=wk1[:], op=ALU.subtract), reads=[twk], writes=[twk])
            P.op("act", lambda e, hs=hs: e.activation(out=snb[tb][:, hs], in_=wk2[:], func=AF.Sin, scale=TWO_PI), reads=[twk], writes=[ttab[tb]])
            P.op("act", lambda e: e.activation(out=wk1[:], in_=wk2[:], func=AF.Copy, scale=-1.0), reads=[twk], writes=[twk])
            P.op("dve", lambda e: e.tensor_tensor(out=wk1[:], in0=wk1[:], in1=wk2[:], op=ALU.min), reads=[twk], writes=[twk])
            P.op("act", lambda e, hs=hs: e.activation(out=csb[tb][:, hs], in_=wk1[:], func=AF.Sin, scale=TWO_PI, bias=cst[:, 0:1]), reads=[twk, tcst], writes=[ttab[tb]])
        P.op("dve", lambda e: e.tensor_scalar(out=rtab[tb][:], in0=ones[:], scalar1=rk, scalar2=None, op0=ALU.mult), reads=[tones, tsc, ttab[tb]], writes=[ttab[tb]])

    if npairs > 0:
        gen_tables(0)
    tasks = []
    g = 0
    for k in range(npairs):
        row0 = k * 32
        tb = k % 2
        for s in range(n_seq):
            c0 = s * S
            ub = (k * n_seq + s) % NU
            gb = (k * n_seq + s) % 2
            for qi in range(nq):
                t0 = qi * HB
                tsl = slice(t0, t0 + HB)
                b2 = g % 2
                bz = g % NZb
                bzp = (g - 1) % NZb
                g += 1

                def p0(k=k, s=s, qi=qi, ub=ub, row0=row0, c0=c0):
                    if qi == 0:
                        P.dma("sp", lambda e: e.dma_start(out=uwin[ub][:], in_=uT[row0:row0 + 32, c0:c0 + S]), writes=[tuw[ub]])
                    tpp = n_seq * nq
                    ti = s * nq + qi
                    assert tpp >= 8, "table double-buffering needs >= 8 tasks per pair"
                    if ti == 7 and k + 1 < npairs:
                        gen_tables(k + 1)

                def p1(k=k, ub=ub, b2=b2, tsl=tsl):
                    P.op("pe", lambda e: e.matmul(out=pbr[b2][:], lhsT=BT[:, k, 0, :], rhs=uwin[ub][:, tsl], start=True, stop=True),
                         reads=[tBT, tuw[ub]], writes=[tpb[b2]])
                    P.op("pe", lambda e: e.matmul(out=pbi[b2][:], lhsT=BT[:, k, 1, :], rhs=uwin[ub][:, tsl], start=True, stop=True),
                         reads=[tBT, tuw[ub]], writes=[tpb[b2]])

                def p2(b2=b2):
                    P.op("act", lambda e: e.activation(out=bsb[b2][0][:], in_=pbr[b2][:], func=AF.Copy), reads=[tpb[b2]], writes=[tbsb[b2]])
                    P.op("act", lambda e: e.activation(out=bsb[b2][1][:], in_=pbi[b2][:], func=AF.Copy), reads=[tpb[b2]], writes=[tbsb[b2]])

                def p3(b2=b2, tb=tb, tsl=tsl):
                    rd = [ttab[tb], tbsb[b2]]
                    P.op("dve", lambda e: e.tensor_tensor(out=A[b2][0][:], in0=csb[tb][:, tsl], in1=bsb[b2][0][:], op=ALU.mult), reads=rd, writes=[tA01[b2]])
                    P.op("dve", lambda e: e.tensor_tensor(out=A[b2][1][:], in0=snb[tb][:, tsl], in1=bsb[b2][1][:], op=ALU.mult), reads=rd, writes=[tA01[b2]])
                    P.op("dve", lambda e: e.tensor_tensor(out=A[b2][2][:], in0=csb[tb][:, tsl], in1=bsb[b2][1][:], op=ALU.mult), reads=rd, writes=[tA23[b2]])
                    P.op("dve", lambda e: e.tensor_tensor(out=A[b2][3][:], in0=snb[tb][:, tsl], in1=bsb[b2][0][:], op=ALU.mult), reads=rd, writes=[tA23[b2]])

                def p4(b2=b2):
                    P.op("dve", lambda e: e.tensor_tensor(out=W[b2][0][:], in0=A[b2][0][:], in1=A[b2][1][:], op=ALU.add), reads=[tA01[b2]], writes=[tW[b2]])
                    P.op("dve", lambda e: e.tensor_tensor(out=W[b2][1][:], in0=A[b2][2][:], in1=A[b2][3][:], op=ALU.subtract), reads=[tA23[b2]], writes=[tW[b2]])

                def p5(b2=b2, bz=bz, bzp=bzp, tb=tb, qi=qi):
                    for j in range(2):
                        if qi == 0:
                            init, rd = zc[:, 0:1], [tW[b2], ttab[tb], tones]
                        else:
                            init, rd = Z[bzp][j][:, HB - 1:HB], [tW[b2], ttab[tb], tZ[bzp]]
                        P.op("dve", lambda e, j=j, init=init: e.tensor_tensor_scan(out=Z[bz][j][:], data0=rtab[tb][:], data1=W[b2][j][:], initial=init,
                                                                                   op0=ALU.mult, op1=ALU.add), reads=rd, writes=[tZ[bz]])

                def p6(b2=b2, bz=bz):
                    for j in range(2):
                        P.op("act", lambda e, j=j: e.activation(out=Zb[b2][j][:], in_=Z[bz][j][:], func=AF.Copy), reads=[tZ[bz]], writes=[tZb[b2]])

                def p7(b2=b2, tb=tb, tsl=tsl):
                    rd = [ttab[tb], tZb[b2]]
                    P.op("dve", lambda e: e.tensor_tensor(out=Bq[b2][0][:], in0=csb[tb][:, tsl], in1=Zb[b2][0][:], op=ALU.mult), reads=rd, writes=[tB[b2]])
                    P.op("dve", lambda e: e.tensor_tensor(out=Bq[b2][1][:], in0=snb[tb][:, tsl], in1=Zb[b2][1][:], op=ALU.mult), reads=rd, writes=[tB[b2]])
                    P.op("dve", lambda e: e.tensor_tensor(out=Bq[b2][2][:], in0=snb[tb][:, tsl], in1=Zb[b2][0][:], op=ALU.mult), reads=rd, writes=[tB2[b2]])
                    P.op("dve", lambda e: e.tensor_tensor(out=Bq[b2][3][:], in0=csb[tb][:, tsl], in1=Zb[b2][1][:], op=ALU.mult), reads=rd, writes=[tB2[b2]])

                def p8(b2=b2, k=k):
                    for i4, ci in enumerate((0, 2, 1, 1)):
                        P.op("pe", lambda e, i4=i4, ci=ci: e.matmul(out=py[b2][:], lhsT=CT[:, k, ci, :], rhs=Bq[b2][i4][:], start=(i4 == 0), stop=(i4 == 3)),
                             reads=[tCT, tB[b2], tB2[b2]], writes=[tpy[b2]])

                def p9(b2=b2, ub=ub, gb=gb, tsl=tsl, k=k, qi=qi, row0=row0, c0=c0):
                    P.op("dve", lambda e: e.scalar_tensor_tensor(out=ytmp[b2][:], in0=uwin[ub][:, tsl], scalar=dwin[:, k:k + 1], in1=py[b2][0:32, :],
                                                                 op0=ALU.mult, op1=ALU.add), reads=[tuw[ub], tdw, tpy[b2]], writes=[tyt[b2]])
                    P.op("act", lambda e: e.activation(out=gout[gb][:, tsl], in_=ytmp[b2][:], func=AF.Gelu), reads=[tyt[b2]], writes=[tgo[gb]])
                    if qi == nq - 1:
                        to = P.tile()
                        P.dma("act", lambda e: e.dma_start(out=gT[row0:row0 + 32, c0:c0 + S], in_=gout[gb][:]), reads=[tgo[gb]], writes=[to])
                        outs.append(to)

                tasks.append([p0, p1, p2, p3, p4, p5, p6, p7, p8, p9])
    run_pipeline(tasks, [0, 1, 2, 3, 4, 5, 6, 7, 8, 9])
    if dbg is not None:
        for nm, ap_ in dbg.items():
            src = {"CT": CT, "BT": BT, "sn": snb[(npairs - 1) % 2], "cs": csb[(npairs - 1) % 2], "r": sc["r"], "qre": sc["qre"], "qim": sc["qim"], "fhi": sc["fhi"], "flo": sc["flo"]}[nm]
            tl = {"CT": tCT, "BT": tBT, "sn": ttab[(npairs - 1) % 2], "cs": ttab[(npairs - 1) % 2]}.get(nm, tsc)
            to = P.tile()
            P.dma("sp", lambda e, ap_=ap_, src=src: e.dma_start(out=ap_, in_=src[:]), reads=[tl], writes=[to])
            outs.append(to)
    P.finish(outs)


def stage_a3(nc, h_in, h_out, gT, wglu_ap, n_tok):
    P = Prog(nc)
    wg = P.sbuf("wg", [128, 8, 2048], BF16)
    twg = P.tile()
    load_weight_bf16(P, wg, twg, wglu_ap, 8)
    gb = [P.sbuf(f"gb{i}", [128, 8, 512], BF16) for i in range(2)]
    tgb = P.tiles(2)
    xt = [P.sbuf(f"xt{i}", [128, 4, D], F32) for i in range(2)]
    txt = P.tiles(2)
    pv = [P.psum(f"pv{i}", [128, 512]) for i in range(2)]
    tpv = P.tiles(2)
    pg = [P.psum(f"pg{i}", [128, 512]) for i in range(2)]
    tpg = P.tiles(2)
    sg = [P.sbuf(f"sg{i}", [128, 512], F32) for i in range(2)]
    tsg = P.tiles(2)
    outs = []
    k = 0
    for u in range(n_tok // 512):
        r0 = u * 512
        ob = u % 2
        P.dma("sp", lambda e, ob=ob, r0=r0: e.dma_start(out=gb[ob][:], in_=gT[:, r0:r0 + 512].rearrange("(c p) n -> p c n", p=128)), writes=[tgb[ob]])
        P.dma("sp", lambda e, ob=ob, r0=r0: e.dma_start(out=xt[ob][:], in_=h_in[r0:r0 + 512, :].rearrange("(t p) d -> p t d", p=128)), writes=[txt[ob]])
        for t in range(4):
            for nb in range(2):
                b = k % 2
                k += 1
                for c in range(8):
                    P.op("pe", lambda e, c=c, b=b, t=t, nb=nb, ob=ob: e.matmul(out=pv[b][:], lhsT=gb[ob][:, c, t * 128:(t + 1) * 128], rhs=wg[:, c, nb * 512:(nb + 1) * 512],
                                                                               start=(c == 0), stop=(c == 7)), reads=[tgb[ob], twg], writes=[tpv[b]])
                for c in range(8):
                    P.op("pe", lambda e, c=c, b=b, t=t, nb=nb, ob=ob: e.matmul(out=pg[b][:], lhsT=gb[ob][:, c, t * 128:(t + 1) * 128], rhs=wg[:, c, 1024 + nb * 512:1024 + (nb + 1) * 512],
                                                                               start=(c == 0), stop=(c == 7)), reads=[tgb[ob], twg], writes=[tpg[b]])
                P.op("act", lambda e, b=b: e.activation(out=sg[b][:], in_=pg[b][:], func=AF.Sigmoid), reads=[tpg[b]], writes=[tsg[b]])
                P.op("dve", lambda e, b=b: e.tensor_tensor(out=sg[b][:], in0=sg[b][:], in1=pv[b][:], op=ALU.mult), reads=[tsg[b], tpv[b]], writes=[tsg[b]])
                P.op("pool", lambda e, b=b, t=t, nb=nb, ob=ob: e.tensor_tensor(out=xt[ob][:, t, nb * 512:(nb + 1) * 512], in0=xt[ob][:, t, nb * 512:(nb + 1) * 512], in1=sg[b][:], op=ALU.add),
                     reads=[tsg[b], txt[ob]], writes=[txt[ob]])
        to = P.tile()
        P.dma("act", lambda e, ob=ob, r0=r0: e.dma_start(out=h_out[r0:r0 + 512, :].rearrange("(t p) d -> p t d", p=128), in_=xt[ob][:]), reads=[txt[ob]], writes=[to])
        outs.append(to)
    P.finish(outs)


N_CORES = 8
SEQ = 2048
N_SEQ = 4
NT = N_SEQ * SEQ

_PARAMS = [
    ("a_norm", [1, 1024]), ("a_w_in", [1, 1024, 1024]), ("a_lam_re", [1, 64, 64]), ("a_lam_im", [1, 64, 64]),
    ("a_b_re", [1, 64, 64, 16]), ("a_b_im", [1, 64, 64, 16]), ("a_c_re", [1, 64, 16, 64]), ("a_c_im", [1, 64, 16, 64]),
    ("a_d", [1, 1024]), ("a_log_dt", [1, 64]), ("a_w_glu", [1, 1024, 2048]), ("kv_norm", [1024]), ("w_kv", [1024, 2048]),
    ("k_norm", [64]), ("b_norm", [1, 1024]), ("b_w_q", [1, 1024, 1024]), ("b_q_norm", [1, 64]), ("b_w_o", [1, 1024, 1024]),
    ("ffn_norm", [2, 1024]), ("ffn_w_up", [2, 1024, 5632]), ("ffn_conv_w", [2, 3, 2816]), ("ffn_conv_b", [2, 2816]),
    ("ffn_w_down", [2, 2816, 1024]),
]


def build_program(N_SEQ=N_SEQ, SEQ=SEQ, debug=False):
    NT = N_SEQ * SEQ
    nc = bass.Bass("TRN2", target_bir_lowering=False)
    x = nc.dram_tensor("x", [NT, D], F32, kind="ExternalInput").ap()
    prm = {n: nc.dram_tensor(n, s, F32, kind="ExternalInput").ap() for n, s in _PARAMS}
    ident = nc.dram_tensor("c_ident", [128, 128], F32, kind="ExternalInput").ap()
    bones = nc.dram_tensor("c_bones", [128, 128], F32, kind="ExternalInput").ap()
    maskb = nc.dram_tensor("c_maskb", [128, 128], F32, kind="ExternalInput").ap()
    iota = nc.dram_tensor("c_iota", [128, 2048], F32, kind="ExternalInput").ap()
    out = nc.dram_tensor("out", [NT, D], F32, kind="ExternalOutput").ap()
    kd = "ExternalOutput" if debug else "Internal"
    h1 = nc.dram_tensor("h1", [NT, D], F32, kind=kd).ap()
    h2 = nc.dram_tensor("h2", [NT, D], F32, kind=kd).ap()
    h3 = nc.dram_tensor("h3", [NT, D], F32, kind=kd).ap()
    uT = nc.dram_tensor("uT", [D, NT], BF16).ap()
    gT = nc.dram_tensor("gT", [D, NT], BF16).ap()
    kT = nc.dram_tensor("kT", [D, NT], BF16).ap()
    qT = nc.dram_tensor("qT", [D, NT], BF16).ap()
    vr = nc.dram_tensor("vr", [NT, D], BF16).ap()
    a2args = (prm["a_lam_re"][0], prm["a_lam_im"][0], prm["a_b_re"][0], prm["a_b_im"][0], prm["a_c_re"][0], prm["a_c_im"][0],
              prm["a_d"][0], prm["a_log_dt"][0], iota)
    keep = contextlib.ExitStack()
    box = {}

    def prep(P):
        box["ctx"] = a2_prep(P, lambda n, sh, dt_: keep.enter_context(nc.sbuf_tensor("keep_" + n, list(sh), dt_)), *a2args, SEQ)

    stage_a1(nc, x, uT, prm["a_norm"][0], prm["a_w_in"][0], ident, NT, prep=prep)
    stage_a2(nc, uT, gT, *a2args, N_SEQ, SEQ, ctx=box["ctx"])
    keep.close()
    stage_a3(nc, x, h1, gT, prm["a_w_glu"][0], NT)
    stage_ffn(nc, h1, h2, prm["ffn_norm"][0], prm["ffn_w_up"][0], prm["ffn_conv_w"][0], prm["ffn_conv_b"][0],
              prm["ffn_w_down"][0], ident, N_SEQ, SEQ)
    stage_kvq(nc, h2, kT, qT, vr, prm["kv_norm"], prm["w_kv"], prm["k_norm"], prm["b_norm"][0], prm["b_w_q"][0], prm["b_q_norm"][0],
              ident, bones, N_SEQ, SEQ)
    stage_att(nc, h2, h3, qT, kT, vr, prm["b_w_o"][0], ident, maskb, N_SEQ, SEQ)
    stage_ffn(nc, h3, out, prm["ffn_norm"][1], prm["ffn_w_up"][1], prm["ffn_conv_w"][1], prm["ffn_conv_b"][1],
              prm["ffn_w_down"][1], ident, N_SEQ, SEQ)
    return nc


def kernel(**inputs):
    x = np.ascontiguousarray(np.asarray(inputs["x"], dtype=np.float32))
    nc = build_program()
    p = np.arange(128)
    consts = {
        "c_ident": np.eye(128, dtype=np.float32),
        "c_bones": (p[:, None] // 64 == p[None, :] // 64).astype(np.float32),
        "c_maskb": np.where(p[None, :] + p[:, None] >= 128, 0.0, -30000.0).astype(np.float32),
        "c_iota": np.ascontiguousarray(np.tile(np.arange(2048, dtype=np.float32), (128, 1))),
    }
    params = {n: np.ascontiguousarray(np.asarray(inputs[n], dtype=np.float32)).reshape(s) for n, s in _PARAMS}
    in_maps = []
    for c in range(N_CORES):
        m = {"x": x[c * N_SEQ:(c + 1) * N_SEQ].reshape(NT, D)}
        m.update(params)
        m.update(consts)
        in_maps.append(m)
    res = run_bass_kernel_spmd(nc, in_maps, core_ids=list(range(N_CORES)))
    outs = [np.asarray(r["out"], dtype=np.float32).reshape(N_SEQ, SEQ, D) for r in res.results]
    return np.concatenate(outs, axis=0)
```

```python
import contextlib
import numpy as np
import concourse.bass as bass
import concourse.mybir as mybir
from concourse.bass_utils import run_bass_kernel_spmd

F32 = mybir.dt.float32
BF16 = mybir.dt.bfloat16
AF = mybir.ActivationFunctionType
ALU = mybir.AluOpType
AX = mybir.AxisListType

ENGS = ("pe", "act", "dve", "pool", "sp")
N_DMA_SEMS = 4

D = 1024
DFF = 2816
NFT = DFF // 128
EPS = 1e-6


class T:
    __slots__ = ("name", "writes", "reads")

    def __init__(self, name="t"):
        self.name = name
        self.writes = {}
        self.reads = {}


class Prog:
    _stage = 0

    def __init__(self, nc):
        self.nc = nc
        Prog._stage += 1
        self.sid = Prog._stage
        self.stack = contextlib.ExitStack()
        self.ops = {e: [] for e in ENGS}
        self.known = {e: {} for e in ENGS}
        self.ndma = {e: 0 for e in ENGS}
        self.cnt = {e: 0 for e in ENGS}
        self.milestones = {e: set() for e in ENGS}
        self._n = 0

    def sbuf(self, name, shape, dtype):
        return self.stack.enter_context(self.nc.sbuf_tensor(f"s{self.sid}_{name}", list(shape), dtype))

    def psum(self, name, shape, dtype=F32):
        return self.stack.enter_context(self.nc.psum_tensor(f"s{self.sid}_{name}", list(shape), dtype))

    def tile(self, name=None):
        return T(name or "t")

    def tiles(self, n):
        return [T() for _ in range(n)]

    def _collect(self, eng, reads, writes):
        need = {}
        for t in reads:
            for k, v in t.writes.items():
                if need.get(k, 0) < v:
                    need[k] = v
        for t in writes:
            for d in (t.writes, t.reads):
                for k, v in d.items():
                    if need.get(k, 0) < v:
                        need[k] = v
        waits = []
        kn = self.known[eng]
        for k, v in need.items():
            if k == ("e", eng) and eng == "pe":
                continue
            if kn.get(k, 0) >= v:
                continue
            kn[k] = v
            waits.append((k, v))
            if k[0] == "e":
                self.milestones[k[1]].add(v)
        return waits

    def op(self, eng, fn, reads=(), writes=()):
        waits = self._collect(eng, reads, writes)
        self.cnt[eng] += 1
        idx = self.cnt[eng]
        key = ("e", eng)
        self.ops[eng].append(dict(kind="op", fn=fn, waits=waits, idx=idx))
        for t in reads:
            if t.reads.get(key, 0) < idx:
                t.reads[key] = idx
        for t in writes:
            t.writes = {key: idx}
            t.reads = {}

    def dma(self, eng, fn, reads=(), writes=()):
        i = self.ndma[eng]
        self.ndma[eng] += 1
        slot = i % N_DMA_SEMS
        gen = i // N_DMA_SEMS
        key = ("d", eng, slot)
        waits = self._collect(eng, reads, writes)
        kn = self.known[eng]
        if gen > 0 and kn.get(key, 0) < 16 * gen:
            kn[key] = 16 * gen
            waits.append((key, 16 * gen))
        val = 16 * (gen + 1)
        self.ops[eng].append(dict(kind="dma", fn=fn, waits=waits, key=key))
        for t in reads:
            if t.reads.get(key, 0) < val:
                t.reads[key] = val
        for t in writes:
            t.writes = {key: val}
            t.reads = {}

    def wait_all(self, eng, tiles):
        waits = self._collect(eng, tiles, ())
        self.ops[eng].append(dict(kind="wait", waits=waits))

    def finish(self, out_tiles):
        self.wait_all("sp", out_tiles)
        nc = self.nc
        with nc.cleanup_on_exit():
            sems = {}
            for e in ENGS:
                if self.milestones[e]:
                    sems[("e", e)] = nc.alloc_semaphore(f"p{self.sid}_{e}")
                for s in range(min(N_DMA_SEMS, self.ndma[e])):
                    sems[("d", e, s)] = nc.alloc_semaphore(f"d{self.sid}_{e}_{s}")
            mmap = {e: {v: i + 1 for i, v in enumerate(sorted(self.milestones[e]))} for e in ENGS}

            def replay(e):
                def body(engine):
                    for o in self.ops[e]:
                        for k, v in o["waits"]:
                            if k[0] == "e":
                                v = mmap[k[1]][v]
                            engine.wait_ge(sems[k], v)
                        if o["kind"] == "op":
                            ins = o["fn"](engine)
                            if o["idx"] in mmap[e]:
                                ins.then_inc(sems[("e", e)], 1)
                        elif o["kind"] == "dma":
                            ins = o["fn"](engine)
                            ins.then_inc(sems[o["key"]], 16)
                return body

            with nc.Block() as block:
                block.tensor(replay("pe"))
                block.scalar(replay("act"))
                block.vector(replay("dve"))
                block.gpsimd(replay("pool"))
                block.sync(replay("sp"))
            nc.all_engine_barrier()
        self.stack.close()


def run_pipeline(tasks, offsets):
    nph = len(offsets)
    for step in range(len(tasks) + max(offsets) + 1):
        for p in reversed(range(nph)):
            t = step - offsets[p]
            if 0 <= t < len(tasks) and tasks[t][p] is not None:
                tasks[t][p]()


def load_weight_bf16(P, dst, tdst, w_ap, kchunks, gain_sb=None, tgain=None, eng="dve"):
    n = w_ap.shape[-1]
    src = w_ap.rearrange("(c p) n -> p c n", p=128)
    step = max(1, 4096 // n)
    for c0 in range(0, kchunks, step):
        c1 = min(kchunks, c0 + step)
        P.dma("pool", lambda e, c0=c0, c1=c1: e.dma_start(out=dst[:, c0:c1, :], in_=src[:, c0:c1, :]), writes=[tdst])
    if gain_sb is not None:
        for c in range(kchunks):
            P.op(eng, lambda e, c=c: e.tensor_scalar(out=dst[:, c, :], in0=dst[:, c, :], scalar1=gain_sb[:, c:c + 1],
                                                      scalar2=None, op0=ALU.mult), reads=[tdst, tgain], writes=[tdst])


def load_gain(P, name, g_ap):
    g_sb = P.sbuf(name, [128, 8], F32)
    t = P.tile()
    P.dma("sp", lambda e: e.dma_start(out=g_sb[:], in_=g_ap.rearrange("(c p) -> p c", p=128), allow_slow_non_contiguous=True), writes=[t])
    return g_sb, t


class NormT:
    def __init__(self, P, ident, tident, n_xt=2, n_xnT=1, junk=None, tjunk=None):
        self.P = P
        self.ident, self.tident = ident, tident
        self.n_xt, self.n_xnT = n_xt, n_xnT
        self.xt = [P.sbuf(f"nt_xt{i}", [128, 4, D], F32) for i in range(n_xt)]
        self.txt = P.tiles(n_xt)
        if junk is None:
            self.junk = P.sbuf("nt_junk", [128, D], BF16)[:]
            self.tjunk = P.tile()
        else:
            self.junk, self.tjunk = junk, tjunk
        self.ss = P.sbuf("nt_ss", [128, 8], F32)
        self.tss = P.tile()
        self.xn = [P.sbuf(f"nt_xn{i}", [128, D], BF16) for i in range(4)]
        self.txn = P.tiles(4)
        self.pst = [P.psum(f"nt_pst{i}", [128, D], BF16) for i in range(2)]
        self.tpst = P.tiles(2)
        self.xnT = [P.sbuf(f"nt_xnT{i}", [128, 8, 512], BF16) for i in range(n_xnT)]
        self.txnT = P.tiles(n_xnT)
        self.k = 0
        self.n = 0
        self.n2 = 0
        self.kp = 0

    def part1(self, src_rows):
        P = self.P
        b = self.n % self.n_xt
        self.n += 1
        xt, txt = self.xt[b], self.txt[b]
        P.dma("sp", lambda e: e.dma_start(out=xt[:], in_=src_rows.rearrange("(t p) d -> p t d", p=128)), writes=[txt])
        ss, tss = self.ss, self.tss
        for t in range(4):
            xn, txn = self.xn[t], self.txn[t]
            self.k += 1
            col = (self.k % 4) * 2
            P.op("dve", lambda e, col=col: e.memset(ss[:, col:col + 1], 0.0), writes=[tss])
            P.op("act", lambda e, t=t, col=col: e.activation(out=self.junk, in_=xt[:, t, :], func=AF.Square,
                                                             accum_out=ss[:, col:col + 1]),
                 reads=[txt], writes=[self.tjunk, tss])
            P.op("dve", lambda e, col=col: e.tensor_scalar(out=ss[:, col + 1:col + 2], in0=ss[:, col:col + 1], scalar1=1.0 / D,
                                                           scalar2=EPS, op0=ALU.mult, op1=ALU.add), reads=[tss], writes=[tss])
            P.op("act", lambda e, col=col: e.activation(out=ss[:, col + 1:col + 2], in_=ss[:, col + 1:col + 2], func=AF.Sqrt),
                 reads=[tss], writes=[tss])
            P.op("dve", lambda e, col=col: e.reciprocal(out=ss[:, col + 1:col + 2], in_=ss[:, col + 1:col + 2]), reads=[tss], writes=[tss])
            P.op("act", lambda e, t=t, col=col, xn=xn: e.activation(out=xn[:], in_=xt[:, t, :], func=AF.Copy,
                                                                     scale=ss[:, col + 1:col + 2]),
                 reads=[txt, tss], writes=[txn])
        return xt, txt

    def part2(self):
        P = self.P
        b2 = self.n2 % self.n_xnT
        self.n2 += 1
        xnT, txnT = self.xnT[b2], self.txnT[b2]
        for t in range(4):
            xn, txn = self.xn[t], self.txn[t]
            kk = self.kp % 2
            self.kp += 1
            pst, tpst = self.pst[kk], self.tpst[kk]
            for c in range(8):
                P.op("pe", lambda e, c=c, xn=xn, pst=pst: e.transpose(out=pst[:, c * 128:(c + 1) * 128],
                                                                       in_=xn[:, c * 128:(c + 1) * 128], identity=self.ident[:]),
                     reads=[txn, self.tident], writes=[tpst])
            P.op("dve", lambda e, t=t, pst=pst, xnT=xnT: e.tensor_copy(out=xnT[:, :, t * 128:(t + 1) * 128],
                                                                       in_=pst[:].rearrange("p (c k) -> p c k", k=128)),
                 reads=[tpst], writes=[txnT])
        return xnT, txnT

    def run(self, src_rows):
        xt, txt = self.part1(src_rows)
        xnT, txnT = self.part2()
        return xt, txt, xnT, txnT


def load_ident(P, ident_ap):
    ident = P.sbuf("ident", [128, 128], BF16)
    t = P.tile()
    P.dma("pool", lambda e: e.dma_start(out=ident[:], in_=ident_ap), writes=[t])
    return ident, t


def stage_ffn(nc, h_in, h_out, g_ap, wup_ap, cw_ap, cb_ap, wdn_ap, ident_ap, n_seq, seq_len):
    P = Prog(nc)
    ident, tident = load_ident(P, ident_ap)
    g_sb, tg = load_gain(P, "g", g_ap)
    wup = P.sbuf("wup", [128, 8, 2 * DFF], BF16)
    twup = P.tile()
    load_weight_bf16(P, wup, twup, wup_ap, 8, g_sb, tg)
    wdn = P.sbuf("wdn", [128, NFT, D], BF16)
    twdn = P.tile()
    load_weight_bf16(P, wdn, twdn, wdn_ap, NFT)
    cw = P.sbuf("cw", [128, NFT, 3], F32)
    cb = P.sbuf("cb", [128, NFT], F32)
    tcw = P.tile()
    for j in range(3):
        P.dma("sp", lambda e, j=j: e.dma_start(out=cw[:, :, j], in_=cw_ap[j].rearrange("(f p) -> p f", p=128), allow_slow_non_contiguous=True), writes=[tcw])
    P.dma("sp", lambda e: e.dma_start(out=cb[:], in_=cb_ap.rearrange("(f p) -> p f", p=128), allow_slow_non_contiguous=True), writes=[tcw])
    gc = [P.sbuf(f"gc{i}", [128, 512], F32) for i in range(2)]
    tgc = P.tiles(2)
    nt = NormT(P, ident, tident, n_xt=1, n_xnT=2, junk=gc[1][:].bitcast(BF16), tjunk=tgc[1])
    halo = P.sbuf("halo", [128, NFT, 2], F32)
    thalo = P.tile()
    gbuf = [P.sbuf(f"gbuf{i}", [128, 514], F32) for i in range(2)]
    tgbuf = P.tiles(2)
    hid = P.sbuf("hid", [128, NFT, 512], BF16)
    thid = P.tile()
    pv = [P.psum(f"pv{i}", [128, 512]) for i in range(2)]
    tpv = P.tiles(2)
    pg = [P.psum(f"pg{i}", [128, 512]) for i in range(2)]
    tpg = P.tiles(2)
    po = [P.psum(f"po{i}", [128, 512]) for i in range(2)]
    tpo = P.tiles(2)
    xr = [P.sbuf(f"xr{i}", [128, D], F32) for i in range(1)] * 2
    txr = P.tiles(1) * 2
    outs = []
    nsup = seq_len // 512
    nblk = n_seq * nsup
    cnt = {"k": 0, "ko": 0, "kx": 0}
    xn_of = {}

    def phA1(u):
        r0 = u * 512
        nt.part1(h_in[r0:r0 + 512, :])

    def phA2(u):
        xn_of[u] = nt.part2()

    def phB(u):
        xnT, txnT = xn_of[u]
        if u % nsup == 0:
            P.op("pool", lambda e: e.memset(halo[:], 0.0), writes=[thalo])
        for ft in range(NFT):
            if ft == 2 and u + 1 < nblk:
                phA1(u + 1)
            b = cnt["k"] % 2
            cnt["k"] += 1
            for c in range(8):
                P.op("pe", lambda e, c=c, b=b, ft=ft: e.matmul(out=pv[b][:], lhsT=wup[:, c, ft * 128:(ft + 1) * 128], rhs=xnT[:, c, :],
                                                               start=(c == 0), stop=(c == 7)), reads=[twup, txnT], writes=[tpv[b]])
            for c in range(8):
                P.op("pe", lambda e, c=c, b=b, ft=ft: e.matmul(out=pg[b][:], lhsT=wup[:, c, DFF + ft * 128:DFF + (ft + 1) * 128], rhs=xnT[:, c, :],
                                                               start=(c == 0), stop=(c == 7)), reads=[twup, txnT], writes=[tpg[b]])
            gb, tgb, g2, tg2 = gbuf[b], tgbuf[b], gc[b], tgc[b]
            P.op("pool", lambda e, gb=gb, ft=ft: e.tensor_copy(out=gb[:, 0:2], in_=halo[:, ft, :]), reads=[thalo], writes=[tgb])
            P.op("act", lambda e, gb=gb, b=b: e.activation(out=gb[:, 2:514], in_=pg[b][:], func=AF.Copy), reads=[tpg[b]], writes=[tgb])
            P.op("pool", lambda e, gb=gb, ft=ft: e.tensor_copy(out=halo[:, ft, :], in_=gb[:, 512:514]), reads=[tgb], writes=[thalo])
            P.op("act", lambda e, gb=gb, g2=g2, ft=ft: e.activation(out=g2[:], in_=gb[:, 2:514], func=AF.Identity,
                                                                    scale=cw[:, ft, 2:3], bias=cb[:, ft:ft + 1]),
                 reads=[tgb, tcw], writes=[tg2])
            P.op("dve", lambda e, gb=gb, g2=g2, ft=ft: e.scalar_tensor_tensor(out=g2[:], in0=gb[:, 1:513], scalar=cw[:, ft, 1:2], in1=g2[:],
                                                                              op0=ALU.mult, op1=ALU.add), reads=[tgb, tcw, tg2], writes=[tg2])
            P.op("dve", lambda e, gb=gb, g2=g2, ft=ft: e.scalar_tensor_tensor(out=g2[:], in0=gb[:, 0:512], scalar=cw[:, ft, 0:1], in1=g2[:],
                                                                              op0=ALU.mult, op1=ALU.add), reads=[tgb, tcw, tg2], writes=[tg2])
            P.op("act", lambda e, g2=g2: e.activation(out=g2[:], in_=g2[:], func=AF.Silu), reads=[tg2], writes=[tg2])
            P.op("dve", lambda e, g2=g2, b=b, ft=ft: e.tensor_tensor(out=hid[:, ft, :], in0=g2[:], in1=pv[b][:], op=ALU.mult),
                 reads=[tg2, tpv[b]], writes=[thid])

    def phC(u):
        r0 = u * 512
        for t in range(4):
            xb = cnt["kx"] % 2
            cnt["kx"] += 1
            rr = r0 + t * 128
            P.dma("sp", lambda e, xb=xb, rr=rr: e.dma_start(out=xr[xb][:], in_=h_in[rr:rr + 128, :]), writes=[txr[xb]])
            for nb in range(2):
                b = cnt["ko"] % 2
                cnt["ko"] += 1
                for ft in range(NFT):
                    P.op("pe", lambda e, ft=ft, t=t, nb=nb, b=b: e.matmul(out=po[b][:], lhsT=hid[:, ft, t * 128:(t + 1) * 128],
                                                                          rhs=wdn[:, ft, nb * 512:(nb + 1) * 512],
                                                                          start=(ft == 0), stop=(ft == NFT - 1)),
                         reads=[thid, twdn], writes=[tpo[b]])
                P.op("dve", lambda e, nb=nb, b=b, xb=xb: e.tensor_tensor(out=xr[xb][:, nb * 512:(nb + 1) * 512], in0=po[b][:],
                                                                         in1=xr[xb][:, nb * 512:(nb + 1) * 512], op=ALU.add),
                     reads=[tpo[b], txr[xb]], writes=[txr[xb]])
            to = P.tile()
            P.dma("act", lambda e, xb=xb, rr=rr: e.dma_start(out=h_out[rr:rr + 128, :], in_=xr[xb][:]), reads=[txr[xb]], writes=[to])
            outs.append(to)

    phA1(0)
    phA2(0)
    for u in range(nblk):
        phB(u)
        if u + 1 < nblk:
            phA2(u + 1)
        phC(u)
    P.finish(outs)


def stage_kvq(nc, h_in, kT_rev, qT, v_rev, kvn_ap, wkv_ap, kn_ap, bn_ap, wq_ap, qn_ap, ident_ap, bones_ap, n_seq, S):
    P = Prog(nc)
    ident, tident = load_ident(P, ident_ap)
    bones = P.sbuf("bones", [128, 128], BF16)
    tbones = P.tile()
    P.dma("pool", lambda e: e.dma_start(out=bones[:], in_=bones_ap), writes=[tbones])
    gkv, tgkv = load_gain(P, "gkv", kvn_ap)
    gb, tgb = load_gain(P, "gb", bn_ap)
    wkv = P.sbuf("wkv", [128, 8, 2048], BF16)
    twkv = P.tile()
    load_weight_bf16(P, wkv, twkv, wkv_ap, 8, gkv, tgkv)
    wq = P.sbuf("wq", [128, 8, 1024], BF16)
    twq = P.tile()
    load_weight_bf16(P, wq, twq, wq_ap, 8, gb, tgb)
    hg = P.sbuf("hg", [128, 4], F32)
    thg = P.tile()
    for hf in range(2):
        P.dma("sp", lambda e, hf=hf: e.dma_start(out=hg[hf * 64:(hf + 1) * 64, 0:1], in_=kn_ap.rearrange("(p o) -> p o", o=1)), writes=[thg])
        P.dma("sp", lambda e, hf=hf: e.dma_start(out=hg[hf * 64:(hf + 1) * 64, 1:2], in_=qn_ap.rearrange("(p o) -> p o", o=1)), writes=[thg])
    P.op("dve", lambda e: e.tensor_scalar(out=hg[:, 1:2], in0=hg[:, 1:2], scalar1=0.125, scalar2=None, op0=ALU.mult), reads=[thg], writes=[thg])
    P.op("dve", lambda e: e.memset(hg[:, 2:3], EPS), reads=[thg], writes=[thg])
    nt = NormT(P, ident, tident)
    pk = [P.psum(f"pk{i}", [128, 512]) for i in range(2)]
    tpk = P.tiles(2)
    pss = [P.psum(f"pss{i}", [128, 512]) for i in range(2)]
    tpss = P.tiles(2)
    sq = [P.sbuf(f"sq{i}", [128, 512], BF16) for i in range(2)]
    tsq = P.tiles(2)
    rs = [P.sbuf(f"rs{i}", [128, 512], F32) for i in range(2)]
    trs = P.tiles(2)
    kbuf = [P.sbuf(f"kbuf{i}", [128, 8, 512], BF16) for i in range(2)]
    tkbuf = P.tiles(2)
    qbuf = [P.sbuf(f"qbuf{i}", [128, 8, 512], BF16) for i in range(2)]
    tqbuf = P.tiles(2)
    vbuf = [P.sbuf(f"vbuf{i}", [128, 4, 1024], BF16) for i in range(2)]
    tvbuf = P.tiles(2)
    xrev = P.sbuf("xrev", [128, 8, 512], BF16)
    txrev = P.tile()
    outs = []
    nsup = S // 512
    k = 0
    for s in range(n_seq):
        for u in range(nsup):
            r0 = s * S + u * 512
            rr = s * S + S - 512 * (u + 1)
            ob = (s * nsup + u) % 2
            if s == 0 and u == 0:
                nt.part1(h_in[r0:r0 + 512, :])
            xnT, txnT = nt.part2()
            if r0 + 512 < n_seq * S:
                nt.part1(h_in[r0 + 512:r0 + 1024, :])
            for which in range(2):
                for ft in range(8):
                    b = k % 2
                    k += 1
                    for c in range(8):
                        if which == 0:
                            P.op("pe", lambda e, c=c, b=b, ft=ft: e.matmul(out=pk[b][:], lhsT=wkv[:, c, ft * 128:(ft + 1) * 128], rhs=xnT[:, c, :],
                                                                           start=(c == 0), stop=(c == 7)), reads=[twkv, txnT], writes=[tpk[b]])
                        else:
                            P.op("pe", lambda e, c=c, b=b, ft=ft: e.matmul(out=pk[b][:], lhsT=wq[:, c, ft * 128:(ft + 1) * 128], rhs=xnT[:, c, :],
                                                                           start=(c == 0), stop=(c == 7)), reads=[twq, txnT], writes=[tpk[b]])
                    P.op("act", lambda e, b=b: e.activation(out=sq[b][:], in_=pk[b][:], func=AF.Square), reads=[tpk[b]], writes=[tsq[b]])
                    P.op("pe", lambda e, b=b: e.matmul(out=pss[b][:], lhsT=bones[:], rhs=sq[b][:], start=True, stop=True),
                         reads=[tbones, tsq[b]], writes=[tpss[b]])
                    P.op("act", lambda e, b=b: e.activation(out=rs[b][:], in_=pss[b][:], func=AF.Ln, scale=1.0 / 64, bias=hg[:, 2:3]),
                         reads=[tpss[b], thg], writes=[trs[b]])
                    P.op("act", lambda e, b=b: e.activation(out=rs[b][:], in_=rs[b][:], func=AF.Exp, scale=-0.5), reads=[trs[b]], writes=[trs[b]])
                    if which == 0:
                        P.op("dve", lambda e, b=b, ft=ft, ob=ob: e.scalar_tensor_tensor(out=kbuf[ob][:, ft, ::-1], in0=pk[b][:], scalar=hg[:, 0:1], in1=rs[b][:],
                                                                                        op0=ALU.mult, op1=ALU.mult),
                             reads=[tpk[b], thg, trs[b]], writes=[tkbuf[ob]])
                    else:
                        P.op("dve", lambda e, b=b, ft=ft, ob=ob: e.scalar_tensor_tensor(out=qbuf[ob][:, ft, :], in0=pk[b][:], scalar=hg[:, 1:2], in1=rs[b][:],
                                                                                        op0=ALU.mult, op1=ALU.mult),
                             reads=[tpk[b], thg, trs[b]], writes=[tqbuf[ob]])
            P.op("dve", lambda e, xnT=xnT: e.tensor_copy(out=xrev[:, :, ::-1], in_=xnT[:]), reads=[txnT], writes=[txrev])
            for t in range(4):
                for nb in range(2):
                    b = k % 2
                    k += 1
                    for c in range(8):
                        P.op("pe", lambda e, c=c, b=b, t=t, nb=nb: e.matmul(out=pk[b][:], lhsT=xrev[:, c, t * 128:(t + 1) * 128],
                                                                            rhs=wkv[:, c, 1024 + nb * 512:1024 + (nb + 1) * 512],
                                                                            start=(c == 0), stop=(c == 7)), reads=[twkv, txrev], writes=[tpk[b]])
                    P.op("act", lambda e, b=b, t=t, nb=nb, ob=ob: e.activation(out=vbuf[ob][:, t, nb * 512:(nb + 1) * 512], in_=pk[b][:], func=AF.Copy),
                         reads=[tpk[b]], writes=[tvbuf[ob]])
            t1, t2, t3 = P.tile(), P.tile(), P.tile()
            P.dma("act", lambda e, ob=ob, rr=rr: e.dma_start(out=kT_rev[:, rr:rr + 512].rearrange("(f p) n -> p f n", p=128), in_=kbuf[ob][:]),
                  reads=[tkbuf[ob]], writes=[t1])
            P.dma("act", lambda e, ob=ob, r0=r0: e.dma_start(out=qT[:, r0:r0 + 512].rearrange("(f p) n -> p f n", p=128), in_=qbuf[ob][:]),
                  reads=[tqbuf[ob]], writes=[t2])
            P.dma("act", lambda e, ob=ob, rr=rr: e.dma_start(out=v_rev[rr:rr + 512, :].rearrange("(t p) d -> p t d", p=128), in_=vbuf[ob][:]),
                  reads=[tvbuf[ob]], writes=[t3])
            outs += [t1, t2, t3]
    P.finish(outs)


def stage_att(nc, h_in, h_out, qT, kT_rev, v_rev, wo_ap, ident_ap, maskb_ap, n_seq, S):
    P = Prog(nc)
    ident, tident = load_ident(P, ident_ap)
    maskb = P.sbuf("maskb", [128, 128], BF16)
    tmask = P.tile()
    P.dma("pool", lambda e: e.dma_start(out=maskb[:], in_=maskb_ap), writes=[tmask])
    wo = P.sbuf("wo", [128, 8, 1024], BF16)
    two = P.tile()
    load_weight_bf16(P, wo, two, wo_ap, 8)
    zeros = P.sbuf("zeros", [128, 512], F32)
    onec = P.sbuf("onec", [128, 1], F32)
    tz = P.tile()
    P.op("pool", lambda e: e.memset(zeros[:], 0.0), writes=[tz])
    P.op("pool", lambda e: e.memset(onec[:], 1.0), reads=[tz], writes=[tz])
    NB = S // 128
    qsb = P.sbuf("qsb", [128, 8, S], BF16)
    ksb = P.sbuf("ksb", [128, 8, S], BF16)
    vsb = P.sbuf("vsb", [128, NB, 1024], BF16)
    oT = P.sbuf("oT", [128, 8, S], BF16)
    tq, tk, tv, toT = P.tile(), P.tile(), P.tile(), P.tile()
    NZ, NS, NPB, NW, NWT, NWS, NPO = 3, 4, 5, 4, 2, 4, 3
    pz = [P.psum(f"pz{i}", [128, 512]) for i in range(NZ)]
    tpz = P.tiles(NZ)
    pwT = [P.psum(f"pwT{i}", [128, 1024], BF16) for i in range(NWT)]
    tpwT = P.tiles(NWT)
    po = [P.psum(f"po{i}", [128, 512]) for i in range(NPO)]
    tpo = P.tiles(NPO)
    pw = pz[0:2]
    tpw = tpz[0:2]
    ssb = [P.sbuf(f"ssb{i}", [128, 512], F32) for i in range(NS)]
    tssb = P.tiles(NS)
    pbuf = [P.sbuf(f"pbuf{i}", [128, 513], F32) for i in range(NPB)]
    tpbuf = P.tiles(NPB)
    wsb = [P.sbuf(f"wsb{i}", [128, 512], BF16) for i in range(NW)]
    twsb = P.tiles(NW)
    wTs = [P.sbuf(f"wTs{i}", [128, 512], BF16) for i in range(NWS)]
    twTs = P.tiles(NWS)
    xt = [P.sbuf(f"xt{i}", [128, D], F32) for i in range(2)]
    txt = P.tiles(2)
    outs = []
    kk = 0
    kx = 0
    gt = 0
    gpo = 0
    for s in range(n_seq):
        c0 = s * S
        for f in range(0, 8, 2):
            P.dma("sp", lambda e, f=f, c0=c0: e.dma_start(out=qsb[:, f:f + 2, :], in_=qT[f * 128:(f + 2) * 128, c0:c0 + S].rearrange("(f p) n -> p f n", p=128)), writes=[tq])
            P.dma("act", lambda e, f=f, c0=c0: e.dma_start(out=ksb[:, f:f + 2, :], in_=kT_rev[f * 128:(f + 2) * 128, c0:c0 + S].rearrange("(f p) n -> p f n", p=128)), writes=[tk])
        for b4 in range(0, NB, 4):
            P.dma("sp", lambda e, b4=b4, c0=c0: e.dma_start(out=vsb[:, b4:b4 + 4, :], in_=v_rev[c0 + b4 * 128:c0 + (b4 + 4) * 128, :].rearrange("(t p) d -> p t d", p=128)), writes=[tv])
        tasks = []
        for ft in range(8):
            for i in range(NB):
                q0 = i * 128
                kp0 = S - 128 - q0
                klen = q0 + 128
                ob = gpo % NPO
                gpo += 1
                ntile = (klen + 511) // 512
                for hp in range(2):
                    h = ft * 2 + hp
                    ps = slice(hp * 64, (hp + 1) * 64)
                    for n in range(ntile):
                        col0 = kp0 + 512 * n
                        wdt = min(512, S - col0)
                        nblk = wdt // 128
                        g = gt
                        gt += 1
                        bz, bs_, bp, bw, bwt, bws = g % NZ, g % NS, g % NPB, g % NW, g % NWT, g % NWS
                        bpp = (g - 1) % NPB

                        def ph0(bz=bz, ft=ft, ps=ps, q0=q0, col0=col0, wdt=wdt, n=n):
                            P.op("pe", lambda e: e.matmul(out=pz[bz][:, 0:wdt], lhsT=qsb[ps, ft, q0:q0 + 128], rhs=ksb[ps, ft, col0:col0 + wdt],
                                                          start=True, stop=(n != 0)), reads=[tq, tk], writes=[tpz[bz]])
                            if n == 0:
                                P.op("pe", lambda e: e.matmul(out=pz[bz][:, 0:128], lhsT=ident[:], rhs=maskb[:], start=False, stop=True),
                                     reads=[tident, tmask], writes=[tpz[bz]])

                        def ph1(bz=bz, bs_=bs_, wdt=wdt):
                            P.op("act", lambda e: e.activation(out=ssb[bs_][:, 0:wdt], in_=pz[bz][:, 0:wdt], func=AF.Sigmoid, scale=-1.0),
                                 reads=[tpz[bz]], writes=[tssb[bs_]])

                        def ph2(bs_=bs_, bp=bp, bpp=bpp, wdt=wdt, n=n):
                            if n == 0:
                                P.op("act", lambda e: e.activation(out=pbuf[bp][:, 0:1], in_=onec[:, 0:1], func=AF.Copy), reads=[tz], writes=[tpbuf[bp]])
                                init = onec[:, 0:1]
                                rd = [tssb[bs_], tz, tpbuf[bp]]
                            else:
                                P.op("act", lambda e: e.activation(out=pbuf[bp][:, 0:1], in_=pbuf[bpp][:, 512:513], func=AF.Copy), reads=[tpbuf[bpp]], writes=[tpbuf[bp]])
                                init = pbuf[bpp][:, 512:513]
                                rd = [tssb[bs_], tz, tpbuf[bp], tpbuf[bpp]]
                            P.op("dve", lambda e: e.tensor_tensor_scan(out=pbuf[bp][:, 1:1 + wdt], data0=ssb[bs_][:, 0:wdt], data1=zeros[:, 0:wdt],
                                                                       initial=init, op0=ALU.mult, op1=ALU.add), reads=rd, writes=[tpbuf[bp]])

                        def ph3(bp=bp, bw=bw, wdt=wdt):
                            P.op("dve", lambda e: e.tensor_tensor(out=wsb[bw][:, 0:wdt], in0=pbuf[bp][:, 0:wdt], in1=pbuf[bp][:, 1:1 + wdt], op=ALU.subtract),
                                 reads=[tpbuf[bp]], writes=[twsb[bw]])

                        def ph4(bw=bw, bwt=bwt, nblk=nblk):
                            for jb in range(nblk):
                                P.op("pe", lambda e, jb=jb: e.transpose(out=pwT[bwt][:, jb * 128:(jb + 1) * 128], in_=wsb[bw][:, jb * 128:(jb + 1) * 128], identity=ident[:]),
                                     reads=[twsb[bw], tident], writes=[tpwT[bwt]])

                        def ph5(bwt=bwt, bws=bws, wdt=wdt, g=g):
                            if True:
                                P.op("act", lambda e: e.activation(out=wTs[bws][:, 0:wdt], in_=pwT[bwt][:, 0:wdt], func=AF.Copy), reads=[tpwT[bwt]], writes=[twTs[bws]])
                            else:
                                P.op("dve", lambda e: e.tensor_copy(out=wTs[bws][:, 0:wdt], in_=pwT[bwt][:, 0:wdt]), reads=[tpwT[bwt]], writes=[twTs[bws]])

                        def ph6(bws=bws, nblk=nblk, col0=col0, h=h, ps=ps, ob=ob, n=n, ntile=ntile, hp=hp, ft=ft, q0=q0):
                            for jb in range(nblk):
                                blk = col0 // 128 + jb
                                first = (n == 0 and jb == 0)
                                last = (n == ntile - 1 and jb == nblk - 1)
                                P.op("pe", lambda e, jb=jb, blk=blk, first=first, last=last: e.matmul(
                                    out=po[ob][ps, 0:128], lhsT=vsb[:, blk, h * 64:(h + 1) * 64], rhs=wTs[bws][:, jb * 128:(jb + 1) * 128], start=first, stop=last),
                                    reads=[tv, twTs[bws]], writes=[tpo[ob]])
                            if hp == 1 and n == ntile - 1:
                                P.op("act", lambda e: e.activation(out=oT[:, ft, q0:q0 + 128], in_=po[ob][:, 0:128], func=AF.Copy), reads=[tpo[ob]], writes=[toT])

                        tasks.append([ph0, ph1, ph2, ph3, ph4, ph5, ph6])
        run_pipeline(tasks, ATT_OFFS)
        for t in range(NB):
            r0 = c0 + t * 128
            xb = kx % 2
            kx += 1
            P.dma("sp", lambda e, xb=xb, r0=r0: e.dma_start(out=xt[xb][:], in_=h_in[r0:r0 + 128, :]), writes=[txt[xb]])
            for nb in range(2):
                b = kk % 2
                kk += 1
                for ft in range(8):
                    P.op("pe", lambda e, b=b, ft=ft, t=t, nb=nb: e.matmul(out=pw[b][:], lhsT=oT[:, ft, t * 128:(t + 1) * 128], rhs=wo[:, ft, nb * 512:(nb + 1) * 512],
                                                                          start=(ft == 0), stop=(ft == 7)), reads=[toT, two], writes=[tpw[b]])
                P.op("dve", lambda e, b=b, xb=xb, nb=nb: e.tensor_tensor(out=xt[xb][:, nb * 512:(nb + 1) * 512], in0=pw[b][:], in1=xt[xb][:, nb * 512:(nb + 1) * 512], op=ALU.add),
                     reads=[tpw[b], txt[xb]], writes=[txt[xb]])
            to = P.tile()
            P.dma("act", lambda e, xb=xb, r0=r0: e.dma_start(out=h_out[r0:r0 + 128, :], in_=xt[xb][:]), reads=[txt[xb]], writes=[to])
            outs.append(to)
    P.finish(outs)


def stage_a1(nc, h_in, uT, g_ap, win_ap, ident_ap, n_tok, prep=None):
    P = Prog(nc)
    if prep is not None:
        prep(P)
    ident, tident = load_ident(P, ident_ap)
    g_sb, tg = load_gain(P, "g", g_ap)
    win = P.sbuf("win", [128, 8, 1024], BF16)
    twin = P.tile()
    load_weight_bf16(P, win, twin, win_ap, 8, g_sb, tg)
    nt = NormT(P, ident, tident)
    pu = [P.psum(f"pu{i}", [128, 512]) for i in range(2)]
    tpu = P.tiles(2)
    ubuf = [P.sbuf(f"ubuf{i}", [128, 8, 512], BF16) for i in range(2)]
    tubuf = P.tiles(2)
    outs = []
    k = 0
    nblk = n_tok // 512
    nt.part1(h_in[0:512, :])
    for u in range(nblk):
        r0 = u * 512
        ob = u % 2
        xnT, txnT = nt.part2()
        if u + 1 < nblk:
            nt.part1(h_in[r0 + 512:r0 + 1024, :])
        for m in range(8):
            b = k % 2
            k += 1
            for c in range(8):
                P.op("pe", lambda e, c=c, b=b, m=m: e.matmul(out=pu[b][:], lhsT=win[:, c, m * 128:(m + 1) * 128], rhs=xnT[:, c, :],
                                                             start=(c == 0), stop=(c == 7)), reads=[twin, txnT], writes=[tpu[b]])
            if m % 2 == 0:
                P.op("act", lambda e, b=b, m=m, ob=ob: e.activation(out=ubuf[ob][:, m, :], in_=pu[b][:], func=AF.Copy), reads=[tpu[b]], writes=[tubuf[ob]])
            else:
                P.op("dve", lambda e, b=b, m=m, ob=ob: e.tensor_copy(out=ubuf[ob][:, m, :], in_=pu[b][:]), reads=[tpu[b]], writes=[tubuf[ob]])
        to = P.tile()
        P.dma("act", lambda e, ob=ob, r0=r0: e.dma_start(out=uT[:, r0:r0 + 512].rearrange("(m p) n -> p m n", p=128), in_=ubuf[ob][:]),
              reads=[tubuf[ob]], writes=[to])
        outs.append(to)
    P.finish(outs)


TWO_PI = 6.283185307179586
ATT_OFFS = [0, 2, 4, 6, 8, 9, 11]


def a2_prep(P, alloc, lre_ap, lim_ap, bre_ap, bim_ap, cre_ap, cim_ap, d_ap, ldt_ap, iota_ap, S):
    I32 = mybir.dt.int32
    NP = 32
    lr = alloc("lr", [128, NP], F32)
    li = alloc("li", [128, NP], F32)
    dt = alloc("dt", [128, NP], F32)
    tprm = P.tile()
    for e_ in range(2):
        ps = slice(e_ * 64, (e_ + 1) * 64)
        P.dma("sp", lambda e, e_=e_, ps=ps: e.dma_start(out=lr[ps, :], in_=lre_ap.rearrange("(k e) n -> e n k", e=2)[e_], allow_slow_non_contiguous=True), writes=[tprm])
        P.dma("sp", lambda e, e_=e_, ps=ps: e.dma_start(out=li[ps, :], in_=lim_ap.rearrange("(k e) n -> e n k", e=2)[e_], allow_slow_non_contiguous=True), writes=[tprm])
        P.dma("sp", lambda e, e_=e_, ps=ps: e.dma_start(out=dt[ps, :], in_=ldt_ap.rearrange("(k e) -> e k", e=2)[e_].partition_broadcast(64), allow_slow_non_contiguous=True), writes=[tprm])
    cst = alloc("cst", [128, 4], F32)
    tcst = P.tile()
    P.op("dve", lambda e: e.memset(cst[:, 0:1], TWO_PI / 4), writes=[tcst])
    P.op("dve", lambda e: e.memset(cst[:, 1:2], 0.0), reads=[tcst], writes=[tcst])
    sc = {}
    for nm in ["f", "fhi", "flo", "r", "t0", "t1", "t2", "t3", "sn", "cs", "ar", "ai", "qre", "qim", "nqim", "den"]:
        sc[nm] = alloc("p_" + nm, [128, NP], F32)
    fhb = alloc("p_fhb", [128, NP], BF16)
    tiq = alloc("p_ti", [128, NP], I32)
    tsc = P.tile()

    def V(fn, eng="dve", extra=()):
        P.op(eng, fn, reads=[tprm, tsc, tcst] + list(extra), writes=[tsc])

    V(lambda e: e.activation(out=sc["t0"][:], in_=dt[:], func=AF.Exp), "act")
    V(lambda e: e.activation(out=sc["t1"][:], in_=sc["t0"][:], func=AF.Ln), "act")
    V(lambda e: e.tensor_tensor(out=sc["t1"][:], in0=dt[:], in1=sc["t1"][:], op=ALU.subtract))
    V(lambda e: e.tensor_scalar(out=sc["t1"][:], in0=sc["t1"][:], scalar1=1.0, scalar2=None, op0=ALU.add))
    V(lambda e: e.tensor_tensor(out=dt[:], in0=sc["t0"][:], in1=sc["t1"][:], op=ALU.mult))
    V(lambda e: e.tensor_tensor(out=sc["t0"][:], in0=li[:], in1=dt[:], op=ALU.mult))
    V(lambda e: e.tensor_scalar(out=sc["f"][:], in0=sc["t0"][:], scalar1=1.0 / TWO_PI, scalar2=None, op0=ALU.mult))
    V(lambda e: e.tensor_copy(out=fhb[:], in_=sc["f"][:]))
    V(lambda e: e.tensor_copy(out=sc["fhi"][:], in_=fhb[:]))
    V(lambda e: e.tensor_tensor(out=sc["flo"][:], in0=sc["f"][:], in1=sc["fhi"][:], op=ALU.subtract))
    V(lambda e: e.tensor_tensor(out=sc["t1"][:], in0=lr[:], in1=dt[:], op=ALU.mult))
    V(lambda e: e.activation(out=sc["t2"][:], in_=sc["t1"][:], func=AF.Exp), "act")
    V(lambda e: e.activation(out=sc["t3"][:], in_=sc["t2"][:], func=AF.Ln), "act")
    V(lambda e: e.tensor_tensor(out=sc["t3"][:], in0=sc["t1"][:], in1=sc["t3"][:], op=ALU.subtract))
    V(lambda e: e.tensor_scalar(out=sc["t3"][:], in0=sc["t3"][:], scalar1=1.0, scalar2=None, op0=ALU.add))
    V(lambda e: e.tensor_tensor(out=sc["r"][:], in0=sc["t2"][:], in1=sc["t3"][:], op=ALU.mult))
    V(lambda e: e.tensor_copy(out=tiq[:], in_=sc["f"][:]))
    V(lambda e: e.tensor_copy(out=sc["t3"][:], in_=tiq[:]))
    V(lambda e: e.tensor_tensor(out=sc["t2"][:], in0=sc["f"][:], in1=sc["t3"][:], op=ALU.subtract))
    V(lambda e: e.activation(out=sc["sn"][:], in_=sc["t2"][:], func=AF.Sin, scale=TWO_PI), "act")
    V(lambda e: e.tensor_scalar(out=sc["t3"][:], in0=sc["t2"][:], scalar1=-1.0, scalar2=None, op0=ALU.mult))
    V(lambda e: e.tensor_tensor(out=sc["t3"][:], in0=sc["t3"][:], in1=sc["t2"][:], op=ALU.min))
    V(lambda e: e.activation(out=sc["cs"][:], in_=sc["t3"][:], func=AF.Sin, scale=TWO_PI, bias=cst[:, 0:1]), "act")
    V(lambda e: e.tensor_tensor(out=sc["ar"][:], in0=sc["r"][:], in1=sc["cs"][:], op=ALU.mult))
    V(lambda e: e.tensor_tensor(out=sc["ai"][:], in0=sc["r"][:], in1=sc["sn"][:], op=ALU.mult))
    V(lambda e: e.tensor_scalar(out=sc["ar"][:], in0=sc["ar"][:], scalar1=-1.0, scalar2=None, op0=ALU.add))
    V(lambda e: e.tensor_tensor(out=sc["t0"][:], in0=lr[:], in1=lr[:], op=ALU.mult))
    V(lambda e: e.tensor_tensor(out=sc["t1"][:], in0=li[:], in1=li[:], op=ALU.mult))
    V(lambda e: e.tensor_tensor(out=sc["den"][:], in0=sc["t0"][:], in1=sc["t1"][:], op=ALU.add))
    V(lambda e: e.reciprocal(out=sc["den"][:], in_=sc["den"][:]))
    V(lambda e: e.tensor_tensor(out=sc["t0"][:], in0=sc["ar"][:], in1=lr[:], op=ALU.mult))
    V(lambda e: e.tensor_tensor(out=sc["t1"][:], in0=sc["ai"][:], in1=li[:], op=ALU.mult))
    V(lambda e: e.tensor_tensor(out=sc["t0"][:], in0=sc["t0"][:], in1=sc["t1"][:], op=ALU.add))
    V(lambda e: e.tensor_tensor(out=sc["qre"][:], in0=sc["t0"][:], in1=sc["den"][:], op=ALU.mult))
    V(lambda e: e.tensor_tensor(out=sc["t0"][:], in0=sc["ai"][:], in1=lr[:], op=ALU.mult))
    V(lambda e: e.tensor_tensor(out=sc["t1"][:], in0=sc["ar"][:], in1=li[:], op=ALU.mult))
    V(lambda e: e.tensor_tensor(out=sc["t0"][:], in0=sc["t0"][:], in1=sc["t1"][:], op=ALU.subtract))
    V(lambda e: e.tensor_tensor(out=sc["qim"][:], in0=sc["t0"][:], in1=sc["den"][:], op=ALU.mult))
    V(lambda e: e.tensor_scalar(out=sc["nqim"][:], in0=sc["qim"][:], scalar1=-1.0, scalar2=None, op0=ALU.mult))
    craw = alloc("craw", [128, NP, 2, 16], F32)
    tcraw = P.tile()
    for e_ in range(2):
        ps = slice(e_ * 64, (e_ + 1) * 64)
        for k1 in range(NP):
            for ri, ap_ in enumerate((cre_ap, cim_ap)):
                P.dma("sp" if ri == 0 else "act", lambda e, e_=e_, ps=ps, k1=k1, ri=ri, ap_=ap_: e.dma_start(
                    out=craw[ps, k1, ri, :], in_=ap_[2 * k1 + e_].rearrange("h n -> n h"), allow_slow_non_contiguous=True),
                    writes=[tcraw])
    CT = alloc("CT", [128, NP, 3, 128], BF16)
    tCT = P.tile()
    ctmp = alloc("ctmp", [128, NP, 16], F32)
    ctmp2 = alloc("ctmp2", [128, NP, 16], F32)
    tctmp = P.tile()
    P.op("pool", lambda e: e.memset(CT[:], 0.0), writes=[tCT])

    def bq(nm):
        return sc[nm][:].unsqueeze(2).to_broadcast([128, NP, 16])

    rd = [tcraw, tsc]
    P.op("dve", lambda e: e.tensor_tensor(out=ctmp[:], in0=craw[:, :, 0, :], in1=bq("qre"), op=ALU.mult), reads=rd, writes=[tctmp])
    P.op("dve", lambda e: e.tensor_tensor(out=ctmp2[:], in0=craw[:, :, 1, :], in1=bq("qim"), op=ALU.mult), reads=rd, writes=[tctmp])
    for e_ in range(2):
        ps = slice(e_ * 64, (e_ + 1) * 64)
        P.op("dve", lambda e, e_=e_, ps=ps: e.tensor_tensor(out=CT[ps, :, 0, e_ * 16:(e_ + 1) * 16], in0=ctmp[ps], in1=ctmp2[ps], op=ALU.subtract),
             reads=[tctmp], writes=[tCT])
    P.op("dve", lambda e: e.tensor_tensor(out=ctmp[:], in0=craw[:, :, 0, :], in1=bq("nqim"), op=ALU.mult), reads=rd + [tCT], writes=[tctmp])
    P.op("dve", lambda e: e.tensor_tensor(out=ctmp2[:], in0=craw[:, :, 1, :], in1=bq("qre"), op=ALU.mult), reads=rd, writes=[tctmp])
    for e_ in range(2):
        ps = slice(e_ * 64, (e_ + 1) * 64)
        P.op("dve", lambda e, e_=e_, ps=ps: e.tensor_tensor(out=CT[ps, :, 1, e_ * 16:(e_ + 1) * 16], in0=ctmp[ps], in1=ctmp2[ps], op=ALU.subtract),
             reads=[tctmp], writes=[tCT])
    BT = alloc("BT", [32, NP, 2, 128], BF16)
    tBT = P.tile()
    P.op("pool", lambda e: e.memset(BT[:], 0.0), writes=[tBT])
    for e_ in range(2):
        for ri, ap_ in enumerate((bre_ap, bim_ap)):
            for k1 in range(NP):
                P.dma("pool", lambda e, e_=e_, ri=ri, ap_=ap_, k1=k1: e.dma_start(
                    out=BT[e_ * 16:(e_ + 1) * 16, k1, ri, e_ * 64:(e_ + 1) * 64], in_=ap_[2 * k1 + e_].rearrange("n h -> h n"),
                    allow_slow_non_contiguous=True), writes=[tBT])
    dwin = alloc("dwin", [32, NP], F32)
    tdw = P.tile()
    P.dma("sp", lambda e: e.dma_start(out=dwin[:], in_=d_ap.rearrange("(k r) -> r k", r=32), allow_slow_non_contiguous=True), writes=[tdw])
    iota = alloc("iota", [128, S], F32)
    tio = P.tile()
    P.dma("sp", lambda e: e.dma_start(out=iota[:], in_=iota_ap[:, 0:S]), writes=[tio])
    return dict(sc=sc, tsc=tsc, CT=CT, tCT=tCT, BT=BT, tBT=tBT, dwin=dwin, tdw=tdw, iota=iota, tio=tio, cst=cst, tcst=tcst)


def stage_a2(nc, uT, gT, lre_ap, lim_ap, bre_ap, bim_ap, cre_ap, cim_ap, d_ap, ldt_ap, iota_ap, n_seq, S, dbg_pairs=None, dbg=None, dbg_stop=9, ctx=None, ident_ap=None):
    P = Prog(nc)
    I32 = mybir.dt.int32
    NP = 32
    if ctx is None:
        ctx = a2_prep(P, P.sbuf, lre_ap, lim_ap, bre_ap, bim_ap, cre_ap, cim_ap, d_ap, ldt_ap, iota_ap, S)
    else:
        ctx = dict(ctx)
        for tn in ("tsc", "tCT", "tBT", "tdw", "tio", "tcst"):
            ctx[tn] = P.tile()
    sc, tsc, CT, tCT, BT, tBT = ctx["sc"], ctx["tsc"], ctx["CT"], ctx["tCT"], ctx["BT"], ctx["tBT"]
    dwin, tdw, iota, tio, cst, tcst = ctx["dwin"], ctx["tdw"], ctx["iota"], ctx["tio"], ctx["cst"], ctx["tcst"]
    HB = 512
    nq = S // HB
    snb = [P.sbuf(f"snb{i}", [128, S], BF16) for i in range(2)]
    csb = [P.sbuf(f"csb{i}", [128, S], BF16) for i in range(2)]
    rtab = [P.sbuf(f"rtab{i}", [128, HB], F32) for i in range(2)]
    ttab = P.tiles(2)
    SH = S // 2
    wk1 = P.sbuf("wk1", [128, SH], F32)
    wk2 = P.sbuf("wk2", [128, SH], F32)
    tiw = P.sbuf("tiw", [128, SH], I32)
    twk = P.tile()
    ones = P.sbuf("ones", [128, HB], F32)
    zc = P.sbuf("zc", [128, 1], F32)
    tones = P.tile()
    P.op("dve", lambda e: e.memset(ones[:], 1.0), writes=[tones])
    P.op("dve", lambda e: e.memset(zc[:], 0.0), reads=[tones], writes=[tones])
    P.op("dve", lambda e: e.tensor_scalar(out=CT[:, :, 2, :], in0=CT[:, :, 0, :], scalar1=-1.0, scalar2=None, op0=ALU.mult), reads=[tCT], writes=[tCT])
    NU = 3
    uwin = [P.sbuf(f"uwin{i}", [32, S], BF16) for i in range(NU)]
    tuw = P.tiles(NU)
    pbr = [P.psum(f"pbr{i}", [128, HB]) for i in range(1)] * 2
    pbi = [P.psum(f"pbi{i}", [128, HB]) for i in range(1)] * 2
    tpb = P.tiles(1) * 2
    pW = [[P.psum(f"pW{i}_{j}", [128, HB]) for j in range(2)] for i in range(2)]
    tpW = P.tiles(2)
    idb = P.sbuf("idb", [128, 2, 128], BF16)
    tidb = P.tile()
    P.dma("pool", lambda e: e.dma_start(out=idb[:, 0, :], in_=ident_ap), writes=[tidb])
    P.op("act", lambda e: e.activation(out=idb[:, 1, :], in_=idb[:, 0, :], func=AF.Copy, scale=-1.0), reads=[tidb], writes=[tidb])
    py = [P.psum(f"py{i}", [128, HB]) for i in range(2)]
    tpy = P.tiles(2)
    bsb = [[P.sbuf(f"bsb{i}_{j}", [128, HB], BF16) for j in range(2)] for i in range(2)]
    tbsb = P.tiles(2)
    A = [[P.sbuf(f"A{i}_{j}", [128, HB], BF16) for j in range(4)] for i in range(2)]
    tA01 = P.tiles(2)
    tA23 = P.tiles(2)
    W = [[P.sbuf(f"W{i}_{j}", [128, HB], F32) for j in range(2)] for i in range(2)]
    tW = P.tiles(2)
    NZb = 3
    Z = [[P.sbuf(f"Z{i}_{j}", [128, HB], F32) for j in range(2)] for i in range(NZb)]
    tZ = P.tiles(NZb)
    Zb = [[P.sbuf(f"Zb{i}_{j}", [128, HB], BF16) for j in range(2)] for i in range(2)]
    tZb = P.tiles(2)
    Bq = [[P.sbuf(f"B{i}_{j}", [128, HB], BF16) for j in range(4)] for i in range(2)]
    tB = P.tiles(2)
    tB2 = P.tiles(2)
    ytmp = [P.sbuf(f"ytmp{i}", [32, HB], F32) for i in range(2)]
    tyt = P.tiles(2)
    gout = [P.sbuf(f"gout{i}", [32, S], BF16) for i in range(2)]
    tgo = P.tiles(2)
    outs = []
    npairs = NP if dbg_pairs is None else dbg_pairs

    def gen_tables(k):
        tb = k % 2
        fh, fl, rk = sc["fhi"][:, k:k + 1], sc["flo"][:, k:k + 1], sc["r"][:, k:k + 1]
        for hh in range(2):
            hs = slice(hh * SH, (hh + 1) * SH)
            P.op("act", lambda e, hs=hs: e.activation(out=wk1[:], in_=iota[:, hs], func=AF.Copy, scale=fh), reads=[tio, tsc], writes=[twk])
            P.op("act", lambda e: e.activation(out=tiw[:], in_=wk1[:], func=AF.Copy), reads=[twk], writes=[twk])
            P.op("act", lambda e: e.activation(out=wk2[:], in_=tiw[:], func=AF.Copy), reads=[twk], writes=[twk])
            P.op("dve", lambda e: e.tensor_tensor(out=wk1[:], in0=wk1[:], in1=wk2[:], op=ALU.subtract), reads=[twk], writes=[twk])
            P.op("dve", lambda e, hs=hs: e.scalar_tensor_tensor(out=wk2[:], in0=iota[:, hs], scalar=fl, in1=wk1[:], op0=ALU.mult, op1=ALU.add), reads=[tio, tsc, twk], writes=[twk])
            P.op("act", lambda e: e.activation(out=tiw[:], in_=wk2[:], func=AF.Copy), reads=[twk], writes=[twk])
            P.op("act", lambda e: e.activation(out=wk1[:], in_=tiw[:], func=AF.Copy), reads=[twk], writes=[twk])
            P.op("dve", lambda e: e.tensor_tensor(out=wk2[:], in0=wk2[:], in1=wk1[:], op=ALU.subtract), reads=[twk], writes=[twk])
            P.op("act", lambda e, hs=hs: e.activation(out=snb[tb][:, hs], in_=wk2[:], func=AF.Sin, scale=TWO_PI), reads=[twk], writes=[ttab[tb]])
            P.op("act", lambda e: e.activation(out=wk1[:], in_=wk2[:], func=AF.Copy, scale=-1.0), reads=[twk], writes=[twk])
            P.op("dve", lambda e: e.tensor_tensor(out=wk1[:], in0=wk1[:], in1=wk2[:], op=ALU.min), reads=[twk], writes=[twk])
            P.op("act", lambda e, hs=hs: e.activation(out=csb[tb][:, hs], in_=wk1[:], func=AF.Sin, scale=TWO_PI, bias=cst[:, 0:1]), reads=[twk, tcst], writes=[ttab[tb]])
        P.op("dve", lambda e: e.tensor_scalar(out=rtab[tb][:], in0=ones[:], scalar1=rk, scalar2=None, op0=ALU.mult), reads=[tones, tsc, ttab[tb]], writes=[ttab[tb]])

    if npairs > 0:
        gen_tables(0)
    tasks = []
    g = 0
    for k in range(npairs):
        row0 = k * 32
        tb = k % 2
        for s in range(n_seq):
            c0 = s * S
            ub = (k * n_seq + s) % NU
            gb = (k * n_seq + s) % 2
            for qi in range(nq):
                t0 = qi * HB
                tsl = slice(t0, t0 + HB)
                b2 = g % 2
                bz = g % NZb
                bzp = (g - 1) % NZb
                g += 1

                def p0(k=k, s=s, qi=qi, ub=ub, row0=row0, c0=c0):
                    if qi == 0:
                        P.dma("sp", lambda e: e.dma_start(out=uwin[ub][:], in_=uT[row0:row0 + 32, c0:c0 + S]), writes=[tuw[ub]])
                    tpp = n_seq * nq
                    ti = s * nq + qi
                    assert tpp >= 8, "table double-buffering needs >= 8 tasks per pair"
                    if ti == 7 and k + 1 < npairs:
                        gen_tables(k + 1)

                def p1(k=k, ub=ub, b2=b2, tsl=tsl):
                    P.op("pe", lambda e: e.matmul(out=pbr[b2][:], lhsT=BT[:, k, 0, :], rhs=uwin[ub][:, tsl], start=True, stop=True),
                         reads=[tBT, tuw[ub]], writes=[tpb[b2]])
                    P.op("pe", lambda e: e.matmul(out=pbi[b2][:], lhsT=BT[:, k, 1, :], rhs=uwin[ub][:, tsl], start=True, stop=True),
                         reads=[tBT, tuw[ub]], writes=[tpb[b2]])

                def p2(b2=b2):
                    P.op("act", lambda e: e.activation(out=bsb[b2][0][:], in_=pbr[b2][:], func=AF.Copy), reads=[tpb[b2]], writes=[tbsb[b2]])
                    P.op("act", lambda e: e.activation(out=bsb[b2][1][:], in_=pbi[b2][:], func=AF.Copy), reads=[tpb[b2]], writes=[tbsb[b2]])

                def p3(b2=b2, tb=tb, tsl=tsl):
                    rd = [ttab[tb], tbsb[b2]]
                    P.op("dve", lambda e: e.tensor_tensor(out=A[b2][0][:], in0=csb[tb][:, tsl], in1=bsb[b2][0][:], op=ALU.mult), reads=rd, writes=[tA01[b2]])
                    P.op("dve", lambda e: e.tensor_tensor(out=A[b2][1][:], in0=snb[tb][:, tsl], in1=bsb[b2][1][:], op=ALU.mult), reads=rd, writes=[tA01[b2]])
                    P.op("dve", lambda e: e.tensor_tensor(out=A[b2][2][:], in0=csb[tb][:, tsl], in1=bsb[b2][1][:], op=ALU.mult), reads=rd, writes=[tA23[b2]])
                    P.op("dve", lambda e: e.tensor_tensor(out=A[b2][3][:], in0=snb[tb][:, tsl], in1=bsb[b2][0][:], op=ALU.mult), reads=rd, writes=[tA23[b2]])

                def p4(b2=b2):
                    for j, (x0, x1, sg) in enumerate(((0, 1, 0), (2, 3, 1))):
                        P.op("pe", lambda e, j=j, x0=x0: e.matmul(out=pW[b2][j][:], lhsT=idb[:, 0, :], rhs=A[b2][x0][:], start=True, stop=False),
                             reads=[tidb, tA01[b2], tA23[b2]], writes=[tpW[b2]])
                        P.op("pe", lambda e, j=j, x1=x1, sg=sg: e.matmul(out=pW[b2][j][:], lhsT=idb[:, sg, :], rhs=A[b2][x1][:], start=False, stop=True),
                             reads=[tidb, tA01[b2], tA23[b2]], writes=[tpW[b2]])

                def p5(b2=b2, bz=bz, bzp=bzp, tb=tb, qi=qi):
                    for j in range(2):
                        if qi == 0:
                            init, rd = zc[:, 0:1], [tpW[b2], ttab[tb], tones]
                        else:
                            init, rd = Z[bzp][j][:, HB - 1:HB], [tpW[b2], ttab[tb], tZ[bzp]]
                        P.op("dve", lambda e, j=j, init=init: e.tensor_tensor_scan(out=Z[bz][j][:], data0=rtab[tb][:], data1=pW[b2][j][:], initial=init,
                                                                                   op0=ALU.mult, op1=ALU.add), reads=rd, writes=[tZ[bz]])

                def p6(b2=b2, bz=bz):
                    for j in range(2):
                        P.op("act", lambda e, j=j: e.activation(out=Zb[b2][j][:], in_=Z[bz][j][:], func=AF.Copy), reads=[tZ[bz]], writes=[tZb[b2]])

                def p7(b2=b2, tb=tb, tsl=tsl):
                    rd = [ttab[tb], tZb[b2]]
                    P.op("dve", lambda e: e.tensor_tensor(out=Bq[b2][0][:], in0=csb[tb][:, tsl], in1=Zb[b2][0][:], op=ALU.mult), reads=rd, writes=[tB[b2]])
                    P.op("dve", lambda e: e.tensor_tensor(out=Bq[b2][1][:], in0=snb[tb][:, tsl], in1=Zb[b2][1][:], op=ALU.mult), reads=rd, writes=[tB[b2]])
                    P.op("dve", lambda e: e.tensor_tensor(out=Bq[b2][2][:], in0=snb[tb][:, tsl], in1=Zb[b2][0][:], op=ALU.mult), reads=rd, writes=[tB2[b2]])
                    P.op("dve", lambda e: e.tensor_tensor(out=Bq[b2][3][:], in0=csb[tb][:, tsl], in1=Zb[b2][1][:], op=ALU.mult), reads=rd, writes=[tB2[b2]])

                def p8(b2=b2, k=k):
                    for i4, ci in enumerate((0, 2, 1, 1)):
                        P.op("pe", lambda e, i4=i4, ci=ci: e.matmul(out=py[b2][:], lhsT=CT[:, k, ci, :], rhs=Bq[b2][i4][:], start=(i4 == 0), stop=(i4 == 3)),
                             reads=[tCT, tB[b2], tB2[b2]], writes=[tpy[b2]])

                def p9(b2=b2, ub=ub, gb=gb, tsl=tsl, k=k, qi=qi, row0=row0, c0=c0):
                    P.op("dve", lambda e: e.scalar_tensor_tensor(out=ytmp[b2][:], in0=uwin[ub][:, tsl], scalar=dwin[:, k:k + 1], in1=py[b2][0:32, :],
                                                                 op0=ALU.mult, op1=ALU.add), reads=[tuw[ub], tdw, tpy[b2]], writes=[tyt[b2]])
                    P.op("act", lambda e: e.activation(out=gout[gb][:, tsl], in_=ytmp[b2][:], func=AF.Gelu), reads=[tyt[b2]], writes=[tgo[gb]])
                    if qi == nq - 1:
                        to = P.tile()
                        P.dma("act", lambda e: e.dma_start(out=gT[row0:row0 + 32, c0:c0 + S], in_=gout[gb][:]), reads=[tgo[gb]], writes=[to])
                        outs.append(to)

                tasks.append([p0, p1, p2, p3, p4, p5, p6, p7, p8, p9])
    run_pipeline(tasks, [0, 1, 2, 3, 4, 5, 6, 7, 8, 9])
    if dbg is not None:
        for nm, ap_ in dbg.items():
            src = {"CT": CT, "BT": BT, "sn": snb[(npairs - 1) % 2], "cs": csb[(npairs - 1) % 2], "r": sc["r"], "qre": sc["qre"], "qim": sc["qim"], "fhi": sc["fhi"], "flo": sc["flo"]}[nm]
            tl = {"CT": tCT, "BT": tBT, "sn": ttab[(npairs - 1) % 2], "cs": ttab[(npairs - 1) % 2]}.get(nm, tsc)
            to = P.tile()
            P.dma("sp", lambda e, ap_=ap_, src=src: e.dma_start(out=ap_, in_=src[:]), reads=[tl], writes=[to])
            outs.append(to)
    P.finish(outs)


def stage_a3(nc, h_in, h_out, gT, wglu_ap, n_tok):
    P = Prog(nc)
    wg = P.sbuf("wg", [128, 8, 2048], BF16)
    twg = P.tile()
    load_weight_bf16(P, wg, twg, wglu_ap, 8)
    gb = [P.sbuf(f"gb{i}", [128, 8, 512], BF16) for i in range(2)]
    tgb = P.tiles(2)
    xt = [P.sbuf(f"xt{i}", [128, 4, D], F32) for i in range(2)]
    txt = P.tiles(2)
    pv = [P.psum(f"pv{i}", [128, 512]) for i in range(2)]
    tpv = P.tiles(2)
    pg = [P.psum(f"pg{i}", [128, 512]) for i in range(2)]
    tpg = P.tiles(2)
    sg = [P.sbuf(f"sg{i}", [128, 512], F32) for i in range(2)]
    tsg = P.tiles(2)
    outs = []
    k = 0
    for u in range(n_tok // 512):
        r0 = u * 512
        ob = u % 2
        P.dma("sp", lambda e, ob=ob, r0=r0: e.dma_start(out=gb[ob][:], in_=gT[:, r0:r0 + 512].rearrange("(c p) n -> p c n", p=128)), writes=[tgb[ob]])
        P.dma("sp", lambda e, ob=ob, r0=r0: e.dma_start(out=xt[ob][:], in_=h_in[r0:r0 + 512, :].rearrange("(t p) d -> p t d", p=128)), writes=[txt[ob]])
        for t in range(4):
            for nb in range(2):
                b = k % 2
                k += 1
                for c in range(8):
                    P.op("pe", lambda e, c=c, b=b, t=t, nb=nb, ob=ob: e.matmul(out=pv[b][:], lhsT=gb[ob][:, c, t * 128:(t + 1) * 128], rhs=wg[:, c, nb * 512:(nb + 1) * 512],
                                                                               start=(c == 0), stop=(c == 7)), reads=[tgb[ob], twg], writes=[tpv[b]])
                for c in range(8):
                    P.op("pe", lambda e, c=c, b=b, t=t, nb=nb, ob=ob: e.matmul(out=pg[b][:], lhsT=gb[ob][:, c, t * 128:(t + 1) * 128], rhs=wg[:, c, 1024 + nb * 512:1024 + (nb + 1) * 512],
                                                                               start=(c == 0), stop=(c == 7)), reads=[tgb[ob], twg], writes=[tpg[b]])
                P.op("act", lambda e, b=b: e.activation(out=sg[b][:], in_=pg[b][:], func=AF.Sigmoid), reads=[tpg[b]], writes=[tsg[b]])
                P.op("dve", lambda e, b=b: e.tensor_tensor(out=sg[b][:], in0=sg[b][:], in1=pv[b][:], op=ALU.mult), reads=[tsg[b], tpv[b]], writes=[tsg[b]])
                P.op("pool", lambda e, b=b, t=t, nb=nb, ob=ob: e.tensor_tensor(out=xt[ob][:, t, nb * 512:(nb + 1) * 512], in0=xt[ob][:, t, nb * 512:(nb + 1) * 512], in1=sg[b][:], op=ALU.add),
                     reads=[tsg[b], txt[ob]], writes=[txt[ob]])
        to = P.tile()
        P.dma("act", lambda e, ob=ob, r0=r0: e.dma_start(out=h_out[r0:r0 + 512, :].rearrange("(t p) d -> p t d", p=128), in_=xt[ob][:]), reads=[txt[ob]], writes=[to])
        outs.append(to)
    P.finish(outs)


N_CORES = 8
SEQ = 2048
N_SEQ = 4
NT = N_SEQ * SEQ

_PARAMS = [
    ("a_norm", [1, 1024]), ("a_w_in", [1, 1024, 1024]), ("a_lam_re", [1, 64, 64]), ("a_lam_im", [1, 64, 64]),
    ("a_b_re", [1, 64, 64, 16]), ("a_b_im", [1, 64, 64, 16]), ("a_c_re", [1, 64, 16, 64]), ("a_c_im", [1, 64, 16, 64]),
    ("a_d", [1, 1024]), ("a_log_dt", [1, 64]), ("a_w_glu", [1, 1024, 2048]), ("kv_norm", [1024]), ("w_kv", [1024, 2048]),
    ("k_norm", [64]), ("b_norm", [1, 1024]), ("b_w_q", [1, 1024, 1024]), ("b_q_norm", [1, 64]), ("b_w_o", [1, 1024, 1024]),
    ("ffn_norm", [2, 1024]), ("ffn_w_up", [2, 1024, 5632]), ("ffn_conv_w", [2, 3, 2816]), ("ffn_conv_b", [2, 2816]),
    ("ffn_w_down", [2, 2816, 1024]),
]


def build_program(N_SEQ=N_SEQ, SEQ=SEQ, debug=False):
    NT = N_SEQ * SEQ
    nc = bass.Bass("TRN2", target_bir_lowering=False)
    x = nc.dram_tensor("x", [NT, D], F32, kind="ExternalInput").ap()
    prm = {n: nc.dram_tensor(n, s, F32, kind="ExternalInput").ap() for n, s in _PARAMS}
    ident = nc.dram_tensor("c_ident", [128, 128], F32, kind="ExternalInput").ap()
    bones = nc.dram_tensor("c_bones", [128, 128], F32, kind="ExternalInput").ap()
    maskb = nc.dram_tensor("c_maskb", [128, 128], F32, kind="ExternalInput").ap()
    iota = nc.dram_tensor("c_iota", [128, 2048], F32, kind="ExternalInput").ap()
    out = nc.dram_tensor("out", [NT, D], F32, kind="ExternalOutput").ap()
    kd = "ExternalOutput" if debug else "Internal"
    h1 = nc.dram_tensor("h1", [NT, D], F32, kind=kd).ap()
    h2 = nc.dram_tensor("h2", [NT, D], F32, kind=kd).ap()
    h3 = nc.dram_tensor("h3", [NT, D], F32, kind=kd).ap()
    uT = nc.dram_tensor("uT", [D, NT], BF16).ap()
    gT = nc.dram_tensor("gT", [D, NT], BF16).ap()
    kT = nc.dram_tensor("kT", [D, NT], BF16).ap()
    qT = nc.dram_tensor("qT", [D, NT], BF16).ap()
    vr = nc.dram_tensor("vr", [NT, D], BF16).ap()
    a2args = (prm["a_lam_re"][0], prm["a_lam_im"][0], prm["a_b_re"][0], prm["a_b_im"][0], prm["a_c_re"][0], prm["a_c_im"][0],
              prm["a_d"][0], prm["a_log_dt"][0], iota)
    keep = contextlib.ExitStack()
    box = {}

    def prep(P):
        box["ctx"] = a2_prep(P, lambda n, sh, dt_: keep.enter_context(nc.sbuf_tensor("keep_" + n, list(sh), dt_)), *a2args, SEQ)

    stage_a1(nc, x, uT, prm["a_norm"][0], prm["a_w_in"][0], ident, NT, prep=prep)
    stage_a2(nc, uT, gT, *a2args, N_SEQ, SEQ, ctx=box["ctx"], ident_ap=ident)
    keep.close()
    stage_a3(nc, x, h1, gT, prm["a_w_glu"][0], NT)
    stage_ffn(nc, h1, h2, prm["ffn_norm"][0], prm["ffn_w_up"][0], prm["ffn_conv_w"][0], prm["ffn_conv_b"][0],
              prm["ffn_w_down"][0], ident, N_SEQ, SEQ)
    stage_kvq(nc, h2, kT, qT, vr, prm["kv_norm"], prm["w_kv"], prm["k_norm"], prm["b_norm"][0], prm["b_w_q"][0], prm["b_q_norm"][0],
              ident, bones, N_SEQ, SEQ)
    stage_att(nc, h2, h3, qT, kT, vr, prm["b_w_o"][0], ident, maskb, N_SEQ, SEQ)
    stage_ffn(nc, h3, out, prm["ffn_norm"][1], prm["ffn_w_up"][1], prm["ffn_conv_w"][1], prm["ffn_conv_b"][1],
              prm["ffn_w_down"][1], ident, N_SEQ, SEQ)
    return nc


def kernel(**inputs):
    x = np.ascontiguousarray(np.asarray(inputs["x"], dtype=np.float32))
    nc = build_program()
    p = np.arange(128)
    consts = {
        "c_ident": np.eye(128, dtype=np.float32),
        "c_bones": (p[:, None] // 64 == p[None, :] // 64).astype(np.float32),
        "c_maskb": np.where(p[None, :] + p[:, None] >= 128, 0.0, -30000.0).astype(np.float32),
        "c_iota": np.ascontiguousarray(np.tile(np.arange(2048, dtype=np.float32), (128, 1))),
    }
    params = {n: np.ascontiguousarray(np.asarray(inputs[n], dtype=np.float32)).reshape(s) for n, s in _PARAMS}
    in_maps = []
    for c in range(N_CORES):
        m = {"x": x[c * N_SEQ:(c + 1) * N_SEQ].reshape(NT, D)}
        m.update(params)
        m.update(consts)
        in_maps.append(m)
    res = run_bass_kernel_spmd(nc, in_maps, core_ids=list(range(N_CORES)))
    outs = [np.asarray(r["out"], dtype=np.float32).reshape(N_SEQ, SEQ, D) for r in res.results]
    return np.concatenate(outs, axis=0)
```

```python
import contextlib
import numpy as np
import concourse.bass as bass
import concourse.mybir as mybir
from concourse.bass_utils import run_bass_kernel_spmd

F32 = mybir.dt.float32
BF16 = mybir.dt.bfloat16
AF = mybir.ActivationFunctionType
ALU = mybir.AluOpType
AX = mybir.AxisListType

ENGS = ("pe", "act", "dve", "pool", "sp")
N_DMA_SEMS = 4

D = 1024
DFF = 2816
NFT = DFF // 128
EPS = 1e-6


class T:
    __slots__ = ("name", "writes", "reads")

    def __init__(self, name="t"):
        self.name = name
        self.writes = {}
        self.reads = {}


class Prog:
    _stage = 0

    def __init__(self, nc):
        self.nc = nc
        Prog._stage += 1
        self.sid = Prog._stage
        self.stack = contextlib.ExitStack()
        self.ops = {e: [] for e in ENGS}
        self.known = {e: {} for e in ENGS}
        self.ndma = {e: 0 for e in ENGS}
        self.cnt = {e: 0 for e in ENGS}
        self.milestones = {e: set() for e in ENGS}
        self._n = 0

    def sbuf(self, name, shape, dtype):
        return self.stack.enter_context(self.nc.sbuf_tensor(f"s{self.sid}_{name}", list(shape), dtype))

    def psum(self, name, shape, dtype=F32):
        return self.stack.enter_context(self.nc.psum_tensor(f"s{self.sid}_{name}", list(shape), dtype))

    def tile(self, name=None):
        return T(name or "t")

    def tiles(self, n):
        return [T() for _ in range(n)]

    def _collect(self, eng, reads, writes):
        need = {}
        for t in reads:
            for k, v in t.writes.items():
                if need.get(k, 0) < v:
                    need[k] = v
        for t in writes:
            for d in (t.writes, t.reads):
                for k, v in d.items():
                    if need.get(k, 0) < v:
                        need[k] = v
        waits = []
        kn = self.known[eng]
        for k, v in need.items():
            if k == ("e", eng) and eng == "pe":
                continue
            if kn.get(k, 0) >= v:
                continue
            kn[k] = v
            waits.append((k, v))
            if k[0] == "e":
                self.milestones[k[1]].add(v)
        return waits

    def op(self, eng, fn, reads=(), writes=()):
        waits = self._collect(eng, reads, writes)
        self.cnt[eng] += 1
        idx = self.cnt[eng]
        key = ("e", eng)
        self.ops[eng].append(dict(kind="op", fn=fn, waits=waits, idx=idx))
        for t in reads:
            if t.reads.get(key, 0) < idx:
                t.reads[key] = idx
        for t in writes:
            t.writes = {key: idx}
            t.reads = {}

    def dma(self, eng, fn, reads=(), writes=()):
        i = self.ndma[eng]
        self.ndma[eng] += 1
        slot = i % N_DMA_SEMS
        gen = i // N_DMA_SEMS
        key = ("d", eng, slot)
        waits = self._collect(eng, reads, writes)
        kn = self.known[eng]
        if gen > 0 and kn.get(key, 0) < 16 * gen:
            kn[key] = 16 * gen
            waits.append((key, 16 * gen))
        val = 16 * (gen + 1)
        self.ops[eng].append(dict(kind="dma", fn=fn, waits=waits, key=key))
        for t in reads:
            if t.reads.get(key, 0) < val:
                t.reads[key] = val
        for t in writes:
            t.writes = {key: val}
            t.reads = {}

    def wait_all(self, eng, tiles):
        waits = self._collect(eng, tiles, ())
        self.ops[eng].append(dict(kind="wait", waits=waits))

    def finish(self, out_tiles):
        self.wait_all("sp", out_tiles)
        nc = self.nc
        with nc.cleanup_on_exit():
            sems = {}
            for e in ENGS:
                if self.milestones[e]:
                    sems[("e", e)] = nc.alloc_semaphore(f"p{self.sid}_{e}")
                for s in range(min(N_DMA_SEMS, self.ndma[e])):
                    sems[("d", e, s)] = nc.alloc_semaphore(f"d{self.sid}_{e}_{s}")
            mmap = {e: {v: i + 1 for i, v in enumerate(sorted(self.milestones[e]))} for e in ENGS}

            def replay(e):
                def body(engine):
                    for o in self.ops[e]:
                        for k, v in o["waits"]:
                            if k[0] == "e":
                                v = mmap[k[1]][v]
                            engine.wait_ge(sems[k], v)
                        if o["kind"] == "op":
                            ins = o["fn"](engine)
                            if o["idx"] in mmap[e]:
                                ins.then_inc(sems[("e", e)], 1)
                        elif o["kind"] == "dma":
                            ins = o["fn"](engine)
                            ins.then_inc(sems[o["key"]], 16)
                return body

            with nc.Block() as block:
                block.tensor(replay("pe"))
                block.scalar(replay("act"))
                block.vector(replay("dve"))
                block.gpsimd(replay("pool"))
                block.sync(replay("sp"))
            nc.all_engine_barrier()
        self.stack.close()


def run_pipeline(tasks, offsets):
    nph = len(offsets)
    for step in range(len(tasks) + max(offsets) + 1):
        for p in reversed(range(nph)):
            t = step - offsets[p]
            if 0 <= t < len(tasks) and tasks[t][p] is not None:
                tasks[t][p]()


def load_weight_bf16(P, dst, tdst, w_ap, kchunks, gain_sb=None, tgain=None, eng="dve"):
    n = w_ap.shape[-1]
    src = w_ap.rearrange("(c p) n -> p c n", p=128)
    step = max(1, 4096 // n)
    for c0 in range(0, kchunks, step):
        c1 = min(kchunks, c0 + step)
        P.dma("pool", lambda e, c0=c0, c1=c1: e.dma_start(out=dst[:, c0:c1, :], in_=src[:, c0:c1, :]), writes=[tdst])
    if gain_sb is not None:
        for c in range(kchunks):
            P.op(eng, lambda e, c=c: e.tensor_scalar(out=dst[:, c, :], in0=dst[:, c, :], scalar1=gain_sb[:, c:c + 1],
                                                      scalar2=None, op0=ALU.mult), reads=[tdst, tgain], writes=[tdst])


def load_gain(P, name, g_ap):
    g_sb = P.sbuf(name, [128, 8], F32)
    t = P.tile()
    P.dma("sp", lambda e: e.dma_start(out=g_sb[:], in_=g_ap.rearrange("(c p) -> p c", p=128), allow_slow_non_contiguous=True), writes=[t])
    return g_sb, t


class NormT:
    def __init__(self, P, ident, tident, n_xt=2, n_xnT=1, junk=None, tjunk=None):
        self.P = P
        self.ident, self.tident = ident, tident
        self.n_xt, self.n_xnT = n_xt, n_xnT
        self.xt = [P.sbuf(f"nt_xt{i}", [128, 4, D], F32) for i in range(n_xt)]
        self.txt = P.tiles(n_xt)
        if junk is None:
            self.junk = P.sbuf("nt_junk", [128, D], BF16)[:]
            self.tjunk = P.tile()
        else:
            self.junk, self.tjunk = junk, tjunk
        self.ss = P.sbuf("nt_ss", [128, 8], F32)
        self.tss = P.tile()
        self.xn = [P.sbuf(f"nt_xn{i}", [128, D], BF16) for i in range(4)]
        self.txn = P.tiles(4)
        self.pst = [P.psum(f"nt_pst{i}", [128, D], BF16) for i in range(2)]
        self.tpst = P.tiles(2)
        self.xnT = [P.sbuf(f"nt_xnT{i}", [128, 8, 512], BF16) for i in range(n_xnT)]
        self.txnT = P.tiles(n_xnT)
        self.k = 0
        self.n = 0
        self.n2 = 0
        self.kp = 0

    def part1(self, src_rows):
        P = self.P
        b = self.n % self.n_xt
        self.n += 1
        xt, txt = self.xt[b], self.txt[b]
        P.dma("sp", lambda e: e.dma_start(out=xt[:], in_=src_rows.rearrange("(t p) d -> p t d", p=128)), writes=[txt])
        ss, tss = self.ss, self.tss
        for t in range(4):
            xn, txn = self.xn[t], self.txn[t]
            self.k += 1
            col = (self.k % 4) * 2
            P.op("dve", lambda e, col=col: e.memset(ss[:, col:col + 1], 0.0), writes=[tss])
            P.op("act", lambda e, t=t, col=col: e.activation(out=self.junk, in_=xt[:, t, :], func=AF.Square,
                                                             accum_out=ss[:, col:col + 1]),
                 reads=[txt], writes=[self.tjunk, tss])
            P.op("dve", lambda e, col=col: e.tensor_scalar(out=ss[:, col + 1:col + 2], in0=ss[:, col:col + 1], scalar1=1.0 / D,
                                                           scalar2=EPS, op0=ALU.mult, op1=ALU.add), reads=[tss], writes=[tss])
            P.op("act", lambda e, col=col: e.activation(out=ss[:, col + 1:col + 2], in_=ss[:, col + 1:col + 2], func=AF.Sqrt),
                 reads=[tss], writes=[tss])
            P.op("dve", lambda e, col=col: e.reciprocal(out=ss[:, col + 1:col + 2], in_=ss[:, col + 1:col + 2]), reads=[tss], writes=[tss])
            P.op("act", lambda e, t=t, col=col, xn=xn: e.activation(out=xn[:], in_=xt[:, t, :], func=AF.Copy,
                                                                     scale=ss[:, col + 1:col + 2]),
                 reads=[txt, tss], writes=[txn])
        return xt, txt

    def part2(self):
        P = self.P
        b2 = self.n2 % self.n_xnT
        self.n2 += 1
        xnT, txnT = self.xnT[b2], self.txnT[b2]
        for t in range(4):
            xn, txn = self.xn[t], self.txn[t]
            kk = self.kp % 2
            self.kp += 1
            pst, tpst = self.pst[kk], self.tpst[kk]
            for c in range(8):
                P.op("pe", lambda e, c=c, xn=xn, pst=pst: e.transpose(out=pst[:, c * 128:(c + 1) * 128],
                                                                       in_=xn[:, c * 128:(c + 1) * 128], identity=self.ident[:]),
                     reads=[txn, self.tident], writes=[tpst])
            P.op("dve", lambda e, t=t, pst=pst, xnT=xnT: e.tensor_copy(out=xnT[:, :, t * 128:(t + 1) * 128],
                                                                       in_=pst[:].rearrange("p (c k) -> p c k", k=128)),
                 reads=[tpst], writes=[txnT])
        return xnT, txnT

    def run(self, src_rows):
        xt, txt = self.part1(src_rows)
        xnT, txnT = self.part2()
        return xt, txt, xnT, txnT


def load_ident(P, ident_ap):
    ident = P.sbuf("ident", [128, 128], BF16)
    t = P.tile()
    P.dma("pool", lambda e: e.dma_start(out=ident[:], in_=ident_ap), writes=[t])
    return ident, t


def stage_ffn(nc, h_in, h_out, g_ap, wup_ap, cw_ap, cb_ap, wdn_ap, ident_ap, n_seq, seq_len):
    P = Prog(nc)
    ident, tident = load_ident(P, ident_ap)
    g_sb, tg = load_gain(P, "g", g_ap)
    wup = P.sbuf("wup", [128, 8, 2 * DFF], BF16)
    twup = P.tile()
    load_weight_bf16(P, wup, twup, wup_ap, 8, g_sb, tg)
    wdn = P.sbuf("wdn", [128, NFT, D], BF16)
    twdn = P.tile()
    load_weight_bf16(P, wdn, twdn, wdn_ap, NFT)
    cw = P.sbuf("cw", [128, NFT, 3], F32)
    cb = P.sbuf("cb", [128, NFT], F32)
    tcw = P.tile()
    for j in range(3):
        P.dma("sp", lambda e, j=j: e.dma_start(out=cw[:, :, j], in_=cw_ap[j].rearrange("(f p) -> p f", p=128), allow_slow_non_contiguous=True), writes=[tcw])
    P.dma("sp", lambda e: e.dma_start(out=cb[:], in_=cb_ap.rearrange("(f p) -> p f", p=128), allow_slow_non_contiguous=True), writes=[tcw])
    gc = [P.sbuf(f"gc{i}", [128, 512], F32) for i in range(2)]
    tgc = P.tiles(2)
    nt = NormT(P, ident, tident, n_xt=1, n_xnT=2, junk=gc[1][:].bitcast(BF16), tjunk=tgc[1])
    halo = P.sbuf("halo", [128, NFT, 2], F32)
    thalo = P.tile()
    gbuf = [P.sbuf(f"gbuf{i}", [128, 514], F32) for i in range(2)]
    tgbuf = P.tiles(2)
    hid = P.sbuf("hid", [128, NFT, 512], BF16)
    thid = P.tile()
    pv = [P.psum(f"pv{i}", [128, 512]) for i in range(2)]
    tpv = P.tiles(2)
    pg = [P.psum(f"pg{i}", [128, 512]) for i in range(2)]
    tpg = P.tiles(2)
    po = [P.psum(f"po{i}", [128, 512]) for i in range(2)]
    tpo = P.tiles(2)
    xr = [P.sbuf(f"xr{i}", [128, D], F32) for i in range(1)] * 2
    txr = P.tiles(1) * 2
    outs = []
    nsup = seq_len // 512
    nblk = n_seq * nsup
    cnt = {"k": 0, "ko": 0, "kx": 0}
    xn_of = {}

    def phA1(u):
        r0 = u * 512
        nt.part1(h_in[r0:r0 + 512, :])

    def phA2(u):
        xn_of[u] = nt.part2()

    def phB(u):
        xnT, txnT = xn_of[u]
        if u % nsup == 0:
            P.op("pool", lambda e: e.memset(halo[:], 0.0), writes=[thalo])
        for ft in range(NFT):
            if ft == 2 and u + 1 < nblk:
                phA1(u + 1)
            b = cnt["k"] % 2
            cnt["k"] += 1
            for c in range(8):
                P.op("pe", lambda e, c=c, b=b, ft=ft: e.matmul(out=pv[b][:], lhsT=wup[:, c, ft * 128:(ft + 1) * 128], rhs=xnT[:, c, :],
                                                               start=(c == 0), stop=(c == 7)), reads=[twup, txnT], writes=[tpv[b]])
            for c in range(8):
                P.op("pe", lambda e, c=c, b=b, ft=ft: e.matmul(out=pg[b][:], lhsT=wup[:, c, DFF + ft * 128:DFF + (ft + 1) * 128], rhs=xnT[:, c, :],
                                                               start=(c == 0), stop=(c == 7)), reads=[twup, txnT], writes=[tpg[b]])
            gb, tgb, g2, tg2 = gbuf[b], tgbuf[b], gc[b], tgc[b]
            P.op("pool", lambda e, gb=gb, ft=ft: e.tensor_copy(out=gb[:, 0:2], in_=halo[:, ft, :]), reads=[thalo], writes=[tgb])
            P.op("act", lambda e, gb=gb, b=b: e.activation(out=gb[:, 2:514], in_=pg[b][:], func=AF.Copy), reads=[tpg[b]], writes=[tgb])
            P.op("pool", lambda e, gb=gb, ft=ft: e.tensor_copy(out=halo[:, ft, :], in_=gb[:, 512:514]), reads=[tgb], writes=[thalo])
            P.op("act", lambda e, gb=gb, g2=g2, ft=ft: e.activation(out=g2[:], in_=gb[:, 2:514], func=AF.Identity,
                                                                    scale=cw[:, ft, 2:3], bias=cb[:, ft:ft + 1]),
                 reads=[tgb, tcw], writes=[tg2])
            P.op("dve", lambda e, gb=gb, g2=g2, ft=ft: e.scalar_tensor_tensor(out=g2[:], in0=gb[:, 1:513], scalar=cw[:, ft, 1:2], in1=g2[:],
                                                                              op0=ALU.mult, op1=ALU.add), reads=[tgb, tcw, tg2], writes=[tg2])
            P.op("dve", lambda e, gb=gb, g2=g2, ft=ft: e.scalar_tensor_tensor(out=g2[:], in0=gb[:, 0:512], scalar=cw[:, ft, 0:1], in1=g2[:],
                                                                              op0=ALU.mult, op1=ALU.add), reads=[tgb, tcw, tg2], writes=[tg2])
            P.op("act", lambda e, g2=g2: e.activation(out=g2[:], in_=g2[:], func=AF.Silu), reads=[tg2], writes=[tg2])
            P.op("dve", lambda e, g2=g2, b=b, ft=ft: e.tensor_tensor(out=hid[:, ft, :], in0=g2[:], in1=pv[b][:], op=ALU.mult),
                 reads=[tg2, tpv[b]], writes=[thid])

    def phC(u):
        r0 = u * 512
        for t in range(4):
            xb = cnt["kx"] % 2
            cnt["kx"] += 1
            rr = r0 + t * 128
            P.dma("sp", lambda e, xb=xb, rr=rr: e.dma_start(out=xr[xb][:], in_=h_in[rr:rr + 128, :]), writes=[txr[xb]])
            for nb in range(2):
                b = cnt["ko"] % 2
                cnt["ko"] += 1
                for ft in range(NFT):
                    P.op("pe", lambda e, ft=ft, t=t, nb=nb, b=b: e.matmul(out=po[b][:], lhsT=hid[:, ft, t * 128:(t + 1) * 128],
                                                                          rhs=wdn[:, ft, nb * 512:(nb + 1) * 512],
                                                                          start=(ft == 0), stop=(ft == NFT - 1)),
                         reads=[thid, twdn], writes=[tpo[b]])
                P.op("dve", lambda e, nb=nb, b=b, xb=xb: e.tensor_tensor(out=xr[xb][:, nb * 512:(nb + 1) * 512], in0=po[b][:],
                                                                         in1=xr[xb][:, nb * 512:(nb + 1) * 512], op=ALU.add),
                     reads=[tpo[b], txr[xb]], writes=[txr[xb]])
            to = P.tile()
            P.dma("act", lambda e, xb=xb, rr=rr: e.dma_start(out=h_out[rr:rr + 128, :], in_=xr[xb][:]), reads=[txr[xb]], writes=[to])
            outs.append(to)

    phA1(0)
    phA2(0)
    for u in range(nblk):
        phB(u)
        if u + 1 < nblk:
            phA2(u + 1)
        phC(u)
    P.finish(outs)


def stage_kvq(nc, h_in, kT_rev, qT, v_rev, kvn_ap, wkv_ap, kn_ap, bn_ap, wq_ap, qn_ap, ident_ap, bones_ap, n_seq, S):
    P = Prog(nc)
    ident, tident = load_ident(P, ident_ap)
    bones = P.sbuf("bones", [128, 128], BF16)
    tbones = P.tile()
    P.dma("pool", lambda e: e.dma_start(out=bones[:], in_=bones_ap), writes=[tbones])
    gkv, tgkv = load_gain(P, "gkv", kvn_ap)
    gb, tgb = load_gain(P, "gb", bn_ap)
    wkv = P.sbuf("wkv", [128, 8, 2048], BF16)
    twkv = P.tile()
    load_weight_bf16(P, wkv, twkv, wkv_ap, 8, gkv, tgkv)
    wq = P.sbuf("wq", [128, 8, 1024], BF16)
    twq = P.tile()
    load_weight_bf16(P, wq, twq, wq_ap, 8, gb, tgb)
    hg = P.sbuf("hg", [128, 4], F32)
    thg = P.tile()
    for hf in range(2):
        P.dma("sp", lambda e, hf=hf: e.dma_start(out=hg[hf * 64:(hf + 1) * 64, 0:1], in_=kn_ap.rearrange("(p o) -> p o", o=1)), writes=[thg])
        P.dma("sp", lambda e, hf=hf: e.dma_start(out=hg[hf * 64:(hf + 1) * 64, 1:2], in_=qn_ap.rearrange("(p o) -> p o", o=1)), writes=[thg])
    P.op("dve", lambda e: e.tensor_scalar(out=hg[:, 1:2], in0=hg[:, 1:2], scalar1=0.125, scalar2=None, op0=ALU.mult), reads=[thg], writes=[thg])
    P.op("dve", lambda e: e.memset(hg[:, 2:3], EPS), reads=[thg], writes=[thg])
    nt = NormT(P, ident, tident)
    pk = [P.psum(f"pk{i}", [128, 512]) for i in range(2)]
    tpk = P.tiles(2)
    pss = [P.psum(f"pss{i}", [128, 512]) for i in range(2)]
    tpss = P.tiles(2)
    sq = [P.sbuf(f"sq{i}", [128, 512], BF16) for i in range(2)]
    tsq = P.tiles(2)
    rs = [P.sbuf(f"rs{i}", [128, 512], F32) for i in range(2)]
    trs = P.tiles(2)
    kbuf = [P.sbuf(f"kbuf{i}", [128, 8, 512], BF16) for i in range(2)]
    tkbuf = P.tiles(2)
    qbuf = [P.sbuf(f"qbuf{i}", [128, 8, 512], BF16) for i in range(2)]
    tqbuf = P.tiles(2)
    vbuf = [P.sbuf(f"vbuf{i}", [128, 4, 1024], BF16) for i in range(2)]
    tvbuf = P.tiles(2)
    xrev = P.sbuf("xrev", [128, 8, 512], BF16)
    txrev = P.tile()
    outs = []
    nsup = S // 512
    k = 0
    for s in range(n_seq):
        for u in range(nsup):
            r0 = s * S + u * 512
            rr = s * S + S - 512 * (u + 1)
            ob = (s * nsup + u) % 2
            if s == 0 and u == 0:
                nt.part1(h_in[r0:r0 + 512, :])
            xnT, txnT = nt.part2()
            if r0 + 512 < n_seq * S:
                nt.part1(h_in[r0 + 512:r0 + 1024, :])
            for which in range(2):
                for ft in range(8):
                    b = k % 2
                    k += 1
                    for c in range(8):
                        if which == 0:
                            P.op("pe", lambda e, c=c, b=b, ft=ft: e.matmul(out=pk[b][:], lhsT=wkv[:, c, ft * 128:(ft + 1) * 128], rhs=xnT[:, c, :],
                                                                           start=(c == 0), stop=(c == 7)), reads=[twkv, txnT], writes=[tpk[b]])
                        else:
                            P.op("pe", lambda e, c=c, b=b, ft=ft: e.matmul(out=pk[b][:], lhsT=wq[:, c, ft * 128:(ft + 1) * 128], rhs=xnT[:, c, :],
                                                                           start=(c == 0), stop=(c == 7)), reads=[twq, txnT], writes=[tpk[b]])
                    P.op("act", lambda e, b=b: e.activation(out=sq[b][:], in_=pk[b][:], func=AF.Square), reads=[tpk[b]], writes=[tsq[b]])
                    P.op("pe", lambda e, b=b: e.matmul(out=pss[b][:], lhsT=bones[:], rhs=sq[b][:], start=True, stop=True),
                         reads=[tbones, tsq[b]], writes=[tpss[b]])
                    P.op("act", lambda e, b=b: e.activation(out=rs[b][:], in_=pss[b][:], func=AF.Ln, scale=1.0 / 64, bias=hg[:, 2:3]),
                         reads=[tpss[b], thg], writes=[trs[b]])
                    P.op("act", lambda e, b=b: e.activation(out=rs[b][:], in_=rs[b][:], func=AF.Exp, scale=-0.5), reads=[trs[b]], writes=[trs[b]])
                    if which == 0:
                        P.op("dve", lambda e, b=b, ft=ft, ob=ob: e.scalar_tensor_tensor(out=kbuf[ob][:, ft, ::-1], in0=pk[b][:], scalar=hg[:, 0:1], in1=rs[b][:],
                                                                                        op0=ALU.mult, op1=ALU.mult),
                             reads=[tpk[b], thg, trs[b]], writes=[tkbuf[ob]])
                    else:
                        P.op("dve", lambda e, b=b, ft=ft, ob=ob: e.scalar_tensor_tensor(out=qbuf[ob][:, ft, :], in0=pk[b][:], scalar=hg[:, 1:2], in1=rs[b][:],
                                                                                        op0=ALU.mult, op1=ALU.mult),
                             reads=[tpk[b], thg, trs[b]], writes=[tqbuf[ob]])
            P.op("dve", lambda e, xnT=xnT: e.tensor_copy(out=xrev[:, :, ::-1], in_=xnT[:]), reads=[txnT], writes=[txrev])
            for t in range(4):
                for nb in range(2):
                    b = k % 2
                    k += 1
                    for c in range(8):
                        P.op("pe", lambda e, c=c, b=b, t=t, nb=nb: e.matmul(out=pk[b][:], lhsT=xrev[:, c, t * 128:(t + 1) * 128],
                                                                            rhs=wkv[:, c, 1024 + nb * 512:1024 + (nb + 1) * 512],
                                                                            start=(c == 0), stop=(c == 7)), reads=[twkv, txrev], writes=[tpk[b]])
                    P.op("act", lambda e, b=b, t=t, nb=nb, ob=ob: e.activation(out=vbuf[ob][:, t, nb * 512:(nb + 1) * 512], in_=pk[b][:], func=AF.Copy),
                         reads=[tpk[b]], writes=[tvbuf[ob]])
            t1, t2, t3 = P.tile(), P.tile(), P.tile()
            P.dma("act", lambda e, ob=ob, rr=rr: e.dma_start(out=kT_rev[:, rr:rr + 512].rearrange("(f p) n -> p f n", p=128), in_=kbuf[ob][:]),
                  reads=[tkbuf[ob]], writes=[t1])
            P.dma("act", lambda e, ob=ob, r0=r0: e.dma_start(out=qT[:, r0:r0 + 512].rearrange("(f p) n -> p f n", p=128), in_=qbuf[ob][:]),
                  reads=[tqbuf[ob]], writes=[t2])
            P.dma("act", lambda e, ob=ob, rr=rr: e.dma_start(out=v_rev[rr:rr + 512, :].rearrange("(t p) d -> p t d", p=128), in_=vbuf[ob][:]),
                  reads=[tvbuf[ob]], writes=[t3])
            outs += [t1, t2, t3]
    P.finish(outs)


def stage_att(nc, h_in, h_out, qT, kT_rev, v_rev, wo_ap, ident_ap, maskb_ap, n_seq, S):
    P = Prog(nc)
    ident, tident = load_ident(P, ident_ap)
    maskb = P.sbuf("maskb", [128, 128], BF16)
    tmask = P.tile()
    P.dma("pool", lambda e: e.dma_start(out=maskb[:], in_=maskb_ap), writes=[tmask])
    wo = P.sbuf("wo", [128, 8, 1024], BF16)
    two = P.tile()
    load_weight_bf16(P, wo, two, wo_ap, 8)
    zeros = P.sbuf("zeros", [128, 512], F32)
    onec = P.sbuf("onec", [128, 1], F32)
    tz = P.tile()
    P.op("pool", lambda e: e.memset(zeros[:], 0.0), writes=[tz])
    P.op("pool", lambda e: e.memset(onec[:], 1.0), reads=[tz], writes=[tz])
    NB = S // 128
    qsb = P.sbuf("qsb", [128, 8, S], BF16)
    ksb = P.sbuf("ksb", [128, 8, S], BF16)
    vsb = P.sbuf("vsb", [128, NB, 1024], BF16)
    oT = P.sbuf("oT", [128, 8, S], BF16)
    tq, tk, tv, toT = P.tile(), P.tile(), P.tile(), P.tile()
    NZ, NS, NPB, NW, NWT, NWS, NPO = 3, 4, 5, 4, 2, 4, 3
    pz = [P.psum(f"pz{i}", [128, 512]) for i in range(NZ)]
    tpz = P.tiles(NZ)
    pwT = [P.psum(f"pwT{i}", [128, 1024], BF16) for i in range(NWT)]
    tpwT = P.tiles(NWT)
    po = [P.psum(f"po{i}", [128, 512]) for i in range(NPO)]
    tpo = P.tiles(NPO)
    pw = pz[0:2]
    tpw = tpz[0:2]
    ssb = [P.sbuf(f"ssb{i}", [128, 512], F32) for i in range(NS)]
    tssb = P.tiles(NS)
    pbuf = [P.sbuf(f"pbuf{i}", [128, 513], F32) for i in range(NPB)]
    tpbuf = P.tiles(NPB)
    wsb = [P.sbuf(f"wsb{i}", [128, 512], BF16) for i in range(NW)]
    twsb = P.tiles(NW)
    wTs = [P.sbuf(f"wTs{i}", [128, 512], BF16) for i in range(NWS)]
    twTs = P.tiles(NWS)
    xt = [P.sbuf(f"xt{i}", [128, D], F32) for i in range(2)]
    txt = P.tiles(2)
    outs = []
    kk = 0
    kx = 0
    gt = 0
    gpo = 0
    for s in range(n_seq):
        c0 = s * S
        for f in range(0, 8, 2):
            P.dma("sp", lambda e, f=f, c0=c0: e.dma_start(out=qsb[:, f:f + 2, :], in_=qT[f * 128:(f + 2) * 128, c0:c0 + S].rearrange("(f p) n -> p f n", p=128)), writes=[tq])
            P.dma("act", lambda e, f=f, c0=c0: e.dma_start(out=ksb[:, f:f + 2, :], in_=kT_rev[f * 128:(f + 2) * 128, c0:c0 + S].rearrange("(f p) n -> p f n", p=128)), writes=[tk])
        for b4 in range(0, NB, 4):
            P.dma("sp", lambda e, b4=b4, c0=c0: e.dma_start(out=vsb[:, b4:b4 + 4, :], in_=v_rev[c0 + b4 * 128:c0 + (b4 + 4) * 128, :].rearrange("(t p) d -> p t d", p=128)), writes=[tv])
        tasks = []
        for ft in range(8):
            for i in range(NB):
                q0 = i * 128
                kp0 = S - 128 - q0
                klen = q0 + 128
                ob = gpo % NPO
                gpo += 1
                ntile = (klen + 511) // 512
                for hp in range(2):
                    h = ft * 2 + hp
                    ps = slice(hp * 64, (hp + 1) * 64)
                    for n in range(ntile):
                        col0 = kp0 + 512 * n
                        wdt = min(512, S - col0)
                        nblk = wdt // 128
                        g = gt
                        gt += 1
                        bz, bs_, bp, bw, bwt, bws = g % NZ, g % NS, g % NPB, g % NW, g % NWT, g % NWS
                        bpp = (g - 1) % NPB

                        def ph0(bz=bz, ft=ft, ps=ps, q0=q0, col0=col0, wdt=wdt, n=n):
                            P.op("pe", lambda e: e.matmul(out=pz[bz][:, 0:wdt], lhsT=qsb[ps, ft, q0:q0 + 128], rhs=ksb[ps, ft, col0:col0 + wdt],
                                                          start=True, stop=(n != 0)), reads=[tq, tk], writes=[tpz[bz]])
                            if n == 0:
                                P.op("pe", lambda e: e.matmul(out=pz[bz][:, 0:128], lhsT=ident[:], rhs=maskb[:], start=False, stop=True),
                                     reads=[tident, tmask], writes=[tpz[bz]])

                        def ph1(bz=bz, bs_=bs_, wdt=wdt):
                            P.op("act", lambda e: e.activation(out=ssb[bs_][:, 0:wdt], in_=pz[bz][:, 0:wdt], func=AF.Sigmoid, scale=-1.0),
                                 reads=[tpz[bz]], writes=[tssb[bs_]])

                        def ph2(bs_=bs_, bp=bp, bpp=bpp, wdt=wdt, n=n):
                            if n == 0:
                                P.op("act", lambda e: e.activation(out=pbuf[bp][:, 0:1], in_=onec[:, 0:1], func=AF.Copy), reads=[tz], writes=[tpbuf[bp]])
                                init = onec[:, 0:1]
                                rd = [tssb[bs_], tz, tpbuf[bp]]
                            else:
                                P.op("act", lambda e: e.activation(out=pbuf[bp][:, 0:1], in_=pbuf[bpp][:, 512:513], func=AF.Copy), reads=[tpbuf[bpp]], writes=[tpbuf[bp]])
                                init = pbuf[bpp][:, 512:513]
                                rd = [tssb[bs_], tz, tpbuf[bp], tpbuf[bpp]]
                            P.op("dve", lambda e: e.tensor_tensor_scan(out=pbuf[bp][:, 1:1 + wdt], data0=ssb[bs_][:, 0:wdt], data1=zeros[:, 0:wdt],
                                                                       initial=init, op0=ALU.mult, op1=ALU.add), reads=rd, writes=[tpbuf[bp]])

                        def ph3(bp=bp, bw=bw, wdt=wdt):
                            P.op("dve", lambda e: e.tensor_tensor(out=wsb[bw][:, 0:wdt], in0=pbuf[bp][:, 0:wdt], in1=pbuf[bp][:, 1:1 + wdt], op=ALU.subtract),
                                 reads=[tpbuf[bp]], writes=[twsb[bw]])

                        def ph4(bw=bw, bwt=bwt, nblk=nblk):
                            for jb in range(nblk):
                                P.op("pe", lambda e, jb=jb: e.transpose(out=pwT[bwt][:, jb * 128:(jb + 1) * 128], in_=wsb[bw][:, jb * 128:(jb + 1) * 128], identity=ident[:]),
                                     reads=[twsb[bw], tident], writes=[tpwT[bwt]])

                        def ph5(bwt=bwt, bws=bws, wdt=wdt, g=g):
                            if True:
                                P.op("act", lambda e: e.activation(out=wTs[bws][:, 0:wdt], in_=pwT[bwt][:, 0:wdt], func=AF.Copy), reads=[tpwT[bwt]], writes=[twTs[bws]])
                            else:
                                P.op("dve", lambda e: e.tensor_copy(out=wTs[bws][:, 0:wdt], in_=pwT[bwt][:, 0:wdt]), reads=[tpwT[bwt]], writes=[twTs[bws]])

                        def ph6(bws=bws, nblk=nblk, col0=col0, h=h, ps=ps, ob=ob, n=n, ntile=ntile, hp=hp, ft=ft, q0=q0):
                            for jb in range(nblk):
                                blk = col0 // 128 + jb
                                first = (n == 0 and jb == 0)
                                last = (n == ntile - 1 and jb == nblk - 1)
                                P.op("pe", lambda e, jb=jb, blk=blk, first=first, last=last: e.matmul(
                                    out=po[ob][ps, 0:128], lhsT=vsb[:, blk, h * 64:(h + 1) * 64], rhs=wTs[bws][:, jb * 128:(jb + 1) * 128], start=first, stop=last),
                                    reads=[tv, twTs[bws]], writes=[tpo[ob]])
                            if hp == 1 and n == ntile - 1:
                                P.op("act", lambda e: e.activation(out=oT[:, ft, q0:q0 + 128], in_=po[ob][:, 0:128], func=AF.Copy), reads=[tpo[ob]], writes=[toT])

                        tasks.append([ph0, ph1, ph2, ph3, ph4, ph5, ph6])
        run_pipeline(tasks, ATT_OFFS)
        for t in range(NB):
            r0 = c0 + t * 128
            xb = kx % 2
            kx += 1
            P.dma("sp", lambda e, xb=xb, r0=r0: e.dma_start(out=xt[xb][:], in_=h_in[r0:r0 + 128, :]), writes=[txt[xb]])
            for nb in range(2):
                b = kk % 2
                kk += 1
                for ft in range(8):
                    P.op("pe", lambda e, b=b, ft=ft, t=t, nb=nb: e.matmul(out=pw[b][:], lhsT=oT[:, ft, t * 128:(t + 1) * 128], rhs=wo[:, ft, nb * 512:(nb + 1) * 512],
                                                                          start=(ft == 0), stop=(ft == 7)), reads=[toT, two], writes=[tpw[b]])
                P.op("dve", lambda e, b=b, xb=xb, nb=nb: e.tensor_tensor(out=xt[xb][:, nb * 512:(nb + 1) * 512], in0=pw[b][:], in1=xt[xb][:, nb * 512:(nb + 1) * 512], op=ALU.add),
                     reads=[tpw[b], txt[xb]], writes=[txt[xb]])
            to = P.tile()
            P.dma("act", lambda e, xb=xb, r0=r0: e.dma_start(out=h_out[r0:r0 + 128, :], in_=xt[xb][:]), reads=[txt[xb]], writes=[to])
            outs.append(to)
    P.finish(outs)


def stage_a1(nc, h_in, uT, g_ap, win_ap, ident_ap, n_tok, prep=None):
    P = Prog(nc)
    if prep is not None:
        prep(P)
    ident, tident = load_ident(P, ident_ap)
    g_sb, tg = load_gain(P, "g", g_ap)
    win = P.sbuf("win", [128, 8, 1024], BF16)
    twin = P.tile()
    load_weight_bf16(P, win, twin, win_ap, 8, g_sb, tg)
    nt = NormT(P, ident, tident)
    pu = [P.psum(f"pu{i}", [128, 512]) for i in range(2)]
    tpu = P.tiles(2)
    ubuf = [P.sbuf(f"ubuf{i}", [128, 8, 512], BF16) for i in range(2)]
    tubuf = P.tiles(2)
    outs = []
    k = 0
    nblk = n_tok // 512
    nt.part1(h_in[0:512, :])
    for u in range(nblk):
        r0 = u * 512
        ob = u % 2
        xnT, txnT = nt.part2()
        if u + 1 < nblk:
            nt.part1(h_in[r0 + 512:r0 + 1024, :])
        for m in range(8):
            b = k % 2
            k += 1
            for c in range(8):
                P.op("pe", lambda e, c=c, b=b, m=m: e.matmul(out=pu[b][:], lhsT=win[:, c, m * 128:(m + 1) * 128], rhs=xnT[:, c, :],
                                                             start=(c == 0), stop=(c == 7)), reads=[twin, txnT], writes=[tpu[b]])
            if m % 2 == 0:
                P.op("act", lambda e, b=b, m=m, ob=ob: e.activation(out=ubuf[ob][:, m, :], in_=pu[b][:], func=AF.Copy), reads=[tpu[b]], writes=[tubuf[ob]])
            else:
                P.op("dve", lambda e, b=b, m=m, ob=ob: e.tensor_copy(out=ubuf[ob][:, m, :], in_=pu[b][:]), reads=[tpu[b]], writes=[tubuf[ob]])
        to = P.tile()
        P.dma("act", lambda e, ob=ob, r0=r0: e.dma_start(out=uT[:, r0:r0 + 512].rearrange("(m p) n -> p m n", p=128), in_=ubuf[ob][:]),
              reads=[tubuf[ob]], writes=[to])
        outs.append(to)
    P.finish(outs)


TWO_PI = 6.283185307179586
ATT_OFFS = [0, 2, 4, 6, 8, 9, 11]


def a2_prep(P, alloc, lre_ap, lim_ap, bre_ap, bim_ap, cre_ap, cim_ap, d_ap, ldt_ap, iota_ap, S, ident_ap=None):
    I32 = mybir.dt.int32
    NP = 32
    lr = alloc("lr", [128, NP], F32)
    li = alloc("li", [128, NP], F32)
    dt = alloc("dt", [128, NP], F32)
    tprm = P.tile()
    for e_ in range(2):
        ps = slice(e_ * 64, (e_ + 1) * 64)
        P.dma("sp", lambda e, e_=e_, ps=ps: e.dma_start(out=lr[ps, :], in_=lre_ap.rearrange("(k e) n -> e n k", e=2)[e_], allow_slow_non_contiguous=True), writes=[tprm])
        P.dma("sp", lambda e, e_=e_, ps=ps: e.dma_start(out=li[ps, :], in_=lim_ap.rearrange("(k e) n -> e n k", e=2)[e_], allow_slow_non_contiguous=True), writes=[tprm])
        P.dma("sp", lambda e, e_=e_, ps=ps: e.dma_start(out=dt[ps, :], in_=ldt_ap.rearrange("(k e) -> e k", e=2)[e_].partition_broadcast(64), allow_slow_non_contiguous=True), writes=[tprm])
    cst = alloc("cst", [128, 4], F32)
    tcst = P.tile()
    P.op("dve", lambda e: e.memset(cst[:, 0:1], TWO_PI / 4), writes=[tcst])
    P.op("dve", lambda e: e.memset(cst[:, 1:2], 0.0), reads=[tcst], writes=[tcst])
    sc = {}
    for nm in ["f", "fhi", "flo", "r", "t0", "t1", "t2", "t3", "sn", "cs", "ar", "ai", "qre", "qim", "nqim", "den"]:
        sc[nm] = alloc("p_" + nm, [128, NP], F32)
    fhb = alloc("p_fhb", [128, NP], BF16)
    tiq = alloc("p_ti", [128, NP], I32)
    tsc = P.tile()

    def V(fn, eng="dve", extra=()):
        P.op(eng, fn, reads=[tprm, tsc, tcst] + list(extra), writes=[tsc])

    V(lambda e: e.activation(out=sc["t0"][:], in_=dt[:], func=AF.Exp), "act")
    V(lambda e: e.activation(out=sc["t1"][:], in_=sc["t0"][:], func=AF.Ln), "act")
    V(lambda e: e.tensor_tensor(out=sc["t1"][:], in0=dt[:], in1=sc["t1"][:], op=ALU.subtract))
    V(lambda e: e.tensor_scalar(out=sc["t1"][:], in0=sc["t1"][:], scalar1=1.0, scalar2=None, op0=ALU.add))
    V(lambda e: e.tensor_tensor(out=dt[:], in0=sc["t0"][:], in1=sc["t1"][:], op=ALU.mult))
    V(lambda e: e.tensor_tensor(out=sc["t0"][:], in0=li[:], in1=dt[:], op=ALU.mult))
    V(lambda e: e.tensor_scalar(out=sc["f"][:], in0=sc["t0"][:], scalar1=1.0 / TWO_PI, scalar2=None, op0=ALU.mult))
    V(lambda e: e.tensor_copy(out=fhb[:], in_=sc["f"][:]))
    V(lambda e: e.tensor_copy(out=sc["fhi"][:], in_=fhb[:]))
    V(lambda e: e.tensor_tensor(out=sc["flo"][:], in0=sc["f"][:], in1=sc["fhi"][:], op=ALU.subtract))
    V(lambda e: e.tensor_tensor(out=sc["t1"][:], in0=lr[:], in1=dt[:], op=ALU.mult))
    V(lambda e: e.activation(out=sc["t2"][:], in_=sc["t1"][:], func=AF.Exp), "act")
    V(lambda e: e.activation(out=sc["t3"][:], in_=sc["t2"][:], func=AF.Ln), "act")
    V(lambda e: e.tensor_tensor(out=sc["t3"][:], in0=sc["t1"][:], in1=sc["t3"][:], op=ALU.subtract))
    V(lambda e: e.tensor_scalar(out=sc["t3"][:], in0=sc["t3"][:], scalar1=1.0, scalar2=None, op0=ALU.add))
    V(lambda e: e.tensor_tensor(out=sc["r"][:], in0=sc["t2"][:], in1=sc["t3"][:], op=ALU.mult))
    V(lambda e: e.tensor_copy(out=tiq[:], in_=sc["f"][:]))
    V(lambda e: e.tensor_copy(out=sc["t3"][:], in_=tiq[:]))
    V(lambda e: e.tensor_tensor(out=sc["t2"][:], in0=sc["f"][:], in1=sc["t3"][:], op=ALU.subtract))
    V(lambda e: e.activation(out=sc["sn"][:], in_=sc["t2"][:], func=AF.Sin, scale=TWO_PI), "act")
    V(lambda e: e.tensor_scalar(out=sc["t3"][:], in0=sc["t2"][:], scalar1=-1.0, scalar2=None, op0=ALU.mult))
    V(lambda e: e.tensor_tensor(out=sc["t3"][:], in0=sc["t3"][:], in1=sc["t2"][:], op=ALU.min))
    V(lambda e: e.activation(out=sc["cs"][:], in_=sc["t3"][:], func=AF.Sin, scale=TWO_PI, bias=cst[:, 0:1]), "act")
    V(lambda e: e.tensor_tensor(out=sc["ar"][:], in0=sc["r"][:], in1=sc["cs"][:], op=ALU.mult))
    V(lambda e: e.tensor_tensor(out=sc["ai"][:], in0=sc["r"][:], in1=sc["sn"][:], op=ALU.mult))
    V(lambda e: e.tensor_scalar(out=sc["ar"][:], in0=sc["ar"][:], scalar1=-1.0, scalar2=None, op0=ALU.add))
    V(lambda e: e.tensor_tensor(out=sc["t0"][:], in0=lr[:], in1=lr[:], op=ALU.mult))
    V(lambda e: e.tensor_tensor(out=sc["t1"][:], in0=li[:], in1=li[:], op=ALU.mult))
    V(lambda e: e.tensor_tensor(out=sc["den"][:], in0=sc["t0"][:], in1=sc["t1"][:], op=ALU.add))
    V(lambda e: e.reciprocal(out=sc["den"][:], in_=sc["den"][:]))
    V(lambda e: e.tensor_tensor(out=sc["t0"][:], in0=sc["ar"][:], in1=lr[:], op=ALU.mult))
    V(lambda e: e.tensor_tensor(out=sc["t1"][:], in0=sc["ai"][:], in1=li[:], op=ALU.mult))
    V(lambda e: e.tensor_tensor(out=sc["t0"][:], in0=sc["t0"][:], in1=sc["t1"][:], op=ALU.add))
    V(lambda e: e.tensor_tensor(out=sc["qre"][:], in0=sc["t0"][:], in1=sc["den"][:], op=ALU.mult))
    V(lambda e: e.tensor_tensor(out=sc["t0"][:], in0=sc["ai"][:], in1=lr[:], op=ALU.mult))
    V(lambda e: e.tensor_tensor(out=sc["t1"][:], in0=sc["ar"][:], in1=li[:], op=ALU.mult))
    V(lambda e: e.tensor_tensor(out=sc["t0"][:], in0=sc["t0"][:], in1=sc["t1"][:], op=ALU.subtract))
    V(lambda e: e.tensor_tensor(out=sc["qim"][:], in0=sc["t0"][:], in1=sc["den"][:], op=ALU.mult))
    V(lambda e: e.tensor_scalar(out=sc["nqim"][:], in0=sc["qim"][:], scalar1=-1.0, scalar2=None, op0=ALU.mult))
    craw = alloc("craw", [128, NP, 2, 16], F32)
    tcraw = P.tile()
    for e_ in range(2):
        ps = slice(e_ * 64, (e_ + 1) * 64)
        for k1 in range(NP):
            for ri, ap_ in enumerate((cre_ap, cim_ap)):
                P.dma("sp" if ri == 0 else "act", lambda e, e_=e_, ps=ps, k1=k1, ri=ri, ap_=ap_: e.dma_start(
                    out=craw[ps, k1, ri, :], in_=ap_[2 * k1 + e_].rearrange("h n -> n h"), allow_slow_non_contiguous=True),
                    writes=[tcraw])
    CT = alloc("CT", [128, NP, 3, 128], BF16)
    tCT = P.tile()
    ctmp = alloc("ctmp", [128, NP, 16], F32)
    ctmp2 = alloc("ctmp2", [128, NP, 16], F32)
    tctmp = P.tile()
    P.op("pool", lambda e: e.memset(CT[:], 0.0), writes=[tCT])

    def bq(nm):
        return sc[nm][:].unsqueeze(2).to_broadcast([128, NP, 16])

    rd = [tcraw, tsc]
    P.op("dve", lambda e: e.tensor_tensor(out=ctmp[:], in0=craw[:, :, 0, :], in1=bq("qre"), op=ALU.mult), reads=rd, writes=[tctmp])
    P.op("dve", lambda e: e.tensor_tensor(out=ctmp2[:], in0=craw[:, :, 1, :], in1=bq("qim"), op=ALU.mult), reads=rd, writes=[tctmp])
    for e_ in range(2):
        ps = slice(e_ * 64, (e_ + 1) * 64)
        P.op("dve", lambda e, e_=e_, ps=ps: e.tensor_tensor(out=CT[ps, :, 0, e_ * 16:(e_ + 1) * 16], in0=ctmp[ps], in1=ctmp2[ps], op=ALU.subtract),
             reads=[tctmp], writes=[tCT])
    P.op("dve", lambda e: e.tensor_tensor(out=ctmp[:], in0=craw[:, :, 0, :], in1=bq("nqim"), op=ALU.mult), reads=rd + [tCT], writes=[tctmp])
    P.op("dve", lambda e: e.tensor_tensor(out=ctmp2[:], in0=craw[:, :, 1, :], in1=bq("qre"), op=ALU.mult), reads=rd, writes=[tctmp])
    for e_ in range(2):
        ps = slice(e_ * 64, (e_ + 1) * 64)
        P.op("dve", lambda e, e_=e_, ps=ps: e.tensor_tensor(out=CT[ps, :, 1, e_ * 16:(e_ + 1) * 16], in0=ctmp[ps], in1=ctmp2[ps], op=ALU.subtract),
             reads=[tctmp], writes=[tCT])
    BT = alloc("BT", [32, NP, 2, 128], BF16)
    tBT = P.tile()
    P.op("pool", lambda e: e.memset(BT[:], 0.0), writes=[tBT])
    for e_ in range(2):
        for ri, ap_ in enumerate((bre_ap, bim_ap)):
            for k1 in range(NP):
                P.dma("pool", lambda e, e_=e_, ri=ri, ap_=ap_, k1=k1: e.dma_start(
                    out=BT[e_ * 16:(e_ + 1) * 16, k1, ri, e_ * 64:(e_ + 1) * 64], in_=ap_[2 * k1 + e_].rearrange("n h -> h n"),
                    allow_slow_non_contiguous=True), writes=[tBT])
    dwin = alloc("dwin", [32, NP], F32)
    tdw = P.tile()
    P.dma("sp", lambda e: e.dma_start(out=dwin[:], in_=d_ap.rearrange("(k r) -> r k", r=32), allow_slow_non_contiguous=True), writes=[tdw])
    iota = alloc("iota", [128, S], F32)
    tio = P.tile()
    P.dma("sp", lambda e: e.dma_start(out=iota[:], in_=iota_ap[:, 0:S]), writes=[tio])
    id32 = alloc("id32", [32, 128], F32)
    Dg = alloc("Dg", [32, NP, 128], BF16)
    tDg = P.tile()
    P.dma("sp", lambda e: e.dma_start(out=id32[:], in_=ident_ap[0:32, :]), writes=[tDg])
    for k in range(NP):
        P.op("dve", lambda e, k=k: e.tensor_scalar(out=Dg[:, k, :], in0=id32[:], scalar1=dwin[:, k:k + 1], scalar2=None, op0=ALU.mult),
             reads=[tDg, tdw], writes=[tDg])
    return dict(Dg=Dg, tDg=tDg, sc=sc, tsc=tsc, CT=CT, tCT=tCT, BT=BT, tBT=tBT, dwin=dwin, tdw=tdw, iota=iota, tio=tio, cst=cst, tcst=tcst)


def stage_a2(nc, uT, gT, lre_ap, lim_ap, bre_ap, bim_ap, cre_ap, cim_ap, d_ap, ldt_ap, iota_ap, n_seq, S, dbg_pairs=None, dbg=None, dbg_stop=9, ctx=None, ident_ap=None):
    P = Prog(nc)
    I32 = mybir.dt.int32
    NP = 32
    if ctx is None:
        ctx = a2_prep(P, P.sbuf, lre_ap, lim_ap, bre_ap, bim_ap, cre_ap, cim_ap, d_ap, ldt_ap, iota_ap, S, ident_ap=ident_ap)
    else:
        ctx = dict(ctx)
        for tn in ("tsc", "tCT", "tBT", "tdw", "tio", "tcst", "tDg"):
            ctx[tn] = P.tile()
    sc, tsc, CT, tCT, BT, tBT = ctx["sc"], ctx["tsc"], ctx["CT"], ctx["tCT"], ctx["BT"], ctx["tBT"]
    dwin, tdw, iota, tio, cst, tcst = ctx["dwin"], ctx["tdw"], ctx["iota"], ctx["tio"], ctx["cst"], ctx["tcst"]
    Dg, tDg = ctx["Dg"], ctx["tDg"]
    HB = 512
    nq = S // HB
    snb = [P.sbuf(f"snb{i}", [128, S], BF16) for i in range(2)]
    csb = [P.sbuf(f"csb{i}", [128, S], BF16) for i in range(2)]
    rtab = [P.sbuf(f"rtab{i}", [128, HB], F32) for i in range(2)]
    ttab = P.tiles(2)
    SH = S // 2
    wk1 = P.sbuf("wk1", [128, SH], F32)
    wk2 = P.sbuf("wk2", [128, SH], F32)
    tiw = P.sbuf("tiw", [128, SH], I32)
    twk = P.tile()
    ones = P.sbuf("ones", [128, HB], F32)
    zc = P.sbuf("zc", [128, 1], F32)
    tones = P.tile()
    P.op("dve", lambda e: e.memset(ones[:], 1.0), writes=[tones])
    P.op("dve", lambda e: e.memset(zc[:], 0.0), reads=[tones], writes=[tones])
    P.op("dve", lambda e: e.tensor_scalar(out=CT[:, :, 2, :], in0=CT[:, :, 0, :], scalar1=-1.0, scalar2=None, op0=ALU.mult), reads=[tCT], writes=[tCT])
    NU = 3
    uwin = [P.sbuf(f"uwin{i}", [32, S], BF16) for i in range(NU)]
    tuw = P.tiles(NU)
    pbr = [P.psum(f"pbr{i}", [128, HB]) for i in range(1)] * 2
    pbi = [P.psum(f"pbi{i}", [128, HB]) for i in range(1)] * 2
    tpb = P.tiles(1) * 2
    pW = [[P.psum(f"pW{i}_{j}", [128, HB]) for j in range(2)] for i in range(2)]
    tpW = P.tiles(2)
    idb = P.sbuf("idb", [128, 2, 128], BF16)
    tidb = P.tile()
    P.dma("pool", lambda e: e.dma_start(out=idb[:, 0, :], in_=ident_ap), writes=[tidb])
    P.op("act", lambda e: e.activation(out=idb[:, 1, :], in_=idb[:, 0, :], func=AF.Copy, scale=-1.0), reads=[tidb], writes=[tidb])
    py = [P.psum(f"py{i}", [128, HB]) for i in range(2)]
    tpy = P.tiles(2)
    bsb = [[P.sbuf(f"bsb{i}_{j}", [128, HB], BF16) for j in range(2)] for i in range(2)]
    tbsb = P.tiles(2)
    A = [[P.sbuf(f"A{i}_{j}", [128, HB], BF16) for j in range(4)] for i in range(2)]
    tA01 = P.tiles(2)
    tA23 = P.tiles(2)
    W = [[P.sbuf(f"W{i}_{j}", [128, HB], F32) for j in range(2)] for i in range(2)]
    tW = P.tiles(2)
    NZb = 3
    Z = [[P.sbuf(f"Z{i}_{j}", [128, HB], F32) for j in range(2)] for i in range(NZb)]
    tZ = P.tiles(NZb)
    Zb = [[P.sbuf(f"Zb{i}_{j}", [128, HB], BF16) for j in range(2)] for i in range(2)]
    tZb = P.tiles(2)
    Bq = [[P.sbuf(f"B{i}_{j}", [128, HB], BF16) for j in range(4)] for i in range(2)]
    tB = P.tiles(2)
    tB2 = P.tiles(2)
    ytmp = [P.sbuf(f"ytmp{i}", [32, HB], F32) for i in range(2)]
    tyt = P.tiles(2)
    gout = [P.sbuf(f"gout{i}", [32, S], BF16) for i in range(2)]
    tgo = P.tiles(2)
    outs = []
    npairs = NP if dbg_pairs is None else dbg_pairs

    def gen_tables(k):
        tb = k % 2
        fh, fl, rk = sc["fhi"][:, k:k + 1], sc["flo"][:, k:k + 1], sc["r"][:, k:k + 1]
        for hh in range(2):
            hs = slice(hh * SH, (hh + 1) * SH)
            P.op("act", lambda e, hs=hs: e.activation(out=wk1[:], in_=iota[:, hs], func=AF.Copy, scale=fh), reads=[tio, tsc], writes=[twk])
            P.op("act", lambda e: e.activation(out=tiw[:], in_=wk1[:], func=AF.Copy), reads=[twk], writes=[twk])
            P.op("act", lambda e: e.activation(out=wk2[:], in_=tiw[:], func=AF.Copy), reads=[twk], writes=[twk])
            P.op("dve", lambda e: e.tensor_tensor(out=wk1[:], in0=wk1[:], in1=wk2[:], op=ALU.subtract), reads=[twk], writes=[twk])
            P.op("dve", lambda e, hs=hs: e.scalar_tensor_tensor(out=wk2[:], in0=iota[:, hs], scalar=fl, in1=wk1[:], op0=ALU.mult, op1=ALU.add), reads=[tio, tsc, twk], writes=[twk])
            P.op("act", lambda e: e.activation(out=tiw[:], in_=wk2[:], func=AF.Copy), reads=[twk], writes=[twk])
            P.op("act", lambda e: e.activation(out=wk1[:], in_=tiw[:], func=AF.Copy), reads=[twk], writes=[twk])
            P.op("dve", lambda e: e.tensor_tensor(out=wk2[:], in0=wk2[:], in1=wk1[:], op=ALU.subtract), reads=[twk], writes=[twk])
            P.op("act", lambda e, hs=hs: e.activation(out=snb[tb][:, hs], in_=wk2[:], func=AF.Sin, scale=TWO_PI), reads=[twk], writes=[ttab[tb]])
            P.op("act", lambda e: e.activation(out=wk1[:], in_=wk2[:], func=AF.Copy, scale=-1.0), reads=[twk], writes=[twk])
            P.op("dve", lambda e: e.tensor_tensor(out=wk1[:], in0=wk1[:], in1=wk2[:], op=ALU.min), reads=[twk], writes=[twk])
            P.op("act", lambda e, hs=hs: e.activation(out=csb[tb][:, hs], in_=wk1[:], func=AF.Sin, scale=TWO_PI, bias=cst[:, 0:1]), reads=[twk, tcst], writes=[ttab[tb]])
        P.op("dve", lambda e: e.tensor_scalar(out=rtab[tb][:], in0=ones[:], scalar1=rk, scalar2=None, op0=ALU.mult), reads=[tones, tsc, ttab[tb]], writes=[ttab[tb]])

    if npairs > 0:
        gen_tables(0)
    tasks = []
    g = 0
    for k in range(npairs):
        row0 = k * 32
        tb = k % 2
        for s in range(n_seq):
            c0 = s * S
            ub = (k * n_seq + s) % NU
            gb = (k * n_seq + s) % 2
            for qi in range(nq):
                t0 = qi * HB
                tsl = slice(t0, t0 + HB)
                b2 = g % 2
                bz = g % NZb
                bzp = (g - 1) % NZb
                g += 1

                def p0(k=k, s=s, qi=qi, ub=ub, row0=row0, c0=c0):
                    if qi == 0:
                        P.dma("sp", lambda e: e.dma_start(out=uwin[ub][:], in_=uT[row0:row0 + 32, c0:c0 + S]), writes=[tuw[ub]])
                    tpp = n_seq * nq
                    ti = s * nq + qi
                    assert tpp >= 8, "table double-buffering needs >= 8 tasks per pair"
                    if ti == 7 and k + 1 < npairs:
                        gen_tables(k + 1)

                def p1(k=k, ub=ub, b2=b2, tsl=tsl):
                    P.op("pe", lambda e: e.matmul(out=pbr[b2][:], lhsT=BT[:, k, 0, :], rhs=uwin[ub][:, tsl], start=True, stop=True),
                         reads=[tBT, tuw[ub]], writes=[tpb[b2]])
                    P.op("pe", lambda e: e.matmul(out=pbi[b2][:], lhsT=BT[:, k, 1, :], rhs=uwin[ub][:, tsl], start=True, stop=True),
                         reads=[tBT, tuw[ub]], writes=[tpb[b2]])

                def p2(b2=b2):
                    P.op("act", lambda e: e.activation(out=bsb[b2][0][:], in_=pbr[b2][:], func=AF.Copy), reads=[tpb[b2]], writes=[tbsb[b2]])
                    P.op("act", lambda e: e.activation(out=bsb[b2][1][:], in_=pbi[b2][:], func=AF.Copy), reads=[tpb[b2]], writes=[tbsb[b2]])

                def p3(b2=b2, tb=tb, tsl=tsl):
                    rd = [ttab[tb], tbsb[b2]]
                    P.op("dve", lambda e: e.tensor_tensor(out=A[b2][0][:], in0=csb[tb][:, tsl], in1=bsb[b2][0][:], op=ALU.mult), reads=rd, writes=[tA01[b2]])
                    P.op("dve", lambda e: e.tensor_tensor(out=A[b2][1][:], in0=snb[tb][:, tsl], in1=bsb[b2][1][:], op=ALU.mult), reads=rd, writes=[tA01[b2]])
                    P.op("dve", lambda e: e.tensor_tensor(out=A[b2][2][:], in0=csb[tb][:, tsl], in1=bsb[b2][1][:], op=ALU.mult), reads=rd, writes=[tA23[b2]])
                    P.op("dve", lambda e: e.tensor_tensor(out=A[b2][3][:], in0=snb[tb][:, tsl], in1=bsb[b2][0][:], op=ALU.mult), reads=rd, writes=[tA23[b2]])

                def p4(b2=b2):
                    for j, (x0, x1, sg) in enumerate(((0, 1, 0), (2, 3, 1))):
                        P.op("pe", lambda e, j=j, x0=x0: e.matmul(out=pW[b2][j][:], lhsT=idb[:, 0, :], rhs=A[b2][x0][:], start=True, stop=False),
                             reads=[tidb, tA01[b2], tA23[b2]], writes=[tpW[b2]])
                        P.op("pe", lambda e, j=j, x1=x1, sg=sg: e.matmul(out=pW[b2][j][:], lhsT=idb[:, sg, :], rhs=A[b2][x1][:], start=False, stop=True),
                             reads=[tidb, tA01[b2], tA23[b2]], writes=[tpW[b2]])

                def p5(b2=b2, bz=bz, bzp=bzp, tb=tb, qi=qi):
                    for j in range(2):
                        if qi == 0:
                            init, rd = zc[:, 0:1], [tpW[b2], ttab[tb], tones]
                        else:
                            init, rd = Z[bzp][j][:, HB - 1:HB], [tpW[b2], ttab[tb], tZ[bzp]]
                        P.op("dve", lambda e, j=j, init=init: e.tensor_tensor_scan(out=Z[bz][j][:], data0=rtab[tb][:], data1=pW[b2][j][:], initial=init,
                                                                                   op0=ALU.mult, op1=ALU.add), reads=rd, writes=[tZ[bz]])

                def p6(b2=b2, bz=bz):
                    for j in range(2):
                        P.op("act", lambda e, j=j: e.activation(out=Zb[b2][j][:], in_=Z[bz][j][:], func=AF.Copy), reads=[tZ[bz]], writes=[tZb[b2]])

                def p7(b2=b2, tb=tb, tsl=tsl):
                    rd = [ttab[tb], tZb[b2]]
                    P.op("dve", lambda e: e.tensor_tensor(out=Bq[b2][0][:], in0=csb[tb][:, tsl], in1=Zb[b2][0][:], op=ALU.mult), reads=rd, writes=[tB[b2]])
                    P.op("dve", lambda e: e.tensor_tensor(out=Bq[b2][1][:], in0=snb[tb][:, tsl], in1=Zb[b2][1][:], op=ALU.mult), reads=rd, writes=[tB[b2]])
                    P.op("dve", lambda e: e.tensor_tensor(out=Bq[b2][2][:], in0=snb[tb][:, tsl], in1=Zb[b2][0][:], op=ALU.mult), reads=rd, writes=[tB2[b2]])
                    P.op("dve", lambda e: e.tensor_tensor(out=Bq[b2][3][:], in0=csb[tb][:, tsl], in1=Zb[b2][1][:], op=ALU.mult), reads=rd, writes=[tB2[b2]])

                def p8(b2=b2, k=k, ub=ub, tsl=tsl):
                    for i4, ci in enumerate((0, 2, 1, 1)):
                        P.op("pe", lambda e, i4=i4, ci=ci: e.matmul(out=py[b2][:], lhsT=CT[:, k, ci, :], rhs=Bq[b2][i4][:], start=(i4 == 0), stop=False),
                             reads=[tCT, tB[b2], tB2[b2]], writes=[tpy[b2]])
                    P.op("pe", lambda e: e.matmul(out=py[b2][:], lhsT=Dg[:, k, :], rhs=uwin[ub][:, tsl], start=False, stop=True),
                         reads=[tDg, tuw[ub]], writes=[tpy[b2]])

                def p9(b2=b2, ub=ub, gb=gb, tsl=tsl, k=k, qi=qi, row0=row0, c0=c0):
                    P.op("act", lambda e: e.activation(out=gout[gb][:, tsl], in_=py[b2][0:32, :], func=AF.Gelu), reads=[tpy[b2]], writes=[tgo[gb]])
                    if qi == nq - 1:
                        to = P.tile()
                        P.dma("act", lambda e: e.dma_start(out=gT[row0:row0 + 32, c0:c0 + S], in_=gout[gb][:]), reads=[tgo[gb]], writes=[to])
                        outs.append(to)

                tasks.append([p0, p1, p2, p3, p4, p5, p6, p7, p8, p9])
    run_pipeline(tasks, [0, 1, 2, 3, 4, 5, 6, 7, 8, 9])
    if dbg is not None:
        for nm, ap_ in dbg.items():
            src = {"CT": CT, "BT": BT, "sn": snb[(npairs - 1) % 2], "cs": csb[(npairs - 1) % 2], "r": sc["r"], "qre": sc["qre"], "qim": sc["qim"], "fhi": sc["fhi"], "flo": sc["flo"]}[nm]
            tl = {"CT": tCT, "BT": tBT, "sn": ttab[(npairs - 1) % 2], "cs": ttab[(npairs - 1) % 2]}.get(nm, tsc)
            to = P.tile()
            P.dma("sp", lambda e, ap_=ap_, src=src: e.dma_start(out=ap_, in_=src[:]), reads=[tl], writes=[to])
            outs.append(to)
    P.finish(outs)


def stage_a3(nc, h_in, h_out, gT, wglu_ap, n_tok):
    P = Prog(nc)
    wg = P.sbuf("wg", [128, 8, 2048], BF16)
    twg = P.tile()
    load_weight_bf16(P, wg, twg, wglu_ap, 8)
    gb = [P.sbuf(f"gb{i}", [128, 8, 512], BF16) for i in range(2)]
    tgb = P.tiles(2)
    xt = [P.sbuf(f"xt{i}", [128, 4, D], F32) for i in range(2)]
    txt = P.tiles(2)
    pv = [P.psum(f"pv{i}", [128, 512]) for i in range(2)]
    tpv = P.tiles(2)
    pg = [P.psum(f"pg{i}", [128, 512]) for i in range(2)]
    tpg = P.tiles(2)
    sg = [P.sbuf(f"sg{i}", [128, 512], F32) for i in range(2)]
    tsg = P.tiles(2)
    outs = []
    k = 0
    for u in range(n_tok // 512):
        r0 = u * 512
        ob = u % 2
        P.dma("sp", lambda e, ob=ob, r0=r0: e.dma_start(out=gb[ob][:], in_=gT[:, r0:r0 + 512].rearrange("(c p) n -> p c n", p=128)), writes=[tgb[ob]])
        P.dma("sp", lambda e, ob=ob, r0=r0: e.dma_start(out=xt[ob][:], in_=h_in[r0:r0 + 512, :].rearrange("(t p) d -> p t d", p=128)), writes=[txt[ob]])
        for t in range(4):
            for nb in range(2):
                b = k % 2
                k += 1
                for c in range(8):
                    P.op("pe", lambda e, c=c, b=b, t=t, nb=nb, ob=ob: e.matmul(out=pv[b][:], lhsT=gb[ob][:, c, t * 128:(t + 1) * 128], rhs=wg[:, c, nb * 512:(nb + 1) * 512],
                                                                               start=(c == 0), stop=(c == 7)), reads=[tgb[ob], twg], writes=[tpv[b]])
                for c in range(8):
                    P.op("pe", lambda e, c=c, b=b, t=t, nb=nb, ob=ob: e.matmul(out=pg[b][:], lhsT=gb[ob][:, c, t * 128:(t + 1) * 128], rhs=wg[:, c, 1024 + nb * 512:1024 + (nb + 1) * 512],
                                                                               start=(c == 0), stop=(c == 7)), reads=[tgb[ob], twg], writes=[tpg[b]])
                P.op("act", lambda e, b=b: e.activation(out=sg[b][:], in_=pg[b][:], func=AF.Sigmoid), reads=[tpg[b]], writes=[tsg[b]])
                P.op("dve", lambda e, b=b: e.tensor_tensor(out=sg[b][:], in0=sg[b][:], in1=pv[b][:], op=ALU.mult), reads=[tsg[b], tpv[b]], writes=[tsg[b]])
                P.op("pool", lambda e, b=b, t=t, nb=nb, ob=ob: e.tensor_tensor(out=xt[ob][:, t, nb * 512:(nb + 1) * 512], in0=xt[ob][:, t, nb * 512:(nb + 1) * 512], in1=sg[b][:], op=ALU.add),
                     reads=[tsg[b], txt[ob]], writes=[txt[ob]])
        to = P.tile()
        P.dma("act", lambda e, ob=ob, r0=r0: e.dma_start(out=h_out[r0:r0 + 512, :].rearrange("(t p) d -> p t d", p=128), in_=xt[ob][:]), reads=[txt[ob]], writes=[to])
        outs.append(to)
    P.finish(outs)


N_CORES = 8
SEQ = 2048
N_SEQ = 4
NT = N_SEQ * SEQ

_PARAMS = [
    ("a_norm", [1, 1024]), ("a_w_in", [1, 1024, 1024]), ("a_lam_re", [1, 64, 64]), ("a_lam_im", [1, 64, 64]),
    ("a_b_re", [1, 64, 64, 16]), ("a_b_im", [1, 64, 64, 16]), ("a_c_re", [1, 64, 16, 64]), ("a_c_im", [1, 64, 16, 64]),
    ("a_d", [1, 1024]), ("a_log_dt", [1, 64]), ("a_w_glu", [1, 1024, 2048]), ("kv_norm", [1024]), ("w_kv", [1024, 2048]),
    ("k_norm", [64]), ("b_norm", [1, 1024]), ("b_w_q", [1, 1024, 1024]), ("b_q_norm", [1, 64]), ("b_w_o", [1, 1024, 1024]),
    ("ffn_norm", [2, 1024]), ("ffn_w_up", [2, 1024, 5632]), ("ffn_conv_w", [2, 3, 2816]), ("ffn_conv_b", [2, 2816]),
    ("ffn_w_down", [2, 2816, 1024]),
]


def build_program(N_SEQ=N_SEQ, SEQ=SEQ, debug=False):
    NT = N_SEQ * SEQ
    nc = bass.Bass("TRN2", target_bir_lowering=False)
    x = nc.dram_tensor("x", [NT, D], F32, kind="ExternalInput").ap()
    prm = {n: nc.dram_tensor(n, s, F32, kind="ExternalInput").ap() for n, s in _PARAMS}
    ident = nc.dram_tensor("c_ident", [128, 128], F32, kind="ExternalInput").ap()
    bones = nc.dram_tensor("c_bones", [128, 128], F32, kind="ExternalInput").ap()
    maskb = nc.dram_tensor("c_maskb", [128, 128], F32, kind="ExternalInput").ap()
    iota = nc.dram_tensor("c_iota", [128, 2048], F32, kind="ExternalInput").ap()
    out = nc.dram_tensor("out", [NT, D], F32, kind="ExternalOutput").ap()
    kd = "ExternalOutput" if debug else "Internal"
    h1 = nc.dram_tensor("h1", [NT, D], F32, kind=kd).ap()
    h2 = nc.dram_tensor("h2", [NT, D], F32, kind=kd).ap()
    h3 = nc.dram_tensor("h3", [NT, D], F32, kind=kd).ap()
    uT = nc.dram_tensor("uT", [D, NT], BF16).ap()
    gT = nc.dram_tensor("gT", [D, NT], BF16).ap()
    kT = nc.dram_tensor("kT", [D, NT], BF16).ap()
    qT = nc.dram_tensor("qT", [D, NT], BF16).ap()
    vr = nc.dram_tensor("vr", [NT, D], BF16).ap()
    a2args = (prm["a_lam_re"][0], prm["a_lam_im"][0], prm["a_b_re"][0], prm["a_b_im"][0], prm["a_c_re"][0], prm["a_c_im"][0],
              prm["a_d"][0], prm["a_log_dt"][0], iota)
    keep = contextlib.ExitStack()
    box = {}

    def prep(P):
        box["ctx"] = a2_prep(P, lambda n, sh, dt_: keep.enter_context(nc.sbuf_tensor("keep_" + n, list(sh), dt_)), *a2args, SEQ, ident_ap=ident)

    stage_a1(nc, x, uT, prm["a_norm"][0], prm["a_w_in"][0], ident, NT, prep=prep)
    stage_a2(nc, uT, gT, *a2args, N_SEQ, SEQ, ctx=box["ctx"], ident_ap=ident)
    keep.close()
    stage_a3(nc, x, h1, gT, prm["a_w_glu"][0], NT)
    stage_ffn(nc, h1, h2, prm["ffn_norm"][0], prm["ffn_w_up"][0], prm["ffn_conv_w"][0], prm["ffn_conv_b"][0],
              prm["ffn_w_down"][0], ident, N_SEQ, SEQ)
    stage_kvq(nc, h2, kT, qT, vr, prm["kv_norm"], prm["w_kv"], prm["k_norm"], prm["b_norm"][0], prm["b_w_q"][0], prm["b_q_norm"][0],
              ident, bones, N_SEQ, SEQ)
    stage_att(nc, h2, h3, qT, kT, vr, prm["b_w_o"][0], ident, maskb, N_SEQ, SEQ)
    stage_ffn(nc, h3, out, prm["ffn_norm"][1], prm["ffn_w_up"][1], prm["ffn_conv_w"][1], prm["ffn_conv_b"][1],
              prm["ffn_w_down"][1], ident, N_SEQ, SEQ)
    return nc


def kernel(**inputs):
    x = np.ascontiguousarray(np.asarray(inputs["x"], dtype=np.float32))
    nc = build_program()
    p = np.arange(128)
    consts = {
        "c_ident": np.eye(128, dtype=np.float32),
        "c_bones": (p[:, None] // 64 == p[None, :] // 64).astype(np.float32),
        "c_maskb": np.where(p[None, :] + p[:, None] >= 128, 0.0, -30000.0).astype(np.float32),
        "c_iota": np.ascontiguousarray(np.tile(np.arange(2048, dtype=np.float32), (128, 1))),
    }
    params = {n: np.ascontiguousarray(np.asarray(inputs[n], dtype=np.float32)).reshape(s) for n, s in _PARAMS}
    in_maps = []
    for c in range(N_CORES):
        m = {"x": x[c * N_SEQ:(c + 1) * N_SEQ].reshape(NT, D)}
        m.update(params)
        m.update(consts)
        in_maps.append(m)
    res = run_bass_kernel_spmd(nc, in_maps, core_ids=list(range(N_CORES)))
    outs = [np.asarray(r["out"], dtype=np.float32).reshape(N_SEQ, SEQ, D) for r in res.results]
    return np.concatenate(outs, axis=0)
```

```python
import contextlib
import numpy as np
import concourse.bass as bass
import concourse.mybir as mybir
from concourse.bass_utils import run_bass_kernel_spmd

F32 = mybir.dt.float32
BF16 = mybir.dt.bfloat16
AF = mybir.ActivationFunctionType
ALU = mybir.AluOpType
AX = mybir.AxisListType

ENGS = ("pe", "act", "dve", "pool", "sp")
N_DMA_SEMS = 4

D = 1024
DFF = 2816
NFT = DFF // 128
EPS = 1e-6


class T:
    __slots__ = ("name", "writes", "reads")

    def __init__(self, name="t"):
        self.name = name
        self.writes = {}
        self.reads = {}


class Prog:
    _stage = 0

    def __init__(self, nc):
        self.nc = nc
        Prog._stage += 1
        self.sid = Prog._stage
        self.stack = contextlib.ExitStack()
        self.ops = {e: [] for e in ENGS}
        self.known = {e: {} for e in ENGS}
        self.ndma = {e: 0 for e in ENGS}
        self.cnt = {e: 0 for e in ENGS}
        self.milestones = {e: set() for e in ENGS}
        self._n = 0

    def sbuf(self, name, shape, dtype):
        return self.stack.enter_context(self.nc.sbuf_tensor(f"s{self.sid}_{name}", list(shape), dtype))

    def psum(self, name, shape, dtype=F32):
        return self.stack.enter_context(self.nc.psum_tensor(f"s{self.sid}_{name}", list(shape), dtype))

    def tile(self, name=None):
        return T(name or "t")

    def tiles(self, n):
        return [T() for _ in range(n)]

    def _collect(self, eng, reads, writes):
        need = {}
        for t in reads:
            for k, v in t.writes.items():
                if need.get(k, 0) < v:
                    need[k] = v
        for t in writes:
            for d in (t.writes, t.reads):
                for k, v in d.items():
                    if need.get(k, 0) < v:
                        need[k] = v
        waits = []
        kn = self.known[eng]
        for k, v in need.items():
            if k == ("e", eng) and eng == "pe":
                continue
            if kn.get(k, 0) >= v:
                continue
            kn[k] = v
            waits.append((k, v))
            if k[0] == "e":
                self.milestones[k[1]].add(v)
        return waits

    def op(self, eng, fn, reads=(), writes=()):
        waits = self._collect(eng, reads, writes)
        self.cnt[eng] += 1
        idx = self.cnt[eng]
        key = ("e", eng)
        self.ops[eng].append(dict(kind="op", fn=fn, waits=waits, idx=idx))
        for t in reads:
            if t.reads.get(key, 0) < idx:
                t.reads[key] = idx
        for t in writes:
            t.writes = {key: idx}
            t.reads = {}

    def dma(self, eng, fn, reads=(), writes=()):
        i = self.ndma[eng]
        self.ndma[eng] += 1
        slot = i % N_DMA_SEMS
        gen = i // N_DMA_SEMS
        key = ("d", eng, slot)
        waits = self._collect(eng, reads, writes)
        kn = self.known[eng]
        if gen > 0 and kn.get(key, 0) < 16 * gen:
            kn[key] = 16 * gen
            waits.append((key, 16 * gen))
        val = 16 * (gen + 1)
        self.ops[eng].append(dict(kind="dma", fn=fn, waits=waits, key=key))
        for t in reads:
            if t.reads.get(key, 0) < val:
                t.reads[key] = val
        for t in writes:
            t.writes = {key: val}
            t.reads = {}

    def wait_all(self, eng, tiles):
        waits = self._collect(eng, tiles, ())
        self.ops[eng].append(dict(kind="wait", waits=waits))

    def finish(self, out_tiles):
        self.wait_all("sp", out_tiles)
        nc = self.nc
        with nc.cleanup_on_exit():
            sems = {}
            for e in ENGS:
                if self.milestones[e]:
                    sems[("e", e)] = nc.alloc_semaphore(f"p{self.sid}_{e}")
                for s in range(min(N_DMA_SEMS, self.ndma[e])):
                    sems[("d", e, s)] = nc.alloc_semaphore(f"d{self.sid}_{e}_{s}")
            mmap = {e: {v: i + 1 for i, v in enumerate(sorted(self.milestones[e]))} for e in ENGS}

            def replay(e):
                def body(engine):
                    for o in self.ops[e]:
                        for k, v in o["waits"]:
                            if k[0] == "e":
                                v = mmap[k[1]][v]
                            engine.wait_ge(sems[k], v)
                        if o["kind"] == "op":
                            ins = o["fn"](engine)
                            if o["idx"] in mmap[e]:
                                ins.then_inc(sems[("e", e)], 1)
                        elif o["kind"] == "dma":
                            ins = o["fn"](engine)
                            ins.then_inc(sems[o["key"]], 16)
                return body

            with nc.Block() as block:
                block.tensor(replay("pe"))
                block.scalar(replay("act"))
                block.vector(replay("dve"))
                block.gpsimd(replay("pool"))
                block.sync(replay("sp"))
            nc.all_engine_barrier()
        self.stack.close()


def run_pipeline(tasks, offsets):
    nph = len(offsets)
    for step in range(len(tasks) + max(offsets) + 1):
        for p in reversed(range(nph)):
            t = step - offsets[p]
            if 0 <= t < len(tasks) and tasks[t][p] is not None:
                tasks[t][p]()


def load_weight_bf16(P, dst, tdst, w_ap, kchunks, gain_sb=None, tgain=None, eng="dve"):
    n = w_ap.shape[-1]
    src = w_ap.rearrange("(c p) n -> p c n", p=128)
    step = max(1, 4096 // n)
    for c0 in range(0, kchunks, step):
        c1 = min(kchunks, c0 + step)
        P.dma("pool", lambda e, c0=c0, c1=c1: e.dma_start(out=dst[:, c0:c1, :], in_=src[:, c0:c1, :]), writes=[tdst])
    if gain_sb is not None:
        for c in range(kchunks):
            P.op(eng, lambda e, c=c: e.tensor_scalar(out=dst[:, c, :], in0=dst[:, c, :], scalar1=gain_sb[:, c:c + 1],
                                                      scalar2=None, op0=ALU.mult), reads=[tdst, tgain], writes=[tdst])


def load_gain(P, name, g_ap):
    g_sb = P.sbuf(name, [128, 8], F32)
    t = P.tile()
    P.dma("sp", lambda e: e.dma_start(out=g_sb[:], in_=g_ap.rearrange("(c p) -> p c", p=128), allow_slow_non_contiguous=True), writes=[t])
    return g_sb, t


class NormT:
    def __init__(self, P, ident, tident, n_xt=2, n_xnT=1, junk=None, tjunk=None):
        self.P = P
        self.ident, self.tident = ident, tident
        self.n_xt, self.n_xnT = n_xt, n_xnT
        self.xt = [P.sbuf(f"nt_xt{i}", [128, 4, D], F32) for i in range(n_xt)]
        self.txt = P.tiles(n_xt)
        if junk is None:
            self.junk = P.sbuf("nt_junk", [128, D], BF16)[:]
            self.tjunk = P.tile()
        else:
            self.junk, self.tjunk = junk, tjunk
        self.ss = P.sbuf("nt_ss", [128, 8], F32)
        self.tss = P.tile()
        self.xn = [P.sbuf(f"nt_xn{i}", [128, D], BF16) for i in range(4)]
        self.txn = P.tiles(4)
        self.pst = [P.psum(f"nt_pst{i}", [128, D], BF16) for i in range(2)]
        self.tpst = P.tiles(2)
        self.xnT = [P.sbuf(f"nt_xnT{i}", [128, 8, 512], BF16) for i in range(n_xnT)]
        self.txnT = P.tiles(n_xnT)
        self.k = 0
        self.n = 0
        self.n2 = 0
        self.kp = 0

    def part1(self, src_rows):
        P = self.P
        b = self.n % self.n_xt
        self.n += 1
        xt, txt = self.xt[b], self.txt[b]
        P.dma("sp", lambda e: e.dma_start(out=xt[:], in_=src_rows.rearrange("(t p) d -> p t d", p=128)), writes=[txt])
        ss, tss = self.ss, self.tss
        for t in range(4):
            xn, txn = self.xn[t], self.txn[t]
            self.k += 1
            col = (self.k % 4) * 2
            P.op("dve", lambda e, col=col: e.memset(ss[:, col:col + 1], 0.0), writes=[tss])
            P.op("act", lambda e, t=t, col=col: e.activation(out=self.junk, in_=xt[:, t, :], func=AF.Square,
                                                             accum_out=ss[:, col:col + 1]),
                 reads=[txt], writes=[self.tjunk, tss])
            P.op("dve", lambda e, col=col: e.tensor_scalar(out=ss[:, col + 1:col + 2], in0=ss[:, col:col + 1], scalar1=1.0 / D,
                                                           scalar2=EPS, op0=ALU.mult, op1=ALU.add), reads=[tss], writes=[tss])
            P.op("act", lambda e, col=col: e.activation(out=ss[:, col + 1:col + 2], in_=ss[:, col + 1:col + 2], func=AF.Sqrt),
                 reads=[tss], writes=[tss])
            P.op("dve", lambda e, col=col: e.reciprocal(out=ss[:, col + 1:col + 2], in_=ss[:, col + 1:col + 2]), reads=[tss], writes=[tss])
            P.op("act", lambda e, t=t, col=col, xn=xn: e.activation(out=xn[:], in_=xt[:, t, :], func=AF.Copy,
                                                                     scale=ss[:, col + 1:col + 2]),
                 reads=[txt, tss], writes=[txn])
        return xt, txt

    def part2(self):
        P = self.P
        b2 = self.n2 % self.n_xnT
        self.n2 += 1
        xnT, txnT = self.xnT[b2], self.txnT[b2]
        for t in range(4):
            xn, txn = self.xn[t], self.txn[t]
            kk = self.kp % 2
            self.kp += 1
            pst, tpst = self.pst[kk], self.tpst[kk]
            for c in range(8):
                P.op("pe", lambda e, c=c, xn=xn, pst=pst: e.transpose(out=pst[:, c * 128:(c + 1) * 128],
                                                                       in_=xn[:, c * 128:(c + 1) * 128], identity=self.ident[:]),
                     reads=[txn, self.tident], writes=[tpst])
            P.op("dve", lambda e, t=t, pst=pst, xnT=xnT: e.tensor_copy(out=xnT[:, :, t * 128:(t + 1) * 128],
                                                                       in_=pst[:].rearrange("p (c k) -> p c k", k=128)),
                 reads=[tpst], writes=[txnT])
        return xnT, txnT

    def run(self, src_rows):
        xt, txt = self.part1(src_rows)
        xnT, txnT = self.part2()
        return xt, txt, xnT, txnT


def load_ident(P, ident_ap):
    ident = P.sbuf("ident", [128, 128], BF16)
    t = P.tile()
    P.dma("pool", lambda e: e.dma_start(out=ident[:], in_=ident_ap), writes=[t])
    return ident, t


def stage_ffn(nc, h_in, h_out, g_ap, wup_ap, cw_ap, cb_ap, wdn_ap, ident_ap, n_seq, seq_len):
    P = Prog(nc)
    ident, tident = load_ident(P, ident_ap)
    g_sb, tg = load_gain(P, "g", g_ap)
    wup = P.sbuf("wup", [128, 8, 2 * DFF], BF16)
    twup = P.tile()
    load_weight_bf16(P, wup, twup, wup_ap, 8, g_sb, tg)
    wdn = P.sbuf("wdn", [128, NFT, D], BF16)
    twdn = P.tile()
    load_weight_bf16(P, wdn, twdn, wdn_ap, NFT)
    cw = P.sbuf("cw", [128, NFT, 3], F32)
    cb = P.sbuf("cb", [128, NFT], F32)
    tcw = P.tile()
    for j in range(3):
        P.dma("sp", lambda e, j=j: e.dma_start(out=cw[:, :, j], in_=cw_ap[j].rearrange("(f p) -> p f", p=128), allow_slow_non_contiguous=True), writes=[tcw])
    P.dma("sp", lambda e: e.dma_start(out=cb[:], in_=cb_ap.rearrange("(f p) -> p f", p=128), allow_slow_non_contiguous=True), writes=[tcw])
    gc = [P.sbuf(f"gc{i}", [128, 512], F32) for i in range(2)]
    tgc = P.tiles(2)
    nt = NormT(P, ident, tident, n_xt=1, n_xnT=2, junk=gc[1][:].bitcast(BF16), tjunk=tgc[1])
    halo = P.sbuf("halo", [128, NFT, 2], F32)
    thalo = P.tile()
    gbuf = [P.sbuf(f"gbuf{i}", [128, 514], F32) for i in range(2)]
    tgbuf = P.tiles(2)
    hid = P.sbuf("hid", [128, NFT, 512], BF16)
    thid = P.tile()
    pv = [P.psum(f"pv{i}", [128, 512]) for i in range(2)]
    tpv = P.tiles(2)
    pg = [P.psum(f"pg{i}", [128, 512]) for i in range(2)]
    tpg = P.tiles(2)
    po = [P.psum(f"po{i}", [128, 512]) for i in range(2)]
    tpo = P.tiles(2)
    xr = [P.sbuf(f"xr{i}", [128, D], F32) for i in range(1)] * 2
    txr = P.tiles(1) * 2
    outs = []
    nsup = seq_len // 512
    nblk = n_seq * nsup
    cnt = {"k": 0, "ko": 0, "kx": 0}
    xn_of = {}

    def phA1(u):
        r0 = u * 512
        nt.part1(h_in[r0:r0 + 512, :])

    def phA2(u):
        xn_of[u] = nt.part2()

    def phB(u):
        xnT, txnT = xn_of[u]
        if u % nsup == 0:
            P.op("pool", lambda e: e.memset(halo[:], 0.0), writes=[thalo])
        for ft in range(NFT):
            if ft == 2 and u + 1 < nblk:
                phA1(u + 1)
            b = cnt["k"] % 2
            cnt["k"] += 1
            for c in range(8):
                P.op("pe", lambda e, c=c, b=b, ft=ft: e.matmul(out=pv[b][:], lhsT=wup[:, c, ft * 128:(ft + 1) * 128], rhs=xnT[:, c, :],
                                                               start=(c == 0), stop=(c == 7)), reads=[twup, txnT], writes=[tpv[b]])
            for c in range(8):
                P.op("pe", lambda e, c=c, b=b, ft=ft: e.matmul(out=pg[b][:], lhsT=wup[:, c, DFF + ft * 128:DFF + (ft + 1) * 128], rhs=xnT[:, c, :],
                                                               start=(c == 0), stop=(c == 7)), reads=[twup, txnT], writes=[tpg[b]])
            gb, tgb, g2, tg2 = gbuf[b], tgbuf[b], gc[b], tgc[b]
            P.op("pool", lambda e, gb=gb, ft=ft: e.tensor_copy(out=gb[:, 0:2], in_=halo[:, ft, :]), reads=[thalo], writes=[tgb])
            P.op("act", lambda e, gb=gb, b=b: e.activation(out=gb[:, 2:514], in_=pg[b][:], func=AF.Copy), reads=[tpg[b]], writes=[tgb])
            P.op("pool", lambda e, gb=gb, ft=ft: e.tensor_copy(out=halo[:, ft, :], in_=gb[:, 512:514]), reads=[tgb], writes=[thalo])
            P.op("act", lambda e, gb=gb, g2=g2, ft=ft: e.activation(out=g2[:], in_=gb[:, 2:514], func=AF.Identity,
                                                                    scale=cw[:, ft, 2:3], bias=cb[:, ft:ft + 1]),
                 reads=[tgb, tcw], writes=[tg2])
            P.op("dve", lambda e, gb=gb, g2=g2, ft=ft: e.scalar_tensor_tensor(out=g2[:], in0=gb[:, 1:513], scalar=cw[:, ft, 1:2], in1=g2[:],
                                                                              op0=ALU.mult, op1=ALU.add), reads=[tgb, tcw, tg2], writes=[tg2])
            P.op("dve", lambda e, gb=gb, g2=g2, ft=ft: e.scalar_tensor_tensor(out=g2[:], in0=gb[:, 0:512], scalar=cw[:, ft, 0:1], in1=g2[:],
                                                                              op0=ALU.mult, op1=ALU.add), reads=[tgb, tcw, tg2], writes=[tg2])
            P.op("act", lambda e, g2=g2: e.activation(out=g2[:], in_=g2[:], func=AF.Silu), reads=[tg2], writes=[tg2])
            P.op("dve", lambda e, g2=g2, b=b, ft=ft: e.tensor_tensor(out=hid[:, ft, :], in0=g2[:], in1=pv[b][:], op=ALU.mult),
                 reads=[tg2, tpv[b]], writes=[thid])

    def phC(u):
        r0 = u * 512
        for t in range(4):
            xb = cnt["kx"] % 2
            cnt["kx"] += 1
            rr = r0 + t * 128
            P.dma("sp", lambda e, xb=xb, rr=rr: e.dma_start(out=xr[xb][:], in_=h_in[rr:rr + 128, :]), writes=[txr[xb]])
            for nb in range(2):
                b = cnt["ko"] % 2
                cnt["ko"] += 1
                for ft in range(NFT):
                    P.op("pe", lambda e, ft=ft, t=t, nb=nb, b=b: e.matmul(out=po[b][:], lhsT=hid[:, ft, t * 128:(t + 1) * 128],
                                                                          rhs=wdn[:, ft, nb * 512:(nb + 1) * 512],
                                                                          start=(ft == 0), stop=(ft == NFT - 1)),
                         reads=[thid, twdn], writes=[tpo[b]])
                P.op("dve", lambda e, nb=nb, b=b, xb=xb: e.tensor_tensor(out=xr[xb][:, nb * 512:(nb + 1) * 512], in0=po[b][:],
                                                                         in1=xr[xb][:, nb * 512:(nb + 1) * 512], op=ALU.add),
                     reads=[tpo[b], txr[xb]], writes=[txr[xb]])
            to = P.tile()
            P.dma("act", lambda e, xb=xb, rr=rr: e.dma_start(out=h_out[rr:rr + 128, :], in_=xr[xb][:]), reads=[txr[xb]], writes=[to])
            outs.append(to)

    phA1(0)
    phA2(0)
    for u in range(nblk):
        phB(u)
        if u + 1 < nblk:
            phA2(u + 1)
        phC(u)
    P.finish(outs)


def stage_kvq(nc, h_in, kT_rev, qT, v_rev, kvn_ap, wkv_ap, kn_ap, bn_ap, wq_ap, qn_ap, ident_ap, bones_ap, n_seq, S):
    P = Prog(nc)
    ident, tident = load_ident(P, ident_ap)
    bones = P.sbuf("bones", [128, 128], BF16)
    tbones = P.tile()
    P.dma("pool", lambda e: e.dma_start(out=bones[:], in_=bones_ap), writes=[tbones])
    gkv, tgkv = load_gain(P, "gkv", kvn_ap)
    gb, tgb = load_gain(P, "gb", bn_ap)
    wkv = P.sbuf("wkv", [128, 8, 2048], BF16)
    twkv = P.tile()
    load_weight_bf16(P, wkv, twkv, wkv_ap, 8, gkv, tgkv)
    wq = P.sbuf("wq", [128, 8, 1024], BF16)
    twq = P.tile()
    load_weight_bf16(P, wq, twq, wq_ap, 8, gb, tgb)
    hg = P.sbuf("hg", [128, 4], F32)
    thg = P.tile()
    for hf in range(2):
        P.dma("sp", lambda e, hf=hf: e.dma_start(out=hg[hf * 64:(hf + 1) * 64, 0:1], in_=kn_ap.rearrange("(p o) -> p o", o=1)), writes=[thg])
        P.dma("sp", lambda e, hf=hf: e.dma_start(out=hg[hf * 64:(hf + 1) * 64, 1:2], in_=qn_ap.rearrange("(p o) -> p o", o=1)), writes=[thg])
    P.op("dve", lambda e: e.tensor_scalar(out=hg[:, 1:2], in0=hg[:, 1:2], scalar1=0.125, scalar2=None, op0=ALU.mult), reads=[thg], writes=[thg])
    P.op("dve", lambda e: e.memset(hg[:, 2:3], EPS), reads=[thg], writes=[thg])
    nt = NormT(P, ident, tident)
    pk = [P.psum(f"pk{i}", [128, 512]) for i in range(3)]
    tpk = P.tiles(3)
    pss = [P.psum(f"pss{i}", [128, 512]) for i in range(3)]
    tpss = P.tiles(3)
    sq = [P.sbuf(f"sq{i}", [128, 512], BF16) for i in range(3)]
    tsq = P.tiles(3)
    rs = [P.sbuf(f"rs{i}", [128, 512], F32) for i in range(3)]
    trs = P.tiles(3)
    kbuf = [P.sbuf(f"kbuf{i}", [128, 8, 512], BF16) for i in range(2)]
    tkbuf = P.tiles(2)
    qbuf = [P.sbuf(f"qbuf{i}", [128, 8, 512], BF16) for i in range(2)]
    tqbuf = P.tiles(2)
    vbuf = [P.sbuf(f"vbuf{i}", [128, 4, 1024], BF16) for i in range(2)]
    tvbuf = P.tiles(2)
    xrev = P.sbuf("xrev", [128, 8, 512], BF16)
    txrev = P.tile()
    outs = []
    nsup = S // 512
    k = 0
    for s in range(n_seq):
        for u in range(nsup):
            r0 = s * S + u * 512
            rr = s * S + S - 512 * (u + 1)
            ob = (s * nsup + u) % 2
            if s == 0 and u == 0:
                nt.part1(h_in[r0:r0 + 512, :])
            xnT, txnT = nt.part2()
            if r0 + 512 < n_seq * S:
                nt.part1(h_in[r0 + 512:r0 + 1024, :])
            pend = None
            for which in range(2):
                for ft in range(8):
                    b = k % 3
                    k += 1
                    for c in range(8):
                        if which == 0:
                            P.op("pe", lambda e, c=c, b=b, ft=ft: e.matmul(out=pk[b][:], lhsT=wkv[:, c, ft * 128:(ft + 1) * 128], rhs=xnT[:, c, :],
                                                                           start=(c == 0), stop=(c == 7)), reads=[twkv, txnT], writes=[tpk[b]])
                        else:
                            P.op("pe", lambda e, c=c, b=b, ft=ft: e.matmul(out=pk[b][:], lhsT=wq[:, c, ft * 128:(ft + 1) * 128], rhs=xnT[:, c, :],
                                                                           start=(c == 0), stop=(c == 7)), reads=[twq, txnT], writes=[tpk[b]])
                    P.op("act", lambda e, b=b: e.activation(out=sq[b][:], in_=pk[b][:], func=AF.Square), reads=[tpk[b]], writes=[tsq[b]])

                    def tail(b=b, ft=ft, which=which, ob=ob):
                        P.op("pe", lambda e: e.matmul(out=pss[b][:], lhsT=bones[:], rhs=sq[b][:], start=True, stop=True),
                             reads=[tbones, tsq[b]], writes=[tpss[b]])
                        P.op("act", lambda e: e.activation(out=rs[b][:], in_=pss[b][:], func=AF.Ln, scale=1.0 / 64, bias=hg[:, 2:3]),
                             reads=[tpss[b], thg], writes=[trs[b]])
                        P.op("act", lambda e: e.activation(out=rs[b][:], in_=rs[b][:], func=AF.Exp, scale=-0.5), reads=[trs[b]], writes=[trs[b]])
                        if which == 0:
                            P.op("dve", lambda e: e.scalar_tensor_tensor(out=kbuf[ob][:, ft, ::-1], in0=pk[b][:], scalar=hg[:, 0:1], in1=rs[b][:],
                                                                         op0=ALU.mult, op1=ALU.mult),
                                 reads=[tpk[b], thg, trs[b]], writes=[tkbuf[ob]])
                        else:
                            P.op("dve", lambda e: e.scalar_tensor_tensor(out=qbuf[ob][:, ft, :], in0=pk[b][:], scalar=hg[:, 1:2], in1=rs[b][:],
                                                                         op0=ALU.mult, op1=ALU.mult),
                                 reads=[tpk[b], thg, trs[b]], writes=[tqbuf[ob]])

                    if pend is not None:
                        pend()
                    pend = tail
            pend()
            P.op("dve", lambda e, xnT=xnT: e.tensor_copy(out=xrev[:, :, ::-1], in_=xnT[:]), reads=[txnT], writes=[txrev])
            for t in range(4):
                for nb in range(2):
                    b = k % 3
                    k += 1
                    for c in range(8):
                        P.op("pe", lambda e, c=c, b=b, t=t, nb=nb: e.matmul(out=pk[b][:], lhsT=xrev[:, c, t * 128:(t + 1) * 128],
                                                                            rhs=wkv[:, c, 1024 + nb * 512:1024 + (nb + 1) * 512],
                                                                            start=(c == 0), stop=(c == 7)), reads=[twkv, txrev], writes=[tpk[b]])
                    P.op("act", lambda e, b=b, t=t, nb=nb, ob=ob: e.activation(out=vbuf[ob][:, t, nb * 512:(nb + 1) * 512], in_=pk[b][:], func=AF.Copy),
                         reads=[tpk[b]], writes=[tvbuf[ob]])
            t1, t2, t3 = P.tile(), P.tile(), P.tile()
            P.dma("act", lambda e, ob=ob, rr=rr: e.dma_start(out=kT_rev[:, rr:rr + 512].rearrange("(f p) n -> p f n", p=128), in_=kbuf[ob][:]),
                  reads=[tkbuf[ob]], writes=[t1])
            P.dma("act", lambda e, ob=ob, r0=r0: e.dma_start(out=qT[:, r0:r0 + 512].rearrange("(f p) n -> p f n", p=128), in_=qbuf[ob][:]),
                  reads=[tqbuf[ob]], writes=[t2])
            P.dma("act", lambda e, ob=ob, rr=rr: e.dma_start(out=v_rev[rr:rr + 512, :].rearrange("(t p) d -> p t d", p=128), in_=vbuf[ob][:]),
                  reads=[tvbuf[ob]], writes=[t3])
            outs += [t1, t2, t3]
    P.finish(outs)


def stage_att(nc, h_in, h_out, qT, kT_rev, v_rev, wo_ap, ident_ap, maskb_ap, n_seq, S):
    P = Prog(nc)
    ident, tident = load_ident(P, ident_ap)
    maskb = P.sbuf("maskb", [128, 128], BF16)
    tmask = P.tile()
    P.dma("pool", lambda e: e.dma_start(out=maskb[:], in_=maskb_ap), writes=[tmask])
    wo = P.sbuf("wo", [128, 8, 1024], BF16)
    two = P.tile()
    load_weight_bf16(P, wo, two, wo_ap, 8)
    zeros = P.sbuf("zeros", [128, 512], F32)
    onec = P.sbuf("onec", [128, 1], F32)
    tz = P.tile()
    P.op("pool", lambda e: e.memset(zeros[:], 0.0), writes=[tz])
    P.op("pool", lambda e: e.memset(onec[:], 1.0), reads=[tz], writes=[tz])
    NB = S // 128
    qsb = P.sbuf("qsb", [128, 8, S], BF16)
    ksb = P.sbuf("ksb", [128, 8, S], BF16)
    vsb = P.sbuf("vsb", [128, NB, 1024], BF16)
    oT = P.sbuf("oT", [128, 8, S], BF16)
    tq, tk, tv, toT = P.tile(), P.tile(), P.tile(), P.tile()
    NZ, NS, NPB, NW, NWT, NWS, NPO = 3, 4, 5, 4, 2, 4, 3
    pz = [P.psum(f"pz{i}", [128, 512]) for i in range(NZ)]
    tpz = P.tiles(NZ)
    pwT = [P.psum(f"pwT{i}", [128, 1024], BF16) for i in range(NWT)]
    tpwT = P.tiles(NWT)
    po = [P.psum(f"po{i}", [128, 512]) for i in range(NPO)]
    tpo = P.tiles(NPO)
    pw = pz[0:2]
    tpw = tpz[0:2]
    ssb = [P.sbuf(f"ssb{i}", [128, 512], F32) for i in range(NS)]
    tssb = P.tiles(NS)
    pbuf = [P.sbuf(f"pbuf{i}", [128, 513], F32) for i in range(NPB)]
    tpbuf = P.tiles(NPB)
    wsb = [P.sbuf(f"wsb{i}", [128, 512], BF16) for i in range(NW)]
    twsb = P.tiles(NW)
    wTs = [P.sbuf(f"wTs{i}", [128, 512], BF16) for i in range(NWS)]
    twTs = P.tiles(NWS)
    xt = [P.sbuf(f"xt{i}", [128, D], F32) for i in range(2)]
    txt = P.tiles(2)
    outs = []
    kk = 0
    kx = 0
    gt = 0
    gpo = 0
    for s in range(n_seq):
        c0 = s * S
        for f in range(0, 8, 2):
            P.dma("sp", lambda e, f=f, c0=c0: e.dma_start(out=qsb[:, f:f + 2, :], in_=qT[f * 128:(f + 2) * 128, c0:c0 + S].rearrange("(f p) n -> p f n", p=128)), writes=[tq])
            P.dma("act", lambda e, f=f, c0=c0: e.dma_start(out=ksb[:, f:f + 2, :], in_=kT_rev[f * 128:(f + 2) * 128, c0:c0 + S].rearrange("(f p) n -> p f n", p=128)), writes=[tk])
        for b4 in range(0, NB, 4):
            P.dma("sp", lambda e, b4=b4, c0=c0: e.dma_start(out=vsb[:, b4:b4 + 4, :], in_=v_rev[c0 + b4 * 128:c0 + (b4 + 4) * 128, :].rearrange("(t p) d -> p t d", p=128)), writes=[tv])
        tasks = []
        for ft in range(8):
            for i in range(NB):
                q0 = i * 128
                kp0 = S - 128 - q0
                klen = q0 + 128
                ob = gpo % NPO
                gpo += 1
                ntile = (klen + 511) // 512
                for hp in range(2):
                    h = ft * 2 + hp
                    ps = slice(hp * 64, (hp + 1) * 64)
                    for n in range(ntile):
                        col0 = kp0 + 512 * n
                        wdt = min(512, S - col0)
                        nblk = wdt // 128
                        g = gt
                        gt += 1
                        bz, bs_, bp, bw, bwt, bws = g % NZ, g % NS, g % NPB, g % NW, g % NWT, g % NWS
                        bpp = (g - 1) % NPB

                        def ph0(bz=bz, ft=ft, ps=ps, q0=q0, col0=col0, wdt=wdt, n=n):
                            P.op("pe", lambda e: e.matmul(out=pz[bz][:, 0:wdt], lhsT=qsb[ps, ft, q0:q0 + 128], rhs=ksb[ps, ft, col0:col0 + wdt],
                                                          start=True, stop=(n != 0)), reads=[tq, tk], writes=[tpz[bz]])
                            if n == 0:
                                P.op("pe", lambda e: e.matmul(out=pz[bz][:, 0:128], lhsT=ident[:], rhs=maskb[:], start=False, stop=True),
                                     reads=[tident, tmask], writes=[tpz[bz]])

                        def ph1(bz=bz, bs_=bs_, wdt=wdt):
                            P.op("act", lambda e: e.activation(out=ssb[bs_][:, 0:wdt], in_=pz[bz][:, 0:wdt], func=AF.Sigmoid, scale=-1.0),
                                 reads=[tpz[bz]], writes=[tssb[bs_]])

                        def ph2(bs_=bs_, bp=bp, bpp=bpp, wdt=wdt, n=n):
                            if n == 0:
                                P.op("act", lambda e: e.activation(out=pbuf[bp][:, 0:1], in_=onec[:, 0:1], func=AF.Copy), reads=[tz], writes=[tpbuf[bp]])
                                init = onec[:, 0:1]
                                rd = [tssb[bs_], tz, tpbuf[bp]]
                            else:
                                P.op("act", lambda e: e.activation(out=pbuf[bp][:, 0:1], in_=pbuf[bpp][:, 512:513], func=AF.Copy), reads=[tpbuf[bpp]], writes=[tpbuf[bp]])
                                init = pbuf[bpp][:, 512:513]
                                rd = [tssb[bs_], tz, tpbuf[bp], tpbuf[bpp]]
                            P.op("dve", lambda e: e.tensor_tensor_scan(out=pbuf[bp][:, 1:1 + wdt], data0=ssb[bs_][:, 0:wdt], data1=zeros[:, 0:wdt],
                                                                       initial=init, op0=ALU.mult, op1=ALU.add), reads=rd, writes=[tpbuf[bp]])

                        def ph3(bp=bp, bw=bw, wdt=wdt):
                            P.op("dve", lambda e: e.tensor_tensor(out=wsb[bw][:, 0:wdt], in0=pbuf[bp][:, 0:wdt], in1=pbuf[bp][:, 1:1 + wdt], op=ALU.subtract),
                                 reads=[tpbuf[bp]], writes=[twsb[bw]])

                        def ph4(bw=bw, bwt=bwt, nblk=nblk):
                            for jb in range(nblk):
                                P.op("pe", lambda e, jb=jb: e.transpose(out=pwT[bwt][:, jb * 128:(jb + 1) * 128], in_=wsb[bw][:, jb * 128:(jb + 1) * 128], identity=ident[:]),
                                     reads=[twsb[bw], tident], writes=[tpwT[bwt]])

                        def ph5(bwt=bwt, bws=bws, wdt=wdt, g=g):
                            if True:
                                P.op("act", lambda e: e.activation(out=wTs[bws][:, 0:wdt], in_=pwT[bwt][:, 0:wdt], func=AF.Copy), reads=[tpwT[bwt]], writes=[twTs[bws]])
                            else:
                                P.op("dve", lambda e: e.tensor_copy(out=wTs[bws][:, 0:wdt], in_=pwT[bwt][:, 0:wdt]), reads=[tpwT[bwt]], writes=[twTs[bws]])

                        def ph6(bws=bws, nblk=nblk, col0=col0, h=h, ps=ps, ob=ob, n=n, ntile=ntile, hp=hp, ft=ft, q0=q0):
                            for jb in range(nblk):
                                blk = col0 // 128 + jb
                                first = (n == 0 and jb == 0)
                                last = (n == ntile - 1 and jb == nblk - 1)
                                P.op("pe", lambda e, jb=jb, blk=blk, first=first, last=last: e.matmul(
                                    out=po[ob][ps, 0:128], lhsT=vsb[:, blk, h * 64:(h + 1) * 64], rhs=wTs[bws][:, jb * 128:(jb + 1) * 128], start=first, stop=last),
                                    reads=[tv, twTs[bws]], writes=[tpo[ob]])
                            if hp == 1 and n == ntile - 1:
                                P.op("act", lambda e: e.activation(out=oT[:, ft, q0:q0 + 128], in_=po[ob][:, 0:128], func=AF.Copy), reads=[tpo[ob]], writes=[toT])

                        tasks.append([ph0, ph1, ph2, ph3, ph4, ph5, ph6])
        run_pipeline(tasks, ATT_OFFS)
        for t in range(NB):
            r0 = c0 + t * 128
            xb = kx % 2
            kx += 1
            P.dma("sp", lambda e, xb=xb, r0=r0: e.dma_start(out=xt[xb][:], in_=h_in[r0:r0 + 128, :]), writes=[txt[xb]])
            for nb in range(2):
                b = kk % 2
                kk += 1
                for ft in range(8):
                    P.op("pe", lambda e, b=b, ft=ft, t=t, nb=nb: e.matmul(out=pw[b][:], lhsT=oT[:, ft, t * 128:(t + 1) * 128], rhs=wo[:, ft, nb * 512:(nb + 1) * 512],
                                                                          start=(ft == 0), stop=(ft == 7)), reads=[toT, two], writes=[tpw[b]])
                P.op("dve", lambda e, b=b, xb=xb, nb=nb: e.tensor_tensor(out=xt[xb][:, nb * 512:(nb + 1) * 512], in0=pw[b][:], in1=xt[xb][:, nb * 512:(nb + 1) * 512], op=ALU.add),
                     reads=[tpw[b], txt[xb]], writes=[txt[xb]])
            to = P.tile()
            P.dma("act", lambda e, xb=xb, r0=r0: e.dma_start(out=h_out[r0:r0 + 128, :], in_=xt[xb][:]), reads=[txt[xb]], writes=[to])
            outs.append(to)
    P.finish(outs)


def stage_a1(nc, h_in, uT, g_ap, win_ap, ident_ap, n_tok, prep=None):
    P = Prog(nc)
    if prep is not None:
        prep(P)
    ident, tident = load_ident(P, ident_ap)
    g_sb, tg = load_gain(P, "g", g_ap)
    win = P.sbuf("win", [128, 8, 1024], BF16)
    twin = P.tile()
    load_weight_bf16(P, win, twin, win_ap, 8, g_sb, tg)
    nt = NormT(P, ident, tident)
    pu = [P.psum(f"pu{i}", [128, 512]) for i in range(2)]
    tpu = P.tiles(2)
    ubuf = [P.sbuf(f"ubuf{i}", [128, 8, 512], BF16) for i in range(2)]
    tubuf = P.tiles(2)
    outs = []
    k = 0
    nblk = n_tok // 512
    nt.part1(h_in[0:512, :])
    for u in range(nblk):
        r0 = u * 512
        ob = u % 2
        xnT, txnT = nt.part2()
        if u + 1 < nblk:
            nt.part1(h_in[r0 + 512:r0 + 1024, :])
        for m in range(8):
            b = k % 2
            k += 1
            for c in range(8):
                P.op("pe", lambda e, c=c, b=b, m=m: e.matmul(out=pu[b][:], lhsT=win[:, c, m * 128:(m + 1) * 128], rhs=xnT[:, c, :],
                                                             start=(c == 0), stop=(c == 7)), reads=[twin, txnT], writes=[tpu[b]])
            if m % 2 == 0:
                P.op("act", lambda e, b=b, m=m, ob=ob: e.activation(out=ubuf[ob][:, m, :], in_=pu[b][:], func=AF.Copy), reads=[tpu[b]], writes=[tubuf[ob]])
            else:
                P.op("dve", lambda e, b=b, m=m, ob=ob: e.tensor_copy(out=ubuf[ob][:, m, :], in_=pu[b][:]), reads=[tpu[b]], writes=[tubuf[ob]])
        to = P.tile()
        P.dma("act", lambda e, ob=ob, r0=r0: e.dma_start(out=uT[:, r0:r0 + 512].rearrange("(m p) n -> p m n", p=128), in_=ubuf[ob][:]),
              reads=[tubuf[ob]], writes=[to])
        outs.append(to)
    P.finish(outs)


TWO_PI = 6.283185307179586
ATT_OFFS = [0, 2, 4, 6, 8, 9, 11]


def a2_prep(P, alloc, lre_ap, lim_ap, bre_ap, bim_ap, cre_ap, cim_ap, d_ap, ldt_ap, iota_ap, S, ident_ap=None):
    I32 = mybir.dt.int32
    NP = 32
    lr = alloc("lr", [128, NP], F32)
    li = alloc("li", [128, NP], F32)
    dt = alloc("dt", [128, NP], F32)
    tprm = P.tile()
    for e_ in range(2):
        ps = slice(e_ * 64, (e_ + 1) * 64)
        P.dma("sp", lambda e, e_=e_, ps=ps: e.dma_start(out=lr[ps, :], in_=lre_ap.rearrange("(k e) n -> e n k", e=2)[e_], allow_slow_non_contiguous=True), writes=[tprm])
        P.dma("sp", lambda e, e_=e_, ps=ps: e.dma_start(out=li[ps, :], in_=lim_ap.rearrange("(k e) n -> e n k", e=2)[e_], allow_slow_non_contiguous=True), writes=[tprm])
        P.dma("sp", lambda e, e_=e_, ps=ps: e.dma_start(out=dt[ps, :], in_=ldt_ap.rearrange("(k e) -> e k", e=2)[e_].partition_broadcast(64), allow_slow_non_contiguous=True), writes=[tprm])
    cst = alloc("cst", [128, 4], F32)
    tcst = P.tile()
    P.op("dve", lambda e: e.memset(cst[:, 0:1], TWO_PI / 4), writes=[tcst])
    P.op("dve", lambda e: e.memset(cst[:, 1:2], 0.0), reads=[tcst], writes=[tcst])
    sc = {}
    for nm in ["f", "fhi", "flo", "r", "t0", "t1", "t2", "t3", "sn", "cs", "ar", "ai", "qre", "qim", "nqim", "den"]:
        sc[nm] = alloc("p_" + nm, [128, NP], F32)
    fhb = alloc("p_fhb", [128, NP], BF16)
    tiq = alloc("p_ti", [128, NP], I32)
    tsc = P.tile()

    def V(fn, eng="dve", extra=()):
        P.op(eng, fn, reads=[tprm, tsc, tcst] + list(extra), writes=[tsc])

    V(lambda e: e.activation(out=sc["t0"][:], in_=dt[:], func=AF.Exp), "act")
    V(lambda e: e.activation(out=sc["t1"][:], in_=sc["t0"][:], func=AF.Ln), "act")
    V(lambda e: e.tensor_tensor(out=sc["t1"][:], in0=dt[:], in1=sc["t1"][:], op=ALU.subtract))
    V(lambda e: e.tensor_scalar(out=sc["t1"][:], in0=sc["t1"][:], scalar1=1.0, scalar2=None, op0=ALU.add))
    V(lambda e: e.tensor_tensor(out=dt[:], in0=sc["t0"][:], in1=sc["t1"][:], op=ALU.mult))
    V(lambda e: e.tensor_tensor(out=sc["t0"][:], in0=li[:], in1=dt[:], op=ALU.mult))
    V(lambda e: e.tensor_scalar(out=sc["f"][:], in0=sc["t0"][:], scalar1=1.0 / TWO_PI, scalar2=None, op0=ALU.mult))
    V(lambda e: e.tensor_copy(out=fhb[:], in_=sc["f"][:]))
    V(lambda e: e.tensor_copy(out=sc["fhi"][:], in_=fhb[:]))
    V(lambda e: e.tensor_tensor(out=sc["flo"][:], in0=sc["f"][:], in1=sc["fhi"][:], op=ALU.subtract))
    V(lambda e: e.tensor_tensor(out=sc["t1"][:], in0=lr[:], in1=dt[:], op=ALU.mult))
    V(lambda e: e.activation(out=sc["t2"][:], in_=sc["t1"][:], func=AF.Exp), "act")
    V(lambda e: e.activation(out=sc["t3"][:], in_=sc["t2"][:], func=AF.Ln), "act")
    V(lambda e: e.tensor_tensor(out=sc["t3"][:], in0=sc["t1"][:], in1=sc["t3"][:], op=ALU.subtract))
    V(lambda e: e.tensor_scalar(out=sc["t3"][:], in0=sc["t3"][:], scalar1=1.0, scalar2=None, op0=ALU.add))
    V(lambda e: e.tensor_tensor(out=sc["r"][:], in0=sc["t2"][:], in1=sc["t3"][:], op=ALU.mult))
    V(lambda e: e.tensor_copy(out=tiq[:], in_=sc["f"][:]))
    V(lambda e: e.tensor_copy(out=sc["t3"][:], in_=tiq[:]))
    V(lambda e: e.tensor_tensor(out=sc["t2"][:], in0=sc["f"][:], in1=sc["t3"][:], op=ALU.subtract))
    V(lambda e: e.activation(out=sc["sn"][:], in_=sc["t2"][:], func=AF.Sin, scale=TWO_PI), "act")
    V(lambda e: e.tensor_scalar(out=sc["t3"][:], in0=sc["t2"][:], scalar1=-1.0, scalar2=None, op0=ALU.mult))
    V(lambda e: e.tensor_tensor(out=sc["t3"][:], in0=sc["t3"][:], in1=sc["t2"][:], op=ALU.min))
    V(lambda e: e.activation(out=sc["cs"][:], in_=sc["t3"][:], func=AF.Sin, scale=TWO_PI, bias=cst[:, 0:1]), "act")
    V(lambda e: e.tensor_tensor(out=sc["ar"][:], in0=sc["r"][:], in1=sc["cs"][:], op=ALU.mult))
    V(lambda e: e.tensor_tensor(out=sc["ai"][:], in0=sc["r"][:], in1=sc["sn"][:], op=ALU.mult))
    V(lambda e: e.tensor_scalar(out=sc["ar"][:], in0=sc["ar"][:], scalar1=-1.0, scalar2=None, op0=ALU.add))
    V(lambda e: e.tensor_tensor(out=sc["t0"][:], in0=lr[:], in1=lr[:], op=ALU.mult))
    V(lambda e: e.tensor_tensor(out=sc["t1"][:], in0=li[:], in1=li[:], op=ALU.mult))
    V(lambda e: e.tensor_tensor(out=sc["den"][:], in0=sc["t0"][:], in1=sc["t1"][:], op=ALU.add))
    V(lambda e: e.reciprocal(out=sc["den"][:], in_=sc["den"][:]))
    V(lambda e: e.tensor_tensor(out=sc["t0"][:], in0=sc["ar"][:], in1=lr[:], op=ALU.mult))
    V(lambda e: e.tensor_tensor(out=sc["t1"][:], in0=sc["ai"][:], in1=li[:], op=ALU.mult))
    V(lambda e: e.tensor_tensor(out=sc["t0"][:], in0=sc["t0"][:], in1=sc["t1"][:], op=ALU.add))
    V(lambda e: e.tensor_tensor(out=sc["qre"][:], in0=sc["t0"][:], in1=sc["den"][:], op=ALU.mult))
    V(lambda e: e.tensor_tensor(out=sc["t0"][:], in0=sc["ai"][:], in1=lr[:], op=ALU.mult))
    V(lambda e: e.tensor_tensor(out=sc["t1"][:], in0=sc["ar"][:], in1=li[:], op=ALU.mult))
    V(lambda e: e.tensor_tensor(out=sc["t0"][:], in0=sc["t0"][:], in1=sc["t1"][:], op=ALU.subtract))
    V(lambda e: e.tensor_tensor(out=sc["qim"][:], in0=sc["t0"][:], in1=sc["den"][:], op=ALU.mult))
    V(lambda e: e.tensor_scalar(out=sc["nqim"][:], in0=sc["qim"][:], scalar1=-1.0, scalar2=None, op0=ALU.mult))
    craw = alloc("craw", [128, NP, 2, 16], F32)
    tcraw = P.tile()
    for e_ in range(2):
        ps = slice(e_ * 64, (e_ + 1) * 64)
        for k1 in range(NP):
            for ri, ap_ in enumerate((cre_ap, cim_ap)):
                P.dma("sp" if ri == 0 else "act", lambda e, e_=e_, ps=ps, k1=k1, ri=ri, ap_=ap_: e.dma_start(
                    out=craw[ps, k1, ri, :], in_=ap_[2 * k1 + e_].rearrange("h n -> n h"), allow_slow_non_contiguous=True),
                    writes=[tcraw])
    CT = alloc("CT", [128, NP, 3, 128], BF16)
    tCT = P.tile()
    ctmp = alloc("ctmp", [128, NP, 16], F32)
    ctmp2 = alloc("ctmp2", [128, NP, 16], F32)
    tctmp = P.tile()
    P.op("pool", lambda e: e.memset(CT[:], 0.0), writes=[tCT])

    def bq(nm):
        return sc[nm][:].unsqueeze(2).to_broadcast([128, NP, 16])

    rd = [tcraw, tsc]
    P.op("dve", lambda e: e.tensor_tensor(out=ctmp[:], in0=craw[:, :, 0, :], in1=bq("qre"), op=ALU.mult), reads=rd, writes=[tctmp])
    P.op("dve", lambda e: e.tensor_tensor(out=ctmp2[:], in0=craw[:, :, 1, :], in1=bq("qim"), op=ALU.mult), reads=rd, writes=[tctmp])
    for e_ in range(2):
        ps = slice(e_ * 64, (e_ + 1) * 64)
        P.op("dve", lambda e, e_=e_, ps=ps: e.tensor_tensor(out=CT[ps, :, 0, e_ * 16:(e_ + 1) * 16], in0=ctmp[ps], in1=ctmp2[ps], op=ALU.subtract),
             reads=[tctmp], writes=[tCT])
    P.op("dve", lambda e: e.tensor_tensor(out=ctmp[:], in0=craw[:, :, 0, :], in1=bq("nqim"), op=ALU.mult), reads=rd + [tCT], writes=[tctmp])
    P.op("dve", lambda e: e.tensor_tensor(out=ctmp2[:], in0=craw[:, :, 1, :], in1=bq("qre"), op=ALU.mult), reads=rd, writes=[tctmp])
    for e_ in range(2):
        ps = slice(e_ * 64, (e_ + 1) * 64)
        P.op("dve", lambda e, e_=e_, ps=ps: e.tensor_tensor(out=CT[ps, :, 1, e_ * 16:(e_ + 1) * 16], in0=ctmp[ps], in1=ctmp2[ps], op=ALU.subtract),
             reads=[tctmp], writes=[tCT])
    BT = alloc("BT", [32, NP, 2, 128], BF16)
    tBT = P.tile()
    P.op("pool", lambda e: e.memset(BT[:], 0.0), writes=[tBT])
    for e_ in range(2):
        for ri, ap_ in enumerate((bre_ap, bim_ap)):
            for k1 in range(NP):
                P.dma("pool", lambda e, e_=e_, ri=ri, ap_=ap_, k1=k1: e.dma_start(
                    out=BT[e_ * 16:(e_ + 1) * 16, k1, ri, e_ * 64:(e_ + 1) * 64], in_=ap_[2 * k1 + e_].rearrange("n h -> h n"),
                    allow_slow_non_contiguous=True), writes=[tBT])
    dwin = alloc("dwin", [32, NP], F32)
    tdw = P.tile()
    P.dma("sp", lambda e: e.dma_start(out=dwin[:], in_=d_ap.rearrange("(k r) -> r k", r=32), allow_slow_non_contiguous=True), writes=[tdw])
    iota = alloc("iota", [128, S], F32)
    tio = P.tile()
    P.dma("sp", lambda e: e.dma_start(out=iota[:], in_=iota_ap[:, 0:S]), writes=[tio])
    id32 = alloc("id32", [32, 128], F32)
    Dg = alloc("Dg", [32, NP, 128], BF16)
    tDg = P.tile()
    P.dma("sp", lambda e: e.dma_start(out=id32[:], in_=ident_ap[0:32, :]), writes=[tDg])
    for k in range(NP):
        P.op("dve", lambda e, k=k: e.tensor_scalar(out=Dg[:, k, :], in0=id32[:], scalar1=dwin[:, k:k + 1], scalar2=None, op0=ALU.mult),
             reads=[tDg, tdw], writes=[tDg])
    return dict(Dg=Dg, tDg=tDg, sc=sc, tsc=tsc, CT=CT, tCT=tCT, BT=BT, tBT=tBT, dwin=dwin, tdw=tdw, iota=iota, tio=tio, cst=cst, tcst=tcst)


def stage_a2(nc, uT, gT, lre_ap, lim_ap, bre_ap, bim_ap, cre_ap, cim_ap, d_ap, ldt_ap, iota_ap, n_seq, S, dbg_pairs=None, dbg=None, dbg_stop=9, ctx=None, ident_ap=None):
    P = Prog(nc)
    I32 = mybir.dt.int32
    NP = 32
    if ctx is None:
        ctx = a2_prep(P, P.sbuf, lre_ap, lim_ap, bre_ap, bim_ap, cre_ap, cim_ap, d_ap, ldt_ap, iota_ap, S, ident_ap=ident_ap)
    else:
        ctx = dict(ctx)
        for tn in ("tsc", "tCT", "tBT", "tdw", "tio", "tcst", "tDg"):
            ctx[tn] = P.tile()
    sc, tsc, CT, tCT, BT, tBT = ctx["sc"], ctx["tsc"], ctx["CT"], ctx["tCT"], ctx["BT"], ctx["tBT"]
    dwin, tdw, iota, tio, cst, tcst = ctx["dwin"], ctx["tdw"], ctx["iota"], ctx["tio"], ctx["cst"], ctx["tcst"]
    Dg, tDg = ctx["Dg"], ctx["tDg"]
    HB = 512
    nq = S // HB
    snb = [P.sbuf(f"snb{i}", [128, S], BF16) for i in range(2)]
    csb = [P.sbuf(f"csb{i}", [128, S], BF16) for i in range(2)]
    rtab = [P.sbuf(f"rtab{i}", [128, HB], F32) for i in range(2)]
    ttab = P.tiles(2)
    SH = S // 2
    wk1 = P.sbuf("wk1", [128, SH], F32)
    wk2 = P.sbuf("wk2", [128, SH], F32)
    tiw = P.sbuf("tiw", [128, SH], I32)
    twk = P.tile()
    ones = P.sbuf("ones", [128, HB], F32)
    zc = P.sbuf("zc", [128, 1], F32)
    tones = P.tile()
    P.op("dve", lambda e: e.memset(ones[:], 1.0), writes=[tones])
    P.op("dve", lambda e: e.memset(zc[:], 0.0), reads=[tones], writes=[tones])
    P.op("dve", lambda e: e.tensor_scalar(out=CT[:, :, 2, :], in0=CT[:, :, 0, :], scalar1=-1.0, scalar2=None, op0=ALU.mult), reads=[tCT], writes=[tCT])
    NU = 3
    uwin = [P.sbuf(f"uwin{i}", [32, S], BF16) for i in range(NU)]
    tuw = P.tiles(NU)
    pbr = [P.psum(f"pbr{i}", [128, HB]) for i in range(1)] * 2
    pbi = [P.psum(f"pbi{i}", [128, HB]) for i in range(1)] * 2
    tpb = P.tiles(1) * 2
    pW = [[P.psum(f"pW{i}_{j}", [128, HB]) for j in range(2)] for i in range(2)]
    tpW = P.tiles(2)
    idb = P.sbuf("idb", [128, 2, 128], BF16)
    tidb = P.tile()
    P.dma("pool", lambda e: e.dma_start(out=idb[:, 0, :], in_=ident_ap), writes=[tidb])
    P.op("act", lambda e: e.activation(out=idb[:, 1, :], in_=idb[:, 0, :], func=AF.Copy, scale=-1.0), reads=[tidb], writes=[tidb])
    py = [P.psum(f"py{i}", [128, HB]) for i in range(2)]
    tpy = P.tiles(2)
    bsb = [[P.sbuf(f"bsb{i}_{j}", [128, HB], BF16) for j in range(2)] for i in range(2)]
    tbsb = P.tiles(2)
    A = [[P.sbuf(f"A{i}_{j}", [128, HB], BF16) for j in range(4)] for i in range(2)]
    tA01 = P.tiles(2)
    tA23 = P.tiles(2)
    W = [[P.sbuf(f"W{i}_{j}", [128, HB], F32) for j in range(2)] for i in range(2)]
    tW = P.tiles(2)
    NZb = 3
    Z = [[P.sbuf(f"Z{i}_{j}", [128, HB], F32) for j in range(2)] for i in range(NZb)]
    tZ = P.tiles(NZb)
    Zb = [[P.sbuf(f"Zb{i}_{j}", [128, HB], BF16) for j in range(2)] for i in range(2)]
    tZb = P.tiles(2)
    Bq = [[P.sbuf(f"B{i}_{j}", [128, HB], BF16) for j in range(4)] for i in range(2)]
    tB = P.tiles(2)
    tB2 = P.tiles(2)
    ytmp = [P.sbuf(f"ytmp{i}", [32, HB], F32) for i in range(2)]
    tyt = P.tiles(2)
    gout = [P.sbuf(f"gout{i}", [32, S], BF16) for i in range(2)]
    tgo = P.tiles(2)
    outs = []
    npairs = NP if dbg_pairs is None else dbg_pairs

    def gen_tables(k):
        tb = k % 2
        fh, fl, rk = sc["fhi"][:, k:k + 1], sc["flo"][:, k:k + 1], sc["r"][:, k:k + 1]
        for hh in range(2):
            hs = slice(hh * SH, (hh + 1) * SH)
            P.op("act", lambda e, hs=hs: e.activation(out=wk1[:], in_=iota[:, hs], func=AF.Copy, scale=fh), reads=[tio, tsc], writes=[twk])
            P.op("act", lambda e: e.activation(out=tiw[:], in_=wk1[:], func=AF.Copy), reads=[twk], writes=[twk])
            P.op("act", lambda e: e.activation(out=wk2[:], in_=tiw[:], func=AF.Copy), reads=[twk], writes=[twk])
            P.op("dve", lambda e: e.tensor_tensor(out=wk1[:], in0=wk1[:], in1=wk2[:], op=ALU.subtract), reads=[twk], writes=[twk])
            P.op("dve", lambda e, hs=hs: e.scalar_tensor_tensor(out=wk2[:], in0=iota[:, hs], scalar=fl, in1=wk1[:], op0=ALU.mult, op1=ALU.add), reads=[tio, tsc, twk], writes=[twk])
            P.op("act", lambda e: e.activation(out=tiw[:], in_=wk2[:], func=AF.Copy), reads=[twk], writes=[twk])
            P.op("act", lambda e: e.activation(out=wk1[:], in_=tiw[:], func=AF.Copy), reads=[twk], writes=[twk])
            P.op("dve", lambda e: e.tensor_tensor(out=wk2[:], in0=wk2[:], in1=wk1[:], op=ALU.subtract), reads=[twk], writes=[twk])
            P.op("act", lambda e, hs=hs: e.activation(out=snb[tb][:, hs], in_=wk2[:], func=AF.Sin, scale=TWO_PI), reads=[twk], writes=[ttab[tb]])
            P.op("act", lambda e: e.activation(out=wk1[:], in_=wk2[:], func=AF.Copy, scale=-1.0), reads=[twk], writes=[twk])
            P.op("dve", lambda e: e.tensor_tensor(out=wk1[:], in0=wk1[:], in1=wk2[:], op=ALU.min), reads=[twk], writes=[twk])
            P.op("act", lambda e, hs=hs: e.activation(out=csb[tb][:, hs], in_=wk1[:], func=AF.Sin, scale=TWO_PI, bias=cst[:, 0:1]), reads=[twk, tcst], writes=[ttab[tb]])
        P.op("dve", lambda e: e.tensor_scalar(out=rtab[tb][:], in0=ones[:], scalar1=rk, scalar2=None, op0=ALU.mult), reads=[tones, tsc, ttab[tb]], writes=[ttab[tb]])

    if npairs > 0:
        gen_tables(0)
    tasks = []
    g = 0
    for k in range(npairs):
        row0 = k * 32
        tb = k % 2
        for s in range(n_seq):
            c0 = s * S
            ub = (k * n_seq + s) % NU
            gb = (k * n_seq + s) % 2
            for qi in range(nq):
                t0 = qi * HB
                tsl = slice(t0, t0 + HB)
                b2 = g % 2
                bz = g % NZb
                bzp = (g - 1) % NZb
                g += 1

                def p0(k=k, s=s, qi=qi, ub=ub, row0=row0, c0=c0):
                    if qi == 0:
                        P.dma("sp", lambda e: e.dma_start(out=uwin[ub][:], in_=uT[row0:row0 + 32, c0:c0 + S]), writes=[tuw[ub]])
                    tpp = n_seq * nq
                    ti = s * nq + qi
                    assert tpp >= 8, "table double-buffering needs >= 8 tasks per pair"
                    if ti == 7 and k + 1 < npairs:
                        gen_tables(k + 1)

                def p1(k=k, ub=ub, b2=b2, tsl=tsl):
                    P.op("pe", lambda e: e.matmul(out=pbr[b2][:], lhsT=BT[:, k, 0, :], rhs=uwin[ub][:, tsl], start=True, stop=True),
                         reads=[tBT, tuw[ub]], writes=[tpb[b2]])
                    P.op("pe", lambda e: e.matmul(out=pbi[b2][:], lhsT=BT[:, k, 1, :], rhs=uwin[ub][:, tsl], start=True, stop=True),
                         reads=[tBT, tuw[ub]], writes=[tpb[b2]])

                def p2(b2=b2):
                    P.op("act", lambda e: e.activation(out=bsb[b2][0][:], in_=pbr[b2][:], func=AF.Copy), reads=[tpb[b2]], writes=[tbsb[b2]])
                    P.op("act", lambda e: e.activation(out=bsb[b2][1][:], in_=pbi[b2][:], func=AF.Copy), reads=[tpb[b2]], writes=[tbsb[b2]])

                def p3(b2=b2, tb=tb, tsl=tsl):
                    rd = [ttab[tb], tbsb[b2]]
                    P.op("dve", lambda e: e.tensor_tensor(out=A[b2][0][:], in0=csb[tb][:, tsl], in1=bsb[b2][0][:], op=ALU.mult), reads=rd, writes=[tA01[b2]])
                    P.op("dve", lambda e: e.tensor_tensor(out=A[b2][1][:], in0=snb[tb][:, tsl], in1=bsb[b2][1][:], op=ALU.mult), reads=rd, writes=[tA01[b2]])
                    P.op("dve", lambda e: e.tensor_tensor(out=A[b2][2][:], in0=csb[tb][:, tsl], in1=bsb[b2][1][:], op=ALU.mult), reads=rd, writes=[tA23[b2]])
                    P.op("dve", lambda e: e.tensor_tensor(out=A[b2][3][:], in0=snb[tb][:, tsl], in1=bsb[b2][0][:], op=ALU.mult), reads=rd, writes=[tA23[b2]])

                def p4(b2=b2):
                    for j, (x0, x1, sg) in enumerate(((0, 1, 0), (2, 3, 1))):
                        P.op("pe", lambda e, j=j, x0=x0: e.matmul(out=pW[b2][j][:], lhsT=idb[:, 0, :], rhs=A[b2][x0][:], start=True, stop=False),
                             reads=[tidb, tA01[b2], tA23[b2]], writes=[tpW[b2]])
                        P.op("pe", lambda e, j=j, x1=x1, sg=sg: e.matmul(out=pW[b2][j][:], lhsT=idb[:, sg, :], rhs=A[b2][x1][:], start=False, stop=True),
                             reads=[tidb, tA01[b2], tA23[b2]], writes=[tpW[b2]])

                def p5(b2=b2, bz=bz, bzp=bzp, tb=tb, qi=qi):
                    for j in range(2):
                        if qi == 0:
                            init, rd = zc[:, 0:1], [tpW[b2], ttab[tb], tones]
                        else:
                            init, rd = Z[bzp][j][:, HB - 1:HB], [tpW[b2], ttab[tb], tZ[bzp]]
                        P.op("dve", lambda e, j=j, init=init: e.tensor_tensor_scan(out=Z[bz][j][:], data0=rtab[tb][:], data1=pW[b2][j][:], initial=init,
                                                                                   op0=ALU.mult, op1=ALU.add), reads=rd, writes=[tZ[bz]])

                def p6(b2=b2, bz=bz):
                    for j in range(2):
                        P.op("act", lambda e, j=j: e.activation(out=Zb[b2][j][:], in_=Z[bz][j][:], func=AF.Copy), reads=[tZ[bz]], writes=[tZb[b2]])

                def p7(b2=b2, tb=tb, tsl=tsl):
                    rd = [ttab[tb], tZb[b2]]
                    P.op("dve", lambda e: e.tensor_tensor(out=Bq[b2][0][:], in0=csb[tb][:, tsl], in1=Zb[b2][0][:], op=ALU.mult), reads=rd, writes=[tB[b2]])
                    P.op("dve", lambda e: e.tensor_tensor(out=Bq[b2][1][:], in0=snb[tb][:, tsl], in1=Zb[b2][1][:], op=ALU.mult), reads=rd, writes=[tB[b2]])
                    P.op("dve", lambda e: e.tensor_tensor(out=Bq[b2][2][:], in0=snb[tb][:, tsl], in1=Zb[b2][0][:], op=ALU.mult), reads=rd, writes=[tB2[b2]])
                    P.op("dve", lambda e: e.tensor_tensor(out=Bq[b2][3][:], in0=csb[tb][:, tsl], in1=Zb[b2][1][:], op=ALU.mult), reads=rd, writes=[tB2[b2]])

                def p8(b2=b2, k=k, ub=ub, tsl=tsl):
                    for i4, ci in enumerate((0, 2, 1, 1)):
                        P.op("pe", lambda e, i4=i4, ci=ci: e.matmul(out=py[b2][:], lhsT=CT[:, k, ci, :], rhs=Bq[b2][i4][:], start=(i4 == 0), stop=False),
                             reads=[tCT, tB[b2], tB2[b2]], writes=[tpy[b2]])
                    P.op("pe", lambda e: e.matmul(out=py[b2][:], lhsT=Dg[:, k, :], rhs=uwin[ub][:, tsl], start=False, stop=True),
                         reads=[tDg, tuw[ub]], writes=[tpy[b2]])

                def p9(b2=b2, ub=ub, gb=gb, tsl=tsl, k=k, qi=qi, row0=row0, c0=c0):
                    P.op("act", lambda e: e.activation(out=gout[gb][:, tsl], in_=py[b2][0:32, :], func=AF.Gelu), reads=[tpy[b2]], writes=[tgo[gb]])
                    if qi == nq - 1:
                        to = P.tile()
                        P.dma("act", lambda e: e.dma_start(out=gT[row0:row0 + 32, c0:c0 + S], in_=gout[gb][:]), reads=[tgo[gb]], writes=[to])
                        outs.append(to)

                tasks.append([p0, p1, p2, p3, p4, p5, p6, p7, p8, p9])
    run_pipeline(tasks, [0, 1, 2, 3, 4, 5, 6, 7, 8, 9])
    if dbg is not None:
        for nm, ap_ in dbg.items():
            src = {"CT": CT, "BT": BT, "sn": snb[(npairs - 1) % 2], "cs": csb[(npairs - 1) % 2], "r": sc["r"], "qre": sc["qre"], "qim": sc["qim"], "fhi": sc["fhi"], "flo": sc["flo"]}[nm]
            tl = {"CT": tCT, "BT": tBT, "sn": ttab[(npairs - 1) % 2], "cs": ttab[(npairs - 1) % 2]}.get(nm, tsc)
            to = P.tile()
            P.dma("sp", lambda e, ap_=ap_, src=src: e.dma_start(out=ap_, in_=src[:]), reads=[tl], writes=[to])
            outs.append(to)
    P.finish(outs)


def stage_a3(nc, h_in, h_out, gT, wglu_ap, n_tok):
    P = Prog(nc)
    wg = P.sbuf("wg", [128, 8, 2048], BF16)
    twg = P.tile()
    load_weight_bf16(P, wg, twg, wglu_ap, 8)
    gb = [P.sbuf(f"gb{i}", [128, 8, 512], BF16) for i in range(2)]
    tgb = P.tiles(2)
    xt = [P.sbuf(f"xt{i}", [128, 4, D], F32) for i in range(2)]
    txt = P.tiles(2)
    pv = [P.psum(f"pv{i}", [128, 512]) for i in range(2)]
    tpv = P.tiles(2)
    pg = [P.psum(f"pg{i}", [128, 512]) for i in range(2)]
    tpg = P.tiles(2)
    sg = [P.sbuf(f"sg{i}", [128, 512], F32) for i in range(2)]
    tsg = P.tiles(2)
    outs = []
    k = 0
    for u in range(n_tok // 512):
        r0 = u * 512
        ob = u % 2
        P.dma("sp", lambda e, ob=ob, r0=r0: e.dma_start(out=gb[ob][:], in_=gT[:, r0:r0 + 512].rearrange("(c p) n -> p c n", p=128)), writes=[tgb[ob]])
        P.dma("sp", lambda e, ob=ob, r0=r0: e.dma_start(out=xt[ob][:], in_=h_in[r0:r0 + 512, :].rearrange("(t p) d -> p t d", p=128)), writes=[txt[ob]])
        for t in range(4):
            for nb in range(2):
                b = k % 2
                k += 1
                for c in range(8):
                    P.op("pe", lambda e, c=c, b=b, t=t, nb=nb, ob=ob: e.matmul(out=pv[b][:], lhsT=gb[ob][:, c, t * 128:(t + 1) * 128], rhs=wg[:, c, nb * 512:(nb + 1) * 512],
                                                                               start=(c == 0), stop=(c == 7)), reads=[tgb[ob], twg], writes=[tpv[b]])
                for c in range(8):
                    P.op("pe", lambda e, c=c, b=b, t=t, nb=nb, ob=ob: e.matmul(out=pg[b][:], lhsT=gb[ob][:, c, t * 128:(t + 1) * 128], rhs=wg[:, c, 1024 + nb * 512:1024 + (nb + 1) * 512],
                                                                               start=(c == 0), stop=(c == 7)), reads=[tgb[ob], twg], writes=[tpg[b]])
                P.op("act", lambda e, b=b: e.activation(out=sg[b][:], in_=pg[b][:], func=AF.Sigmoid), reads=[tpg[b]], writes=[tsg[b]])
                P.op("dve", lambda e, b=b: e.tensor_tensor(out=sg[b][:], in0=sg[b][:], in1=pv[b][:], op=ALU.mult), reads=[tsg[b], tpv[b]], writes=[tsg[b]])
                P.op("pool", lambda e, b=b, t=t, nb=nb, ob=ob: e.tensor_tensor(out=xt[ob][:, t, nb * 512:(nb + 1) * 512], in0=xt[ob][:, t, nb * 512:(nb + 1) * 512], in1=sg[b][:], op=ALU.add),
                     reads=[tsg[b], txt[ob]], writes=[txt[ob]])
        to = P.tile()
        P.dma("act", lambda e, ob=ob, r0=r0: e.dma_start(out=h_out[r0:r0 + 512, :].rearrange("(t p) d -> p t d", p=128), in_=xt[ob][:]), reads=[txt[ob]], writes=[to])
        outs.append(to)
    P.finish(outs)


N_CORES = 8
SEQ = 2048
N_SEQ = 4
NT = N_SEQ * SEQ

_PARAMS = [
    ("a_norm", [1, 1024]), ("a_w_in", [1, 1024, 1024]), ("a_lam_re", [1, 64, 64]), ("a_lam_im", [1, 64, 64]),
    ("a_b_re", [1, 64, 64, 16]), ("a_b_im", [1, 64, 64, 16]), ("a_c_re", [1, 64, 16, 64]), ("a_c_im", [1, 64, 16, 64]),
    ("a_d", [1, 1024]), ("a_log_dt", [1, 64]), ("a_w_glu", [1, 1024, 2048]), ("kv_norm", [1024]), ("w_kv", [1024, 2048]),
    ("k_norm", [64]), ("b_norm", [1, 1024]), ("b_w_q", [1, 1024, 1024]), ("b_q_norm", [1, 64]), ("b_w_o", [1, 1024, 1024]),
    ("ffn_norm", [2, 1024]), ("ffn_w_up", [2, 1024, 5632]), ("ffn_conv_w", [2, 3, 2816]), ("ffn_conv_b", [2, 2816]),
    ("ffn_w_down", [2, 2816, 1024]),
]


def build_program(N_SEQ=N_SEQ, SEQ=SEQ, debug=False):
    NT = N_SEQ * SEQ
    nc = bass.Bass("TRN2", target_bir_lowering=False)
    x = nc.dram_tensor("x", [NT, D], F32, kind="ExternalInput").ap()
    prm = {n: nc.dram_tensor(n, s, F32, kind="ExternalInput").ap() for n, s in _PARAMS}
    ident = nc.dram_tensor("c_ident", [128, 128], F32, kind="ExternalInput").ap()
    bones = nc.dram_tensor("c_bones", [128, 128], F32, kind="ExternalInput").ap()
    maskb = nc.dram_tensor("c_maskb", [128, 128], F32, kind="ExternalInput").ap()
    iota = nc.dram_tensor("c_iota", [128, 2048], F32, kind="ExternalInput").ap()
    out = nc.dram_tensor("out", [NT, D], F32, kind="ExternalOutput").ap()
    kd = "ExternalOutput" if debug else "Internal"
    h1 = nc.dram_tensor("h1", [NT, D], F32, kind=kd).ap()
    h2 = nc.dram_tensor("h2", [NT, D], F32, kind=kd).ap()
    h3 = nc.dram_tensor("h3", [NT, D], F32, kind=kd).ap()
    uT = nc.dram_tensor("uT", [D, NT], BF16).ap()
    gT = nc.dram_tensor("gT", [D, NT], BF16).ap()
    kT = nc.dram_tensor("kT", [D, NT], BF16).ap()
    qT = nc.dram_tensor("qT", [D, NT], BF16).ap()
    vr = nc.dram_tensor("vr", [NT, D], BF16).ap()
    a2args = (prm["a_lam_re"][0], prm["a_lam_im"][0], prm["a_b_re"][0], prm["a_b_im"][0], prm["a_c_re"][0], prm["a_c_im"][0],
              prm["a_d"][0], prm["a_log_dt"][0], iota)
    keep = contextlib.ExitStack()
    box = {}

    def prep(P):
        box["ctx"] = a2_prep(P, lambda n, sh, dt_: keep.enter_context(nc.sbuf_tensor("keep_" + n, list(sh), dt_)), *a2args, SEQ, ident_ap=ident)

    stage_a1(nc, x, uT, prm["a_norm"][0], prm["a_w_in"][0], ident, NT, prep=prep)
    stage_a2(nc, uT, gT, *a2args, N_SEQ, SEQ, ctx=box["ctx"], ident_ap=ident)
    keep.close()
    stage_a3(nc, x, h1, gT, prm["a_w_glu"][0], NT)
    stage_ffn(nc, h1, h2, prm["ffn_norm"][0], prm["ffn_w_up"][0], prm["ffn_conv_w"][0], prm["ffn_conv_b"][0],
              prm["ffn_w_down"][0], ident, N_SEQ, SEQ)
    stage_kvq(nc, h2, kT, qT, vr, prm["kv_norm"], prm["w_kv"], prm["k_norm"], prm["b_norm"][0], prm["b_w_q"][0], prm["b_q_norm"][0],
              ident, bones, N_SEQ, SEQ)
    stage_att(nc, h2, h3, qT, kT, vr, prm["b_w_o"][0], ident, maskb, N_SEQ, SEQ)
    stage_ffn(nc, h3, out, prm["ffn_norm"][1], prm["ffn_w_up"][1], prm["ffn_conv_w"][1], prm["ffn_conv_b"][1],
              prm["ffn_w_down"][1], ident, N_SEQ, SEQ)
    return nc


def kernel(**inputs):
    x = np.ascontiguousarray(np.asarray(inputs["x"], dtype=np.float32))
    nc = build_program()
    p = np.arange(128)
    consts = {
        "c_ident": np.eye(128, dtype=np.float32),
        "c_bones": (p[:, None] // 64 == p[None, :] // 64).astype(np.float32),
        "c_maskb": np.where(p[None, :] + p[:, None] >= 128, 0.0, -30000.0).astype(np.float32),
        "c_iota": np.ascontiguousarray(np.tile(np.arange(2048, dtype=np.float32), (128, 1))),
    }
    params = {n: np.ascontiguousarray(np.asarray(inputs[n], dtype=np.float32)).reshape(s) for n, s in _PARAMS}
    in_maps = []
    for c in range(N_CORES):
        m = {"x": x[c * N_SEQ:(c + 1) * N_SEQ].reshape(NT, D)}
        m.update(params)
        m.update(consts)
        in_maps.append(m)
    res = run_bass_kernel_spmd(nc, in_maps, core_ids=list(range(N_CORES)))
    outs = [np.asarray(r["out"], dtype=np.float32).reshape(N_SEQ, SEQ, D) for r in res.results]
    return np.concatenate(outs, axis=0)
```
